# Optimizing a Trainium2 kernel written in Bass

```python
import math
import jax
import jax.numpy as jnp
from jax import lax
import numpy as np

D_MODEL = 1024
BATCH = 16
SEQ = 2048
DEPTH = 2

GRID_W = 64
CTX_LEN = 256
N_MIXERS = 4
GROUP_W = D_MODEL // N_MIXERS
D_MIX = N_MIXERS * GROUP_W
NORM_EPS = 1e-6
ROPE_THETA = 10000.0
Q_BLOCK = 128

RW_HD = 64
RW_HEADS = GROUP_W // RW_HD
D_DECAY_LORA = 64
D_AAA_LORA = 64
D_GATE_LORA = 128
RW_LN_EPS = 64e-5
RW_SPLITS = (GROUP_W, GROUP_W, GROUP_W, D_DECAY_LORA, D_DECAY_LORA, D_AAA_LORA, D_AAA_LORA, D_GATE_LORA)
RW_IN = 3 * GROUP_W + 2 * D_DECAY_LORA + 2 * D_AAA_LORA + D_GATE_LORA

DA_HD = 32
DA_VD = 2 * DA_HD
DA_HEADS = GROUP_W // DA_VD
DA_IN = 3 * GROUP_W

GLA_DV = 64
GLA_HEADS = GROUP_W // GLA_DV
GLA_DK = GLA_DV // 2
GLA_KW = GLA_HEADS * GLA_DK
GLA_GATE_RANK = 16
GLA_TAU = 16.0
GLA_CHUNK = 64
GLA_SPLITS = (GLA_KW, GLA_KW, GROUP_W, GLA_GATE_RANK, GLA_GATE_RANK, GROUP_W)
GLA_IN = 2 * GLA_KW + 2 * GROUP_W + 2 * GLA_GATE_RANK

GQA_HD = 64
GQA_HEADS = GROUP_W // GQA_HD
GQA_KV_HEADS = 2
GQA_GROUP = GQA_HEADS // GQA_KV_HEADS
GQA_KVW = GQA_KV_HEADS * GQA_HD
GQA_SPLITS = (GROUP_W, GQA_KVW, GQA_KVW)
GQA_IN = GROUP_W + 2 * GQA_KVW

MIXER_IN = (RW_IN, DA_IN, GLA_IN, GQA_IN)
N_IN = RW_IN + DA_IN + GLA_IN + GQA_IN
D_FF = 2816

kernel_name = 'hybrid_parallel_heads_diffusion_block'


def _cumsplit(x, widths):
    return jnp.split(x, np.cumsum(widths)[:-1].tolist(), axis=-1)


def rms_norm(x, g, eps=NORM_EPS):
    xf = x.astype(jnp.float32)
    y = xf * lax.rsqrt(jnp.mean(xf * xf, axis=-1, keepdims=True) + eps)
    return (y * g.astype(jnp.float32)).astype(x.dtype)


def modulate(x, g, shift, scale):
    return rms_norm(x, g) * (1 + scale) + shift


def shift_prev(u):
    return jnp.pad(u[:, :-1], ((0, 0), (1, 0), (0, 0)))


def shift_next(u):
    return jnp.pad(u[:, 1:], ((0, 0), (0, 1), (0, 0)))


def axial_rope_tables(rows, head_dim):
    row = jnp.repeat(jnp.arange(rows, dtype=jnp.float32), GRID_W)
    col = jnp.tile(jnp.arange(GRID_W, dtype=jnp.float32), rows)
    n_freq = head_dim // 4
    inv = ROPE_THETA ** (-jnp.arange(n_freq, dtype=jnp.float32) / n_freq)
    ang = jnp.concatenate([row[:, None] * inv, col[:, None] * inv], axis=-1)
    return jnp.cos(ang), jnp.sin(ang)


def apply_rope(x, cos, sin):
    half = x.shape[-1] // 2
    shape = (1, x.shape[1]) + (1,) * (x.ndim - 3) + (half,)
    cos = cos.reshape(shape)
    sin = sin.reshape(shape)
    xf = x.astype(jnp.float32)
    x1, x2 = xf[..., :half], xf[..., half:]
    return jnp.concatenate([x1 * cos - x2 * sin, x2 * cos + x1 * sin], axis=-1).astype(x.dtype)


def sweep_query_blocks(fn, q):
    B, T = q.shape[:2]
    nb = T // Q_BLOCK
    qb = q.reshape((B, nb, Q_BLOCK) + q.shape[2:]).swapaxes(0, 1)
    ob = lax.map(fn, qb)
    return ob.swapaxes(0, 1).reshape((B, T) + ob.shape[3:])


def rwkv7_prepare(pa, mu, w0, w2, a0, a2, g2, k_k, k_a):
    B, T, _ = pa.shape
    pa = pa + mu[0] * (shift_prev(pa) - pa) + mu[1] * (shift_next(pa) - pa)
    r, k, v, wf, wb, af, ab, g = _cumsplit(pa, RW_SPLITS)
    hv = lambda t: t.reshape(B, T, RW_HEADS, RW_HD).astype(jnp.float32)
    out_gate = jax.nn.sigmoid(g) @ g2
    kk = hv(k * k_k)
    kk = kk * lax.rsqrt(jnp.sum(kk * kk, axis=-1, keepdims=True) + 1e-12)
    per_dir = []
    for d, (wd, ad) in enumerate(((wf, af), (wb, ab))):
        w_log = -jax.nn.softplus(-(w0[d] + jnp.tanh(wd) @ w2[d])) - 0.5
        decay = jnp.exp(-jnp.exp(w_log.astype(jnp.float32)))
        a = jax.nn.sigmoid(a0[d] + ad @ a2[d])
        kd = k * (1 + (a - 1) * k_a)
        per_dir.append((hv(decay), hv(kd), hv(a)))
    return hv(r), hv(v), kk, per_dir, out_gate


def rwkv7_scan(S0, r, decay, k, v, kk, a, reverse, with_outputs):
    seq = (decay, k, v, -kk, kk * a) + ((r,) if with_outputs else ())
    xs = tuple(jnp.moveaxis(t, 1, 0) for t in seq)

    def step(S, inp):
        w_t, k_t, v_t, a_t, b_t = inp[:5]
        sa = jnp.einsum('bhvk,bhk->bhv', S, a_t)
        S = S * w_t[:, :, None, :] + sa[..., None] * b_t[:, :, None, :] + v_t[..., None] * k_t[:, :, None, :]
        y = jnp.einsum('bhvk,bhk->bhv', S, inp[5]) if with_outputs else None
        return S, y

    S, ys = lax.scan(step, S0, xs, reverse=reverse)
    return S, (jnp.moveaxis(ys, 0, 1) if with_outputs else None)


def rwkv7_output(y, r, v, dirs, r_k, ln_g, ln_b, gate):
    B, T = y.shape[:2]
    mean = jnp.mean(y, axis=-1, keepdims=True)
    var = jnp.mean(jnp.square(y - mean), axis=-1, keepdims=True)
    yn = ((y - mean) * lax.rsqrt(var + RW_LN_EPS)).reshape(B, T, GROUP_W) * ln_g + ln_b
    bonus = sum(jnp.sum(r * kd * r_k, axis=-1, keepdims=True) * v for (_, kd, _) in dirs)
    return (yn + bonus.reshape(B, T, GROUP_W)) * gate


def rwkv7_mixer(pa_l, pa_c, mu, w0, w2, a0, a2, g2, k_k, k_a, r_k, ln_g, ln_b, need_ctx):
    r_l, v_l, kk_l, dirs_l, gate_l = rwkv7_prepare(pa_l, mu, w0, w2, a0, a2, g2, k_k, k_a)
    r_c, v_c, kk_c, dirs_c, gate_c = rwkv7_prepare(pa_c, mu, w0, w2, a0, a2, g2, k_k, k_a)
    B = pa_l.shape[0]
    y_l = 0.0
    y_c = 0.0
    for d in range(2):
        rev = d == 1
        dec_c, kd_c, a_c = dirs_c[d]
        S0 = jnp.zeros((B, RW_HEADS, RW_HD, RW_HD), jnp.float32)
        S_c, yc = rwkv7_scan(S0, r_c, dec_c, kd_c, v_c, kk_c, a_c, rev, need_ctx)
        dec_l, kd_l, a_l = dirs_l[d]
        _, yl = rwkv7_scan(S_c, r_l, dec_l, kd_l, v_l, kk_l, a_l, rev, True)
        y_l = y_l + yl
        if need_ctx:
            y_c = y_c + yc
    o_l = rwkv7_output(y_l, r_l, v_l, dirs_l, r_k, ln_g, ln_b, gate_l)
    o_c = rwkv7_output(y_c, r_c, v_c, dirs_c, r_k, ln_g, ln_b, gate_c) if need_ctx else None
    return o_l, o_c


def diff_attention_core(q, k, v, lam):
    s = jnp.einsum('bqhmd,bkhmd->bhmqk', q, k).astype(jnp.float32) * (DA_HD ** -0.5)
    p = jax.nn.softmax(s, axis=-1)
    att = p[:, :, 0] - lam * p[:, :, 1]
    return jnp.einsum('bhqk,bkhe->bqhe', att, v.astype(jnp.float32))


def diff_attention_mixer(pb_l, pb_c, qk_g, lam_vecs, subln_g, layer_idx, cos, sin, need_ctx):
    lam_init = 0.8 - 0.6 * math.exp(-0.3 * layer_idx)
    lv = lam_vecs.astype(jnp.float32)
    lam = jnp.exp(jnp.sum(lv[0] * lv[1])) - jnp.exp(jnp.sum(lv[2] * lv[3])) + lam_init

    def heads(p):
        B, T, _ = p.shape
        q, k, v = jnp.split(p, 3, axis=-1)
        q = rms_norm(q.reshape(B, T, DA_HEADS, 2, DA_HD), qk_g[0])
        k = rms_norm(k.reshape(B, T, DA_HEADS, 2, DA_HD), qk_g[1])
        return q, k, v.reshape(B, T, DA_HEADS, DA_VD)

    def finish(o):
        B, T = o.shape[:2]
        return (rms_norm(o, subln_g) * (1 - lam_init)).reshape(B, T, GROUP_W)

    q_l, k_l, v_l = heads(pb_l)
    q_c, k_c, v_c = heads(pb_c)
    q_l = apply_rope(q_l, cos, sin)
    k_l = apply_rope(k_l, cos, sin)
    k_all = jnp.concatenate([k_l, k_c], axis=1)
    v_all = jnp.concatenate([v_l, v_c], axis=1)
    o_l = finish(sweep_query_blocks(lambda qb: diff_attention_core(qb, k_all, v_all, lam), q_l))
    o_c = finish(diff_attention_core(q_c, k_c, v_c, lam)) if need_ctx else None
    return o_l, o_c


def gla_prepare(pc, a2, ab):
    B, T, _ = pc.shape
    q, k, v, gf, gb, r = _cumsplit(pc, GLA_SPLITS)
    hk = lambda t: t.reshape(B, T, GLA_HEADS, GLA_DK).astype(jnp.float32)
    q = hk(q) * (GLA_DK ** -0.5)
    k = hk(k)
    v = v.reshape(B, T, GLA_HEADS, GLA_DV).astype(jnp.float32)
    log_gates = [hk(jax.nn.log_sigmoid((g @ a2[d] + ab[d]).astype(jnp.float32)) / GLA_TAU)
                 for d, g in enumerate((gf, gb))]
    return q, k, v, log_gates, r


def gla_chunked(S0, k, v, lg, q=None):
    B, T, H, _ = k.shape
    nc = T // GLA_CHUNK
    chunks = lambda t: t.reshape(B, nc, GLA_CHUNK, H, t.shape[-1]).transpose(1, 0, 3, 2, 4)
    k, v, lg = chunks(k), chunks(v), chunks(lg)
    b = jnp.cumsum(lg, axis=3)
    b_end = b[:, :, :, -1:, :]
    k_end = k * jnp.exp(b_end - b)
    dec = jnp.exp(b_end[:, :, :, 0, :])
    if q is None:
        def step_state(S, inp):
            ke, vv, dd = inp
            return S * dd[..., None] + jnp.einsum('bhld,bhle->bhde', ke, vv), None
        S, _ = lax.scan(step_state, S0, (k_end, v, dec))
        return S, None
    q = chunks(q)
    q_in = q * jnp.exp(b)
    k_in = k * jnp.exp(-b)
    mask = jnp.tril(jnp.ones((GLA_CHUNK, GLA_CHUNK), dtype=bool))
    att = jnp.where(mask, jnp.einsum('cbhid,cbhjd->cbhij', q_in, k_in), 0.0)
    o_intra = jnp.einsum('cbhij,cbhje->cbhie', att, v)

    def step(S, inp):
        qi, ke, vv, dd = inp
        o = jnp.einsum('bhld,bhde->bhle', qi, S)
        return S * dd[..., None] + jnp.einsum('bhld,bhle->bhde', ke, vv), o

    S, o_inter = lax.scan(step, S0, (q_in, k_end, v, dec))
    o = (o_intra + o_inter).transpose(1, 0, 3, 2, 4).reshape(B, T, H, -1)
    return S, o


def gla_direction(S0, q, k, v, lg, reverse, with_outputs):
    flip = (lambda t: jnp.flip(t, axis=1)) if reverse else (lambda t: t)
    S, o = gla_chunked(S0, flip(k), flip(v), flip(lg), flip(q) if with_outputs else None)
    return S, (flip(o) if with_outputs else None)


def gla_mixer(pc_l, pc_c, a2, ab, norm_g, need_ctx):
    q_l, k_l, v_l, lg_l, r_l = gla_prepare(pc_l, a2, ab)
    q_c, k_c, v_c, lg_c, r_c = gla_prepare(pc_c, a2, ab)
    B = pc_l.shape[0]
    o_l = 0.0
    o_c = 0.0
    for d in range(2):
        S0 = jnp.zeros((B, GLA_HEADS, GLA_DK, GLA_DV), jnp.float32)
        S_c, oc = gla_direction(S0, q_c, k_c, v_c, lg_c[d], d == 1, need_ctx)
        _, ol = gla_direction(S_c, q_l, k_l, v_l, lg_l[d], d == 1, True)
        o_l = o_l + ol
        if need_ctx:
            o_c = o_c + oc

    def finish(o, r):
        B_, T = o.shape[:2]
        return rms_norm(o, norm_g).reshape(B_, T, GROUP_W) * jax.nn.silu(r)

    return finish(o_l, r_l), (finish(o_c, r_c) if need_ctx else None)


def gqa_core(q, k, v):
    s = jnp.einsum('bqhgd,bkhd->bhgqk', q, k).astype(jnp.float32) * (GQA_HD ** -0.5)
    p = jax.nn.softmax(s, axis=-1)
    return jnp.einsum('bhgqk,bkhd->bqhgd', p, v.astype(jnp.float32))


def gqa_mixer(pd_l, pd_c, qk_g, cos, sin, need_ctx):
    def heads(p):
        B, T, _ = p.shape
        q, k, v = _cumsplit(p, GQA_SPLITS)
        q = rms_norm(q.reshape(B, T, GQA_KV_HEADS, GQA_GROUP, GQA_HD), qk_g[0])
        k = rms_norm(k.reshape(B, T, GQA_KV_HEADS, GQA_HD), qk_g[1])
        return q, k, v.reshape(B, T, GQA_KV_HEADS, GQA_HD)

    q_l, k_l, v_l = heads(pd_l)
    q_c, k_c, v_c = heads(pd_c)
    q_l = apply_rope(q_l, cos, sin)
    k_l = apply_rope(k_l, cos, sin)
    k_all = jnp.concatenate([k_l, k_c], axis=1)
    v_all = jnp.concatenate([v_l, v_c], axis=1)
    B, T = pd_l.shape[:2]
    o_l = sweep_query_blocks(lambda qb: gqa_core(qb, k_all, v_all), q_l).reshape(B, T, GROUP_W)
    o_c = gqa_core(q_c, k_c, v_c).reshape(B, pd_c.shape[1], GROUP_W) if need_ctx else None
    return o_l, o_c


def conv_ffn(h, w_up, conv_w, conv_b, w_down):
    u, g = jnp.split(h @ w_up, 2, axis=-1)
    g = conv_w[0] * shift_prev(g) + conv_w[1] * g + conv_w[2] * shift_next(g) + conv_b
    return (jax.nn.silu(g) * u) @ w_down


def setup_inputs(seed: int = 0) -> dict:
    key = jax.random.key(seed)
    ks = iter(jax.random.split(key, 40))
    nrm = lambda shape, s: jax.random.normal(next(ks), shape, jnp.float32) * s
    L = DEPTH
    D = D_MODEL
    return {
        'x': nrm((BATCH, SEQ, D), 1.0),
        'c': nrm((BATCH, D), 1.0),
        'ctx': nrm((BATCH, CTX_LEN, D), 1.0),
        'c_ctx': nrm((D,), 1.0),
        'mod_w': nrm((L, D, 6 * D), 0.3 * D ** -0.5),
        'mod_b': nrm((L, 6 * D), 0.02),
        'norm_mix_g': 1.0 + nrm((L, D), 0.05),
        'norm_ffn_g': 1.0 + nrm((L, D), 0.05),
        'w_in': nrm((L, D, N_IN), D ** -0.5),
        'w_out': nrm((L, D_MIX, D), D_MIX ** -0.5),
        'rw_mu': jax.random.uniform(next(ks), (L, 2, RW_IN), jnp.float32, 0.0, 0.5),
        'rw_w0': nrm((L, 2, GROUP_W), 1.0) - 2.0,
        'rw_w2': nrm((L, 2, D_DECAY_LORA, GROUP_W), 0.5 * D_DECAY_LORA ** -0.5),
        'rw_a0': nrm((L, 2, GROUP_W), 0.5),
        'rw_a2': nrm((L, 2, D_AAA_LORA, GROUP_W), 0.5 * D_AAA_LORA ** -0.5),
        'rw_g2': nrm((L, D_GATE_LORA, GROUP_W), D_GATE_LORA ** -0.5),
        'rw_kk': 0.85 + nrm((L, GROUP_W), 0.05),
        'rw_ka': 1.0 + nrm((L, GROUP_W), 0.05),
        'rw_rk': nrm((L, RW_HEADS, RW_HD), 0.1),
        'rw_ln_g': 1.0 + nrm((L, GROUP_W), 0.05),
        'rw_ln_b': nrm((L, GROUP_W), 0.02),
        'da_qk_g': 1.0 + nrm((L, 2, DA_HD), 0.05),
        'da_lam': nrm((L, 4, DA_HD), 0.1),
        'da_subln_g': 1.0 + nrm((L, DA_VD), 0.05),
        'gla_a2': nrm((L, 2, GLA_GATE_RANK, GLA_KW), GLA_GATE_RANK ** -0.5),
        'gla_ab': nrm((L, 2, GLA_KW), 0.5),
        'gla_norm_g': 1.0 + nrm((L, GLA_DV), 0.05),
        'gqa_qk_g': 1.0 + nrm((L, 2, GQA_HD), 0.05),
        'ffn_w_up': nrm((L, D, 2 * D_FF), D ** -0.5),
        'ffn_conv_w': nrm((L, 3, D_FF), 0.6),
        'ffn_conv_b': nrm((L, D_FF), 0.02),
        'ffn_w_down': nrm((L, D_FF, D), D_FF ** -0.5),
    }


def reference(x, c, ctx, c_ctx, mod_w, mod_b, norm_mix_g, norm_ffn_g, w_in, w_out,
              rw_mu, rw_w0, rw_w2, rw_a0, rw_a2, rw_g2, rw_kk, rw_ka, rw_rk, rw_ln_g, rw_ln_b,
              da_qk_g, da_lam, da_subln_g, gla_a2, gla_ab, gla_norm_g, gqa_qk_g,
              ffn_w_up, ffn_conv_w, ffn_conv_b, ffn_w_down):
    n_lat = x.shape[1]
    rows = n_lat // GRID_W
    cos_da, sin_da = axial_rope_tables(rows, DA_HD)
    cos_gq, sin_gq = axial_rope_tables(rows, GQA_HD)
    xc = ctx
    for i in range(DEPTH):
        need_ctx = i < DEPTH - 1
        mod_l = [m[:, None, :] for m in jnp.split(jax.nn.silu(c) @ mod_w[i] + mod_b[i], 6, axis=-1)]
        mod_c = jnp.split(jax.nn.silu(c_ctx) @ mod_w[i] + mod_b[i], 6, axis=-1)
        h_l = modulate(x, norm_mix_g[i], mod_l[0], mod_l[1])
        h_c = modulate(xc, norm_mix_g[i], mod_c[0], mod_c[1])
        pa_l, pb_l, pc_l, pd_l = _cumsplit(h_l @ w_in[i], MIXER_IN)
        pa_c, pb_c, pc_c, pd_c = _cumsplit(h_c @ w_in[i], MIXER_IN)
        oa_l, oa_c = rwkv7_mixer(pa_l, pa_c, rw_mu[i], rw_w0[i], rw_w2[i], rw_a0[i], rw_a2[i], rw_g2[i],
                                 rw_kk[i], rw_ka[i], rw_rk[i], rw_ln_g[i], rw_ln_b[i], need_ctx)
        ob_l, ob_c = diff_attention_mixer(pb_l, pb_c, da_qk_g[i], da_lam[i], da_subln_g[i], i,
                                          cos_da, sin_da, need_ctx)
        oc_l, oc_c = gla_mixer(pc_l, pc_c, gla_a2[i], gla_ab[i], gla_norm_g[i], need_ctx)
        od_l, od_c = gqa_mixer(pd_l, pd_c, gqa_qk_g[i], cos_gq, sin_gq, need_ctx)
        o_l = jnp.concatenate([oa_l, ob_l, oc_l, od_l], axis=-1).astype(x.dtype)
        x = x + mod_l[2] * (o_l @ w_out[i])
        if need_ctx:
            o_c = jnp.concatenate([oa_c, ob_c, oc_c, od_c], axis=-1).astype(xc.dtype)
            xc = xc + mod_c[2] * (o_c @ w_out[i])
        x = x + mod_l[5] * conv_ffn(modulate(x, norm_ffn_g[i], mod_l[3], mod_l[4]),
                                    ffn_w_up[i], ffn_conv_w[i], ffn_conv_b[i], ffn_w_down[i])
        if need_ctx:
            xc = xc + mod_c[5] * conv_ffn(modulate(xc, norm_ffn_g[i], mod_c[3], mod_c[4]),
                                          ffn_w_up[i], ffn_conv_w[i], ffn_conv_b[i], ffn_w_down[i])
    return x
```

```python
import numpy as np
import concourse.bass as bass
import concourse.mybir as mybir
from concourse.bass_utils import run_bass_kernel_spmd

F32 = mybir.dt.float32
BF16 = mybir.dt.bfloat16
AF = mybir.ActivationFunctionType
ALU = mybir.AluOpType
AX = mybir.AxisListType

ENGS = ("tensor", "vector", "scalar", "gpsimd", "sync")


class Trk:
    __slots__ = ("name", "w", "r")

    def __init__(self, name):
        self.name = name
        self.w = None
        self.r = {}


class V:
    __slots__ = ("ap", "trk")

    def __init__(self, ap, trk):
        self.ap = ap
        self.trk = trk


class Buf:
    def __init__(self, t, name):
        self.t = t
        self.name = name
        self.trk = Trk(name)

    def __getitem__(self, idx):
        return V(self.t[idx], self.trk)

    def v(self, ap):
        return V(ap, self.trk)


def _trks(v):
    return v.trk if isinstance(v.trk, (list, tuple)) else (v.trk,)


class FW:
    def __init__(self, nc, n_dma_sems=32):
        self.nc = nc
        self.prog = {e: [] for e in ENGS}
        self.sem = {e: nc.alloc_semaphore(name=f"s_{e}") for e in ENGS}
        self.cnt = {e: 0 for e in ENGS}
        self.waited = {e: {} for e in ENGS}
        self.dsem = [nc.alloc_semaphore(name=f"d_{i}") for i in range(n_dma_sems)]
        self.dcnt = [0] * n_dma_sems
        self.dnext = 0
        self.gnext = 0
        self.semobj = {}
        for e in ENGS:
            self.semobj[("e", e)] = self.sem[e]
        for i, s in enumerate(self.dsem):
            self.semobj[("d", i)] = s
        self.ninst = 0
        self.stack = []

    def push(self):
        self.stack.append([])

    def pop(self):
        self.barrier()
        for g in reversed(self.stack.pop()):
            g.__exit__(None, None, None)

    def sbuf(self, name, shape, dtype=F32):
        self.uid = getattr(self, "uid", 0) + 1
        name = f"{name}_u{self.uid}"
        g = self.nc.sbuf_tensor(name, list(shape), dtype)
        t = g.__enter__()
        self.stack[-1].append(g)
        return Buf(t, name)

    def psum(self, name, shape, dtype=F32):
        self.uid = getattr(self, "uid", 0) + 1
        name = f"{name}_u{self.uid}"
        g = self.nc.psum_tensor(name, list(shape), dtype)
        t = g.__enter__()
        self.stack[-1].append(g)
        return Buf(t, name)

    def dram(self, name, shape, dtype=F32, kind="Internal"):
        return Buf(self.nc.dram_tensor(name, list(shape), dtype, kind=kind).ap(), name)

    def _wait(self, eng, ev):
        if ev is None:
            return
        key, val = ev
        if eng == "tensor" and key == ("e", "tensor"):
            return
        if self.waited[eng].get(key, 0) >= val:
            return
        self.waited[eng][key] = val
        self.prog[eng].append(("wait", key, val))

    def _deps(self, eng, reads, writes):
        for v in reads:
            for t in _trks(v):
                self._wait(eng, t.w)
        for v in writes:
            for t in _trks(v):
                self._wait(eng, t.w)
                for kv in list(t.r.items()):
                    self._wait(eng, kv)

    def _mark(self, ev, reads, writes):
        for v in reads:
            for t in _trks(v):
                if t.r.get(ev[0], 0) < ev[1]:
                    t.r[ev[0]] = ev[1]
        for v in writes:
            for t in _trks(v):
                t.w = ev
                t.r = {}

    def op(self, eng, meth, reads, writes, *args, **kw):
        self._deps(eng, reads, writes)
        self.cnt[eng] += 1
        ev = (("e", eng), self.cnt[eng])
        sem = self.sem[eng]
        a2 = [a.ap if isinstance(a, V) else a for a in args]
        k2 = {k: (a.ap if isinstance(a, V) else a) for k, a in kw.items()}

        def emit(e, inc, wait=None, meth=meth, a2=a2, k2=k2, sem=sem):
            ins = getattr(e, meth)(*a2, **k2)
            if wait is not None:
                ins._wait_ge(wait[0], wait[1])
            if inc:
                ins.then_inc(sem, 1)
        self.prog[eng].append(("op", emit, self.cnt[eng]))
        self._mark(ev, reads, writes)
        self.ninst += 1
        return ev

    def dma(self, out, in_, eng="sync", **kw):
        self._deps(eng, [in_], [out])
        nd = len(self.dsem)
        if eng == "gpsimd":
            k = nd - 8 + self.gnext
            self.gnext = (self.gnext + 1) % 8
        else:
            k = self.dnext
            self.dnext = (self.dnext + 1) % (nd - 8)
        if self.dcnt[k] > 0:
            self._wait(eng, (("d", k), self.dcnt[k]))
        self.dcnt[k] += 16
        ev = (("d", k), self.dcnt[k])
        sem = self.dsem[k]
        oa, ia = out.ap, in_.ap

        def emit(e, oa=oa, ia=ia, sem=sem, kw=kw):
            e.dma_start(out=oa, in_=ia, **kw).then_inc(sem, 16)
        self.prog[eng].append(("dma", emit))
        self._mark(ev, [in_], [out])
        self.ninst += 1
        return ev

    def _all_events(self):
        evs = [(("e", e), self.cnt[e]) for e in ENGS if self.cnt[e] > 0]
        evs += [(("d", i), c) for i, c in enumerate(self.dcnt) if c > 0]
        return evs

    def barrier(self):
        evs = self._all_events()
        for e in ENGS:
            for ev in evs:
                self._wait(e, ev)

    def finish(self):
        for ev in self._all_events():
            self._wait("sync", ev)
        import bisect
        needed = {e: set() for e in ENGS}
        for ename in ENGS:
            for it in self.prog[ename]:
                if it[0] == "wait" and it[1][0] == "e":
                    needed[it[1][1]].add(it[2])
        ranks = {e: sorted(needed[e]) for e in ENGS}
        self.max_sem = {e: len(ranks[e]) for e in ENGS}
        with self.nc.Block() as block:
            for ename in ENGS:
                lst = self.prog[ename]

                def body(e, lst=lst, ename=ename):
                    pending = []
                    for it in lst:
                        if it[0] == "wait":
                            key, val = it[1], it[2]
                            if key[0] == "e":
                                val = bisect.bisect_left(ranks[key[1]], val) + 1
                            pending.append((self.semobj[key], val))
                        elif it[0] == "op":
                            for (sm, vl) in pending[:-1]:
                                e.wait_ge(sm, vl)
                            it[1](e, it[2] in needed[ename], pending[-1] if pending else None)
                            pending = []
                        else:
                            for (sm, vl) in pending:
                                e.wait_ge(sm, vl)
                            pending = []
                            it[1](e)
                    for (sm, vl) in pending:
                        e.wait_ge(sm, vl)
                getattr(block, ename)(body)

    def mm(self, out, lhsT, rhs, start=True, stop=True):
        return self.op("tensor", "matmul", [lhsT, rhs], [out], out, lhsT, rhs, start=start, stop=stop)

    def transpose(self, out, in_, ident):
        return self.op("tensor", "transpose", [in_, ident], [out], out, in_, ident)

    def act(self, out, in_, func, bias=None, scale=None, accum_out=None):
        reads = [in_]
        kw = {}
        if bias is not None:
            kw["bias"] = bias
            if isinstance(bias, V):
                reads.append(bias)
        if scale is not None:
            kw["scale"] = scale
            if isinstance(scale, V):
                reads.append(scale)
        writes = [out]
        if accum_out is not None:
            kw["accum_out"] = accum_out
            writes.append(accum_out)
        return self.op("scalar", "activation", reads, writes, out, in_, func, **kw)

    def tt(self, out, in0, in1, op, eng="vector"):
        return self.op(eng, "tensor_tensor", [in0, in1], [out], out, in0, in1, op)

    def ts(self, out, in0, s1, op0, s2=None, op1=None, eng="vector"):
        reads = [in0] + [s for s in (s1, s2) if isinstance(s, V)]
        if op1 is None:
            return self.op(eng, "tensor_scalar", reads, [out], out, in0, s1, None, op0)
        return self.op(eng, "tensor_scalar", reads, [out], out, in0, s1, s2, op0, op1)

    def stt(self, out, in0, scalar, in1, op0, op1, eng="vector"):
        eng = "vector"
        reads = [in0, in1] + ([scalar] if isinstance(scalar, V) else [])
        return self.op(eng, "scalar_tensor_tensor", reads, [out], out, in0, scalar, in1, op0, op1)

    def copy(self, out, in_, eng="vector"):
        if eng == "scalar":
            return self.op("scalar", "copy", [in_], [out], out, in_)
        return self.op(eng, "tensor_copy", [in_], [out], out, in_)

    def memset(self, out, val, eng="vector"):
        return self.op(eng, "memset", [], [out], out, val)

    def recip(self, out, in_):
        return self.op("vector", "reciprocal", [in_], [out], out, in_)


D = 1024
KT = 8
N_IN = 3232
D_FF = 2816
FT = 22
GRID_W = 64
EPS = 1e-6
RW_OFF, DA_OFF, GLA_OFF, GQA_OFF = 0, 1152, 1920, 2720
DEPTH = 2


class Cfg:
    def __init__(self, TC=256, TL=2048, NB=2, depth=DEPTH, stop=None, mix=("rwkv", "da", "gla", "gqa")):
        self.TC, self.TL, self.NB, self.depth = TC, TL, NB, depth
        self.T = TC + TL
        self.stop = stop
        self.mix = mix

    def blocks(self):
        out = []
        s = 0
        while s < self.TC:
            n = min(512, self.TC - s)
            out.append((s, n, True))
            s += n
        while s < self.T:
            n = min(512, self.T - s)
            out.append((s, n, False))
            s += n
        return out


def pack_layout():
    cols = {}
    n = 0

    def add(name, k):
        nonlocal n
        cols[name] = (n, k)
        n += k
    add("nmg", 8)
    add("nfg", 8)
    add("mod_b", 48)
    add("rw_mu0", 9)
    add("rw_mu1", 9)
    add("rw_c0", 9)
    add("rw_omka", 2)
    add("rw_w0", 4)
    add("rw_a0", 4)
    add("rw_kk", 2)
    add("rw_ka", 2)
    add("rw_rk", 2)
    add("rw_ln_g", 2)
    add("rw_ln_b", 2)
    add("da_qg", 1)
    add("da_kg", 1)
    add("da_sub", 1)
    add("gla_ab", 2)
    add("gla_ng", 1)
    add("gq_qg", 1)
    add("gq_kg", 1)
    add("conv_w", 66)
    add("conv_b", 22)
    return cols, n


PACK, NPACK = pack_layout()


def host_pack(inp, l):
    P = np.zeros((128, NPACK), np.float32)

    def put(name, vec, k):
        c0, kk = PACK[name]
        assert kk == k
        P[:, c0:c0 + k] = np.asarray(vec, np.float32).reshape(k, 128).T
    put("nmg", inp["norm_mix_g"][l], 8)
    put("nfg", inp["norm_ffn_g"][l], 8)
    put("mod_b", inp["mod_b"][l], 48)
    put("rw_mu0", inp["rw_mu"][l, 0], 9)
    put("rw_mu1", inp["rw_mu"][l, 1], 9)
    put("rw_w0", inp["rw_w0"][l].reshape(-1), 4)
    put("rw_a0", inp["rw_a0"][l].reshape(-1), 4)
    put("rw_kk", inp["rw_kk"][l], 2)
    put("rw_ka", inp["rw_ka"][l], 2)
    put("rw_rk", inp["rw_rk"][l].reshape(-1), 2)
    put("rw_ln_g", inp["rw_ln_g"][l], 2)
    put("rw_ln_b", inp["rw_ln_b"][l], 2)
    put("da_qg", np.tile(inp["da_qk_g"][l, 0], 4), 1)
    put("da_kg", np.tile(inp["da_qk_g"][l, 1], 4), 1)
    put("da_sub", np.tile(inp["da_subln_g"][l], 2), 1)
    put("gla_ab", inp["gla_ab"][l].reshape(-1), 2)
    put("gla_ng", np.tile(inp["gla_norm_g"][l], 2), 1)
    put("gq_qg", np.tile(inp["gqa_qk_g"][l, 0], 2), 1)
    put("gq_kg", np.tile(inp["gqa_qk_g"][l, 1], 2), 1)
    put("conv_w", inp["ffn_conv_w"][l].reshape(-1), 66)
    put("conv_b", inp["ffn_conv_b"][l], 22)
    return P


def host_consts(cfg):
    c = {}
    c["ident"] = np.eye(128, dtype=np.float32)
    c["ones"] = np.ones((128, 128), np.float32)
    b64 = np.zeros((128, 128), np.float32)
    b64[:64, :64] = 1
    b64[64:, 64:] = 1
    c["blk64"] = b64
    b32 = np.zeros((128, 128), np.float32)
    for i in range(4):
        b32[32 * i:32 * i + 32, 32 * i:32 * i + 32] = 1
    c["blk32"] = b32
    TL = cfg.TL
    rows = TL // GRID_W
    row = np.repeat(np.arange(rows, dtype=np.float32), GRID_W)
    col = np.tile(np.arange(GRID_W, dtype=np.float32), rows)

    def tables(hd):
        nf = hd // 4
        inv = (10000.0 ** (-np.arange(nf, dtype=np.float32) / nf)).astype(np.float32)
        ang = np.concatenate([row[:, None] * inv, col[:, None] * inv], axis=-1)
        cos, sin = np.cos(ang).astype(np.float32), np.sin(ang).astype(np.float32)
        half = hd // 2
        cosf = np.concatenate([cos, cos], axis=-1)
        sinf = np.concatenate([sin, sin], axis=-1)
        rep = 128 // hd
        cT = np.tile(cosf, (1, rep)).T.copy()
        sT = np.tile(sinf, (1, rep)).T.copy()
        R = np.zeros((128, 128), np.float32)
        for m in range(128):
            if m % hd < half:
                R[m + half, m] = -1.0
            else:
                R[m - half, m] = 1.0
        return cT, sT, R
    hm = np.zeros((128, 4), np.float32)
    for p in range(128):
        hm[p, p // 32] = 1.0
    c["hmask4s"] = hm * np.float32(32 ** -0.5)
    jj, tt_ = np.meshgrid(np.arange(128), np.arange(128), indexing="ij")
    c["tri4_0"] = np.tile((jj <= tt_).astype(np.float32)[:, None, :], (1, 4, 1))
    c["tri4_1"] = np.tile((jj >= tt_).astype(np.float32)[:, None, :], (1, 4, 1))
    h2 = np.zeros((128, 4), np.float32)
    h2[:64, 0] = 1; h2[64:, 1] = 1; h2[:64, 2] = -1; h2[64:, 3] = -1
    c["hm2"] = h2
    c["I2"] = np.tile(np.eye(128, dtype=np.float32)[:, None, :], (1, 2, 1))
    c["maskN_0"] = np.tile((tt_ < jj).astype(np.float32)[:, None, :], (1, 4, 1))
    c["maskN_1"] = np.tile((tt_ > jj).astype(np.float32)[:, None, :], (1, 4, 1))
    sf, inf_ = (jj < tt_).astype(np.float32), (jj <= tt_).astype(np.float32)
    sr, inr = (jj > tt_).astype(np.float32), (jj >= tt_).astype(np.float32)
    c["maskAB_0"] = np.tile(np.concatenate([sf, inf_], 1)[:, None, :], (1, 2, 1))
    c["maskAB_1"] = np.tile(np.concatenate([sr, inr], 1)[:, None, :], (1, 2, 1))
    dm = np.zeros((128, 2), np.float32)
    for p in range(128):
        dm[p, (p % 64) // 32] = 1.0
    c["dmask"] = dm
    c["cos_gq"], c["sin_gq"], c["rot_gq"] = tables(64)
    c["cos_da"], c["sin_da"], c["rot_da"] = tables(32)
    return c


def build(cfg):
    nc = bass.Bass("TRN2", target_bir_lowering=False)
    fw = FW(nc)
    NB, T, TC, TL = cfg.NB, cfg.T, cfg.TC, cfg.TL
    NJ = NB + 1
    L = cfg.depth

    def din(name, shape, dt=F32):
        return fw.dram(name, shape, dt, kind="ExternalInput")

    xT_d = din("xT", [NB, D, T])
    cT_d = din("cT", [128, KT, NJ])
    pack_d = din("pack", [L, 128, NPACK])
    mod_w = din("mod_w", [L, D, 6 * D])
    w_in = din("w_in", [L, D, N_IN])
    w_out = din("w_out", [L, D, D])
    w_up = din("ffn_w_up", [L, D, 2 * D_FF])
    w_down = din("ffn_w_down", [L, D_FF, D])
    consts = {}
    for nm in ("ident", "ones", "blk64", "blk32", "rot_gq", "rot_da"):
        consts[nm] = din("c_" + nm, [128, 128])
    for nm in ("cos_gq", "sin_gq", "cos_da", "sin_da"):
        consts[nm] = din("c_" + nm, [128, TL])
    consts["dmask"] = din("c_dmask", [128, 2])
    lamb_d = din("lamb", [L, 128, 128])
    gla_a2_d = din("gla_a2", [L, 2, 16, 128])
    rw_w2_d = din("rw_w2", [L, 128, 256])
    rw_a2_d = din("rw_a2", [L, 128, 256])
    rw_g2_d = din("rw_g2", [L, 128, 256])
    consts["hm2"] = din("c_hm2", [128, 4])
    consts["I2"] = din("c_I2", [128, 2, 128])
    for d_ in range(2):
        consts[f"maskN_{d_}"] = din(f"c_maskN_{d_}", [128, 4, 128])
        consts[f"maskAB_{d_}"] = din(f"c_maskAB_{d_}", [128, 2, 256])
    consts["hmask4s"] = din("c_hmask4s", [128, 4])
    consts["tri4_0"] = din("c_tri4_0", [128, 4, 128])
    consts["tri4_1"] = din("c_tri4_1", [128, 4, 128])
    yT_d = fw.dram("yT", [NB, D, TL], F32, kind="ExternalOutput")
    dbg = {}

    PT = fw.dram("PT", [N_IN, T], F32)
    OT = fw.dram("OT", [D, T], BF16)
    ACT = fw.dram("ACTs", [D_FF, T], BF16)

    fw.push()
    ident = fw.sbuf("ident", [128, 128])
    ones = fw.sbuf("ones", [128, 128])
    blk64 = fw.sbuf("blk64", [128, 128])
    fw.dma(ident[:], consts["ident"][:])
    fw.dma(ones[:], consts["ones"][:])
    fw.dma(blk64[:], consts["blk64"][:])
    identb = fw.sbuf("identb", [128, 128], BF16)
    fw.copy(identb[:], ident[:])
    onesb_g = fw.sbuf("onesb_g", [128, 128], BF16)
    fw.copy(onesb_g[:], ones[:])
    blk64b = fw.sbuf("blk64b", [128, 128], BF16)
    fw.copy(blk64b[:], blk64[:])
    pk = [fw.sbuf(f"pk{l}", [128, NPACK]) for l in range(L)]
    for l in range(L):
        fw.dma(pk[l][:], pack_d[l])
    modT = [fw.sbuf(f"modT{l}", [128, 48, NJ]) for l in range(L)]
    gs = [fw.sbuf(f"gs{l}", [128, 2, KT, NJ]) for l in range(L)]
    epsb = fw.sbuf("epsb", [128, 1])
    fw.memset(epsb[:], EPS)

    def pcol(l, name, i=0):
        c0, k = PACK[name]
        return pk[l][:, c0 + i:c0 + i + 1]

    fw.push()
    cs = fw.sbuf("cs", [128, KT, NJ])
    fw.dma(cs[:], cT_d[:])
    fw.act(cs[:], cs[:], AF.Silu)
    mps = fw.psum("mps", [128, 48, NJ])
    wm = [fw.sbuf(f"wm{i}", [128, KT, 512]) for i in range(2)]
    for l in range(L):
        mwv = mod_w.t[l].rearrange("(k p) c -> p k c", p=128)
        for g in range(12):
            wb = wm[g % 2]
            fw.dma(wb[:], mod_w.v(mwv[:, :, g * 512:(g + 1) * 512]))
            for ci in range(4):
                ct = g * 4 + ci
                for k in range(KT):
                    fw.mm(mps[:, ct, :], wb[:, k, ci * 128:(ci + 1) * 128], cs[:, k, :],
                          start=(k == 0), stop=(k == KT - 1))
        c0 = PACK["mod_b"][0]
        for j in range(NJ):
            fw.tt(modT[l][:, :, j], mps[:, :, j], pk[l][:, c0:c0 + 48], ALU.add)
        for j in range(NJ):
            c0 = PACK["nmg"][0]
            fw.stt(gs[l][:, 0, :, j], modT[l][:, 8:16, j], 1.0, pk[l][:, c0:c0 + 8], ALU.add, ALU.mult)
            c0 = PACK["nfg"][0]
            fw.stt(gs[l][:, 1, :, j], modT[l][:, 32:40, j], 1.0, pk[l][:, c0:c0 + 8], ALU.add, ALU.mult)
    fw.pop()

    blocks = cfg.blocks()
    if cfg.stop == "mix":
        fw.push()
        zt = fw.sbuf("zt", [128, T], BF16)
        fw.memset(zt[:], 0.0)
        for k in range(KT):
            fw.dma(OT[k * 128:(k + 1) * 128, :], zt[:])
        fw.pop()

    def mixer_cols():
        tl = []
        for i in range(9):
            tl.append((RW_OFF + 128 * i, 128))
        for i in range(6):
            tl.append((DA_OFF + 128 * i, 128))
        for i in range(6):
            tl.append((GLA_OFF + 128 * i, 128))
        tl.append((GLA_OFF + 768, 32))
        for i in range(4):
            tl.append((GQA_OFF + 128 * i, 128))
        return tl

    def norm_phase(xT, hT, l, which, b, sq, rstd, nps):
        shift_base = 0 if which == 0 else 24
        sqb = [fw.sbuf(f"nsqb{i}", [128, 512], BF16) for i in range(3)]
        tmpf = [fw.sbuf(f"ntmp{i}", [128, 512]) for i in range(3)]
        nps2 = fw.psum("nps_b", [128, 512])
        ci = 0
        for bi, (s, n, is_ctx) in enumerate(blocks):
            j = NB if is_ctx else b
            ps = nps if bi % 2 == 0 else nps2
            for k in range(KT):
                q = sqb[ci % 3]
                ci += 1
                fw.act(q[:, :n], xT[:, k, s:s + n], AF.Square)
                fw.mm(ps[:, :n], onesb_g[:], q[:, :n], start=(k == 0), stop=(k == KT - 1))
            rs = sq if bi % 2 == 0 else rstd
            fw.act(rs[:, :n], ps[:, :n], AF.Sqrt, bias=epsb[:, 0:1], scale=1.0 / D)
            fw.recip(rs[:, :n], rs[:, :n])
            for k in range(KT):
                t_ = tmpf[ci % 3]
                ci += 1
                fw.tt(t_[:, :n], xT[:, k, s:s + n], rs[:, :n], ALU.mult)
                fw.act(hT[:, k, s:s + n], t_[:, :n], AF.Identity,
                       bias=modT[l][:, shift_base + k, j:j + 1], scale=gs[l][:, which, k, j:j + 1])

    def wout_phase(xT, l, b, need_ctx):
        fw.push()
        oT = fw.sbuf("oT", [128, KT, T], BF16)
        ov = OT.t.rearrange("(k p) t -> p k t", p=128)
        for k in range(KT):
            fw.dma(oT[:, k, :], OT.v(ov[:, k, :]))
        wt = [fw.sbuf(f"wo{i}", [128, KT, 128], BF16) for i in range(3)]
        pps = [fw.psum(f"ops{i}", [128, 512]) for i in range(4)]
        wv = w_out.t[l].rearrange("(k p) c -> p k c", p=128)
        ei = 0
        for jt in range(KT):
            wb = wt[jt % 3]
            fw.dma(wb[:], w_out.v(wv[:, :, jt * 128:(jt + 1) * 128]), eng="gpsimd")
            for (s, n, is_ctx) in blocks:
                if is_ctx and not need_ctx:
                    continue
                j = NB if is_ctx else b
                ps = pps[ei % 4]
                ei += 1
                for k in range(KT):
                    fw.mm(ps[:, :n], wb[:, k, :], oT[:, k, s:s + n], start=(k == 0), stop=(k == KT - 1))
                fw.stt(xT[:, jt, s:s + n], ps[:, :n], modT[l][:, 16 + jt, j:j + 1], xT[:, jt, s:s + n],
                       ALU.mult, ALU.add)
        fw.pop()

    def ffn_phase(xT, l, b, need_ctx):
        segs = ([(0, TC)] if need_ctx else []) + [(TC, T)]
        fblocks = [bl for bl in blocks if (need_ctx or not bl[2])]
        fw.push()
        hT = fw.sbuf("hT2", [128, KT, T], BF16)
        sq = fw.sbuf("sq2", [128, 512])
        rstd = fw.sbuf("rstd2", [128, 512])
        nps = fw.psum("nps2", [128, 512])
        norm_phase(xT, hT, l, 1, b, sq, rstd, nps)
        wu = [fw.sbuf(f"wu{i}", [128, KT, 128], BF16) for i in range(3)]
        wg = [fw.sbuf(f"wg{i}", [128, KT, 128], BF16) for i in range(3)]
        ups = [fw.psum(f"ups{i}", [128, 512]) for i in range(2)]
        gps = [fw.psum(f"gps{i}", [128, 512]) for i in range(2)]
        uT = [fw.sbuf(f"uT{i}", [128, T]) for i in range(2)]
        gT = [fw.sbuf(f"gT{i}", [128, T]) for i in range(2)]
        tmp = [fw.sbuf(f"ftmp{i}", [128, T]) for i in range(2)]
        aT = [fw.sbuf(f"aT{i}", [128, T], BF16) for i in range(2)]
        wv = w_up.t[l].rearrange("(k p) c -> p k c", p=128)
        cw0 = PACK["conv_w"][0]
        cb0 = PACK["conv_b"][0]
        ei = 0
        def wload(i):
            fw.dma(wu[i % 3][:], w_up.v(wv[:, :, i * 128:(i + 1) * 128]), eng="gpsimd")
            fw.dma(wg[i % 3][:], w_up.v(wv[:, :, D_FF + i * 128:D_FF + (i + 1) * 128]), eng="gpsimd")
        wload(0)
        for i in range(FT):
            r = i % 2
            if i + 1 < FT:
                wload(i + 1)
            for (s, n, is_ctx) in fblocks:
                pu, pg = ups[ei % 2], gps[ei % 2]
                ei += 1
                for k in range(KT):
                    fw.mm(pu[:, :n], wu[i % 3][:, k, :], hT[:, k, s:s + n], start=(k == 0), stop=(k == KT - 1))
                for k in range(KT):
                    fw.mm(pg[:, :n], wg[i % 3][:, k, :], hT[:, k, s:s + n], start=(k == 0), stop=(k == KT - 1))
                fw.copy(uT[r][:, s:s + n], pu[:, :n], eng="scalar")
                fw.copy(gT[r][:, s:s + n], pg[:, :n], eng="scalar")
                fw.act(tmp[r][:, s:s + n], pg[:, :n], AF.Identity, bias=pk[l][:, cb0 + i:cb0 + i + 1],
                       scale=pk[l][:, cw0 + FT + i:cw0 + FT + i + 1])
            w0 = pk[l][:, cw0 + i:cw0 + i + 1]
            w1 = pk[l][:, cw0 + FT + i:cw0 + FT + i + 1]
            w2 = pk[l][:, cw0 + 2 * FT + i:cw0 + 2 * FT + i + 1]
            cb = pk[l][:, cb0 + i:cb0 + i + 1]
            for (s, e) in segs:
                fw.stt(tmp[r][:, s + 1:e], gT[r][:, s:e - 1], w0, tmp[r][:, s + 1:e], ALU.mult, ALU.add)
                fw.stt(tmp[r][:, s:e - 1], gT[r][:, s + 1:e], w2, tmp[r][:, s:e - 1], ALU.mult, ALU.add)
                fw.act(tmp[r][:, s:e], tmp[r][:, s:e], AF.Silu)
                fw.tt(aT[r][:, s:e], tmp[r][:, s:e], uT[r][:, s:e], ALU.mult)
                fw.dma(ACT[i * 128:(i + 1) * 128, s:e], aT[r][:, s:e])
        fw.pop()
        fw.push()
        wd = [fw.sbuf(f"wd{jt}", [128, FT, 128], BF16) for jt in range(KT)]
        wdv = w_down.t[l].rearrange("(f p) c -> p f c", p=128)
        for jt in range(KT):
            fw.dma(wd[jt][:], w_down.v(wdv[:, :, jt * 128:(jt + 1) * 128]), eng="gpsimd")
        ab = [fw.sbuf(f"ab{i}", [128, FT, 512], BF16) for i in range(2)]
        dps = [fw.psum(f"dps{i}", [128, 512]) for i in range(4)]
        av = ACT.t.rearrange("(f p) t -> p f t", p=128)
        ei = 0
        for bi, (s, n, is_ctx) in enumerate(fblocks):
            j = NB if is_ctx else b
            a = ab[bi % 2]
            fw.dma(a[:, :, :n], ACT.v(av[:, :, s:s + n]))
            for jt in range(KT):
                ps = dps[ei % 4]
                ei += 1
                for f in range(FT):
                    fw.mm(ps[:, :n], wd[jt][:, f, :], a[:, f, :n], start=(f == 0), stop=(f == FT - 1))
                fw.stt(xT[:, jt, s:s + n], ps[:, :n], modT[l][:, 40 + jt, j:j + 1], xT[:, jt, s:s + n],
                       ALU.mult, ALU.add)
        fw.pop()

    NT = T // 128
    NTC = TC // 128
    lat_blocks = [bl for bl in blocks if not bl[2]]
    ctx_blocks = [bl for bl in blocks if bl[2]]

    def head_norm_rope(raw, outs, blkb, hd, g_ap, rotb, cosT, sinT, s, n, is_ctx, tset, masks=None):
        sqb, rs, qg, t1, t2, nps, npr = tset
        fw.act(sqb[:, :n], raw, AF.Square)
        fw.mm(nps[:, :n], blkb[:], sqb[:, :n])
        fw.act(rs[:, :n], nps[:, :n], AF.Sqrt, bias=epsb[:, 0:1], scale=1.0 / hd)
        fw.recip(rs[:, :n], rs[:, :n])
        fw.stt(qg[:, :n], raw, g_ap, rs[:, :n], ALU.mult, ALU.mult)
        if is_ctx:
            res = qg
        else:
            fw.mm(npr[:, :n], rotb[:], qg[:, :n])
            fw.tt(t1[:, :n], qg[:, :n], cosT[:, s - TC:s - TC + n], ALU.mult)
            fw.tt(t2[:, :n], npr[:, :n], sinT[:, s - TC:s - TC + n], ALU.mult)
            fw.tt(t1[:, :n], t1[:, :n], t2[:, :n], ALU.add, eng="gpsimd")
            res = t1
        if masks is None:
            fw.copy(outs[0], res[:, :n], eng="gpsimd")
        else:
            for m, o in enumerate(outs):
                fw.ts(o, res[:, :n], masks[:, m:m + 1], ALU.mult, eng="gpsimd")

    def prep_sets(tag):
        sets = []
        for i in range(2):
            sets.append((fw.sbuf(f"{tag}sqb{i}", [128, 512], BF16), fw.sbuf(f"{tag}rs{i}", [128, 512]),
                         fw.sbuf(f"{tag}qg{i}", [128, 512], BF16), fw.sbuf(f"{tag}t1{i}", [128, 512]),
                         fw.sbuf(f"{tag}t2{i}", [128, 512]), fw.psum(f"{tag}nps{i}", [128, 512]),
                         fw.psum(f"{tag}npr{i}", [128, 512])))
        return sets

    def make_vdup(vrow0, nheads, Vd, vtmp, tps):
        ntile = (nheads * 64) // 128
        for vt in range(ntile):
            fw.dma(vtmp[:, :], PT[vrow0 + vt * 128:vrow0 + (vt + 1) * 128, :])
            for i in range(NT):
                fw.transpose(tps[:, i % 4, :], vtmp[:, i * 128:(i + 1) * 128], ident[:])
                for hh in range(2):
                    h = vt * 2 + hh
                    fw.copy(Vd[h][:, i, 0:64], tps[:, i % 4, hh * 64:(hh + 1) * 64], eng="vector")
                    fw.copy(Vd[h][:, i, 64:128], tps[:, i % 4, hh * 64:(hh + 1) * 64], eng="scalar")

    def attn_head(qviews, kviews, Vd_h, nmaps, scale, qb, sps_l, pT_l, oacc, dacc, dsum, cnt):
        (s, n, is_ctx) = qb
        kts = list(range(NTC)) if is_ctx else list(range(NT))
        steps = [(ki, kt, m) for ki, kt in enumerate(kts) for m in range(nmaps)]
        c0 = cnt[0]
        cnt[0] += len(steps)

        def score(i):
            ki, kt, m = steps[i]
            sp = sps_l[(c0 + i) % len(sps_l)]
            fw.mm(sp[:, :n], kviews[m](kt), qviews[m](s, n))
        depth = len(sps_l) - 1
        for i in range(min(depth, len(steps))):
            score(i)
        for i, (ki, kt, m) in enumerate(steps):
            if i + depth < len(steps):
                score(i + depth)
            sp = sps_l[(c0 + i) % len(sps_l)]
            pT = pT_l[(c0 + i) % len(pT_l)]
            fw.act(pT[:, :n], sp[:, :n], AF.Exp, scale=scale)
            fw.mm(oacc[m][:, :n], Vd_h[:, kt, :], pT[:, :n], start=(ki == 0), stop=(ki == len(kts) - 1))
            ds = dsum[m]
            de = "vector" if m == 0 else "gpsimd"
            if ki == 0:
                fw.copy(ds[:, :n], pT[:, :n], eng=de)
            else:
                fw.tt(ds[:, :n], pT[:, :n], ds[:, :n], ALU.add, eng=de)
        for m in range(nmaps):
            fw.mm(dacc[m][:, :n], ones[:], dsum[m][:, :n])

    def gqa_phase(l, b, need_ctx):
        fw.push()
        onesb = onesb_g
        qn = fw.sbuf("qn", [128, 2, T], BF16)
        kd = fw.sbuf("kd", [128, 2, T], BF16)
        Vd = [fw.sbuf(f"Vd{h}", [128, NT, 128], BF16) for h in range(2)]
        qblocks = (ctx_blocks if need_ctx else []) + lat_blocks
        fw.push()
        cosT = fw.sbuf("cosT", [128, TL]); sinT = fw.sbuf("sinT", [128, TL]); rot = fw.sbuf("rot", [128, 128])
        fw.dma(cosT[:], consts["cos_gq"][:]); fw.dma(sinT[:], consts["sin_gq"][:]); fw.dma(rot[:], consts["rot_gq"][:])
        rotb = fw.sbuf("rotb", [128, 128], BF16)
        fw.copy(rotb[:], rot[:])
        raw = [fw.sbuf(f"raw{i}", [128, 512]) for i in range(3)]
        tsets = prep_sets("g")
        tps = fw.psum("tps", [128, 4, 128])
        vtmp = fw.sbuf("vtmp", [128, T])
        ri = 0
        for t in range(2):
            for (s, n, is_ctx) in qblocks:
                r = raw[ri % 3]; ri += 1
                fw.dma(r[:, :n], PT[GQA_OFF + t * 128:GQA_OFF + (t + 1) * 128, s:s + n])
                head_norm_rope(r[:, :n], [qn[:, t, s:s + n]], blk64b, 64, pcol(l, "gq_qg"), rotb, cosT, sinT,
                               s, n, is_ctx, tsets[ri % 2])
            for (s, n, is_ctx) in blocks:
                r = raw[ri % 3]; ri += 1
                for hh in range(2):
                    fw.dma(r[hh * 64:(hh + 1) * 64, :n], PT[GQA_OFF + 256 + t * 64:GQA_OFF + 256 + (t + 1) * 64, s:s + n])
                head_norm_rope(r[:, :n], [kd[:, t, s:s + n]], blk64b, 64, pcol(l, "gq_kg"), rotb, cosT, sinT,
                               s, n, is_ctx, tsets[ri % 2])
        make_vdup(GQA_OFF + 384, 2, Vd, vtmp, tps)
        fw.pop()
        sps_l = [fw.psum(f"sps{i}", [128, 512]) for i in range(4)]
        pT_l = [fw.sbuf(f"pT{i}", [128, 512], BF16) for i in range(4)]
        oacc = [fw.psum("oacc0", [128, 512])]
        dacc = [fw.psum("dacc0", [128, 512])]
        dsum_l = [fw.sbuf(f"dsum{i}", [128, 512]) for i in range(2)]
        rec = fw.sbuf("rec", [128, 512])
        ob = [fw.sbuf(f"ob{i}", [128, 512], BF16) for i in range(2)]
        cnt = [0]
        oi = 0
        for h in range(4):
            t, g = h // 2, h % 2
            ph = 64 * g
            qv = [lambda s, n, t=t, ph=ph: qn[ph:ph + 64, t, s:s + n]]
            kv = [lambda kt, t=t, ph=ph: kd[ph:ph + 64, t, kt * 128:(kt + 1) * 128]]
            for qb in qblocks:
                (s, n, is_ctx) = qb
                attn_head(qv, kv, Vd[t], 1, 0.125, qb, sps_l, pT_l, oacc, dacc, [dsum_l[oi % 2]], cnt)
                fw.recip(rec[ph:ph + 64, :n], dacc[0][ph:ph + 64, :n])
                o = ob[oi % 2]; oi += 1
                fw.tt(o[ph:ph + 64, :n], oacc[0][ph:ph + 64, :n], rec[ph:ph + 64, :n], ALU.mult)
                fw.dma(OT[768 + h * 64:768 + (h + 1) * 64, s:s + n], o[ph:ph + 64, :n])
        fw.pop()

    def da_phase(l, b, need_ctx):
        lam_init = 0.8 - 0.6 * float(np.exp(-0.3 * l))
        fw.push()
        onesb = onesb_g
        lamb = fw.sbuf("lamb_s", [128, 128])
        fw.dma(lamb[:], lamb_d[l])
        lt = fw.sbuf("lt", [128, 64])
        lsum = fw.sbuf("lsum", [128, 2])
        nlam = fw.sbuf("nlam", [128, 1])
        sg = fw.sbuf("sg", [128, 1])
        fw.tt(lt[:, 0:32], lamb[:, 0:32], lamb[:, 32:64], ALU.mult)
        fw.tt(lt[:, 32:64], lamb[:, 64:96], lamb[:, 96:128], ALU.mult)
        fw.op("vector", "reduce_sum", [lt[:]], [lsum[:]], lsum[:, 0:1].ap, lt[:, 0:32].ap, AX.X)
        fw.op("vector", "reduce_sum", [lt[:]], [lsum[:]], lsum[:, 1:2].ap, lt[:, 32:64].ap, AX.X)
        fw.act(lsum[:], lsum[:], AF.Exp)
        fw.stt(nlam[:], lsum[:, 1:2], -lam_init, lsum[:, 0:1], ALU.add, ALU.subtract)
        fw.ts(sg[:], pcol(l, "da_sub"), 1.0 - lam_init, ALU.mult)
        qm = [fw.sbuf(f"qm{m}", [128, 2, T], BF16) for m in range(2)]
        kn = fw.sbuf("kn", [128, 2, T], BF16)
        Vd = [fw.sbuf(f"Vd{h}", [128, NT, 128], BF16) for h in range(4)]
        qblocks = (ctx_blocks if need_ctx else []) + lat_blocks
        fw.push()
        cosT = fw.sbuf("cosT", [128, TL]); sinT = fw.sbuf("sinT", [128, TL]); rot = fw.sbuf("rot", [128, 128])
        fw.dma(cosT[:], consts["cos_da"][:]); fw.dma(sinT[:], consts["sin_da"][:]); fw.dma(rot[:], consts["rot_da"][:])
        rotb = fw.sbuf("rotb", [128, 128], BF16)
        fw.copy(rotb[:], rot[:])
        blk32 = fw.sbuf("blk32", [128, 128])
        fw.dma(blk32[:], consts["blk32"][:])
        blk32b = fw.sbuf("blk32b", [128, 128], BF16)
        fw.copy(blk32b[:], blk32[:])
        dmask = fw.sbuf("dmask", [128, 2])
        fw.dma(dmask[:], consts["dmask"][:])
        raw = [fw.sbuf(f"raw{i}", [128, 512]) for i in range(3)]
        tsets = prep_sets("d")
        tps = fw.psum("tps", [128, 4, 128])
        vtmp = fw.sbuf("vtmp", [128, T])
        ri = 0
        for t in range(2):
            for (s, n, is_ctx) in qblocks:
                r = raw[ri % 3]; ri += 1
                fw.dma(r[:, :n], PT[DA_OFF + t * 128:DA_OFF + (t + 1) * 128, s:s + n])
                head_norm_rope(r[:, :n], [qm[0][:, t, s:s + n], qm[1][:, t, s:s + n]], blk32b, 32,
                               pcol(l, "da_qg"), rotb, cosT, sinT, s, n, is_ctx, tsets[ri % 2], masks=dmask)
            for (s, n, is_ctx) in blocks:
                r = raw[ri % 3]; ri += 1
                fw.dma(r[:, :n], PT[DA_OFF + 256 + t * 128:DA_OFF + 256 + (t + 1) * 128, s:s + n])
                head_norm_rope(r[:, :n], [kn[:, t, s:s + n]], blk32b, 32, pcol(l, "da_kg"), rotb, cosT, sinT,
                               s, n, is_ctx, tsets[ri % 2])
        make_vdup(DA_OFF + 512, 4, Vd, vtmp, tps)
        fw.pop()
        nps = fw.psum("anps", [128, 512])
        tmps = [fw.sbuf(f"nt{i}", [128, 512]) for i in range(2)]
        sps_l = [fw.psum(f"sps{i}", [128, 512]) for i in range(3)]
        pT_l = [fw.sbuf(f"pT{i}", [128, 512], BF16) for i in range(4)]
        oacc = [fw.psum(f"oacc{m}", [128, 512]) for m in range(2)]
        dacc = [fw.psum(f"dacc{m}", [128, 512]) for m in range(2)]
        dsum_l = [[fw.sbuf(f"dsum{i}_{m}", [128, 512]) for m in range(2)] for i in range(2)]
        hq = [0]
        rec = [fw.sbuf(f"rec{m}", [128, 512]) for m in range(2)]
        o1 = fw.sbuf("o1", [128, 512])
        osb = fw.sbuf("osb", [128, 512])
        ob = [fw.sbuf(f"ob{i}", [128, 512], BF16) for i in range(2)]
        sq, rs = tmps[0], tmps[1]
        cnt = [0]
        oi = 0
        for t in range(2):
            for qb in qblocks:
                (s, n, is_ctx) = qb
                for g in range(2):
                    h = 2 * t + g
                    ph = 64 * g
                    qv = [lambda s, n, t=t, ph=ph, m=m: qm[m][ph:ph + 64, t, s:s + n] for m in range(2)]
                    kv = [lambda kt, t=t, ph=ph: kn[ph:ph + 64, t, kt * 128:(kt + 1) * 128]] * 2
                    attn_head(qv, kv, Vd[h], 2, 32 ** -0.5, qb, sps_l, pT_l, oacc, dacc, dsum_l[hq[0] % 2], cnt)
                    hq[0] += 1
                    for m in range(2):
                        fw.recip(rec[m][ph:ph + 64, :n], dacc[m][ph:ph + 64, :n])
                    fw.tt(osb[ph:ph + 64, :n], oacc[0][ph:ph + 64, :n], rec[0][ph:ph + 64, :n], ALU.mult)
                    fw.tt(o1[ph:ph + 64, :n], oacc[1][ph:ph + 64, :n], rec[1][ph:ph + 64, :n], ALU.mult)
                    fw.stt(osb[ph:ph + 64, :n], o1[ph:ph + 64, :n], nlam[ph:ph + 64, 0:1], osb[ph:ph + 64, :n],
                           ALU.mult, ALU.add)
                fw.act(sq[:, :n], osb[:, :n], AF.Square)
                fw.mm(nps[:, :n], blk64[:], sq[:, :n])
                fw.act(rs[:, :n], nps[:, :n], AF.Sqrt, bias=epsb[:, 0:1], scale=1.0 / 64)
                fw.recip(rs[:, :n], rs[:, :n])
                o = ob[oi % 2]; oi += 1
                fw.stt(o[:, :n], osb[:, :n], sg[:, 0:1], rs[:, :n], ALU.mult, ALU.mult)
                fw.dma(OT[256 + t * 128:256 + (t + 1) * 128, s:s + n], o[:, :n])
        fw.pop()

    def chunk_order(rev):
        if not rev:
            return list(range(NT))
        return list(range(NTC - 1, -1, -1)) + list(range(NT - 1, NTC - 1, -1))

    def cumsum_chunks(A, B, rev):
        cur, oth = A, B
        s = 1
        while s < 128:
            cv = cur.t[:, :].rearrange("p (c i) -> p c i", i=128)
            ov = oth.t[:, :].rearrange("p (c i) -> p c i", i=128)
            if not rev:
                fw.tt(oth.v(ov[:, :, s:]), cur.v(cv[:, :, s:]), cur.v(cv[:, :, :128 - s]), ALU.add)
                fw.copy(oth.v(ov[:, :, :s]), cur.v(cv[:, :, :s]), eng="gpsimd")
            else:
                fw.tt(oth.v(ov[:, :, :128 - s]), cur.v(cv[:, :, :128 - s]), cur.v(cv[:, :, s:]), ALU.add)
                fw.copy(oth.v(ov[:, :, 128 - s:]), cur.v(cv[:, :, 128 - s:]), eng="gpsimd")
            cur, oth = oth, cur
            s *= 2
        return cur, oth

    def gla_phase(l, b, need_ctx):
        fw.push()
        Fb = [fw.sbuf(f"F{i}", [128, T]) for i in range(4)]
        F1, F2, F3, F4 = Fb
        Vdup = fw.sbuf("Vdup", [128, NT, 4, 128], BF16)
        QM = [fw.sbuf(f"QM{h}", [128, T], BF16) for h in range(4)]
        KTt = fw.sbuf("KTt", [128, T], BF16)
        KH = fw.sbuf("KH", [128, T], BF16)
        oaccT = fw.sbuf("oaccT", [128, 2, T])
        Pend = fw.sbuf("Pend", [128, NT])
        gfb = fw.sbuf("gfb", [48, T])
        a2 = fw.sbuf("a2", [48, 128])
        nab = fw.sbuf("nab", [128, 2])
        hm4 = fw.sbuf("hm4", [128, 4])
        tri = fw.sbuf("tri", [128, 4, 128])
        fw.dma(hm4[:], consts["hmask4s"][:])
        for d in range(2):
            fw.dma(a2[32 * d:32 * d + 16, :], gla_a2_d[l, d])
            fw.dma(gfb[32 * d:32 * d + 16, :], PT[GLA_OFF + 512 + 16 * d:GLA_OFF + 528 + 16 * d, :])
        c0 = PACK["gla_ab"][0]
        fw.ts(nab[:], pk[l][:, c0:c0 + 2], -1.0, ALU.mult)
        S = fw.sbuf("Sst", [128, 64])
        Sdup = fw.sbuf("Sdup", [128, 128], BF16)
        KHt = [fw.sbuf(f"KHt{i}", [128, 640], BF16) for i in range(2)]
        for i in range(2):
            fw.memset(KHt[i][:], 0.0)
        AT = [fw.sbuf(f"AT{i}", [128, 4, 128], BF16) for i in range(2)]
        tpsb = fw.psum("tpsb", [128, 4, 128], BF16)
        tps = fw.psum("tps", [128, 4, 128])
        aps = [fw.psum(f"aps{i}", [128, 4, 128]) for i in range(2)]
        ops = [fw.psum(f"ops{i}", [128, 4, 128]) for i in range(2)]
        sps = fw.psum("sps", [128, 64])
        zps = fw.psum("zps", [128, 512])
        for vt in range(2):
            fw.dma(F1[:], PT[GLA_OFF + 256 + vt * 128:GLA_OFF + 256 + (vt + 1) * 128, :])
            for i in range(NT):
                fw.transpose(tps[:, i % 4, :], F1[:, i * 128:(i + 1) * 128], ident[:])
                for hh in range(2):
                    h = vt * 2 + hh
                    fw.copy(Vdup[:, i, h, 0:64], tps[:, i % 4, hh * 64:(hh + 1) * 64], eng="vector")
                    fw.copy(Vdup[:, i, h, 64:128], tps[:, i % 4, hh * 64:(hh + 1) * 64], eng="scalar")
        fw.dma(F1[:], PT[GLA_OFF:GLA_OFF + 128, :])
        fw.dma(F2[:], PT[GLA_OFF + 128:GLA_OFF + 256, :])
        for d in range(2):
            rev = d == 1
            fw.dma(tri[:], consts[f"tri4_{d}"][:])
            for (s, n, is_ctx) in blocks:
                fw.mm(zps[:, :n], a2[32 * d:32 * d + 16, :], gfb[32 * d:32 * d + 16, s:s + n])
                fw.act(F3[:, s:s + n], zps[:, :n], AF.Exp, bias=nab[:, d:d + 1], scale=-1.0)
            fw.act(F3[:], F3[:], AF.Ln, bias=1.0)
            fw.ts(F3[:], F3[:], -1.0 / 16.0, ALU.mult)
            bb, ff = cumsum_chunks(F3, F4, rev)
            bv = bb.t[:, :].rearrange("p (c i) -> p c i", i=128)
            eidx = 0 if rev else 127
            fw.act(Pend[:], bb.v(bv[:, :, eidx]), AF.Exp)
            fw.act(ff[:], bb[:], AF.Exp)
            for h in range(4):
                fw.stt(QM[h][:], F1[:], hm4[:, h:h + 1], ff[:], ALU.mult, ALU.mult,
                       eng=("gpsimd" if h % 2 else "vector"))
            fw.act(ff[:], bb[:], AF.Exp, scale=-1.0)
            fw.tt(KTt[:], F2[:], ff[:], ALU.mult)
            for c in range(NT):
                fw.act(ff[:, c * 128:(c + 1) * 128], bb[:, c * 128:(c + 1) * 128], AF.Exp,
                       bias=bb[:, c * 128 + eidx:c * 128 + eidx + 1], scale=-1.0)
            fw.tt(KH[:], F2[:], ff[:], ALU.mult, eng="gpsimd")
            first = True
            for ci, c in enumerate(chunk_order(rev)):
                cs = slice(c * 128, (c + 1) * 128)
                is_ctx = c < NTC
                kht = KHt[ci % 2]
                at = AT[ci % 2]
                ap_, op_ = aps[ci % 2], ops[ci % 2]
                fw.transpose(tpsb[:, 0, :], KH[:, cs], identb[:])
                kv = kht.t[:, :].rearrange("p (h x) -> p h x", x=160)
                fw.copy(kht.v(kv[:, :, 0:32]), tpsb.v(tpsb.t[:, 0, :].rearrange("p (h x) -> p h x", x=32)))
                want_out = need_ctx or not is_ctx
                if want_out:
                    for h in range(4):
                        fw.mm(ap_[:, h, :], KTt[:, cs], QM[h][:, cs])
                    fw.tt(at[:], ap_[:], tri[:], ALU.mult)
                    for h in range(4):
                        if not first:
                            fw.mm(op_[:, h, :], Sdup[:], QM[h][:, cs], start=True, stop=False)
                        fw.mm(op_[:, h, :], Vdup[:, c, h, :], at[:, h, :], start=first, stop=True)
                    o4 = op_.t[:, :, :].rearrange("p (a g) t -> p a g t", g=2)
                    for g in range(2):
                        dst = oaccT[64 * g:64 * g + 64, :, cs]
                        src = op_.v(o4[64 * g:64 * g + 64, :, g, :])
                        if d == 0:
                            fw.copy(dst, src, eng=("vector" if g == 0 else "scalar"))
                        else:
                            fw.tt(dst, src, dst, ALU.add, eng="vector")
                for h in range(4):
                    fw.mm(sps[:, :], kht[:, h * 128:(h + 1) * 128], Vdup[:, c, h, 0:64], start=(h == 0), stop=(h == 3))
                if first:
                    fw.copy(S[:], sps[:])
                else:
                    fw.stt(S[:], S[:], Pend[:, c:c + 1], sps[:], ALU.mult, ALU.add)
                fw.copy(Sdup[:, 0:64], S[:], eng="scalar")
                fw.copy(Sdup[:, 64:128], S[:], eng="gpsimd")
                first = False
        sq, rs, rr = F3, F4, F1
        ob = [fw.sbuf(f"gob{i}", [128, 512], BF16) for i in range(2)]
        oi = 0
        qblocks = (ctx_blocks if need_ctx else []) + lat_blocks
        for t in range(2):
            for (s, n, is_ctx) in qblocks:
                fw.dma(rr[:, :n], PT[GLA_OFF + 544 + t * 128:GLA_OFF + 544 + (t + 1) * 128, s:s + n])
                fw.act(rr[:, :n], rr[:, :n], AF.Silu)
                fw.act(sq[:, :n], oaccT[:, t, s:s + n], AF.Square)
                fw.mm(zps[:, :n], blk64[:], sq[:, :n])
                fw.act(rs[:, :n], zps[:, :n], AF.Sqrt, bias=epsb[:, 0:1], scale=1.0 / 64)
                fw.recip(rs[:, :n], rs[:, :n])
                fw.stt(sq[:, :n], oaccT[:, t, s:s + n], pcol(l, "gla_ng"), rs[:, :n], ALU.mult, ALU.mult)
                o = ob[oi % 2]; oi += 1
                fw.tt(o[:, :n], sq[:, :n], rr[:, :n], ALU.mult, eng="gpsimd")
                fw.dma(OT[512 + t * 128:512 + (t + 1) * 128, s:s + n], o[:, :n])
        fw.pop()

    RWS = fw.dram("RWS", [2, 2, 8, 128, T], BF16)
    VDs = fw.dram("VDs", [128, NT, 4, 128], BF16)
    BON = fw.dram("BON", [256, T], F32)
    PENDs = fw.dram("PENDs", [128, 2, 2, NT], F32)
    GATE = fw.dram("GATE", [256, T], F32)

    def shift_mix(dst, raw, l, ct):
        m0 = pcol(l, "rw_mu0", ct)
        m1 = pcol(l, "rw_mu1", ct)
        c0 = pcol(l, "rw_c0", ct)
        fw.ts(dst[:, :], raw[:, :], c0, ALU.mult)
        for (s, e) in ((0, TC), (TC, T)):
            fw.stt(dst[:, s + 1:e], raw[:, s:e - 1], m0, dst[:, s + 1:e], ALU.mult, ALU.add)
            fw.stt(dst[:, s:e - 1], raw[:, s + 1:e], m1, dst[:, s:e - 1], ALU.mult, ALU.add, eng="gpsimd")

    def rwkv_phase(l, b, need_ctx):
        fw.push()
        lora = [fw.sbuf(f"lora{i}", [128, T], BF16) for i in range(3)]
        w2s = fw.sbuf("w2s", [128, 256], BF16)
        a2s = fw.sbuf("a2s", [128, 256], BF16)
        g2s = fw.sbuf("g2s", [128, 256], BF16)
        fw.dma(w2s[:], rw_w2_d[l], eng="gpsimd")
        fw.dma(a2s[:], rw_a2_d[l], eng="gpsimd")
        fw.dma(g2s[:], rw_g2_d[l], eng="gpsimd")
        hm2 = fw.sbuf("hm2", [128, 4])
        fw.dma(hm2[:], consts["hm2"][:])
        c0 = PACK["rw_mu0"][0]
        c1 = PACK["rw_mu1"][0]
        cc = PACK["rw_c0"][0]
        fw.tt(pk[l][:, cc:cc + 9], pk[l][:, c0:c0 + 9], pk[l][:, c1:c1 + 9], ALU.add)
        fw.ts(pk[l][:, cc:cc + 9], pk[l][:, cc:cc + 9], -1.0, ALU.mult, 1.0, ALU.add)
        ck = PACK["rw_ka"][0]
        co = PACK["rw_omka"][0]
        fw.ts(pk[l][:, co:co + 2], pk[l][:, ck:ck + 2], -1.0, ALU.mult, 1.0, ALU.add)
        Bf = [fw.sbuf(f"B{i}", [128, T]) for i in range(10)]
        stgb = [fw.sbuf(f"stgb{i}", [128, T], BF16) for i in range(2)]
        zps = [fw.psum(f"zps{i}", [128, 512]) for i in range(3)]
        tps = fw.psum("tps", [128, 4, 128])
        vd = [fw.sbuf(f"vd{i}", [128, 4, 128], BF16) for i in range(2)]
        eps12 = fw.sbuf("eps12", [128, 1])
        fw.memset(eps12[:], 1e-12)
        raw, sh = Bf[0], Bf[1]
        for i, fn in ((0, AF.Tanh), (1, None), (2, AF.Sigmoid)):
            fw.dma(raw[:], PT[RW_OFF + (6 + i) * 128:RW_OFF + (7 + i) * 128, :])
            shift_mix(sh, raw, l, 6 + i)
            if fn is None:
                fw.copy(lora[i][:], sh[:])
            else:
                fw.act(lora[i][:], sh[:], fn)
        for vt in range(2):
            fw.dma(raw[:], PT[RW_OFF + 512 + vt * 128:RW_OFF + 512 + (vt + 1) * 128, :])
            shift_mix(sh, raw, l, 4 + vt)
            for i in range(NT):
                fw.transpose(tps[:, i % 4, :], sh[:, i * 128:(i + 1) * 128], ident[:])
                v_ = vd[i % 2]
                for hh in range(2):
                    fw.copy(v_[:, hh, 0:64], tps[:, i % 4, hh * 64:(hh + 1) * 64], eng="vector")
                    fw.copy(v_[:, hh, 64:128], tps[:, i % 4, hh * 64:(hh + 1) * 64], eng="scalar")
                fw.dma(VDs[:, i, 2 * vt:2 * vt + 2, :], v_[:, 0:2, :])
        Pend = fw.dram("Pend_d", [2, 2, 128, NT], F32) if False else None
        pend_s = fw.sbuf("pend_s", [128, 2, 2, NT])
        Fk, Fkk, Fr, Fbon, Ll, Aa, Bb, Fa, Fkd, Fb = Bf
        for tau in range(2):
            fw.dma(raw[:], PT[RW_OFF + 256 + tau * 128:RW_OFF + 256 + (tau + 1) * 128, :]) if False else None
            fw.dma(Ll[:], PT[RW_OFF + 256 + tau * 128:RW_OFF + 256 + (tau + 1) * 128, :])
            shift_mix(Fk, Ll, l, 2 + tau)
            fw.dma(Ll[:], PT[RW_OFF + tau * 128:RW_OFF + (tau + 1) * 128, :])
            shift_mix(Fr, Ll, l, tau)
            fw.ts(Fkk[:], Fk[:], pcol(l, "rw_kk", tau), ALU.mult)
            for (s, n, is_ctx) in blocks:
                zp = zps[0]
                fw.act(Aa[:, s:s + n], Fkk[:, s:s + n], AF.Square)
                fw.mm(zp[:, :n], blk64[:], Aa[:, s:s + n])
                fw.act(Aa[:, s:s + n], zp[:, :n], AF.Sqrt, bias=eps12[:, 0:1], scale=1.0)
            fw.recip(Aa[:], Aa[:])
            fw.tt(Fkk[:], Fkk[:], Aa[:], ALU.mult)
            for bi, (s, n, is_ctx) in enumerate(blocks):
                zp = zps[bi % 3]
                fw.mm(zp[:, :n], g2s[:, tau * 128:(tau + 1) * 128], lora[2][:, s:s + n])
                fw.copy(Aa[:, s:s + n], zp[:, :n], eng="scalar")
            fw.dma(GATE[tau * 128:(tau + 1) * 128, :], Aa[:])
            for d in range(2):
                rev = d == 1
                eidx = 0 if rev else 127
                ph = 64 * d
                for bi, (s, n, is_ctx) in enumerate(blocks):
                    zp = zps[bi % 3]
                    fw.mm(zp[:, :n], w2s[ph:ph + 64, tau * 128:(tau + 1) * 128], lora[0][ph:ph + 64, s:s + n])
                    fw.act(Ll[:, s:s + n], zp[:, :n], AF.Sigmoid, bias=pcol(l, "rw_w0", d * 2 + tau))
                    zp2 = zps[(bi + 1) % 3]
                    fw.mm(zp2[:, :n], a2s[ph:ph + 64, tau * 128:(tau + 1) * 128], lora[1][ph:ph + 64, s:s + n])
                    fw.act(Fa[:, s:s + n], zp2[:, :n], AF.Sigmoid, bias=pcol(l, "rw_a0", d * 2 + tau))
                fw.ts(Ll[:], Ll[:], -0.6065306597126334, ALU.mult)
                cur = Ll
                pp = [Aa, Bb]
                st = 1
                k_ = 0
                while st < 128:
                    oth = pp[k_ % 2]
                    cv = cur.t[:, :].rearrange("p (c i) -> p c i", i=128)
                    ov = oth.t[:, :].rearrange("p (c i) -> p c i", i=128)
                    if not rev:
                        fw.tt(oth.v(ov[:, :, st:]), cur.v(cv[:, :, st:]), cur.v(cv[:, :, :128 - st]), ALU.add)
                        fw.copy(oth.v(ov[:, :, :st]), cur.v(cv[:, :, :st]), eng="gpsimd")
                    else:
                        fw.tt(oth.v(ov[:, :, :128 - st]), cur.v(cv[:, :, :128 - st]), cur.v(cv[:, :, st:]), ALU.add)
                        fw.copy(oth.v(ov[:, :, 128 - st:]), cur.v(cv[:, :, 128 - st:]), eng="gpsimd")
                    cur = oth
                    st *= 2
                    k_ += 1
                assert cur is Aa
                cum = Aa
                cvw = cum.t[:, :].rearrange("p (c i) -> p c i", i=128)
                fw.act(pend_s[:, d, tau, :], cum.v(cvw[:, :, eidx]), AF.Exp)
                fw.tt(Ll[:], cum[:], Ll[:], ALU.subtract)
                fw.ts(Fkd[:], Fa[:], pcol(l, "rw_ka", tau), ALU.mult, pcol(l, "rw_omka", tau), ALU.add)
                fw.tt(Fkd[:], Fkd[:], Fk[:], ALU.mult, eng="gpsimd")
                fw.tt(Fb[:], Fkk[:], Fa[:], ALU.mult, eng="gpsimd")
                fw.stt(Fa[:], Fr[:], pcol(l, "rw_rk", tau), Fkd[:], ALU.mult, ALU.mult)
                for bi, (s, n, is_ctx) in enumerate(blocks):
                    zp = zps[bi % 3]
                    fw.mm(zp[:, :n], blk64[:], Fa[:, s:s + n])
                    if d == 0:
                        fw.copy(Fbon[:, s:s + n], zp[:, :n], eng="scalar")
                    else:
                        fw.tt(Fbon[:, s:s + n], zp[:, :n], Fbon[:, s:s + n], ALU.add)
                si = [0]

                def emit(arr_idx, fn):
                    o = stgb[si[0] % 2]
                    si[0] += 1
                    fn(o)
                    fw.dma(RWS[d, tau, arr_idx], o[:])
                fw.act(Bb[:], Ll[:], AF.Exp)
                for hh in range(2):
                    emit(hh, lambda o, hh=hh: fw.stt(o[:], Fkk[:], hm2[:, 2 + hh:3 + hh], Bb[:], ALU.mult, ALU.mult,
                                                    eng=("vector" if hh == 0 else "gpsimd")))
                fw.act(Bb[:], cum[:], AF.Exp)
                for hh in range(2):
                    emit(2 + hh, lambda o, hh=hh: fw.stt(o[:], Fr[:], hm2[:, hh:hh + 1], Bb[:], ALU.mult, ALU.mult,
                                                        eng=("vector" if hh == 0 else "gpsimd")))
                fw.act(Bb[:], cum[:], AF.Exp, scale=-1.0)
                emit(4, lambda o: fw.tt(o[:], Fb[:], Bb[:], ALU.mult))
                emit(5, lambda o: fw.tt(o[:], Fkd[:], Bb[:], ALU.mult, eng="gpsimd"))
                for c in range(NT):
                    fw.act(Bb[:, c * 128:(c + 1) * 128], cum[:, c * 128:(c + 1) * 128], AF.Exp,
                           bias=cum[:, c * 128 + eidx:c * 128 + eidx + 1], scale=-1.0)
                emit(6, lambda o: fw.tt(o[:], Fb[:], Bb[:], ALU.mult))
                emit(7, lambda o: fw.tt(o[:], Fkd[:], Bb[:], ALU.mult, eng="gpsimd"))
            fw.dma(Ll[:], PT[RW_OFF + 512 + tau * 128:RW_OFF + 512 + (tau + 1) * 128, :])
            shift_mix(Aa, Ll, l, 4 + tau)
            fw.tt(Aa[:], Aa[:], Fbon[:], ALU.mult)
            fw.dma(BON[tau * 128:(tau + 1) * 128, :], Aa[:])
        fw.dma(PENDs[:], pend_s[:])
        fw.pop()

        fw.push()
        pend = fw.sbuf("pend", [128, 2, 2, NT])
        fw.dma(pend[:], PENDs[:])
        Vdup = fw.sbuf("Vdup", [128, NT, 4, 128], BF16)
        fw.dma(Vdup[:], VDs[:])
        yaccT = fw.sbuf("yaccT", [128, 2, T])
        fw.memset(yaccT[:, 0, :], 0.0)
        fw.memset(yaccT[:, 1, :], 0.0, eng="gpsimd")
        I4 = fw.sbuf("I4", [128, 2, 128])
        fw.dma(I4[:], consts["I2"][:])
        identb_ = identb
        PB = [fw.psum(f"PB{i}", [128, 512]) for i in range(6)]
        pbi = [0]

        def bank():
            p = PB[pbi[0] % len(PB)]
            pbi[0] += 1
            return p

        def v4(bk, w=128):
            return bk.t[:, 0:4 * w].rearrange("p (h x) -> p h x", x=w)

        evi = [0]

        def evac(out, in_):
            e = "scalar" if evi[0] % 2 == 0 else "vector"
            evi[0] += 1
            fw.copy(out, in_, eng=e)

        def chunk_gen(d):
            rev = d == 1
            maskN = fw.sbuf(f"maskN{d}", [128, 4, 128])
            maskAB = fw.sbuf(f"maskAB{d}", [128, 2, 256])
            fw.dma(maskN[:], consts[f"maskN_{d}"][:])
            fw.dma(maskAB[:], consts[f"maskAB_{d}"][:])
            CH = [[fw.sbuf(f"CH{d}{i}_{tau}", [128, 8, 128], BF16) for tau in range(2)] for i in range(2)]
            BKt = [fw.sbuf(f"BKt{d}{i}", [128, 4, 384], BF16) for i in range(2)]
            for i in range(2):
                fw.memset(BKt[i][:], 0.0)
            X = [fw.sbuf(f"X{d}{i}", [128, 4, 128], BF16) for i in range(2)]
            XT = [fw.sbuf(f"XT{d}{i}", [128, 4, 128], BF16) for i in range(2)]
            Wt = [fw.sbuf(f"Wt{d}{i}", [128, 4, 128], BF16) for i in range(2)]
            AB = [[fw.sbuf(f"AB{d}{i}_{tau}", [128, 2, 256], BF16) for tau in range(2)] for i in range(2)]
            AK = [[fw.sbuf(f"AK{d}{i}_{tau}", [128, 2, 256], BF16) for tau in range(2)] for i in range(2)]
            Z = fw.sbuf(f"Zz{d}", [128, 4, 64], BF16)
            Udup = fw.sbuf(f"Udup{d}", [128, 4, 128], BF16)
            ST = [fw.sbuf(f"ST{d}{tau}", [128, 64]) for tau in range(2)]
            STd = [fw.sbuf(f"STd{d}{tau}", [128, 128], BF16) for tau in range(2)]
            tpsb = fw.psum(f"tpsb{d}", [128, 4, 128], BF16)
            for tau in range(2):
                fw.memset(ST[tau][:], 0.0)
                fw.memset(STd[tau][:], 0.0)
            order = chunk_order(rev)

            def load(ci):
                c = order[ci]
                cs = slice(c * 128, (c + 1) * 128)
                for tau in range(2):
                    fw.dma(CH[ci % 2][tau][:], RWS.v(RWS.t[d, tau, :, :, cs].rearrange("a p t -> p a t")))
            load(0)
            yield
            for ci, c in enumerate(order):
                cs = slice(c * 128, (c + 1) * 128)
                is_ctx = c < NTC
                want_out = need_ctx or not is_ctx
                ch = CH[ci % 2]
                bkt = BKt[ci % 2]
                if ci + 1 < len(order):
                    load(ci + 1)
                for tau in range(2):
                    fw.transpose(tpsb[:, 2 * tau, :], ch[tau][:, 6, :], identb_[:])
                    fw.transpose(tpsb[:, 2 * tau + 1, :], ch[tau][:, 7, :], identb_[:])
                bv = bkt.t[:, :, :].rearrange("p a (h x) -> p a h x", x=192)
                fw.copy(bkt.v(bv[:, :, :, 0:64]), tpsb.v(tpsb.t[:, :, :].rearrange("p a (h x) -> p a h x", x=64)),
                        eng="scalar")
                nb = bank()
                for h in range(4):
                    tau, hh = h // 2, h % 2
                    fw.mm(nb.v(v4(nb)[:, h, :]), ch[tau][:, hh, :], ch[tau][:, 4, :])
                x0 = X[0]
                fw.tt(x0[:], nb.v(v4(nb)), maskN[:], ALU.mult)
                yield
                ab, ak = AB[ci % 2], AK[ci % 2]
                for tau in range(2):
                    b2, b3 = bank(), bank()
                    for hh in range(2):
                        for (bk, arr) in ((b2, 4), (b3, 5)):
                            o = bk.t[:, :].rearrange("p (h x) -> p h x", x=256)
                            fw.mm(bk.v(o[:, hh, 0:128]), ch[tau][:, arr, :], ch[tau][:, hh, :])
                            fw.mm(bk.v(o[:, hh, 128:256]), ch[tau][:, arr, :], ch[tau][:, 2 + hh, :])
                    fw.tt(ab[tau][:], b2.v(b2.t[:, :].rearrange("p (h x) -> p h x", x=256)), maskAB[:], ALU.mult)
                    fw.tt(ak[tau][:], b3.v(b3.t[:, :].rearrange("p (h x) -> p h x", x=256)), maskAB[:], ALU.mult)
                    yield
                xt0 = XT[0]
                w0 = Wt[0]
                for tau in range(2):
                    fw.copy(xt0[:, 2 * tau:2 * tau + 2, :], ab[tau][:, :, 0:128], eng="gpsimd")
                    fw.tt(w0[:, 2 * tau:2 * tau + 2, :], ab[tau][:, :, 0:128], I4[:], ALU.add, eng="gpsimd")
                xc, xtc, wc = x0, xt0, w0
                for p in range(6):
                    xn, xtn, wn = X[(p + 1) % 2], XT[(p + 1) % 2], Wt[(p + 1) % 2]
                    bx = bank()
                    for h in range(4):
                        fw.mm(bx.v(v4(bx)[:, h, :]), xtc[:, h, :], xc[:, h, :])
                    if p < 5:
                        bxt = bank()
                        for h in range(4):
                            fw.mm(bxt.v(v4(bxt)[:, h, :]), xc[:, h, :], xtc[:, h, :])
                    evac(xn[:], bx.v(v4(bx)))
                    if p < 5:
                        evac(xtn[:], bxt.v(v4(bxt)))
                    yield
                    bw = bank()
                    for h in range(4):
                        fw.mm(bw.v(v4(bw)[:, h, :]), identb_[:], wc[:, h, :], start=True, stop=False)
                        fw.mm(bw.v(v4(bw)[:, h, :]), xn[:, h, :], wc[:, h, :], start=False, stop=True)
                    evac(wn[:], bw.v(v4(bw)))
                    xc, xtc, wc = xn, xtn, wn
                    yield
                wT = wc
                gb = bank()
                g4 = gb.t[:, 0:256].rearrange("p (h x) -> p h x", x=64)
                for h in range(4):
                    tau, hh = h // 2, h % 2
                    fw.mm(gb.v(g4[:, h, :]), ch[tau][:, hh, :], STd[tau][:, 0:64], start=True, stop=False)
                    fw.mm(gb.v(g4[:, h, :]), ak[tau][:, hh, 0:128], Vdup[:, c, h, 0:64], start=False, stop=True)
                fw.copy(Z[:], gb.v(g4), eng="scalar")
                yield
                ub = bank()
                u4 = ub.t[:, 0:256].rearrange("p (h x) -> p h x", x=64)
                for h in range(4):
                    fw.mm(ub.v(u4[:, h, :]), wT[:, h, :], Z[:, h, :])
                fw.copy(Udup[:, :, 0:64], ub.v(u4), eng="vector")
                fw.copy(Udup[:, :, 64:128], ub.v(u4), eng="scalar")
                yield
                if want_out:
                    yb = bank()
                    for h in range(4):
                        tau, hh = h // 2, h % 2
                        o = yb.v(v4(yb)[:, h, :])
                        fw.mm(o, STd[tau][:], ch[tau][:, 2 + hh, :], start=True, stop=False)
                        fw.mm(o, Udup[:, h, :], ab[tau][:, hh, 128:256], start=False, stop=False)
                        fw.mm(o, Vdup[:, c, h, :], ak[tau][:, hh, 128:256], start=False, stop=True)
                    o4 = yb.t[:, :].rearrange("p (a g t) -> p a g t", g=2, t=128)
                    for g in range(2):
                        dst = yaccT[64 * g:64 * g + 64, :, cs]
                        src = yb.v(o4[64 * g:64 * g + 64, :, g, :])
                        fw.tt(dst, src, dst, ALU.add, eng="vector")
                for tau in range(2):
                    sb = bank()
                    for hh in range(2):
                        h = 2 * tau + hh
                        fw.mm(sb[:, 0:64], bkt[:, 2 * tau, hh * 128:(hh + 1) * 128], Udup[:, h, 0:64],
                              start=(hh == 0), stop=False)
                        fw.mm(sb[:, 0:64], bkt[:, 2 * tau + 1, hh * 128:(hh + 1) * 128], Vdup[:, c, h, 0:64],
                              start=False, stop=(hh == 1))
                    fw.stt(ST[tau][:], ST[tau][:], pend[:, d, tau, c:c + 1], sb[:, 0:64], ALU.mult, ALU.add)
                    fw.copy(STd[tau][:, 0:64], ST[tau][:], eng="scalar")
                    fw.copy(STd[tau][:, 64:128], ST[tau][:], eng="gpsimd")
                yield

        gens = [chunk_gen(0), chunk_gen(1)]
        alive = [True, True]
        while any(alive):
            for gi, g in enumerate(gens):
                if alive[gi]:
                    try:
                        next(g)
                    except StopIteration:
                        alive[gi] = False
        epsln = fw.sbuf("epsln", [128, 1])
        fw.memset(epsln[:], 64e-5)
        tb = [fw.sbuf(f"r3_{i}", [128, 512]) for i in range(5)]
        ob = [fw.sbuf(f"rob{i}", [128, 512], BF16) for i in range(2)]
        oi = 0
        qblocks = (ctx_blocks if need_ctx else []) + lat_blocks
        for tau in range(2):
            for (s, n, is_ctx) in qblocks:
                yc, sq, rs, bo, ga = tb
                fw.dma(bo[:, :n], BON[tau * 128:(tau + 1) * 128, s:s + n])
                fw.dma(ga[:, :n], GATE[tau * 128:(tau + 1) * 128, s:s + n])
                mb = bank()
                fw.mm(mb[:, :n], blk64[:], yaccT[:, tau, s:s + n])
                fw.stt(yc[:, :n], mb[:, :n], -1.0 / 64, yaccT[:, tau, s:s + n], ALU.mult, ALU.add)
                fw.act(sq[:, :n], yc[:, :n], AF.Square)
                vb = bank()
                fw.mm(vb[:, :n], blk64[:], sq[:, :n])
                fw.act(rs[:, :n], vb[:, :n], AF.Sqrt, bias=epsln[:, 0:1], scale=1.0 / 64)
                fw.recip(rs[:, :n], rs[:, :n])
                fw.stt(yc[:, :n], yc[:, :n], pcol(l, "rw_ln_g", tau), rs[:, :n], ALU.mult, ALU.mult)
                fw.stt(yc[:, :n], yc[:, :n], pcol(l, "rw_ln_b", tau), bo[:, :n], ALU.add, ALU.add, eng="gpsimd")
                o = ob[oi % 2]; oi += 1
                fw.tt(o[:, :n], yc[:, :n], ga[:, :n], ALU.mult, eng="gpsimd")
                fw.dma(OT[tau * 128:(tau + 1) * 128, s:s + n], o[:, :n])
        fw.pop()

    def mixers_0(l, b, need_ctx):
        if "rwkv" in cfg.mix:
            rwkv_phase(l, b, need_ctx)
        if "gla" in cfg.mix:
            gla_phase(l, b, need_ctx)
        if "gqa" in cfg.mix:
            gqa_phase(l, b, need_ctx)
        if "da" in cfg.mix:
            da_phase(l, b, need_ctx)

    def mixers(l, b, need_ctx):
        if "rwkv" in cfg.mix:
            rwkv_phase(l, b, need_ctx)
        if "da" in cfg.mix:
            da_phase(l, b, need_ctx)
        if "gla" in cfg.mix:
            gla_phase(l, b, need_ctx)
        if "gqa" in cfg.mix:
            gqa_phase(l, b, need_ctx)

    for b in range(NB):
        fw.push()
        xT = fw.sbuf("xT_s", [128, KT, T])
        xv = xT_d.t[b].rearrange("(k p) t -> p k t", p=128)
        for k in range(KT):
            fw.dma(xT[:, k, :], xT_d.v(xv[:, k, :]))
        for l in range(L):
            need_ctx = l < L - 1
            fw.push()
            hT = fw.sbuf("hT", [128, KT, T], BF16)
            sq = fw.sbuf("sq", [128, 512])
            rstd = fw.sbuf("rstd", [128, 512])
            nps = fw.psum("nps", [128, 512])
            norm_phase(xT, hT, l, 0, b, sq, rstd, nps)
            wt = [fw.sbuf(f"wt{i}", [128, KT, 128], BF16) for i in range(3)]
            pps = [fw.psum(f"pps{i}", [128, 512]) for i in range(4)]
            stg = [fw.sbuf(f"stg{i}", [128, 512]) for i in range(4)]
            wv = w_in.t[l].rearrange("(k p) c -> p k c", p=128)
            ei = 0
            for ti, (c0, ncol) in enumerate(mixer_cols()):
                wb = wt[ti % 3]
                fw.dma(wb[:, :, :ncol], w_in.v(wv[:, :, c0:c0 + ncol]), eng="gpsimd")
                for (s, n, is_ctx) in blocks:
                    ps = pps[ei % 4]
                    st = stg[ei % 4]
                    for k in range(KT):
                        fw.mm(ps[:ncol, :n], wb[:, k, :ncol], hT[:, k, s:s + n],
                              start=(k == 0), stop=(k == KT - 1))
                    fw.copy(st[:ncol, :n], ps[:ncol, :n], eng=("vector" if ei % 2 == 0 else "scalar"))
                    fw.dma(PT[c0:c0 + ncol, s:s + n], st[:ncol, :n])
                    ei += 1
            fw.pop()
            if cfg.stop == "proj":
                break
            if cfg.stop == "ffn":
                fw.push()
                tb = fw.sbuf("tb", [128, T])
                tbb = fw.sbuf("tbb", [128, T], BF16)
                for k in range(KT):
                    fw.dma(tb[:], PT[k * 128:(k + 1) * 128, :])
                    fw.copy(tbb[:], tb[:])
                    fw.dma(OT[k * 128:(k + 1) * 128, :], tbb[:])
                fw.pop()
            else:
                mixers(l, b, need_ctx)
            if cfg.stop == "mix":
                break
            wout_phase(xT, l, b, need_ctx)
            ffn_phase(xT, l, b, need_ctx)
            if cfg.stop == "ffn":
                break
        yv = yT_d.t[b].rearrange("(k p) t -> p k t", p=128)
        for k in range(KT):
            fw.dma(yT_d.v(yv[:, k, :]), xT[:, k, TC:T])
        fw.pop()
        if cfg.stop is not None:
            break

    if cfg.stop is not None:
        dbg_pt = fw.dram("dbg_PT", [N_IN, T], F32, kind="ExternalOutput")
        fw.dma(dbg_pt[:], PT[:])
        dbg_ot = fw.dram("dbg_OT", [D, T], BF16, kind="ExternalOutput")
        fw.dma(dbg_ot[:], OT[:])
    fw.pop()
    fw.finish()
    return nc


_NC_CACHE = {}


def kernel(**inputs):
    n_cores = 8
    B = inputs["x"].shape[0]
    NB = B // n_cores
    cfg = Cfg(TC=inputs["ctx"].shape[1], TL=inputs["x"].shape[1], NB=NB, depth=DEPTH)
    nc = build(cfg)
    in_maps = [prep_inputs(inputs, cfg, i * NB) for i in range(n_cores)]
    res = run_bass_kernel_spmd(nc, in_maps, core_ids=list(range(n_cores)))
    out = np.empty((B, cfg.TL, D), np.float32)
    for i in range(n_cores):
        yT = np.asarray(res.results[i]["yT"])
        out[i * NB:(i + 1) * NB] = yT.transpose(0, 2, 1)
    return out


def prep_inputs(inp, cfg, b0):
    NB = cfg.NB
    m = {}
    x = np.asarray(inp["x"], np.float32)[b0:b0 + NB]
    ctx = np.asarray(inp["ctx"], np.float32)[b0:b0 + NB]
    xc = np.concatenate([ctx, x], axis=1)
    m["xT"] = np.ascontiguousarray(xc.transpose(0, 2, 1))
    cvec = np.concatenate([np.asarray(inp["c"], np.float32)[b0:b0 + NB],
                           np.asarray(inp["c_ctx"], np.float32)[None]], axis=0)
    m["cT"] = np.ascontiguousarray(cvec.reshape(NB + 1, KT, 128).transpose(2, 1, 0))
    L = cfg.depth
    m["pack"] = np.stack([host_pack(inp, l) for l in range(L)])
    for nm in ("rw_w2", "rw_a2"):
        m[nm] = np.ascontiguousarray(np.asarray(inp[nm], np.float32)[:L].reshape(L, 128, 256))
    m["rw_g2"] = np.ascontiguousarray(np.asarray(inp["rw_g2"], np.float32)[:L])
    m["lamb"] = np.stack([np.tile(np.asarray(inp["da_lam"], np.float32)[l].reshape(1, 128), (128, 1)) for l in range(L)])
    for nm in ("mod_w", "w_in", "w_out", "ffn_w_up", "ffn_w_down", "gla_a2"):
        m[nm] = np.ascontiguousarray(np.asarray(inp[nm], np.float32)[:L])
    for k, v in host_consts(cfg).items():
        m["c_" + k] = v
    return m
```

```python
import numpy as np
import concourse.bass as bass
import concourse.mybir as mybir
from concourse.bass_utils import run_bass_kernel_spmd

F32 = mybir.dt.float32
BF16 = mybir.dt.bfloat16
AF = mybir.ActivationFunctionType
ALU = mybir.AluOpType
AX = mybir.AxisListType

ENGS = ("tensor", "vector", "scalar", "gpsimd", "sync")


class Trk:
    __slots__ = ("name", "w", "r")

    def __init__(self, name):
        self.name = name
        self.w = None
        self.r = {}


class V:
    __slots__ = ("ap", "trk")

    def __init__(self, ap, trk):
        self.ap = ap
        self.trk = trk


class Buf:
    def __init__(self, t, name):
        self.t = t
        self.name = name
        self.trk = Trk(name)

    def __getitem__(self, idx):
        return V(self.t[idx], self.trk)

    def v(self, ap):
        return V(ap, self.trk)


def _trks(v):
    return v.trk if isinstance(v.trk, (list, tuple)) else (v.trk,)


class FW:
    def __init__(self, nc, n_dma_sems=32):
        self.nc = nc
        self.prog = {e: [] for e in ENGS}
        self.sem = {e: nc.alloc_semaphore(name=f"s_{e}") for e in ENGS}
        self.cnt = {e: 0 for e in ENGS}
        self.waited = {e: {} for e in ENGS}
        self.dsem = [nc.alloc_semaphore(name=f"d_{i}") for i in range(n_dma_sems)]
        self.dcnt = [0] * n_dma_sems
        self.dnext = 0
        self.gnext = 0
        self.semobj = {}
        for e in ENGS:
            self.semobj[("e", e)] = self.sem[e]
        for i, s in enumerate(self.dsem):
            self.semobj[("d", i)] = s
        self.ninst = 0
        self.stack = []

    def push(self):
        self.stack.append([])

    def pop(self):
        self.barrier()
        for g in reversed(self.stack.pop()):
            g.__exit__(None, None, None)

    def sbuf(self, name, shape, dtype=F32):
        self.uid = getattr(self, "uid", 0) + 1
        name = f"{name}_u{self.uid}"
        g = self.nc.sbuf_tensor(name, list(shape), dtype)
        t = g.__enter__()
        self.stack[-1].append(g)
        return Buf(t, name)

    def psum(self, name, shape, dtype=F32):
        self.uid = getattr(self, "uid", 0) + 1
        name = f"{name}_u{self.uid}"
        g = self.nc.psum_tensor(name, list(shape), dtype)
        t = g.__enter__()
        self.stack[-1].append(g)
        return Buf(t, name)

    def dram(self, name, shape, dtype=F32, kind="Internal"):
        return Buf(self.nc.dram_tensor(name, list(shape), dtype, kind=kind).ap(), name)

    def _wait(self, eng, ev):
        if ev is None:
            return
        key, val = ev
        if eng == "tensor" and key == ("e", "tensor"):
            return
        if self.waited[eng].get(key, 0) >= val:
            return
        self.waited[eng][key] = val
        self.prog[eng].append(("wait", key, val))

    def _deps(self, eng, reads, writes):
        for v in reads:
            for t in _trks(v):
                self._wait(eng, t.w)
        for v in writes:
            for t in _trks(v):
                self._wait(eng, t.w)
                for kv in list(t.r.items()):
                    self._wait(eng, kv)

    def _mark(self, ev, reads, writes):
        for v in reads:
            for t in _trks(v):
                if t.r.get(ev[0], 0) < ev[1]:
                    t.r[ev[0]] = ev[1]
        for v in writes:
            for t in _trks(v):
                t.w = ev
                t.r = {}

    def op(self, eng, meth, reads, writes, *args, **kw):
        self._deps(eng, reads, writes)
        self.cnt[eng] += 1
        ev = (("e", eng), self.cnt[eng])
        sem = self.sem[eng]
        a2 = [a.ap if isinstance(a, V) else a for a in args]
        k2 = {k: (a.ap if isinstance(a, V) else a) for k, a in kw.items()}

        def emit(e, inc, wait=None, meth=meth, a2=a2, k2=k2, sem=sem):
            ins = getattr(e, meth)(*a2, **k2)
            if wait is not None:
                ins._wait_ge(wait[0], wait[1])
            if inc:
                ins.then_inc(sem, 1)
        self.prog[eng].append(("op", emit, self.cnt[eng]))
        self._mark(ev, reads, writes)
        self.ninst += 1
        return ev

    def dma(self, out, in_, eng="sync", **kw):
        self._deps(eng, [in_], [out])
        nd = len(self.dsem)
        if eng == "gpsimd":
            k = nd - 8 + self.gnext
            self.gnext = (self.gnext + 1) % 8
        else:
            k = self.dnext
            self.dnext = (self.dnext + 1) % (nd - 8)
        if self.dcnt[k] > 0:
            self._wait(eng, (("d", k), self.dcnt[k]))
        self.dcnt[k] += 16
        ev = (("d", k), self.dcnt[k])
        sem = self.dsem[k]
        oa, ia = out.ap, in_.ap

        def emit(e, oa=oa, ia=ia, sem=sem, kw=kw):
            e.dma_start(out=oa, in_=ia, **kw).then_inc(sem, 16)
        self.prog[eng].append(("dma", emit))
        self._mark(ev, [in_], [out])
        self.ninst += 1
        return ev

    def _all_events(self):
        evs = [(("e", e), self.cnt[e]) for e in ENGS if self.cnt[e] > 0]
        evs += [(("d", i), c) for i, c in enumerate(self.dcnt) if c > 0]
        return evs

    def barrier(self):
        evs = self._all_events()
        for e in ENGS:
            for ev in evs:
                self._wait(e, ev)

    def finish(self):
        for ev in self._all_events():
            self._wait("sync", ev)
        import bisect
        needed = {e: set() for e in ENGS}
        for ename in ENGS:
            for it in self.prog[ename]:
                if it[0] == "wait" and it[1][0] == "e":
                    needed[it[1][1]].add(it[2])
        ranks = {e: sorted(needed[e]) for e in ENGS}
        self.max_sem = {e: len(ranks[e]) for e in ENGS}
        with self.nc.Block() as block:
            for ename in ENGS:
                lst = self.prog[ename]

                def body(e, lst=lst, ename=ename):
                    pending = []
                    for it in lst:
                        if it[0] == "wait":
                            key, val = it[1], it[2]
                            if key[0] == "e":
                                val = bisect.bisect_left(ranks[key[1]], val) + 1
                            pending.append((self.semobj[key], val))
                        elif it[0] == "op":
                            for (sm, vl) in pending[:-1]:
                                e.wait_ge(sm, vl)
                            it[1](e, it[2] in needed[ename], pending[-1] if pending else None)
                            pending = []
                        else:
                            for (sm, vl) in pending:
                                e.wait_ge(sm, vl)
                            pending = []
                            it[1](e)
                    for (sm, vl) in pending:
                        e.wait_ge(sm, vl)
                getattr(block, ename)(body)

    def mm(self, out, lhsT, rhs, start=True, stop=True):
        return self.op("tensor", "matmul", [lhsT, rhs], [out], out, lhsT, rhs, start=start, stop=stop)

    def transpose(self, out, in_, ident):
        return self.op("tensor", "transpose", [in_, ident], [out], out, in_, ident)

    def act(self, out, in_, func, bias=None, scale=None, accum_out=None):
        reads = [in_]
        kw = {}
        if bias is not None:
            kw["bias"] = bias
            if isinstance(bias, V):
                reads.append(bias)
        if scale is not None:
            kw["scale"] = scale
            if isinstance(scale, V):
                reads.append(scale)
        writes = [out]
        if accum_out is not None:
            kw["accum_out"] = accum_out
            writes.append(accum_out)
        return self.op("scalar", "activation", reads, writes, out, in_, func, **kw)

    def tt(self, out, in0, in1, op, eng="vector"):
        return self.op(eng, "tensor_tensor", [in0, in1], [out], out, in0, in1, op)

    def ts(self, out, in0, s1, op0, s2=None, op1=None, eng="vector"):
        reads = [in0] + [s for s in (s1, s2) if isinstance(s, V)]
        if op1 is None:
            return self.op(eng, "tensor_scalar", reads, [out], out, in0, s1, None, op0)
        return self.op(eng, "tensor_scalar", reads, [out], out, in0, s1, s2, op0, op1)

    def stt(self, out, in0, scalar, in1, op0, op1, eng="vector"):
        eng = "vector"
        reads = [in0, in1] + ([scalar] if isinstance(scalar, V) else [])
        return self.op(eng, "scalar_tensor_tensor", reads, [out], out, in0, scalar, in1, op0, op1)

    def copy(self, out, in_, eng="vector"):
        if eng == "scalar":
            return self.op("scalar", "copy", [in_], [out], out, in_)
        return self.op(eng, "tensor_copy", [in_], [out], out, in_)

    def memset(self, out, val, eng="vector"):
        return self.op(eng, "memset", [], [out], out, val)

    def recip(self, out, in_):
        return self.op("vector", "reciprocal", [in_], [out], out, in_)


D = 1024
KT = 8
N_IN = 3232
D_FF = 2816
FT = 22
GRID_W = 64
EPS = 1e-6
RW_OFF, DA_OFF, GLA_OFF, GQA_OFF = 0, 1152, 1920, 2720
DEPTH = 2


class Cfg:
    def __init__(self, TC=256, TL=2048, NB=2, depth=DEPTH, stop=None, mix=("rwkv", "da", "gla", "gqa")):
        self.TC, self.TL, self.NB, self.depth = TC, TL, NB, depth
        self.T = TC + TL
        self.stop = stop
        self.mix = mix

    def blocks(self):
        out = []
        s = 0
        while s < self.TC:
            n = min(512, self.TC - s)
            out.append((s, n, True))
            s += n
        while s < self.T:
            n = min(512, self.T - s)
            out.append((s, n, False))
            s += n
        return out


def pack_layout():
    cols = {}
    n = 0

    def add(name, k):
        nonlocal n
        cols[name] = (n, k)
        n += k
    add("nmg", 8)
    add("nfg", 8)
    add("mod_b", 48)
    add("rw_mu0", 9)
    add("rw_mu1", 9)
    add("rw_c0", 9)
    add("rw_omka", 2)
    add("rw_w0", 4)
    add("rw_a0", 4)
    add("rw_kk", 2)
    add("rw_ka", 2)
    add("rw_rk", 2)
    add("rw_ln_g", 2)
    add("rw_ln_b", 2)
    add("da_qg", 1)
    add("da_kg", 1)
    add("da_sub", 1)
    add("gla_ab", 2)
    add("gla_ng", 1)
    add("gq_qg", 1)
    add("gq_kg", 1)
    add("conv_w", 66)
    add("conv_b", 22)
    return cols, n


PACK, NPACK = pack_layout()


def host_pack(inp, l):
    P = np.zeros((128, NPACK), np.float32)

    def put(name, vec, k):
        c0, kk = PACK[name]
        assert kk == k
        P[:, c0:c0 + k] = np.asarray(vec, np.float32).reshape(k, 128).T
    put("nmg", inp["norm_mix_g"][l], 8)
    put("nfg", inp["norm_ffn_g"][l], 8)
    put("mod_b", inp["mod_b"][l], 48)
    put("rw_mu0", inp["rw_mu"][l, 0], 9)
    put("rw_mu1", inp["rw_mu"][l, 1], 9)
    put("rw_w0", inp["rw_w0"][l].reshape(-1), 4)
    put("rw_a0", inp["rw_a0"][l].reshape(-1), 4)
    put("rw_kk", inp["rw_kk"][l], 2)
    put("rw_ka", inp["rw_ka"][l], 2)
    put("rw_rk", inp["rw_rk"][l].reshape(-1), 2)
    put("rw_ln_g", inp["rw_ln_g"][l], 2)
    put("rw_ln_b", inp["rw_ln_b"][l], 2)
    put("da_qg", np.tile(inp["da_qk_g"][l, 0], 4), 1)
    put("da_kg", np.tile(inp["da_qk_g"][l, 1], 4), 1)
    put("da_sub", np.tile(inp["da_subln_g"][l], 2), 1)
    put("gla_ab", inp["gla_ab"][l].reshape(-1), 2)
    put("gla_ng", np.tile(inp["gla_norm_g"][l], 2), 1)
    put("gq_qg", np.tile(inp["gqa_qk_g"][l, 0], 2), 1)
    put("gq_kg", np.tile(inp["gqa_qk_g"][l, 1], 2), 1)
    put("conv_w", inp["ffn_conv_w"][l].reshape(-1), 66)
    put("conv_b", inp["ffn_conv_b"][l], 22)
    return P


def host_consts(cfg):
    c = {}
    c["ident"] = np.eye(128, dtype=np.float32)
    c["ones"] = np.ones((128, 128), np.float32)
    b64 = np.zeros((128, 128), np.float32)
    b64[:64, :64] = 1
    b64[64:, 64:] = 1
    c["blk64"] = b64
    b32 = np.zeros((128, 128), np.float32)
    for i in range(4):
        b32[32 * i:32 * i + 32, 32 * i:32 * i + 32] = 1
    c["blk32"] = b32
    TL = cfg.TL
    rows = TL // GRID_W
    row = np.repeat(np.arange(rows, dtype=np.float32), GRID_W)
    col = np.tile(np.arange(GRID_W, dtype=np.float32), rows)

    def tables(hd):
        nf = hd // 4
        inv = (10000.0 ** (-np.arange(nf, dtype=np.float32) / nf)).astype(np.float32)
        ang = np.concatenate([row[:, None] * inv, col[:, None] * inv], axis=-1)
        cos, sin = np.cos(ang).astype(np.float32), np.sin(ang).astype(np.float32)
        half = hd // 2
        cosf = np.concatenate([cos, cos], axis=-1)
        sinf = np.concatenate([sin, sin], axis=-1)
        rep = 128 // hd
        cT = np.tile(cosf, (1, rep)).T.copy()
        sT = np.tile(sinf, (1, rep)).T.copy()
        R = np.zeros((128, 128), np.float32)
        for m in range(128):
            if m % hd < half:
                R[m + half, m] = -1.0
            else:
                R[m - half, m] = 1.0
        return cT, sT, R
    hm = np.zeros((128, 4), np.float32)
    for p in range(128):
        hm[p, p // 32] = 1.0
    c["hmask4s"] = hm * np.float32(32 ** -0.5)
    jj, tt_ = np.meshgrid(np.arange(128), np.arange(128), indexing="ij")
    c["tri4_0"] = np.tile((jj <= tt_).astype(np.float32)[:, None, :], (1, 4, 1))
    c["tri4_1"] = np.tile((jj >= tt_).astype(np.float32)[:, None, :], (1, 4, 1))
    h2 = np.zeros((128, 4), np.float32)
    h2[:64, 0] = 1; h2[64:, 1] = 1; h2[:64, 2] = -1; h2[64:, 3] = -1
    c["hm2"] = h2
    c["I2"] = np.tile(np.eye(128, dtype=np.float32)[:, None, :], (1, 2, 1))
    c["maskN_0"] = np.tile((tt_ < jj).astype(np.float32)[:, None, :], (1, 4, 1))
    c["maskN_1"] = np.tile((tt_ > jj).astype(np.float32)[:, None, :], (1, 4, 1))
    sf, inf_ = (jj < tt_).astype(np.float32), (jj <= tt_).astype(np.float32)
    sr, inr = (jj > tt_).astype(np.float32), (jj >= tt_).astype(np.float32)
    c["maskAB_0"] = np.tile(np.concatenate([sf, inf_], 1)[:, None, :], (1, 2, 1))
    c["maskAB_1"] = np.tile(np.concatenate([sr, inr], 1)[:, None, :], (1, 2, 1))
    dm = np.zeros((128, 2), np.float32)
    for p in range(128):
        dm[p, (p % 64) // 32] = 1.0
    c["dmask"] = dm
    c["cos_gq"], c["sin_gq"], c["rot_gq"] = tables(64)
    c["cos_da"], c["sin_da"], c["rot_da"] = tables(32)
    return c


def build(cfg):
    nc = bass.Bass("TRN2", target_bir_lowering=False)
    fw = FW(nc)
    NB, T, TC, TL = cfg.NB, cfg.T, cfg.TC, cfg.TL
    NJ = NB + 1
    L = cfg.depth

    def din(name, shape, dt=F32):
        return fw.dram(name, shape, dt, kind="ExternalInput")

    xT_d = din("xT", [NB, D, T])
    cT_d = din("cT", [128, KT, NJ])
    pack_d = din("pack", [L, 128, NPACK])
    mod_w = din("mod_w", [L, D, 6 * D])
    w_in = din("w_in", [L, D, N_IN])
    w_out = din("w_out", [L, D, D])
    w_up = din("ffn_w_up", [L, D, 2 * D_FF])
    w_down = din("ffn_w_down", [L, D_FF, D])
    consts = {}
    for nm in ("ident", "ones", "blk64", "blk32", "rot_gq", "rot_da"):
        consts[nm] = din("c_" + nm, [128, 128])
    for nm in ("cos_gq", "sin_gq", "cos_da", "sin_da"):
        consts[nm] = din("c_" + nm, [128, TL])
    consts["dmask"] = din("c_dmask", [128, 2])
    lamb_d = din("lamb", [L, 128, 128])
    gla_a2_d = din("gla_a2", [L, 2, 16, 128])
    rw_w2_d = din("rw_w2", [L, 128, 256])
    rw_a2_d = din("rw_a2", [L, 128, 256])
    rw_g2_d = din("rw_g2", [L, 128, 256])
    consts["hm2"] = din("c_hm2", [128, 4])
    consts["I2"] = din("c_I2", [128, 2, 128])
    for d_ in range(2):
        consts[f"maskN_{d_}"] = din(f"c_maskN_{d_}", [128, 4, 128])
        consts[f"maskAB_{d_}"] = din(f"c_maskAB_{d_}", [128, 2, 256])
    consts["hmask4s"] = din("c_hmask4s", [128, 4])
    consts["tri4_0"] = din("c_tri4_0", [128, 4, 128])
    consts["tri4_1"] = din("c_tri4_1", [128, 4, 128])
    yT_d = fw.dram("yT", [NB, D, TL], F32, kind="ExternalOutput")
    dbg = {}

    PT = fw.dram("PT", [N_IN, T], F32)
    OT = fw.dram("OT", [D, T], BF16)
    ACT = fw.dram("ACTs", [D_FF, T], BF16)

    fw.push()
    ident = fw.sbuf("ident", [128, 128])
    ones = fw.sbuf("ones", [128, 128])
    blk64 = fw.sbuf("blk64", [128, 128])
    fw.dma(ident[:], consts["ident"][:])
    fw.dma(ones[:], consts["ones"][:])
    fw.dma(blk64[:], consts["blk64"][:])
    identb = fw.sbuf("identb", [128, 128], BF16)
    fw.copy(identb[:], ident[:])
    onesb_g = fw.sbuf("onesb_g", [128, 128], BF16)
    fw.copy(onesb_g[:], ones[:])
    blk64b = fw.sbuf("blk64b", [128, 128], BF16)
    fw.copy(blk64b[:], blk64[:])
    pk = [fw.sbuf(f"pk{l}", [128, NPACK]) for l in range(L)]
    for l in range(L):
        fw.dma(pk[l][:], pack_d[l])
    modT = [fw.sbuf(f"modT{l}", [128, 48, NJ]) for l in range(L)]
    gs = [fw.sbuf(f"gs{l}", [128, 2, KT, NJ]) for l in range(L)]
    epsb = fw.sbuf("epsb", [128, 1])
    fw.memset(epsb[:], EPS)

    def pcol(l, name, i=0):
        c0, k = PACK[name]
        return pk[l][:, c0 + i:c0 + i + 1]

    fw.push()
    cs = fw.sbuf("cs", [128, KT, NJ])
    fw.dma(cs[:], cT_d[:])
    fw.act(cs[:], cs[:], AF.Silu)
    mps = fw.psum("mps", [128, 48, NJ])
    wm = [fw.sbuf(f"wm{i}", [128, KT, 512]) for i in range(2)]
    for l in range(L):
        mwv = mod_w.t[l].rearrange("(k p) c -> p k c", p=128)
        for g in range(12):
            wb = wm[g % 2]
            fw.dma(wb[:], mod_w.v(mwv[:, :, g * 512:(g + 1) * 512]))
            for ci in range(4):
                ct = g * 4 + ci
                for k in range(KT):
                    fw.mm(mps[:, ct, :], wb[:, k, ci * 128:(ci + 1) * 128], cs[:, k, :],
                          start=(k == 0), stop=(k == KT - 1))
        c0 = PACK["mod_b"][0]
        for j in range(NJ):
            fw.tt(modT[l][:, :, j], mps[:, :, j], pk[l][:, c0:c0 + 48], ALU.add)
        for j in range(NJ):
            c0 = PACK["nmg"][0]
            fw.stt(gs[l][:, 0, :, j], modT[l][:, 8:16, j], 1.0, pk[l][:, c0:c0 + 8], ALU.add, ALU.mult)
            c0 = PACK["nfg"][0]
            fw.stt(gs[l][:, 1, :, j], modT[l][:, 32:40, j], 1.0, pk[l][:, c0:c0 + 8], ALU.add, ALU.mult)
    fw.pop()

    blocks = cfg.blocks()
    if cfg.stop == "mix":
        fw.push()
        zt = fw.sbuf("zt", [128, T], BF16)
        fw.memset(zt[:], 0.0)
        for k in range(KT):
            fw.dma(OT[k * 128:(k + 1) * 128, :], zt[:])
        fw.pop()

    def mixer_cols():
        tl = []
        for i in range(9):
            tl.append((RW_OFF + 128 * i, 128))
        for i in range(6):
            tl.append((DA_OFF + 128 * i, 128))
        for i in range(6):
            tl.append((GLA_OFF + 128 * i, 128))
        tl.append((GLA_OFF + 768, 32))
        for i in range(4):
            tl.append((GQA_OFF + 128 * i, 128))
        return tl

    def norm_phase(xT, hT, l, which, b, sq, rstd, nps):
        shift_base = 0 if which == 0 else 24
        sqb = [fw.sbuf(f"nsqb{i}", [128, 512], BF16) for i in range(3)]
        tmpf = [fw.sbuf(f"ntmp{i}", [128, 512]) for i in range(3)]
        nps2 = fw.psum("nps_b", [128, 512])
        ci = 0
        for bi, (s, n, is_ctx) in enumerate(blocks):
            j = NB if is_ctx else b
            ps = nps if bi % 2 == 0 else nps2
            for k in range(KT):
                q = sqb[ci % 3]
                ci += 1
                fw.act(q[:, :n], xT[:, k, s:s + n], AF.Square)
                fw.mm(ps[:, :n], onesb_g[:], q[:, :n], start=(k == 0), stop=(k == KT - 1))
            rs = sq if bi % 2 == 0 else rstd
            fw.act(rs[:, :n], ps[:, :n], AF.Sqrt, bias=epsb[:, 0:1], scale=1.0 / D)
            fw.recip(rs[:, :n], rs[:, :n])
            for k in range(KT):
                t_ = tmpf[ci % 3]
                ci += 1
                fw.tt(t_[:, :n], xT[:, k, s:s + n], rs[:, :n], ALU.mult)
                fw.act(hT[:, k, s:s + n], t_[:, :n], AF.Identity,
                       bias=modT[l][:, shift_base + k, j:j + 1], scale=gs[l][:, which, k, j:j + 1])

    def wout_phase(xT, l, b, need_ctx):
        fw.push()
        oT = fw.sbuf("oT", [128, KT, T], BF16)
        ov = OT.t.rearrange("(k p) t -> p k t", p=128)
        for k in range(KT):
            fw.dma(oT[:, k, :], OT.v(ov[:, k, :]))
        wt = [fw.sbuf(f"wo{i}", [128, KT, 256], BF16) for i in range(2)]
        pps = [fw.psum(f"ops{i}", [128, 512]) for i in range(4)]
        wv = w_out.t[l].rearrange("(k p) c -> p k c", p=128)
        ei = 0
        for jt in range(KT):
            wb_full = wt[(jt // 2) % 2]
            if jt % 2 == 0:
                fw.dma(wb_full[:], w_out.v(wv[:, :, jt * 128:(jt + 2) * 128]), eng="gpsimd")
            wb = Buf(wb_full.t[:, :, (jt % 2) * 128:(jt % 2 + 1) * 128], "wo_half")
            wb.trk = wb_full.trk
            for (s, n, is_ctx) in blocks:
                if is_ctx and not need_ctx:
                    continue
                j = NB if is_ctx else b
                ps = pps[ei % 4]
                ei += 1
                for k in range(KT):
                    fw.mm(ps[:, :n], wb[:, k, :], oT[:, k, s:s + n], start=(k == 0), stop=(k == KT - 1))
                fw.stt(xT[:, jt, s:s + n], ps[:, :n], modT[l][:, 16 + jt, j:j + 1], xT[:, jt, s:s + n],
                       ALU.mult, ALU.add)
        fw.pop()

    def ffn_phase(xT, l, b, need_ctx):
        segs = ([(0, TC)] if need_ctx else []) + [(TC, T)]
        fblocks = [bl for bl in blocks if (need_ctx or not bl[2])]
        fw.push()
        hT = fw.sbuf("hT2", [128, KT, T], BF16)
        sq = fw.sbuf("sq2", [128, 512])
        rstd = fw.sbuf("rstd2", [128, 512])
        nps = fw.psum("nps2", [128, 512])
        norm_phase(xT, hT, l, 1, b, sq, rstd, nps)
        wu = [fw.sbuf(f"wu{i}", [128, KT, 256], BF16) for i in range(2)]
        wg = [fw.sbuf(f"wg{i}", [128, KT, 256], BF16) for i in range(2)]
        ups = [fw.psum(f"ups{i}", [128, 512]) for i in range(2)]
        gps = [fw.psum(f"gps{i}", [128, 512]) for i in range(2)]
        uT = [fw.sbuf(f"uT{i}", [128, T]) for i in range(2)]
        gT = [fw.sbuf(f"gT{i}", [128, T]) for i in range(2)]
        tmp = [fw.sbuf(f"ftmp{i}", [128, T]) for i in range(2)]
        aT = [fw.sbuf(f"aT{i}", [128, T], BF16) for i in range(2)]
        wv = w_up.t[l].rearrange("(k p) c -> p k c", p=128)
        cw0 = PACK["conv_w"][0]
        cb0 = PACK["conv_b"][0]
        ei = 0
        def wload(p):
            fw.dma(wu[p % 2][:], w_up.v(wv[:, :, p * 256:(p + 1) * 256]), eng="gpsimd")
            fw.dma(wg[p % 2][:], w_up.v(wv[:, :, D_FF + p * 256:D_FF + (p + 1) * 256]), eng="gpsimd")
        wload(0)
        for i in range(FT):
            r = i % 2
            if i % 2 == 1 and i // 2 + 1 < FT // 2:
                wload(i // 2 + 1)
            for (s, n, is_ctx) in fblocks:
                pu, pg = ups[ei % 2], gps[ei % 2]
                ei += 1
                for k in range(KT):
                    fw.mm(pu[:, :n], wu[(i // 2) % 2][:, k, (i % 2) * 128:(i % 2 + 1) * 128], hT[:, k, s:s + n],
                          start=(k == 0), stop=(k == KT - 1))
                for k in range(KT):
                    fw.mm(pg[:, :n], wg[(i // 2) % 2][:, k, (i % 2) * 128:(i % 2 + 1) * 128], hT[:, k, s:s + n],
                          start=(k == 0), stop=(k == KT - 1))
                fw.copy(uT[r][:, s:s + n], pu[:, :n], eng="scalar")
                fw.copy(gT[r][:, s:s + n], pg[:, :n], eng="scalar")
                fw.act(tmp[r][:, s:s + n], pg[:, :n], AF.Identity, bias=pk[l][:, cb0 + i:cb0 + i + 1],
                       scale=pk[l][:, cw0 + FT + i:cw0 + FT + i + 1])
            w0 = pk[l][:, cw0 + i:cw0 + i + 1]
            w1 = pk[l][:, cw0 + FT + i:cw0 + FT + i + 1]
            w2 = pk[l][:, cw0 + 2 * FT + i:cw0 + 2 * FT + i + 1]
            cb = pk[l][:, cb0 + i:cb0 + i + 1]
            for (s, e) in segs:
                fw.stt(tmp[r][:, s + 1:e], gT[r][:, s:e - 1], w0, tmp[r][:, s + 1:e], ALU.mult, ALU.add)
                fw.stt(tmp[r][:, s:e - 1], gT[r][:, s + 1:e], w2, tmp[r][:, s:e - 1], ALU.mult, ALU.add)
                fw.act(tmp[r][:, s:e], tmp[r][:, s:e], AF.Silu)
                fw.tt(aT[r][:, s:e], tmp[r][:, s:e], uT[r][:, s:e], ALU.mult)
                fw.dma(ACT[i * 128:(i + 1) * 128, s:e], aT[r][:, s:e])
        fw.pop()
        fw.push()
        wd = [fw.sbuf(f"wd{jt}", [128, FT, 128], BF16) for jt in range(KT)]
        wdv = w_down.t[l].rearrange("(f p) c -> p f c", p=128)
        for jt in range(KT):
            fw.dma(wd[jt][:], w_down.v(wdv[:, :, jt * 128:(jt + 1) * 128]), eng="gpsimd")
        ab = [fw.sbuf(f"ab{i}", [128, FT, 512], BF16) for i in range(2)]
        dps = [fw.psum(f"dps{i}", [128, 512]) for i in range(4)]
        av = ACT.t.rearrange("(f p) t -> p f t", p=128)
        ei = 0
        for bi, (s, n, is_ctx) in enumerate(fblocks):
            j = NB if is_ctx else b
            a = ab[bi % 2]
            fw.dma(a[:, :, :n], ACT.v(av[:, :, s:s + n]))
            for jt in range(KT):
                ps = dps[ei % 4]
                ei += 1
                for f in range(FT):
                    fw.mm(ps[:, :n], wd[jt][:, f, :], a[:, f, :n], start=(f == 0), stop=(f == FT - 1))
                fw.stt(xT[:, jt, s:s + n], ps[:, :n], modT[l][:, 40 + jt, j:j + 1], xT[:, jt, s:s + n],
                       ALU.mult, ALU.add)
        fw.pop()

    NT = T // 128
    NTC = TC // 128
    lat_blocks = [bl for bl in blocks if not bl[2]]
    ctx_blocks = [bl for bl in blocks if bl[2]]

    def head_norm_rope(raw, outs, blkb, hd, g_ap, rotb, cosT, sinT, s, n, is_ctx, tset, masks=None):
        sqb, rs, qg, t1, t2, nps, npr = tset
        fw.act(sqb[:, :n], raw, AF.Square)
        fw.mm(nps[:, :n], blkb[:], sqb[:, :n])
        fw.act(rs[:, :n], nps[:, :n], AF.Sqrt, bias=epsb[:, 0:1], scale=1.0 / hd)
        fw.recip(rs[:, :n], rs[:, :n])
        fw.stt(qg[:, :n], raw, g_ap, rs[:, :n], ALU.mult, ALU.mult)
        if is_ctx:
            res = qg
        else:
            fw.mm(npr[:, :n], rotb[:], qg[:, :n])
            fw.tt(t1[:, :n], qg[:, :n], cosT[:, s - TC:s - TC + n], ALU.mult)
            fw.tt(t2[:, :n], npr[:, :n], sinT[:, s - TC:s - TC + n], ALU.mult)
            fw.tt(t1[:, :n], t1[:, :n], t2[:, :n], ALU.add, eng="gpsimd")
            res = t1
        if masks is None:
            fw.copy(outs[0], res[:, :n], eng="gpsimd")
        else:
            for m, o in enumerate(outs):
                fw.ts(o, res[:, :n], masks[:, m:m + 1], ALU.mult, eng="gpsimd")

    def prep_sets(tag):
        sets = []
        for i in range(2):
            sets.append((fw.sbuf(f"{tag}sqb{i}", [128, 512], BF16), fw.sbuf(f"{tag}rs{i}", [128, 512]),
                         fw.sbuf(f"{tag}qg{i}", [128, 512], BF16), fw.sbuf(f"{tag}t1{i}", [128, 512]),
                         fw.sbuf(f"{tag}t2{i}", [128, 512]), fw.psum(f"{tag}nps{i}", [128, 512]),
                         fw.psum(f"{tag}npr{i}", [128, 512])))
        return sets

    def make_vdup(vrow0, nheads, Vd, vtmp, tps):
        ntile = (nheads * 64) // 128
        for vt in range(ntile):
            fw.dma(vtmp[:, :], PT[vrow0 + vt * 128:vrow0 + (vt + 1) * 128, :])
            for i in range(NT):
                fw.transpose(tps[:, i % 4, :], vtmp[:, i * 128:(i + 1) * 128], ident[:])
                for hh in range(2):
                    h = vt * 2 + hh
                    fw.copy(Vd[h][:, i, 0:64], tps[:, i % 4, hh * 64:(hh + 1) * 64], eng="vector")
                    fw.copy(Vd[h][:, i, 64:128], tps[:, i % 4, hh * 64:(hh + 1) * 64], eng="scalar")

    def attn_head(qviews, kviews, Vd_h, nmaps, scale, qb, sps_l, pT_l, oacc, dacc, dsum, cnt):
        (s, n, is_ctx) = qb
        kts = list(range(NTC)) if is_ctx else list(range(NT))
        steps = [(ki, kt, m) for ki, kt in enumerate(kts) for m in range(nmaps)]
        c0 = cnt[0]
        cnt[0] += len(steps)

        def score(i):
            ki, kt, m = steps[i]
            sp = sps_l[(c0 + i) % len(sps_l)]
            fw.mm(sp[:, :n], kviews[m](kt), qviews[m](s, n))
        depth = len(sps_l) - 1
        for i in range(min(depth, len(steps))):
            score(i)
        for i, (ki, kt, m) in enumerate(steps):
            if i + depth < len(steps):
                score(i + depth)
            sp = sps_l[(c0 + i) % len(sps_l)]
            pT = pT_l[(c0 + i) % len(pT_l)]
            fw.act(pT[:, :n], sp[:, :n], AF.Exp, scale=scale)
            fw.mm(oacc[m][:, :n], Vd_h[:, kt, :], pT[:, :n], start=(ki == 0), stop=(ki == len(kts) - 1))
            ds = dsum[m]
            de = "vector" if m == 0 else "gpsimd"
            if ki == 0:
                fw.copy(ds[:, :n], pT[:, :n], eng=de)
            else:
                fw.tt(ds[:, :n], pT[:, :n], ds[:, :n], ALU.add, eng=de)
        for m in range(nmaps):
            fw.mm(dacc[m][:, :n], ones[:], dsum[m][:, :n])

    def gqa_phase(l, b, need_ctx):
        fw.push()
        onesb = onesb_g
        qn = fw.sbuf("qn", [128, 2, T], BF16)
        kd = fw.sbuf("kd", [128, 2, T], BF16)
        Vd = [fw.sbuf(f"Vd{h}", [128, NT, 128], BF16) for h in range(2)]
        qblocks = (ctx_blocks if need_ctx else []) + lat_blocks
        fw.push()
        cosT = fw.sbuf("cosT", [128, TL]); sinT = fw.sbuf("sinT", [128, TL]); rot = fw.sbuf("rot", [128, 128])
        fw.dma(cosT[:], consts["cos_gq"][:]); fw.dma(sinT[:], consts["sin_gq"][:]); fw.dma(rot[:], consts["rot_gq"][:])
        rotb = fw.sbuf("rotb", [128, 128], BF16)
        fw.copy(rotb[:], rot[:])
        raw = [fw.sbuf(f"raw{i}", [128, 512]) for i in range(3)]
        tsets = prep_sets("g")
        tps = fw.psum("tps", [128, 4, 128])
        vtmp = fw.sbuf("vtmp", [128, T])
        ri = 0
        for t in range(2):
            for (s, n, is_ctx) in qblocks:
                r = raw[ri % 3]; ri += 1
                fw.dma(r[:, :n], PT[GQA_OFF + t * 128:GQA_OFF + (t + 1) * 128, s:s + n])
                head_norm_rope(r[:, :n], [qn[:, t, s:s + n]], blk64b, 64, pcol(l, "gq_qg"), rotb, cosT, sinT,
                               s, n, is_ctx, tsets[ri % 2])
            for (s, n, is_ctx) in blocks:
                r = raw[ri % 3]; ri += 1
                for hh in range(2):
                    fw.dma(r[hh * 64:(hh + 1) * 64, :n], PT[GQA_OFF + 256 + t * 64:GQA_OFF + 256 + (t + 1) * 64, s:s + n])
                head_norm_rope(r[:, :n], [kd[:, t, s:s + n]], blk64b, 64, pcol(l, "gq_kg"), rotb, cosT, sinT,
                               s, n, is_ctx, tsets[ri % 2])
        make_vdup(GQA_OFF + 384, 2, Vd, vtmp, tps)
        fw.pop()
        sps_l = [fw.psum(f"sps{i}", [128, 512]) for i in range(4)]
        pT_l = [fw.sbuf(f"pT{i}", [128, 512], BF16) for i in range(4)]
        oacc = [fw.psum("oacc0", [128, 512])]
        dacc = [fw.psum("dacc0", [128, 512])]
        dsum_l = [fw.sbuf(f"dsum{i}", [128, 512]) for i in range(2)]
        rec = fw.sbuf("rec", [128, 512])
        ob = [fw.sbuf(f"ob{i}", [128, 512], BF16) for i in range(2)]
        cnt = [0]
        oi = 0
        for h in range(4):
            t, g = h // 2, h % 2
            ph = 64 * g
            qv = [lambda s, n, t=t, ph=ph: qn[ph:ph + 64, t, s:s + n]]
            kv = [lambda kt, t=t, ph=ph: kd[ph:ph + 64, t, kt * 128:(kt + 1) * 128]]
            for qb in qblocks:
                (s, n, is_ctx) = qb
                attn_head(qv, kv, Vd[t], 1, 0.125, qb, sps_l, pT_l, oacc, dacc, [dsum_l[oi % 2]], cnt)
                fw.recip(rec[ph:ph + 64, :n], dacc[0][ph:ph + 64, :n])
                o = ob[oi % 2]; oi += 1
                fw.tt(o[ph:ph + 64, :n], oacc[0][ph:ph + 64, :n], rec[ph:ph + 64, :n], ALU.mult)
                fw.dma(OT[768 + h * 64:768 + (h + 1) * 64, s:s + n], o[ph:ph + 64, :n])
        fw.pop()

    def da_phase(l, b, need_ctx):
        lam_init = 0.8 - 0.6 * float(np.exp(-0.3 * l))
        fw.push()
        onesb = onesb_g
        lamb = fw.sbuf("lamb_s", [128, 128])
        fw.dma(lamb[:], lamb_d[l])
        lt = fw.sbuf("lt", [128, 64])
        lsum = fw.sbuf("lsum", [128, 2])
        nlam = fw.sbuf("nlam", [128, 1])
        sg = fw.sbuf("sg", [128, 1])
        fw.tt(lt[:, 0:32], lamb[:, 0:32], lamb[:, 32:64], ALU.mult)
        fw.tt(lt[:, 32:64], lamb[:, 64:96], lamb[:, 96:128], ALU.mult)
        fw.op("vector", "reduce_sum", [lt[:]], [lsum[:]], lsum[:, 0:1].ap, lt[:, 0:32].ap, AX.X)
        fw.op("vector", "reduce_sum", [lt[:]], [lsum[:]], lsum[:, 1:2].ap, lt[:, 32:64].ap, AX.X)
        fw.act(lsum[:], lsum[:], AF.Exp)
        fw.stt(nlam[:], lsum[:, 1:2], -lam_init, lsum[:, 0:1], ALU.add, ALU.subtract)
        fw.ts(sg[:], pcol(l, "da_sub"), 1.0 - lam_init, ALU.mult)
        qm = [fw.sbuf(f"qm{m}", [128, 2, T], BF16) for m in range(2)]
        kn = fw.sbuf("kn", [128, 2, T], BF16)
        Vd = [fw.sbuf(f"Vd{h}", [128, NT, 128], BF16) for h in range(4)]
        qblocks = (ctx_blocks if need_ctx else []) + lat_blocks
        fw.push()
        cosT = fw.sbuf("cosT", [128, TL]); sinT = fw.sbuf("sinT", [128, TL]); rot = fw.sbuf("rot", [128, 128])
        fw.dma(cosT[:], consts["cos_da"][:]); fw.dma(sinT[:], consts["sin_da"][:]); fw.dma(rot[:], consts["rot_da"][:])
        rotb = fw.sbuf("rotb", [128, 128], BF16)
        fw.copy(rotb[:], rot[:])
        blk32 = fw.sbuf("blk32", [128, 128])
        fw.dma(blk32[:], consts["blk32"][:])
        blk32b = fw.sbuf("blk32b", [128, 128], BF16)
        fw.copy(blk32b[:], blk32[:])
        dmask = fw.sbuf("dmask", [128, 2])
        fw.dma(dmask[:], consts["dmask"][:])
        raw = [fw.sbuf(f"raw{i}", [128, 512]) for i in range(3)]
        tsets = prep_sets("d")
        tps = fw.psum("tps", [128, 4, 128])
        vtmp = fw.sbuf("vtmp", [128, T])
        ri = 0
        for t in range(2):
            for (s, n, is_ctx) in qblocks:
                r = raw[ri % 3]; ri += 1
                fw.dma(r[:, :n], PT[DA_OFF + t * 128:DA_OFF + (t + 1) * 128, s:s + n])
                head_norm_rope(r[:, :n], [qm[0][:, t, s:s + n], qm[1][:, t, s:s + n]], blk32b, 32,
                               pcol(l, "da_qg"), rotb, cosT, sinT, s, n, is_ctx, tsets[ri % 2], masks=dmask)
            for (s, n, is_ctx) in blocks:
                r = raw[ri % 3]; ri += 1
                fw.dma(r[:, :n], PT[DA_OFF + 256 + t * 128:DA_OFF + 256 + (t + 1) * 128, s:s + n])
                head_norm_rope(r[:, :n], [kn[:, t, s:s + n]], blk32b, 32, pcol(l, "da_kg"), rotb, cosT, sinT,
                               s, n, is_ctx, tsets[ri % 2])
        make_vdup(DA_OFF + 512, 4, Vd, vtmp, tps)
        fw.pop()
        nps = fw.psum("anps", [128, 512])
        tmps = [fw.sbuf(f"nt{i}", [128, 512]) for i in range(2)]
        sps_l = [fw.psum(f"sps{i}", [128, 512]) for i in range(3)]
        pT_l = [fw.sbuf(f"pT{i}", [128, 512], BF16) for i in range(4)]
        oacc = [fw.psum(f"oacc{m}", [128, 512]) for m in range(2)]
        dacc = [fw.psum(f"dacc{m}", [128, 512]) for m in range(2)]
        dsum_l = [[fw.sbuf(f"dsum{i}_{m}", [128, 512]) for m in range(2)] for i in range(2)]
        hq = [0]
        rec = [fw.sbuf(f"rec{m}", [128, 512]) for m in range(2)]
        o1 = fw.sbuf("o1", [128, 512])
        osb = fw.sbuf("osb", [128, 512])
        ob = [fw.sbuf(f"ob{i}", [128, 512], BF16) for i in range(2)]
        sq, rs = tmps[0], tmps[1]
        cnt = [0]
        oi = 0
        for t in range(2):
            for qb in qblocks:
                (s, n, is_ctx) = qb
                for g in range(2):
                    h = 2 * t + g
                    ph = 64 * g
                    qv = [lambda s, n, t=t, ph=ph, m=m: qm[m][ph:ph + 64, t, s:s + n] for m in range(2)]
                    kv = [lambda kt, t=t, ph=ph: kn[ph:ph + 64, t, kt * 128:(kt + 1) * 128]] * 2
                    attn_head(qv, kv, Vd[h], 2, 32 ** -0.5, qb, sps_l, pT_l, oacc, dacc, dsum_l[hq[0] % 2], cnt)
                    hq[0] += 1
                    for m in range(2):
                        fw.recip(rec[m][ph:ph + 64, :n], dacc[m][ph:ph + 64, :n])
                    fw.tt(osb[ph:ph + 64, :n], oacc[0][ph:ph + 64, :n], rec[0][ph:ph + 64, :n], ALU.mult)
                    fw.tt(o1[ph:ph + 64, :n], oacc[1][ph:ph + 64, :n], rec[1][ph:ph + 64, :n], ALU.mult)
                    fw.stt(osb[ph:ph + 64, :n], o1[ph:ph + 64, :n], nlam[ph:ph + 64, 0:1], osb[ph:ph + 64, :n],
                           ALU.mult, ALU.add)
                fw.act(sq[:, :n], osb[:, :n], AF.Square)
                fw.mm(nps[:, :n], blk64[:], sq[:, :n])
                fw.act(rs[:, :n], nps[:, :n], AF.Sqrt, bias=epsb[:, 0:1], scale=1.0 / 64)
                fw.recip(rs[:, :n], rs[:, :n])
                o = ob[oi % 2]; oi += 1
                fw.stt(o[:, :n], osb[:, :n], sg[:, 0:1], rs[:, :n], ALU.mult, ALU.mult)
                fw.dma(OT[256 + t * 128:256 + (t + 1) * 128, s:s + n], o[:, :n])
        fw.pop()

    def chunk_order(rev):
        if not rev:
            return list(range(NT))
        return list(range(NTC - 1, -1, -1)) + list(range(NT - 1, NTC - 1, -1))

    def cumsum_chunks(A, B, rev):
        cur, oth = A, B
        s = 1
        while s < 128:
            cv = cur.t[:, :].rearrange("p (c i) -> p c i", i=128)
            ov = oth.t[:, :].rearrange("p (c i) -> p c i", i=128)
            if not rev:
                fw.tt(oth.v(ov[:, :, s:]), cur.v(cv[:, :, s:]), cur.v(cv[:, :, :128 - s]), ALU.add)
                fw.copy(oth.v(ov[:, :, :s]), cur.v(cv[:, :, :s]), eng="gpsimd")
            else:
                fw.tt(oth.v(ov[:, :, :128 - s]), cur.v(cv[:, :, :128 - s]), cur.v(cv[:, :, s:]), ALU.add)
                fw.copy(oth.v(ov[:, :, 128 - s:]), cur.v(cv[:, :, 128 - s:]), eng="gpsimd")
            cur, oth = oth, cur
            s *= 2
        return cur, oth

    def gla_phase(l, b, need_ctx):
        fw.push()
        Fb = [fw.sbuf(f"F{i}", [128, T]) for i in range(4)]
        F1, F2, F3, F4 = Fb
        Vdup = fw.sbuf("Vdup", [128, NT, 4, 128], BF16)
        QM = [fw.sbuf(f"QM{h}", [128, T], BF16) for h in range(4)]
        KTt = fw.sbuf("KTt", [128, T], BF16)
        KH = fw.sbuf("KH", [128, T], BF16)
        oaccT = fw.sbuf("oaccT", [128, 2, T])
        Pend = fw.sbuf("Pend", [128, NT])
        gfb = fw.sbuf("gfb", [48, T])
        a2 = fw.sbuf("a2", [48, 128])
        nab = fw.sbuf("nab", [128, 2])
        hm4 = fw.sbuf("hm4", [128, 4])
        tri = fw.sbuf("tri", [128, 4, 128])
        fw.dma(hm4[:], consts["hmask4s"][:])
        for d in range(2):
            fw.dma(a2[32 * d:32 * d + 16, :], gla_a2_d[l, d])
            fw.dma(gfb[32 * d:32 * d + 16, :], PT[GLA_OFF + 512 + 16 * d:GLA_OFF + 528 + 16 * d, :])
        c0 = PACK["gla_ab"][0]
        fw.ts(nab[:], pk[l][:, c0:c0 + 2], -1.0, ALU.mult)
        S = fw.sbuf("Sst", [128, 64])
        Sdup = fw.sbuf("Sdup", [128, 128], BF16)
        KHt = [fw.sbuf(f"KHt{i}", [128, 640], BF16) for i in range(2)]
        for i in range(2):
            fw.memset(KHt[i][:], 0.0)
        AT = [fw.sbuf(f"AT{i}", [128, 4, 128], BF16) for i in range(2)]
        tpsb = fw.psum("tpsb", [128, 4, 128], BF16)
        tps = fw.psum("tps", [128, 4, 128])
        aps = [fw.psum(f"aps{i}", [128, 4, 128]) for i in range(2)]
        ops = [fw.psum(f"ops{i}", [128, 4, 128]) for i in range(2)]
        sps = fw.psum("sps", [128, 64])
        zps = fw.psum("zps", [128, 512])
        for vt in range(2):
            fw.dma(F1[:], PT[GLA_OFF + 256 + vt * 128:GLA_OFF + 256 + (vt + 1) * 128, :])
            for i in range(NT):
                fw.transpose(tps[:, i % 4, :], F1[:, i * 128:(i + 1) * 128], ident[:])
                for hh in range(2):
                    h = vt * 2 + hh
                    fw.copy(Vdup[:, i, h, 0:64], tps[:, i % 4, hh * 64:(hh + 1) * 64], eng="vector")
                    fw.copy(Vdup[:, i, h, 64:128], tps[:, i % 4, hh * 64:(hh + 1) * 64], eng="scalar")
        fw.dma(F1[:], PT[GLA_OFF:GLA_OFF + 128, :])
        fw.dma(F2[:], PT[GLA_OFF + 128:GLA_OFF + 256, :])
        for d in range(2):
            rev = d == 1
            fw.dma(tri[:], consts[f"tri4_{d}"][:])
            for (s, n, is_ctx) in blocks:
                fw.mm(zps[:, :n], a2[32 * d:32 * d + 16, :], gfb[32 * d:32 * d + 16, s:s + n])
                fw.act(F3[:, s:s + n], zps[:, :n], AF.Exp, bias=nab[:, d:d + 1], scale=-1.0)
            fw.act(F3[:], F3[:], AF.Ln, bias=1.0)
            fw.ts(F3[:], F3[:], -1.0 / 16.0, ALU.mult)
            bb, ff = cumsum_chunks(F3, F4, rev)
            bv = bb.t[:, :].rearrange("p (c i) -> p c i", i=128)
            eidx = 0 if rev else 127
            fw.act(Pend[:], bb.v(bv[:, :, eidx]), AF.Exp)
            fw.act(ff[:], bb[:], AF.Exp)
            for h in range(4):
                fw.stt(QM[h][:], F1[:], hm4[:, h:h + 1], ff[:], ALU.mult, ALU.mult,
                       eng=("gpsimd" if h % 2 else "vector"))
            fw.act(ff[:], bb[:], AF.Exp, scale=-1.0)
            fw.tt(KTt[:], F2[:], ff[:], ALU.mult)
            for c in range(NT):
                fw.act(ff[:, c * 128:(c + 1) * 128], bb[:, c * 128:(c + 1) * 128], AF.Exp,
                       bias=bb[:, c * 128 + eidx:c * 128 + eidx + 1], scale=-1.0)
            fw.tt(KH[:], F2[:], ff[:], ALU.mult, eng="gpsimd")
            first = True
            for ci, c in enumerate(chunk_order(rev)):
                cs = slice(c * 128, (c + 1) * 128)
                is_ctx = c < NTC
                kht = KHt[ci % 2]
                at = AT[ci % 2]
                ap_, op_ = aps[ci % 2], ops[ci % 2]
                fw.transpose(tpsb[:, 0, :], KH[:, cs], identb[:])
                kv = kht.t[:, :].rearrange("p (h x) -> p h x", x=160)
                fw.copy(kht.v(kv[:, :, 0:32]), tpsb.v(tpsb.t[:, 0, :].rearrange("p (h x) -> p h x", x=32)))
                want_out = need_ctx or not is_ctx
                if want_out:
                    for h in range(4):
                        fw.mm(ap_[:, h, :], KTt[:, cs], QM[h][:, cs])
                    fw.tt(at[:], ap_[:], tri[:], ALU.mult)
                    for h in range(4):
                        if not first:
                            fw.mm(op_[:, h, :], Sdup[:], QM[h][:, cs], start=True, stop=False)
                        fw.mm(op_[:, h, :], Vdup[:, c, h, :], at[:, h, :], start=first, stop=True)
                    o4 = op_.t[:, :, :].rearrange("p (a g) t -> p a g t", g=2)
                    for g in range(2):
                        dst = oaccT[64 * g:64 * g + 64, :, cs]
                        src = op_.v(o4[64 * g:64 * g + 64, :, g, :])
                        if d == 0:
                            fw.copy(dst, src, eng=("vector" if g == 0 else "scalar"))
                        else:
                            fw.tt(dst, src, dst, ALU.add, eng="vector")
                for h in range(4):
                    fw.mm(sps[:, :], kht[:, h * 128:(h + 1) * 128], Vdup[:, c, h, 0:64], start=(h == 0), stop=(h == 3))
                if first:
                    fw.copy(S[:], sps[:])
                else:
                    fw.stt(S[:], S[:], Pend[:, c:c + 1], sps[:], ALU.mult, ALU.add)
                fw.copy(Sdup[:, 0:64], S[:], eng="scalar")
                fw.copy(Sdup[:, 64:128], S[:], eng="gpsimd")
                first = False
        sq, rs, rr = F3, F4, F1
        ob = [fw.sbuf(f"gob{i}", [128, 512], BF16) for i in range(2)]
        oi = 0
        qblocks = (ctx_blocks if need_ctx else []) + lat_blocks
        for t in range(2):
            for (s, n, is_ctx) in qblocks:
                fw.dma(rr[:, :n], PT[GLA_OFF + 544 + t * 128:GLA_OFF + 544 + (t + 1) * 128, s:s + n])
                fw.act(rr[:, :n], rr[:, :n], AF.Silu)
                fw.act(sq[:, :n], oaccT[:, t, s:s + n], AF.Square)
                fw.mm(zps[:, :n], blk64[:], sq[:, :n])
                fw.act(rs[:, :n], zps[:, :n], AF.Sqrt, bias=epsb[:, 0:1], scale=1.0 / 64)
                fw.recip(rs[:, :n], rs[:, :n])
                fw.stt(sq[:, :n], oaccT[:, t, s:s + n], pcol(l, "gla_ng"), rs[:, :n], ALU.mult, ALU.mult)
                o = ob[oi % 2]; oi += 1
                fw.tt(o[:, :n], sq[:, :n], rr[:, :n], ALU.mult, eng="gpsimd")
                fw.dma(OT[512 + t * 128:512 + (t + 1) * 128, s:s + n], o[:, :n])
        fw.pop()

    RWS = fw.dram("RWS", [2, 2, 8, 128, T], BF16)
    VDs = fw.dram("VDs", [128, NT, 4, 128], BF16)
    BON = fw.dram("BON", [256, T], F32)
    PENDs = fw.dram("PENDs", [128, 2, 2, NT], F32)
    GATE = fw.dram("GATE", [256, T], F32)

    def shift_mix(dst, raw, l, ct):
        m0 = pcol(l, "rw_mu0", ct)
        m1 = pcol(l, "rw_mu1", ct)
        c0 = pcol(l, "rw_c0", ct)
        fw.ts(dst[:, :], raw[:, :], c0, ALU.mult)
        for (s, e) in ((0, TC), (TC, T)):
            fw.stt(dst[:, s + 1:e], raw[:, s:e - 1], m0, dst[:, s + 1:e], ALU.mult, ALU.add)
            fw.stt(dst[:, s:e - 1], raw[:, s + 1:e], m1, dst[:, s:e - 1], ALU.mult, ALU.add, eng="gpsimd")

    def rwkv_phase(l, b, need_ctx):
        fw.push()
        lora = [fw.sbuf(f"lora{i}", [128, T], BF16) for i in range(3)]
        w2s = fw.sbuf("w2s", [128, 256], BF16)
        a2s = fw.sbuf("a2s", [128, 256], BF16)
        g2s = fw.sbuf("g2s", [128, 256], BF16)
        fw.dma(w2s[:], rw_w2_d[l], eng="gpsimd")
        fw.dma(a2s[:], rw_a2_d[l], eng="gpsimd")
        fw.dma(g2s[:], rw_g2_d[l], eng="gpsimd")
        hm2 = fw.sbuf("hm2", [128, 4])
        fw.dma(hm2[:], consts["hm2"][:])
        c0 = PACK["rw_mu0"][0]
        c1 = PACK["rw_mu1"][0]
        cc = PACK["rw_c0"][0]
        fw.tt(pk[l][:, cc:cc + 9], pk[l][:, c0:c0 + 9], pk[l][:, c1:c1 + 9], ALU.add)
        fw.ts(pk[l][:, cc:cc + 9], pk[l][:, cc:cc + 9], -1.0, ALU.mult, 1.0, ALU.add)
        ck = PACK["rw_ka"][0]
        co = PACK["rw_omka"][0]
        fw.ts(pk[l][:, co:co + 2], pk[l][:, ck:ck + 2], -1.0, ALU.mult, 1.0, ALU.add)
        Bf = [fw.sbuf(f"B{i}", [128, T]) for i in range(10)]
        stgb = [fw.sbuf(f"stgb{i}", [128, T], BF16) for i in range(2)]
        zps = [fw.psum(f"zps{i}", [128, 512]) for i in range(3)]
        tps = fw.psum("tps", [128, 4, 128])
        vd = [fw.sbuf(f"vd{i}", [128, 4, 128], BF16) for i in range(2)]
        eps12 = fw.sbuf("eps12", [128, 1])
        fw.memset(eps12[:], 1e-12)
        raw, sh = Bf[0], Bf[1]
        for i, fn in ((0, AF.Tanh), (1, None), (2, AF.Sigmoid)):
            fw.dma(raw[:], PT[RW_OFF + (6 + i) * 128:RW_OFF + (7 + i) * 128, :])
            shift_mix(sh, raw, l, 6 + i)
            if fn is None:
                fw.copy(lora[i][:], sh[:])
            else:
                fw.act(lora[i][:], sh[:], fn)
        for vt in range(2):
            fw.dma(raw[:], PT[RW_OFF + 512 + vt * 128:RW_OFF + 512 + (vt + 1) * 128, :])
            shift_mix(sh, raw, l, 4 + vt)
            for i in range(NT):
                fw.transpose(tps[:, i % 4, :], sh[:, i * 128:(i + 1) * 128], ident[:])
                v_ = vd[i % 2]
                for hh in range(2):
                    fw.copy(v_[:, hh, 0:64], tps[:, i % 4, hh * 64:(hh + 1) * 64], eng="vector")
                    fw.copy(v_[:, hh, 64:128], tps[:, i % 4, hh * 64:(hh + 1) * 64], eng="scalar")
                fw.dma(VDs[:, i, 2 * vt:2 * vt + 2, :], v_[:, 0:2, :])
        Pend = fw.dram("Pend_d", [2, 2, 128, NT], F32) if False else None
        pend_s = fw.sbuf("pend_s", [128, 2, 2, NT])
        Fk, Fkk, Fr, Fbon, Ll, Aa, Bb, Fa, Fkd, Fb = Bf
        for tau in range(2):
            fw.dma(raw[:], PT[RW_OFF + 256 + tau * 128:RW_OFF + 256 + (tau + 1) * 128, :]) if False else None
            fw.dma(Ll[:], PT[RW_OFF + 256 + tau * 128:RW_OFF + 256 + (tau + 1) * 128, :])
            shift_mix(Fk, Ll, l, 2 + tau)
            fw.dma(Ll[:], PT[RW_OFF + tau * 128:RW_OFF + (tau + 1) * 128, :])
            shift_mix(Fr, Ll, l, tau)
            fw.ts(Fkk[:], Fk[:], pcol(l, "rw_kk", tau), ALU.mult)
            for (s, n, is_ctx) in blocks:
                zp = zps[0]
                fw.act(Aa[:, s:s + n], Fkk[:, s:s + n], AF.Square)
                fw.mm(zp[:, :n], blk64[:], Aa[:, s:s + n])
                fw.act(Aa[:, s:s + n], zp[:, :n], AF.Sqrt, bias=eps12[:, 0:1], scale=1.0)
            fw.recip(Aa[:], Aa[:])
            fw.tt(Fkk[:], Fkk[:], Aa[:], ALU.mult)
            for bi, (s, n, is_ctx) in enumerate(blocks):
                zp = zps[bi % 3]
                fw.mm(zp[:, :n], g2s[:, tau * 128:(tau + 1) * 128], lora[2][:, s:s + n])
                fw.copy(Aa[:, s:s + n], zp[:, :n], eng="scalar")
            fw.dma(GATE[tau * 128:(tau + 1) * 128, :], Aa[:])
            for d in range(2):
                rev = d == 1
                eidx = 0 if rev else 127
                ph = 64 * d
                for bi, (s, n, is_ctx) in enumerate(blocks):
                    zp = zps[bi % 3]
                    fw.mm(zp[:, :n], w2s[ph:ph + 64, tau * 128:(tau + 1) * 128], lora[0][ph:ph + 64, s:s + n])
                    fw.act(Ll[:, s:s + n], zp[:, :n], AF.Sigmoid, bias=pcol(l, "rw_w0", d * 2 + tau))
                    zp2 = zps[(bi + 1) % 3]
                    fw.mm(zp2[:, :n], a2s[ph:ph + 64, tau * 128:(tau + 1) * 128], lora[1][ph:ph + 64, s:s + n])
                    fw.act(Fa[:, s:s + n], zp2[:, :n], AF.Sigmoid, bias=pcol(l, "rw_a0", d * 2 + tau))
                fw.ts(Ll[:], Ll[:], -0.6065306597126334, ALU.mult)
                cur = Ll
                pp = [Aa, Bb]
                st = 1
                k_ = 0
                while st < 128:
                    oth = pp[k_ % 2]
                    cv = cur.t[:, :].rearrange("p (c i) -> p c i", i=128)
                    ov = oth.t[:, :].rearrange("p (c i) -> p c i", i=128)
                    if not rev:
                        fw.tt(oth.v(ov[:, :, st:]), cur.v(cv[:, :, st:]), cur.v(cv[:, :, :128 - st]), ALU.add)
                        fw.copy(oth.v(ov[:, :, :st]), cur.v(cv[:, :, :st]), eng="gpsimd")
                    else:
                        fw.tt(oth.v(ov[:, :, :128 - st]), cur.v(cv[:, :, :128 - st]), cur.v(cv[:, :, st:]), ALU.add)
                        fw.copy(oth.v(ov[:, :, 128 - st:]), cur.v(cv[:, :, 128 - st:]), eng="gpsimd")
                    cur = oth
                    st *= 2
                    k_ += 1
                assert cur is Aa
                cum = Aa
                cvw = cum.t[:, :].rearrange("p (c i) -> p c i", i=128)
                fw.act(pend_s[:, d, tau, :], cum.v(cvw[:, :, eidx]), AF.Exp)
                fw.tt(Ll[:], cum[:], Ll[:], ALU.subtract)
                fw.ts(Fkd[:], Fa[:], pcol(l, "rw_ka", tau), ALU.mult, pcol(l, "rw_omka", tau), ALU.add)
                fw.tt(Fkd[:], Fkd[:], Fk[:], ALU.mult, eng="gpsimd")
                fw.tt(Fb[:], Fkk[:], Fa[:], ALU.mult, eng="gpsimd")
                fw.stt(Fa[:], Fr[:], pcol(l, "rw_rk", tau), Fkd[:], ALU.mult, ALU.mult)
                for bi, (s, n, is_ctx) in enumerate(blocks):
                    zp = zps[bi % 3]
                    fw.mm(zp[:, :n], blk64[:], Fa[:, s:s + n])
                    if d == 0:
                        fw.copy(Fbon[:, s:s + n], zp[:, :n], eng="scalar")
                    else:
                        fw.tt(Fbon[:, s:s + n], zp[:, :n], Fbon[:, s:s + n], ALU.add)
                si = [0]

                def emit(arr_idx, fn):
                    o = stgb[si[0] % 2]
                    si[0] += 1
                    fn(o)
                    fw.dma(RWS[d, tau, arr_idx], o[:])
                fw.act(Bb[:], Ll[:], AF.Exp)
                for hh in range(2):
                    emit(hh, lambda o, hh=hh: fw.stt(o[:], Fkk[:], hm2[:, 2 + hh:3 + hh], Bb[:], ALU.mult, ALU.mult,
                                                    eng=("vector" if hh == 0 else "gpsimd")))
                fw.act(Bb[:], cum[:], AF.Exp)
                for hh in range(2):
                    emit(2 + hh, lambda o, hh=hh: fw.stt(o[:], Fr[:], hm2[:, hh:hh + 1], Bb[:], ALU.mult, ALU.mult,
                                                        eng=("vector" if hh == 0 else "gpsimd")))
                fw.act(Bb[:], cum[:], AF.Exp, scale=-1.0)
                emit(4, lambda o: fw.tt(o[:], Fb[:], Bb[:], ALU.mult))
                emit(5, lambda o: fw.tt(o[:], Fkd[:], Bb[:], ALU.mult, eng="gpsimd"))
                for c in range(NT):
                    fw.act(Bb[:, c * 128:(c + 1) * 128], cum[:, c * 128:(c + 1) * 128], AF.Exp,
                           bias=cum[:, c * 128 + eidx:c * 128 + eidx + 1], scale=-1.0)
                emit(6, lambda o: fw.tt(o[:], Fb[:], Bb[:], ALU.mult))
                emit(7, lambda o: fw.tt(o[:], Fkd[:], Bb[:], ALU.mult, eng="gpsimd"))
            fw.dma(Ll[:], PT[RW_OFF + 512 + tau * 128:RW_OFF + 512 + (tau + 1) * 128, :])
            shift_mix(Aa, Ll, l, 4 + tau)
            fw.tt(Aa[:], Aa[:], Fbon[:], ALU.mult)
            fw.dma(BON[tau * 128:(tau + 1) * 128, :], Aa[:])
        fw.dma(PENDs[:], pend_s[:])
        fw.pop()

        fw.push()
        pend = fw.sbuf("pend", [128, 2, 2, NT])
        fw.dma(pend[:], PENDs[:])
        Vdup = fw.sbuf("Vdup", [128, NT, 4, 128], BF16)
        fw.dma(Vdup[:], VDs[:])
        yaccT = fw.sbuf("yaccT", [128, 2, T])
        fw.memset(yaccT[:, 0, :], 0.0)
        fw.memset(yaccT[:, 1, :], 0.0, eng="gpsimd")
        I4 = fw.sbuf("I4", [128, 2, 128])
        fw.dma(I4[:], consts["I2"][:])
        identb_ = identb
        PB = [fw.psum(f"PB{i}", [128, 512]) for i in range(6)]
        pbi = [0]

        def bank():
            p = PB[pbi[0] % len(PB)]
            pbi[0] += 1
            return p

        def v4(bk, w=128):
            return bk.t[:, 0:4 * w].rearrange("p (h x) -> p h x", x=w)

        evi = [0]

        def evac(out, in_):
            e = "scalar" if evi[0] % 2 == 0 else "vector"
            evi[0] += 1
            fw.copy(out, in_, eng=e)

        def chunk_gen(d):
            rev = d == 1
            maskN = fw.sbuf(f"maskN{d}", [128, 4, 128])
            maskAB = fw.sbuf(f"maskAB{d}", [128, 2, 256])
            fw.dma(maskN[:], consts[f"maskN_{d}"][:])
            fw.dma(maskAB[:], consts[f"maskAB_{d}"][:])
            CH = [[fw.sbuf(f"CH{d}{i}_{tau}", [128, 8, 128], BF16) for tau in range(2)] for i in range(2)]
            BKt = [fw.sbuf(f"BKt{d}{i}", [128, 4, 384], BF16) for i in range(2)]
            for i in range(2):
                fw.memset(BKt[i][:], 0.0)
            X = [fw.sbuf(f"X{d}{i}", [128, 4, 128], BF16) for i in range(2)]
            XT = [fw.sbuf(f"XT{d}{i}", [128, 4, 128], BF16) for i in range(2)]
            Wt = [fw.sbuf(f"Wt{d}{i}", [128, 4, 128], BF16) for i in range(2)]
            AB = [[fw.sbuf(f"AB{d}{i}_{tau}", [128, 2, 256], BF16) for tau in range(2)] for i in range(2)]
            AK = [[fw.sbuf(f"AK{d}{i}_{tau}", [128, 2, 256], BF16) for tau in range(2)] for i in range(2)]
            Z = fw.sbuf(f"Zz{d}", [128, 4, 64], BF16)
            Udup = fw.sbuf(f"Udup{d}", [128, 4, 128], BF16)
            ST = [fw.sbuf(f"ST{d}{tau}", [128, 64]) for tau in range(2)]
            STd = [fw.sbuf(f"STd{d}{tau}", [128, 128], BF16) for tau in range(2)]
            tpsb = fw.psum(f"tpsb{d}", [128, 4, 128], BF16)
            for tau in range(2):
                fw.memset(ST[tau][:], 0.0)
                fw.memset(STd[tau][:], 0.0)
            order = chunk_order(rev)

            def load(ci):
                c = order[ci]
                cs = slice(c * 128, (c + 1) * 128)
                for tau in range(2):
                    fw.dma(CH[ci % 2][tau][:], RWS.v(RWS.t[d, tau, :, :, cs].rearrange("a p t -> p a t")))
            load(0)
            yield
            for ci, c in enumerate(order):
                cs = slice(c * 128, (c + 1) * 128)
                is_ctx = c < NTC
                want_out = need_ctx or not is_ctx
                ch = CH[ci % 2]
                bkt = BKt[ci % 2]
                if ci + 1 < len(order):
                    load(ci + 1)
                for tau in range(2):
                    fw.transpose(tpsb[:, 2 * tau, :], ch[tau][:, 6, :], identb_[:])
                    fw.transpose(tpsb[:, 2 * tau + 1, :], ch[tau][:, 7, :], identb_[:])
                bv = bkt.t[:, :, :].rearrange("p a (h x) -> p a h x", x=192)
                fw.copy(bkt.v(bv[:, :, :, 0:64]), tpsb.v(tpsb.t[:, :, :].rearrange("p a (h x) -> p a h x", x=64)),
                        eng="scalar")
                nb = bank()
                for h in range(4):
                    tau, hh = h // 2, h % 2
                    fw.mm(nb.v(v4(nb)[:, h, :]), ch[tau][:, hh, :], ch[tau][:, 4, :])
                x0 = X[0]
                fw.tt(x0[:], nb.v(v4(nb)), maskN[:], ALU.mult)
                yield
                ab, ak = AB[ci % 2], AK[ci % 2]
                for tau in range(2):
                    b2, b3 = bank(), bank()
                    for hh in range(2):
                        for (bk, arr) in ((b2, 4), (b3, 5)):
                            o = bk.t[:, :].rearrange("p (h x) -> p h x", x=256)
                            fw.mm(bk.v(o[:, hh, 0:128]), ch[tau][:, arr, :], ch[tau][:, hh, :])
                            fw.mm(bk.v(o[:, hh, 128:256]), ch[tau][:, arr, :], ch[tau][:, 2 + hh, :])
                    fw.tt(ab[tau][:], b2.v(b2.t[:, :].rearrange("p (h x) -> p h x", x=256)), maskAB[:], ALU.mult)
                    fw.tt(ak[tau][:], b3.v(b3.t[:, :].rearrange("p (h x) -> p h x", x=256)), maskAB[:], ALU.mult)
                    yield
                xt0 = XT[0]
                w0 = Wt[0]
                for tau in range(2):
                    fw.copy(xt0[:, 2 * tau:2 * tau + 2, :], ab[tau][:, :, 0:128], eng="gpsimd")
                    fw.tt(w0[:, 2 * tau:2 * tau + 2, :], ab[tau][:, :, 0:128], I4[:], ALU.add, eng="gpsimd")
                xc, xtc, wc = x0, xt0, w0
                for p in range(6):
                    xn, xtn, wn = X[(p + 1) % 2], XT[(p + 1) % 2], Wt[(p + 1) % 2]
                    bx = bank()
                    for h in range(4):
                        fw.mm(bx.v(v4(bx)[:, h, :]), xtc[:, h, :], xc[:, h, :])
                    if p < 5:
                        bxt = bank()
                        for h in range(4):
                            fw.mm(bxt.v(v4(bxt)[:, h, :]), xc[:, h, :], xtc[:, h, :])
                    evac(xn[:], bx.v(v4(bx)))
                    if p < 5:
                        evac(xtn[:], bxt.v(v4(bxt)))
                    yield
                    bw = bank()
                    for h in range(4):
                        fw.mm(bw.v(v4(bw)[:, h, :]), identb_[:], wc[:, h, :], start=True, stop=False)
                        fw.mm(bw.v(v4(bw)[:, h, :]), xn[:, h, :], wc[:, h, :], start=False, stop=True)
                    evac(wn[:], bw.v(v4(bw)))
                    xc, xtc, wc = xn, xtn, wn
                    yield
                wT = wc
                gb = bank()
                g4 = gb.t[:, 0:256].rearrange("p (h x) -> p h x", x=64)
                for h in range(4):
                    tau, hh = h // 2, h % 2
                    fw.mm(gb.v(g4[:, h, :]), ch[tau][:, hh, :], STd[tau][:, 0:64], start=True, stop=False)
                    fw.mm(gb.v(g4[:, h, :]), ak[tau][:, hh, 0:128], Vdup[:, c, h, 0:64], start=False, stop=True)
                fw.copy(Z[:], gb.v(g4), eng="scalar")
                yield
                ub = bank()
                u4 = ub.t[:, 0:256].rearrange("p (h x) -> p h x", x=64)
                for h in range(4):
                    fw.mm(ub.v(u4[:, h, :]), wT[:, h, :], Z[:, h, :])
                fw.copy(Udup[:, :, 0:64], ub.v(u4), eng="vector")
                fw.copy(Udup[:, :, 64:128], ub.v(u4), eng="scalar")
                yield
                if want_out:
                    yb = bank()
                    for h in range(4):
                        tau, hh = h // 2, h % 2
                        o = yb.v(v4(yb)[:, h, :])
                        fw.mm(o, STd[tau][:], ch[tau][:, 2 + hh, :], start=True, stop=False)
                        fw.mm(o, Udup[:, h, :], ab[tau][:, hh, 128:256], start=False, stop=False)
                        fw.mm(o, Vdup[:, c, h, :], ak[tau][:, hh, 128:256], start=False, stop=True)
                    o4 = yb.t[:, :].rearrange("p (a g t) -> p a g t", g=2, t=128)
                    for g in range(2):
                        dst = yaccT[64 * g:64 * g + 64, :, cs]
                        src = yb.v(o4[64 * g:64 * g + 64, :, g, :])
                        fw.tt(dst, src, dst, ALU.add, eng="vector")
                for tau in range(2):
                    sb = bank()
                    for hh in range(2):
                        h = 2 * tau + hh
                        fw.mm(sb[:, 0:64], bkt[:, 2 * tau, hh * 128:(hh + 1) * 128], Udup[:, h, 0:64],
                              start=(hh == 0), stop=False)
                        fw.mm(sb[:, 0:64], bkt[:, 2 * tau + 1, hh * 128:(hh + 1) * 128], Vdup[:, c, h, 0:64],
                              start=False, stop=(hh == 1))
                    fw.stt(ST[tau][:], ST[tau][:], pend[:, d, tau, c:c + 1], sb[:, 0:64], ALU.mult, ALU.add)
                    fw.copy(STd[tau][:, 0:64], ST[tau][:], eng="scalar")
                    fw.copy(STd[tau][:, 64:128], ST[tau][:], eng="gpsimd")
                yield

        gens = [chunk_gen(0), chunk_gen(1)]
        alive = [True, True]
        while any(alive):
            for gi, g in enumerate(gens):
                if alive[gi]:
                    try:
                        next(g)
                    except StopIteration:
                        alive[gi] = False
        epsln = fw.sbuf("epsln", [128, 1])
        fw.memset(epsln[:], 64e-5)
        tb = [fw.sbuf(f"r3_{i}", [128, 512]) for i in range(5)]
        ob = [fw.sbuf(f"rob{i}", [128, 512], BF16) for i in range(2)]
        oi = 0
        qblocks = (ctx_blocks if need_ctx else []) + lat_blocks
        for tau in range(2):
            for (s, n, is_ctx) in qblocks:
                yc, sq, rs, bo, ga = tb
                fw.dma(bo[:, :n], BON[tau * 128:(tau + 1) * 128, s:s + n])
                fw.dma(ga[:, :n], GATE[tau * 128:(tau + 1) * 128, s:s + n])
                mb = bank()
                fw.mm(mb[:, :n], blk64[:], yaccT[:, tau, s:s + n])
                fw.stt(yc[:, :n], mb[:, :n], -1.0 / 64, yaccT[:, tau, s:s + n], ALU.mult, ALU.add)
                fw.act(sq[:, :n], yc[:, :n], AF.Square)
                vb = bank()
                fw.mm(vb[:, :n], blk64[:], sq[:, :n])
                fw.act(rs[:, :n], vb[:, :n], AF.Sqrt, bias=epsln[:, 0:1], scale=1.0 / 64)
                fw.recip(rs[:, :n], rs[:, :n])
                fw.stt(yc[:, :n], yc[:, :n], pcol(l, "rw_ln_g", tau), rs[:, :n], ALU.mult, ALU.mult)
                fw.stt(yc[:, :n], yc[:, :n], pcol(l, "rw_ln_b", tau), bo[:, :n], ALU.add, ALU.add, eng="gpsimd")
                o = ob[oi % 2]; oi += 1
                fw.tt(o[:, :n], yc[:, :n], ga[:, :n], ALU.mult, eng="gpsimd")
                fw.dma(OT[tau * 128:(tau + 1) * 128, s:s + n], o[:, :n])
        fw.pop()

    def mixers_0(l, b, need_ctx):
        if "rwkv" in cfg.mix:
            rwkv_phase(l, b, need_ctx)
        if "gla" in cfg.mix:
            gla_phase(l, b, need_ctx)
        if "gqa" in cfg.mix:
            gqa_phase(l, b, need_ctx)
        if "da" in cfg.mix:
            da_phase(l, b, need_ctx)

    def mixers(l, b, need_ctx):
        if "rwkv" in cfg.mix:
            rwkv_phase(l, b, need_ctx)
        if "da" in cfg.mix:
            da_phase(l, b, need_ctx)
        if "gla" in cfg.mix:
            gla_phase(l, b, need_ctx)
        if "gqa" in cfg.mix:
            gqa_phase(l, b, need_ctx)

    for b in range(NB):
        fw.push()
        xT = fw.sbuf("xT_s", [128, KT, T])
        xv = xT_d.t[b].rearrange("(k p) t -> p k t", p=128)
        for k in range(KT):
            fw.dma(xT[:, k, :], xT_d.v(xv[:, k, :]))
        for l in range(L):
            need_ctx = l < L - 1
            fw.push()
            hT = fw.sbuf("hT", [128, KT, T], BF16)
            sq = fw.sbuf("sq", [128, 512])
            rstd = fw.sbuf("rstd", [128, 512])
            nps = fw.psum("nps", [128, 512])
            norm_phase(xT, hT, l, 0, b, sq, rstd, nps)
            wt = [fw.sbuf(f"wt{i}", [128, KT, 256], BF16) for i in range(3)]
            pps = [fw.psum(f"pps{i}", [128, 512]) for i in range(4)]
            stg = [fw.sbuf(f"stg{i}", [128, 512]) for i in range(4)]
            wv = w_in.t[l].rearrange("(k p) c -> p k c", p=128)
            ei = 0
            tiles_ = mixer_cols()
            groups_ = [tiles_[i:i + 2] for i in range(0, len(tiles_), 2)]
            for gi, grp in enumerate(groups_):
                g0 = grp[0][0]
                gn = sum(nc_ for _, nc_ in grp)
                wb = wt[gi % 3]
                fw.dma(wb[:, :, :gn], w_in.v(wv[:, :, g0:g0 + gn]), eng="gpsimd")
                for (c0, ncol) in grp:
                    off = c0 - g0
                    for (s, n, is_ctx) in blocks:
                        ps = pps[ei % 4]
                        st = stg[ei % 4]
                        for k in range(KT):
                            fw.mm(ps[:ncol, :n], wb[:, k, off:off + ncol], hT[:, k, s:s + n],
                                  start=(k == 0), stop=(k == KT - 1))
                        fw.copy(st[:ncol, :n], ps[:ncol, :n], eng=("vector" if ei % 2 == 0 else "scalar"))
                        fw.dma(PT[c0:c0 + ncol, s:s + n], st[:ncol, :n])
                        ei += 1
            fw.pop()
            if cfg.stop == "proj":
                break
            if cfg.stop == "ffn":
                fw.push()
                tb = fw.sbuf("tb", [128, T])
                tbb = fw.sbuf("tbb", [128, T], BF16)
                for k in range(KT):
                    fw.dma(tb[:], PT[k * 128:(k + 1) * 128, :])
                    fw.copy(tbb[:], tb[:])
                    fw.dma(OT[k * 128:(k + 1) * 128, :], tbb[:])
                fw.pop()
            else:
                mixers(l, b, need_ctx)
            if cfg.stop == "mix":
                break
            wout_phase(xT, l, b, need_ctx)
            ffn_phase(xT, l, b, need_ctx)
            if cfg.stop == "ffn":
                break
        yv = yT_d.t[b].rearrange("(k p) t -> p k t", p=128)
        for k in range(KT):
            fw.dma(yT_d.v(yv[:, k, :]), xT[:, k, TC:T])
        fw.pop()
        if cfg.stop is not None:
            break

    if cfg.stop is not None:
        dbg_pt = fw.dram("dbg_PT", [N_IN, T], F32, kind="ExternalOutput")
        fw.dma(dbg_pt[:], PT[:])
        dbg_ot = fw.dram("dbg_OT", [D, T], BF16, kind="ExternalOutput")
        fw.dma(dbg_ot[:], OT[:])
    fw.pop()
    fw.finish()
    return nc


_NC_CACHE = {}


def kernel(**inputs):
    n_cores = 8
    B = inputs["x"].shape[0]
    NB = B // n_cores
    cfg = Cfg(TC=inputs["ctx"].shape[1], TL=inputs["x"].shape[1], NB=NB, depth=DEPTH)
    nc = build(cfg)
    in_maps = [prep_inputs(inputs, cfg, i * NB) for i in range(n_cores)]
    res = run_bass_kernel_spmd(nc, in_maps, core_ids=list(range(n_cores)))
    out = np.empty((B, cfg.TL, D), np.float32)
    for i in range(n_cores):
        yT = np.asarray(res.results[i]["yT"])
        out[i * NB:(i + 1) * NB] = yT.transpose(0, 2, 1)
    return out


def prep_inputs(inp, cfg, b0):
    NB = cfg.NB
    m = {}
    x = np.asarray(inp["x"], np.float32)[b0:b0 + NB]
    ctx = np.asarray(inp["ctx"], np.float32)[b0:b0 + NB]
    xc = np.concatenate([ctx, x], axis=1)
    m["xT"] = np.ascontiguousarray(xc.transpose(0, 2, 1))
    cvec = np.concatenate([np.asarray(inp["c"], np.float32)[b0:b0 + NB],
                           np.asarray(inp["c_ctx"], np.float32)[None]], axis=0)
    m["cT"] = np.ascontiguousarray(cvec.reshape(NB + 1, KT, 128).transpose(2, 1, 0))
    L = cfg.depth
    m["pack"] = np.stack([host_pack(inp, l) for l in range(L)])
    for nm in ("rw_w2", "rw_a2"):
        m[nm] = np.ascontiguousarray(np.asarray(inp[nm], np.float32)[:L].reshape(L, 128, 256))
    m["rw_g2"] = np.ascontiguousarray(np.asarray(inp["rw_g2"], np.float32)[:L])
    m["lamb"] = np.stack([np.tile(np.asarray(inp["da_lam"], np.float32)[l].reshape(1, 128), (128, 1)) for l in range(L)])
    for nm in ("mod_w", "w_in", "w_out", "ffn_w_up", "ffn_w_down", "gla_a2"):
        m[nm] = np.ascontiguousarray(np.asarray(inp[nm], np.float32)[:L])
    for k, v in host_consts(cfg).items():
        m["c_" + k] = v
    return m
```

```python
import numpy as np
import concourse.bass as bass
import concourse.mybir as mybir
from concourse.bass_utils import run_bass_kernel_spmd

F32 = mybir.dt.float32
BF16 = mybir.dt.bfloat16
AF = mybir.ActivationFunctionType
ALU = mybir.AluOpType
AX = mybir.AxisListType

ENGS = ("tensor", "vector", "scalar", "gpsimd", "sync")


class Trk:
    __slots__ = ("name", "w", "r")

    def __init__(self, name):
        self.name = name
        self.w = None
        self.r = {}


class V:
    __slots__ = ("ap", "trk")

    def __init__(self, ap, trk):
        self.ap = ap
        self.trk = trk


class Buf:
    def __init__(self, t, name):
        self.t = t
        self.name = name
        self.trk = Trk(name)

    def __getitem__(self, idx):
        return V(self.t[idx], self.trk)

    def v(self, ap):
        return V(ap, self.trk)


def _trks(v):
    return v.trk if isinstance(v.trk, (list, tuple)) else (v.trk,)


class FW:
    def __init__(self, nc, n_dma_sems=32):
        self.nc = nc
        self.prog = {e: [] for e in ENGS}
        self.sem = {e: nc.alloc_semaphore(name=f"s_{e}") for e in ENGS}
        self.cnt = {e: 0 for e in ENGS}
        self.waited = {e: {} for e in ENGS}
        self.dsem = [nc.alloc_semaphore(name=f"d_{i}") for i in range(n_dma_sems)]
        self.dcnt = [0] * n_dma_sems
        self.dnext = 0
        self.gnext = 0
        self.semobj = {}
        for e in ENGS:
            self.semobj[("e", e)] = self.sem[e]
        for i, s in enumerate(self.dsem):
            self.semobj[("d", i)] = s
        self.ninst = 0
        self.stack = []

    def push(self):
        self.stack.append([])

    def pop(self):
        self.barrier()
        for g in reversed(self.stack.pop()):
            g.__exit__(None, None, None)

    def sbuf(self, name, shape, dtype=F32):
        self.uid = getattr(self, "uid", 0) + 1
        name = f"{name}_u{self.uid}"
        g = self.nc.sbuf_tensor(name, list(shape), dtype)
        t = g.__enter__()
        self.stack[-1].append(g)
        return Buf(t, name)

    def psum(self, name, shape, dtype=F32):
        self.uid = getattr(self, "uid", 0) + 1
        name = f"{name}_u{self.uid}"
        g = self.nc.psum_tensor(name, list(shape), dtype)
        t = g.__enter__()
        self.stack[-1].append(g)
        return Buf(t, name)

    def dram(self, name, shape, dtype=F32, kind="Internal"):
        return Buf(self.nc.dram_tensor(name, list(shape), dtype, kind=kind).ap(), name)

    def _wait(self, eng, ev):
        if ev is None:
            return
        key, val = ev
        if eng == "tensor" and key == ("e", "tensor"):
            return
        if self.waited[eng].get(key, 0) >= val:
            return
        self.waited[eng][key] = val
        self.prog[eng].append(("wait", key, val))

    def _deps(self, eng, reads, writes):
        for v in reads:
            for t in _trks(v):
                self._wait(eng, t.w)
        for v in writes:
            for t in _trks(v):
                self._wait(eng, t.w)
                for kv in list(t.r.items()):
                    self._wait(eng, kv)

    def _mark(self, ev, reads, writes):
        for v in reads:
            for t in _trks(v):
                if t.r.get(ev[0], 0) < ev[1]:
                    t.r[ev[0]] = ev[1]
        for v in writes:
            for t in _trks(v):
                t.w = ev
                t.r = {}

    def op(self, eng, meth, reads, writes, *args, **kw):
        self._deps(eng, reads, writes)
        self.cnt[eng] += 1
        ev = (("e", eng), self.cnt[eng])
        sem = self.sem[eng]
        a2 = [a.ap if isinstance(a, V) else a for a in args]
        k2 = {k: (a.ap if isinstance(a, V) else a) for k, a in kw.items()}

        def emit(e, inc, wait=None, meth=meth, a2=a2, k2=k2, sem=sem):
            ins = getattr(e, meth)(*a2, **k2)
            if wait is not None:
                ins._wait_ge(wait[0], wait[1])
            if inc:
                ins.then_inc(sem, 1)
        self.prog[eng].append(("op", emit, self.cnt[eng]))
        self._mark(ev, reads, writes)
        self.ninst += 1
        return ev

    def dma(self, out, in_, eng="sync", **kw):
        self._deps(eng, [in_], [out])
        nd = len(self.dsem)
        if eng == "gpsimd":
            k = nd - 8 + self.gnext
            self.gnext = (self.gnext + 1) % 8
        else:
            k = self.dnext
            self.dnext = (self.dnext + 1) % (nd - 8)
        if self.dcnt[k] > 0:
            self._wait(eng, (("d", k), self.dcnt[k]))
        self.dcnt[k] += 16
        ev = (("d", k), self.dcnt[k])
        sem = self.dsem[k]
        oa, ia = out.ap, in_.ap

        def emit(e, oa=oa, ia=ia, sem=sem, kw=kw):
            e.dma_start(out=oa, in_=ia, **kw).then_inc(sem, 16)
        self.prog[eng].append(("dma", emit))
        self._mark(ev, [in_], [out])
        self.ninst += 1
        return ev

    def _all_events(self):
        evs = [(("e", e), self.cnt[e]) for e in ENGS if self.cnt[e] > 0]
        evs += [(("d", i), c) for i, c in enumerate(self.dcnt) if c > 0]
        return evs

    def barrier(self):
        evs = self._all_events()
        for e in ENGS:
            for ev in evs:
                self._wait(e, ev)

    def finish(self):
        for ev in self._all_events():
            self._wait("sync", ev)
        import bisect
        needed = {e: set() for e in ENGS}
        for ename in ENGS:
            for it in self.prog[ename]:
                if it[0] == "wait" and it[1][0] == "e":
                    needed[it[1][1]].add(it[2])
        ranks = {e: sorted(needed[e]) for e in ENGS}
        self.max_sem = {e: len(ranks[e]) for e in ENGS}
        with self.nc.Block() as block:
            for ename in ENGS:
                lst = self.prog[ename]

                def body(e, lst=lst, ename=ename):
                    pending = []
                    for it in lst:
                        if it[0] == "wait":
                            key, val = it[1], it[2]
                            if key[0] == "e":
                                val = bisect.bisect_left(ranks[key[1]], val) + 1
                            pending.append((self.semobj[key], val))
                        elif it[0] == "op":
                            for (sm, vl) in pending[:-1]:
                                e.wait_ge(sm, vl)
                            it[1](e, it[2] in needed[ename], pending[-1] if pending else None)
                            pending = []
                        else:
                            for (sm, vl) in pending:
                                e.wait_ge(sm, vl)
                            pending = []
                            it[1](e)
                    for (sm, vl) in pending:
                        e.wait_ge(sm, vl)
                getattr(block, ename)(body)

    def mm(self, out, lhsT, rhs, start=True, stop=True):
        return self.op("tensor", "matmul", [lhsT, rhs], [out], out, lhsT, rhs, start=start, stop=stop)

    def transpose(self, out, in_, ident):
        return self.op("tensor", "transpose", [in_, ident], [out], out, in_, ident)

    def act(self, out, in_, func, bias=None, scale=None, accum_out=None):
        reads = [in_]
        kw = {}
        if bias is not None:
            kw["bias"] = bias
            if isinstance(bias, V):
                reads.append(bias)
        if scale is not None:
            kw["scale"] = scale
            if isinstance(scale, V):
                reads.append(scale)
        writes = [out]
        if accum_out is not None:
            kw["accum_out"] = accum_out
            writes.append(accum_out)
        return self.op("scalar", "activation", reads, writes, out, in_, func, **kw)

    def tt(self, out, in0, in1, op, eng="vector"):
        return self.op(eng, "tensor_tensor", [in0, in1], [out], out, in0, in1, op)

    def ts(self, out, in0, s1, op0, s2=None, op1=None, eng="vector"):
        reads = [in0] + [s for s in (s1, s2) if isinstance(s, V)]
        if op1 is None:
            return self.op(eng, "tensor_scalar", reads, [out], out, in0, s1, None, op0)
        return self.op(eng, "tensor_scalar", reads, [out], out, in0, s1, s2, op0, op1)

    def stt(self, out, in0, scalar, in1, op0, op1, eng="vector"):
        eng = "vector"
        reads = [in0, in1] + ([scalar] if isinstance(scalar, V) else [])
        return self.op(eng, "scalar_tensor_tensor", reads, [out], out, in0, scalar, in1, op0, op1)

    def copy(self, out, in_, eng="vector"):
        if eng == "scalar":
            return self.op("scalar", "copy", [in_], [out], out, in_)
        return self.op(eng, "tensor_copy", [in_], [out], out, in_)

    def memset(self, out, val, eng="vector"):
        return self.op(eng, "memset", [], [out], out, val)

    def recip(self, out, in_):
        return self.op("vector", "reciprocal", [in_], [out], out, in_)


D = 1024
KT = 8
N_IN = 3232
D_FF = 2816
FT = 22
GRID_W = 64
EPS = 1e-6
RW_OFF, DA_OFF, GLA_OFF, GQA_OFF = 0, 1152, 1920, 2720
DEPTH = 2


class Cfg:
    def __init__(self, TC=256, TL=2048, NB=2, depth=DEPTH, stop=None, mix=("rwkv", "da", "gla", "gqa")):
        self.TC, self.TL, self.NB, self.depth = TC, TL, NB, depth
        self.T = TC + TL
        self.stop = stop
        self.mix = mix

    def blocks(self):
        out = []
        s = 0
        while s < self.TC:
            n = min(512, self.TC - s)
            out.append((s, n, True))
            s += n
        while s < self.T:
            n = min(512, self.T - s)
            out.append((s, n, False))
            s += n
        return out


def pack_layout():
    cols = {}
    n = 0

    def add(name, k):
        nonlocal n
        cols[name] = (n, k)
        n += k
    add("nmg", 8)
    add("nfg", 8)
    add("mod_b", 48)
    add("rw_mu0", 9)
    add("rw_mu1", 9)
    add("rw_c0", 9)
    add("rw_omka", 2)
    add("rw_w0", 4)
    add("rw_a0", 4)
    add("rw_kk", 2)
    add("rw_ka", 2)
    add("rw_rk", 2)
    add("rw_ln_g", 2)
    add("rw_ln_b", 2)
    add("da_qg", 1)
    add("da_kg", 1)
    add("da_sub", 1)
    add("gla_ab", 2)
    add("gla_ng", 1)
    add("gq_qg", 1)
    add("gq_kg", 1)
    add("conv_w", 66)
    add("conv_b", 22)
    return cols, n


PACK, NPACK = pack_layout()


def host_pack(inp, l):
    P = np.zeros((128, NPACK), np.float32)

    def put(name, vec, k):
        c0, kk = PACK[name]
        assert kk == k
        P[:, c0:c0 + k] = np.asarray(vec, np.float32).reshape(k, 128).T
    put("nmg", inp["norm_mix_g"][l], 8)
    put("nfg", inp["norm_ffn_g"][l], 8)
    put("mod_b", inp["mod_b"][l], 48)
    put("rw_mu0", inp["rw_mu"][l, 0], 9)
    put("rw_mu1", inp["rw_mu"][l, 1], 9)
    put("rw_w0", inp["rw_w0"][l].reshape(-1), 4)
    put("rw_a0", inp["rw_a0"][l].reshape(-1), 4)
    put("rw_kk", inp["rw_kk"][l], 2)
    put("rw_ka", inp["rw_ka"][l], 2)
    put("rw_rk", inp["rw_rk"][l].reshape(-1), 2)
    put("rw_ln_g", inp["rw_ln_g"][l], 2)
    put("rw_ln_b", inp["rw_ln_b"][l], 2)
    put("da_qg", np.tile(inp["da_qk_g"][l, 0], 4), 1)
    put("da_kg", np.tile(inp["da_qk_g"][l, 1], 4), 1)
    put("da_sub", np.tile(inp["da_subln_g"][l], 2), 1)
    put("gla_ab", inp["gla_ab"][l].reshape(-1), 2)
    put("gla_ng", np.tile(inp["gla_norm_g"][l], 2), 1)
    put("gq_qg", np.tile(inp["gqa_qk_g"][l, 0], 2), 1)
    put("gq_kg", np.tile(inp["gqa_qk_g"][l, 1], 2), 1)
    put("conv_w", inp["ffn_conv_w"][l].reshape(-1), 66)
    put("conv_b", inp["ffn_conv_b"][l], 22)
    return P


def host_consts(cfg):
    c = {}
    c["ident"] = np.eye(128, dtype=np.float32)
    c["ones"] = np.ones((128, 128), np.float32)
    b64 = np.zeros((128, 128), np.float32)
    b64[:64, :64] = 1
    b64[64:, 64:] = 1
    c["blk64"] = b64
    b32 = np.zeros((128, 128), np.float32)
    for i in range(4):
        b32[32 * i:32 * i + 32, 32 * i:32 * i + 32] = 1
    c["blk32"] = b32
    TL = cfg.TL
    rows = TL // GRID_W
    row = np.repeat(np.arange(rows, dtype=np.float32), GRID_W)
    col = np.tile(np.arange(GRID_W, dtype=np.float32), rows)

    def tables(hd):
        nf = hd // 4
        inv = (10000.0 ** (-np.arange(nf, dtype=np.float32) / nf)).astype(np.float32)
        ang = np.concatenate([row[:, None] * inv, col[:, None] * inv], axis=-1)
        cos, sin = np.cos(ang).astype(np.float32), np.sin(ang).astype(np.float32)
        half = hd // 2
        cosf = np.concatenate([cos, cos], axis=-1)
        sinf = np.concatenate([sin, sin], axis=-1)
        rep = 128 // hd
        cT = np.tile(cosf, (1, rep)).T.copy()
        sT = np.tile(sinf, (1, rep)).T.copy()
        R = np.zeros((128, 128), np.float32)
        for m in range(128):
            if m % hd < half:
                R[m + half, m] = -1.0
            else:
                R[m - half, m] = 1.0
        return cT, sT, R
    hm = np.zeros((128, 4), np.float32)
    for p in range(128):
        hm[p, p // 32] = 1.0
    c["hmask4s"] = hm * np.float32(32 ** -0.5)
    jj, tt_ = np.meshgrid(np.arange(128), np.arange(128), indexing="ij")
    c["tri4_0"] = np.tile((jj <= tt_).astype(np.float32)[:, None, :], (1, 4, 1))
    c["tri4_1"] = np.tile((jj >= tt_).astype(np.float32)[:, None, :], (1, 4, 1))
    h2 = np.zeros((128, 4), np.float32)
    h2[:64, 0] = 1; h2[64:, 1] = 1; h2[:64, 2] = -1; h2[64:, 3] = -1
    c["hm2"] = h2
    c["I2"] = np.tile(np.eye(128, dtype=np.float32)[:, None, :], (1, 2, 1))
    c["maskN_0"] = np.tile((tt_ < jj).astype(np.float32)[:, None, :], (1, 4, 1))
    c["maskN_1"] = np.tile((tt_ > jj).astype(np.float32)[:, None, :], (1, 4, 1))
    sf, inf_ = (jj < tt_).astype(np.float32), (jj <= tt_).astype(np.float32)
    sr, inr = (jj > tt_).astype(np.float32), (jj >= tt_).astype(np.float32)
    c["maskAB_0"] = np.tile(np.concatenate([sf, inf_], 1)[:, None, :], (1, 2, 1))
    c["maskAB_1"] = np.tile(np.concatenate([sr, inr], 1)[:, None, :], (1, 2, 1))
    dm = np.zeros((128, 2), np.float32)
    for p in range(128):
        dm[p, (p % 64) // 32] = 1.0
    c["dmask"] = dm
    c["cos_gq"], c["sin_gq"], c["rot_gq"] = tables(64)
    c["cos_da"], c["sin_da"], c["rot_da"] = tables(32)
    return c


def build(cfg):
    nc = bass.Bass("TRN2", target_bir_lowering=False)
    fw = FW(nc)
    NB, T, TC, TL = cfg.NB, cfg.T, cfg.TC, cfg.TL
    NJ = NB + 1
    L = cfg.depth

    def din(name, shape, dt=F32):
        return fw.dram(name, shape, dt, kind="ExternalInput")

    xT_d = din("xT", [NB, D, T])
    cT_d = din("cT", [128, KT, NJ])
    pack_d = din("pack", [L, 128, NPACK])
    mod_w = din("mod_w", [L, D, 6 * D])
    w_in = din("w_in", [L, D, N_IN])
    w_out = din("w_out", [L, D, D])
    w_up = din("ffn_w_up", [L, D, 2 * D_FF])
    w_down = din("ffn_w_down", [L, D_FF, D])
    consts = {}
    for nm in ("ident", "ones", "blk64", "blk32", "rot_gq", "rot_da"):
        consts[nm] = din("c_" + nm, [128, 128])
    for nm in ("cos_gq", "sin_gq", "cos_da", "sin_da"):
        consts[nm] = din("c_" + nm, [128, TL])
    consts["dmask"] = din("c_dmask", [128, 2])
    lamb_d = din("lamb", [L, 128, 128])
    gla_a2_d = din("gla_a2", [L, 2, 16, 128])
    rw_w2_d = din("rw_w2", [L, 128, 256])
    rw_a2_d = din("rw_a2", [L, 128, 256])
    rw_g2_d = din("rw_g2", [L, 128, 256])
    consts["hm2"] = din("c_hm2", [128, 4])
    consts["I2"] = din("c_I2", [128, 2, 128])
    for d_ in range(2):
        consts[f"maskN_{d_}"] = din(f"c_maskN_{d_}", [128, 4, 128])
        consts[f"maskAB_{d_}"] = din(f"c_maskAB_{d_}", [128, 2, 256])
    consts["hmask4s"] = din("c_hmask4s", [128, 4])
    consts["tri4_0"] = din("c_tri4_0", [128, 4, 128])
    consts["tri4_1"] = din("c_tri4_1", [128, 4, 128])
    yT_d = fw.dram("yT", [NB, D, TL], F32, kind="ExternalOutput")
    dbg = {}

    PT = fw.dram("PT", [N_IN, T], F32)
    OT = fw.dram("OT", [D, T], BF16)
    ACT = fw.dram("ACTs", [D_FF, T], BF16)

    fw.push()
    ident = fw.sbuf("ident", [128, 128])
    ones = fw.sbuf("ones", [128, 128])
    blk64 = fw.sbuf("blk64", [128, 128])
    fw.dma(ident[:], consts["ident"][:])
    fw.dma(ones[:], consts["ones"][:])
    fw.dma(blk64[:], consts["blk64"][:])
    identb = fw.sbuf("identb", [128, 128], BF16)
    fw.copy(identb[:], ident[:])
    onesb_g = fw.sbuf("onesb_g", [128, 128], BF16)
    fw.copy(onesb_g[:], ones[:])
    blk64b = fw.sbuf("blk64b", [128, 128], BF16)
    fw.copy(blk64b[:], blk64[:])
    pk = [fw.sbuf(f"pk{l}", [128, NPACK]) for l in range(L)]
    for l in range(L):
        fw.dma(pk[l][:], pack_d[l])
    modT = [fw.sbuf(f"modT{l}", [128, 48, NJ]) for l in range(L)]
    gs = [fw.sbuf(f"gs{l}", [128, 2, KT, NJ]) for l in range(L)]
    epsb = fw.sbuf("epsb", [128, 1])
    fw.memset(epsb[:], EPS)

    def pcol(l, name, i=0):
        c0, k = PACK[name]
        return pk[l][:, c0 + i:c0 + i + 1]

    fw.push()
    cs = fw.sbuf("cs", [128, KT, NJ])
    fw.dma(cs[:], cT_d[:])
    fw.act(cs[:], cs[:], AF.Silu)
    mps = fw.psum("mps", [128, 48, NJ])
    wm = [fw.sbuf(f"wm{i}", [128, KT, 512]) for i in range(2)]
    for l in range(L):
        mwv = mod_w.t[l].rearrange("(k p) c -> p k c", p=128)
        for g in range(12):
            wb = wm[g % 2]
            fw.dma(wb[:], mod_w.v(mwv[:, :, g * 512:(g + 1) * 512]))
            for ci in range(4):
                ct = g * 4 + ci
                for k in range(KT):
                    fw.mm(mps[:, ct, :], wb[:, k, ci * 128:(ci + 1) * 128], cs[:, k, :],
                          start=(k == 0), stop=(k == KT - 1))
        c0 = PACK["mod_b"][0]
        for j in range(NJ):
            fw.tt(modT[l][:, :, j], mps[:, :, j], pk[l][:, c0:c0 + 48], ALU.add)
        for j in range(NJ):
            c0 = PACK["nmg"][0]
            fw.stt(gs[l][:, 0, :, j], modT[l][:, 8:16, j], 1.0, pk[l][:, c0:c0 + 8], ALU.add, ALU.mult)
            c0 = PACK["nfg"][0]
            fw.stt(gs[l][:, 1, :, j], modT[l][:, 32:40, j], 1.0, pk[l][:, c0:c0 + 8], ALU.add, ALU.mult)
    fw.pop()

    blocks = cfg.blocks()
    if cfg.stop == "mix":
        fw.push()
        zt = fw.sbuf("zt", [128, T], BF16)
        fw.memset(zt[:], 0.0)
        for k in range(KT):
            fw.dma(OT[k * 128:(k + 1) * 128, :], zt[:])
        fw.pop()

    def mixer_cols():
        tl = []
        for i in range(9):
            tl.append((RW_OFF + 128 * i, 128))
        for i in range(6):
            tl.append((DA_OFF + 128 * i, 128))
        for i in range(6):
            tl.append((GLA_OFF + 128 * i, 128))
        tl.append((GLA_OFF + 768, 32))
        for i in range(4):
            tl.append((GQA_OFF + 128 * i, 128))
        return tl

    def norm_phase(xT, hT, l, which, b, sq, rstd, nps):
        shift_base = 0 if which == 0 else 24
        sqb = [fw.sbuf(f"nsqb{i}", [128, 512], BF16) for i in range(3)]
        tmpf = [fw.sbuf(f"ntmp{i}", [128, 512]) for i in range(3)]
        nps2 = fw.psum("nps_b", [128, 512])
        ci = 0
        for bi, (s, n, is_ctx) in enumerate(blocks):
            j = NB if is_ctx else b
            ps = nps if bi % 2 == 0 else nps2
            for k in range(KT):
                q = sqb[ci % 3]
                ci += 1
                fw.act(q[:, :n], xT[:, k, s:s + n], AF.Square)
                fw.mm(ps[:, :n], onesb_g[:], q[:, :n], start=(k == 0), stop=(k == KT - 1))
            rs = sq if bi % 2 == 0 else rstd
            fw.act(rs[:, :n], ps[:, :n], AF.Sqrt, bias=epsb[:, 0:1], scale=1.0 / D)
            fw.recip(rs[:, :n], rs[:, :n])
            for k in range(KT):
                t_ = tmpf[ci % 3]
                ci += 1
                fw.tt(t_[:, :n], xT[:, k, s:s + n], rs[:, :n], ALU.mult)
                fw.act(hT[:, k, s:s + n], t_[:, :n], AF.Identity,
                       bias=modT[l][:, shift_base + k, j:j + 1], scale=gs[l][:, which, k, j:j + 1])

    def wout_phase(xT, l, b, need_ctx):
        fw.push()
        oT = fw.sbuf("oT", [128, KT, T], BF16)
        ov = OT.t.rearrange("(k p) t -> p k t", p=128)
        for k in range(KT):
            fw.dma(oT[:, k, :], OT.v(ov[:, k, :]))
        wt = [fw.sbuf(f"wo{i}", [128, KT, 256], BF16) for i in range(2)]
        pps = [fw.psum(f"ops{i}", [128, 512]) for i in range(4)]
        wv = w_out.t[l].rearrange("(k p) c -> p k c", p=128)
        ei = 0
        for jt in range(KT):
            wb_full = wt[(jt // 2) % 2]
            if jt % 2 == 0:
                fw.dma(wb_full[:], w_out.v(wv[:, :, jt * 128:(jt + 2) * 128]), eng="gpsimd")
            wb = Buf(wb_full.t[:, :, (jt % 2) * 128:(jt % 2 + 1) * 128], "wo_half")
            wb.trk = wb_full.trk
            for (s, n, is_ctx) in blocks:
                if is_ctx and not need_ctx:
                    continue
                j = NB if is_ctx else b
                ps = pps[ei % 4]
                ei += 1
                for k in range(KT):
                    fw.mm(ps[:, :n], wb[:, k, :], oT[:, k, s:s + n], start=(k == 0), stop=(k == KT - 1))
                fw.stt(xT[:, jt, s:s + n], ps[:, :n], modT[l][:, 16 + jt, j:j + 1], xT[:, jt, s:s + n],
                       ALU.mult, ALU.add)
        fw.pop()

    def ffn_phase(xT, l, b, need_ctx):
        segs = ([(0, TC)] if need_ctx else []) + [(TC, T)]
        fblocks = [bl for bl in blocks if (need_ctx or not bl[2])]
        fw.push()
        hT = fw.sbuf("hT2", [128, KT, T], BF16)
        sq = fw.sbuf("sq2", [128, 512])
        rstd = fw.sbuf("rstd2", [128, 512])
        nps = fw.psum("nps2", [128, 512])
        norm_phase(xT, hT, l, 1, b, sq, rstd, nps)
        wu = [fw.sbuf(f"wu{i}", [128, KT, 256], BF16) for i in range(2)]
        wg = [fw.sbuf(f"wg{i}", [128, KT, 256], BF16) for i in range(2)]
        ups = [fw.psum(f"ups{i}", [128, 512]) for i in range(2)]
        gps = [fw.psum(f"gps{i}", [128, 512]) for i in range(2)]
        uT = [fw.sbuf(f"uT{i}", [128, T]) for i in range(2)]
        gT = [fw.sbuf(f"gT{i}", [128, T]) for i in range(2)]
        tmp = [fw.sbuf(f"ftmp{i}", [128, T]) for i in range(2)]
        aT = [fw.sbuf(f"aT{i}", [128, T], BF16) for i in range(2)]
        wv = w_up.t[l].rearrange("(k p) c -> p k c", p=128)
        cw0 = PACK["conv_w"][0]
        cb0 = PACK["conv_b"][0]
        ei = 0
        def wload(p):
            fw.dma(wu[p % 2][:], w_up.v(wv[:, :, p * 256:(p + 1) * 256]), eng="gpsimd")
            fw.dma(wg[p % 2][:], w_up.v(wv[:, :, D_FF + p * 256:D_FF + (p + 1) * 256]), eng="gpsimd")
        wload(0)
        for i in range(FT):
            r = i % 2
            if i % 2 == 1 and i // 2 + 1 < FT // 2:
                wload(i // 2 + 1)
            for (s, n, is_ctx) in fblocks:
                pu, pg = ups[ei % 2], gps[ei % 2]
                ei += 1
                for k in range(KT):
                    fw.mm(pu[:, :n], wu[(i // 2) % 2][:, k, (i % 2) * 128:(i % 2 + 1) * 128], hT[:, k, s:s + n],
                          start=(k == 0), stop=(k == KT - 1))
                for k in range(KT):
                    fw.mm(pg[:, :n], wg[(i // 2) % 2][:, k, (i % 2) * 128:(i % 2 + 1) * 128], hT[:, k, s:s + n],
                          start=(k == 0), stop=(k == KT - 1))
                fw.copy(uT[r][:, s:s + n], pu[:, :n], eng="scalar")
                fw.copy(gT[r][:, s:s + n], pg[:, :n], eng="scalar")
                fw.act(tmp[r][:, s:s + n], pg[:, :n], AF.Identity, bias=pk[l][:, cb0 + i:cb0 + i + 1],
                       scale=pk[l][:, cw0 + FT + i:cw0 + FT + i + 1])
            w0 = pk[l][:, cw0 + i:cw0 + i + 1]
            w1 = pk[l][:, cw0 + FT + i:cw0 + FT + i + 1]
            w2 = pk[l][:, cw0 + 2 * FT + i:cw0 + 2 * FT + i + 1]
            cb = pk[l][:, cb0 + i:cb0 + i + 1]
            for (s, e) in segs:
                fw.stt(tmp[r][:, s + 1:e], gT[r][:, s:e - 1], w0, tmp[r][:, s + 1:e], ALU.mult, ALU.add)
                fw.stt(tmp[r][:, s:e - 1], gT[r][:, s + 1:e], w2, tmp[r][:, s:e - 1], ALU.mult, ALU.add)
                fw.act(tmp[r][:, s:e], tmp[r][:, s:e], AF.Silu)
                fw.tt(aT[r][:, s:e], tmp[r][:, s:e], uT[r][:, s:e], ALU.mult)
                fw.dma(ACT[i * 128:(i + 1) * 128, s:e], aT[r][:, s:e])
        fw.pop()
        fw.push()
        wd = [fw.sbuf(f"wd{jt}", [128, FT, 128], BF16) for jt in range(KT)]
        wdv = w_down.t[l].rearrange("(f p) c -> p f c", p=128)
        for jt in range(KT):
            fw.dma(wd[jt][:], w_down.v(wdv[:, :, jt * 128:(jt + 1) * 128]), eng="gpsimd")
        ab = [fw.sbuf(f"ab{i}", [128, FT, 512], BF16) for i in range(2)]
        dps = [fw.psum(f"dps{i}", [128, 512]) for i in range(4)]
        av = ACT.t.rearrange("(f p) t -> p f t", p=128)
        ei = 0
        for bi, (s, n, is_ctx) in enumerate(fblocks):
            j = NB if is_ctx else b
            a = ab[bi % 2]
            fw.dma(a[:, :, :n], ACT.v(av[:, :, s:s + n]))
            for jt in range(KT):
                ps = dps[ei % 4]
                ei += 1
                for f in range(FT):
                    fw.mm(ps[:, :n], wd[jt][:, f, :], a[:, f, :n], start=(f == 0), stop=(f == FT - 1))
                fw.stt(xT[:, jt, s:s + n], ps[:, :n], modT[l][:, 40 + jt, j:j + 1], xT[:, jt, s:s + n],
                       ALU.mult, ALU.add)
        fw.pop()

    NT = T // 128
    NTC = TC // 128
    lat_blocks = [bl for bl in blocks if not bl[2]]
    ctx_blocks = [bl for bl in blocks if bl[2]]

    def head_norm_rope(raw, outs, blkb, hd, g_ap, rotb, cosT, sinT, s, n, is_ctx, tset, masks=None):
        sqb, rs, qg, t1, t2, nps, npr = tset
        fw.act(sqb[:, :n], raw, AF.Square)
        fw.mm(nps[:, :n], blkb[:], sqb[:, :n])
        fw.act(rs[:, :n], nps[:, :n], AF.Sqrt, bias=epsb[:, 0:1], scale=1.0 / hd)
        fw.recip(rs[:, :n], rs[:, :n])
        fw.stt(qg[:, :n], raw, g_ap, rs[:, :n], ALU.mult, ALU.mult)
        if is_ctx:
            res = qg
        else:
            fw.mm(npr[:, :n], rotb[:], qg[:, :n])
            fw.tt(t1[:, :n], qg[:, :n], cosT[:, s - TC:s - TC + n], ALU.mult)
            fw.tt(t2[:, :n], npr[:, :n], sinT[:, s - TC:s - TC + n], ALU.mult)
            fw.tt(t1[:, :n], t1[:, :n], t2[:, :n], ALU.add, eng="gpsimd")
            res = t1
        if masks is None:
            fw.copy(outs[0], res[:, :n], eng="gpsimd")
        else:
            for m, o in enumerate(outs):
                fw.ts(o, res[:, :n], masks[:, m:m + 1], ALU.mult, eng="gpsimd")

    def prep_sets(tag):
        sets = []
        for i in range(3):
            sets.append((fw.sbuf(f"{tag}sqb{i}", [128, 512], BF16), fw.sbuf(f"{tag}rs{i}", [128, 512]),
                         fw.sbuf(f"{tag}qg{i}", [128, 512], BF16), fw.sbuf(f"{tag}t1{i}", [128, 512]),
                         fw.sbuf(f"{tag}t2{i}", [128, 512]), fw.psum(f"{tag}nps{i}", [128, 512]),
                         fw.psum(f"{tag}npr{i}", [128, 512])))
        return sets

    def make_vdup(vrow0, nheads, Vd, vtmp, tps):
        ntile = (nheads * 64) // 128
        for vt in range(ntile):
            fw.dma(vtmp[:, :], PT[vrow0 + vt * 128:vrow0 + (vt + 1) * 128, :])
            for i in range(NT):
                fw.transpose(tps[:, i % 4, :], vtmp[:, i * 128:(i + 1) * 128], ident[:])
                for hh in range(2):
                    h = vt * 2 + hh
                    fw.copy(Vd[h][:, i, 0:64], tps[:, i % 4, hh * 64:(hh + 1) * 64], eng="vector")
                    fw.copy(Vd[h][:, i, 64:128], tps[:, i % 4, hh * 64:(hh + 1) * 64], eng="scalar")

    def attn_head(qviews, kviews, Vd_h, nmaps, scale, qb, sps_l, pT_l, oacc, dacc, dsum, cnt):
        (s, n, is_ctx) = qb
        kts = list(range(NTC)) if is_ctx else list(range(NT))
        steps = [(ki, kt, m) for ki, kt in enumerate(kts) for m in range(nmaps)]
        c0 = cnt[0]
        cnt[0] += len(steps)

        def score(i):
            ki, kt, m = steps[i]
            sp = sps_l[(c0 + i) % len(sps_l)]
            fw.mm(sp[:, :n], kviews[m](kt), qviews[m](s, n))
        depth = len(sps_l) - 1
        for i in range(min(depth, len(steps))):
            score(i)
        for i, (ki, kt, m) in enumerate(steps):
            if i + depth < len(steps):
                score(i + depth)
            sp = sps_l[(c0 + i) % len(sps_l)]
            pT = pT_l[(c0 + i) % len(pT_l)]
            fw.act(pT[:, :n], sp[:, :n], AF.Exp, scale=scale)
            fw.mm(oacc[m][:, :n], Vd_h[:, kt, :], pT[:, :n], start=(ki == 0), stop=(ki == len(kts) - 1))
            ds = dsum[m]
            de = "vector" if m == 0 else "gpsimd"
            if ki == 0:
                fw.copy(ds[:, :n], pT[:, :n], eng=de)
            else:
                fw.tt(ds[:, :n], pT[:, :n], ds[:, :n], ALU.add, eng=de)
        for m in range(nmaps):
            fw.mm(dacc[m][:, :n], ones[:], dsum[m][:, :n])

    def gqa_phase(l, b, need_ctx):
        fw.push()
        onesb = onesb_g
        qn = fw.sbuf("qn", [128, 2, T], BF16)
        kd = fw.sbuf("kd", [128, 2, T], BF16)
        Vd = [fw.sbuf(f"Vd{h}", [128, NT, 128], BF16) for h in range(2)]
        qblocks = (ctx_blocks if need_ctx else []) + lat_blocks
        fw.push()
        cosT = fw.sbuf("cosT", [128, TL]); sinT = fw.sbuf("sinT", [128, TL]); rot = fw.sbuf("rot", [128, 128])
        fw.dma(cosT[:], consts["cos_gq"][:]); fw.dma(sinT[:], consts["sin_gq"][:]); fw.dma(rot[:], consts["rot_gq"][:])
        rotb = fw.sbuf("rotb", [128, 128], BF16)
        fw.copy(rotb[:], rot[:])
        raw = [fw.sbuf(f"raw{i}", [128, 512]) for i in range(3)]
        tsets = prep_sets("g")
        tps = fw.psum("tps", [128, 4, 128])
        vtmp = fw.sbuf("vtmp", [128, T])
        ri = 0
        for t in range(2):
            for (s, n, is_ctx) in qblocks:
                r = raw[ri % 3]; ri += 1
                fw.dma(r[:, :n], PT[GQA_OFF + t * 128:GQA_OFF + (t + 1) * 128, s:s + n])
                head_norm_rope(r[:, :n], [qn[:, t, s:s + n]], blk64b, 64, pcol(l, "gq_qg"), rotb, cosT, sinT,
                               s, n, is_ctx, tsets[ri % 3])
            for (s, n, is_ctx) in blocks:
                r = raw[ri % 3]; ri += 1
                for hh in range(2):
                    fw.dma(r[hh * 64:(hh + 1) * 64, :n], PT[GQA_OFF + 256 + t * 64:GQA_OFF + 256 + (t + 1) * 64, s:s + n])
                head_norm_rope(r[:, :n], [kd[:, t, s:s + n]], blk64b, 64, pcol(l, "gq_kg"), rotb, cosT, sinT,
                               s, n, is_ctx, tsets[ri % 3])
        make_vdup(GQA_OFF + 384, 2, Vd, vtmp, tps)
        fw.pop()
        sps_l = [fw.psum(f"sps{i}", [128, 512]) for i in range(4)]
        pT_l = [fw.sbuf(f"pT{i}", [128, 512], BF16) for i in range(4)]
        oacc = [fw.psum("oacc0", [128, 512])]
        dacc = [fw.psum("dacc0", [128, 512])]
        dsum_l = [fw.sbuf(f"dsum{i}", [128, 512]) for i in range(2)]
        rec = fw.sbuf("rec", [128, 512])
        ob = [fw.sbuf(f"ob{i}", [128, 512], BF16) for i in range(2)]
        cnt = [0]
        oi = 0
        for h in range(4):
            t, g = h // 2, h % 2
            ph = 64 * g
            qv = [lambda s, n, t=t, ph=ph: qn[ph:ph + 64, t, s:s + n]]
            kv = [lambda kt, t=t, ph=ph: kd[ph:ph + 64, t, kt * 128:(kt + 1) * 128]]
            for qb in qblocks:
                (s, n, is_ctx) = qb
                attn_head(qv, kv, Vd[t], 1, 0.125, qb, sps_l, pT_l, oacc, dacc, [dsum_l[oi % 2]], cnt)
                fw.recip(rec[ph:ph + 64, :n], dacc[0][ph:ph + 64, :n])
                o = ob[oi % 2]; oi += 1
                fw.tt(o[ph:ph + 64, :n], oacc[0][ph:ph + 64, :n], rec[ph:ph + 64, :n], ALU.mult)
                fw.dma(OT[768 + h * 64:768 + (h + 1) * 64, s:s + n], o[ph:ph + 64, :n])
        fw.pop()

    def da_phase(l, b, need_ctx):
        lam_init = 0.8 - 0.6 * float(np.exp(-0.3 * l))
        fw.push()
        onesb = onesb_g
        lamb = fw.sbuf("lamb_s", [128, 128])
        fw.dma(lamb[:], lamb_d[l])
        lt = fw.sbuf("lt", [128, 64])
        lsum = fw.sbuf("lsum", [128, 2])
        nlam = fw.sbuf("nlam", [128, 1])
        sg = fw.sbuf("sg", [128, 1])
        fw.tt(lt[:, 0:32], lamb[:, 0:32], lamb[:, 32:64], ALU.mult)
        fw.tt(lt[:, 32:64], lamb[:, 64:96], lamb[:, 96:128], ALU.mult)
        fw.op("vector", "reduce_sum", [lt[:]], [lsum[:]], lsum[:, 0:1].ap, lt[:, 0:32].ap, AX.X)
        fw.op("vector", "reduce_sum", [lt[:]], [lsum[:]], lsum[:, 1:2].ap, lt[:, 32:64].ap, AX.X)
        fw.act(lsum[:], lsum[:], AF.Exp)
        fw.stt(nlam[:], lsum[:, 1:2], -lam_init, lsum[:, 0:1], ALU.add, ALU.subtract)
        fw.ts(sg[:], pcol(l, "da_sub"), 1.0 - lam_init, ALU.mult)
        qm = [fw.sbuf(f"qm{m}", [128, 2, T], BF16) for m in range(2)]
        kn = fw.sbuf("kn", [128, 2, T], BF16)
        Vd = [fw.sbuf(f"Vd{h}", [128, NT, 128], BF16) for h in range(4)]
        qblocks = (ctx_blocks if need_ctx else []) + lat_blocks
        fw.push()
        cosT = fw.sbuf("cosT", [128, TL]); sinT = fw.sbuf("sinT", [128, TL]); rot = fw.sbuf("rot", [128, 128])
        fw.dma(cosT[:], consts["cos_da"][:]); fw.dma(sinT[:], consts["sin_da"][:]); fw.dma(rot[:], consts["rot_da"][:])
        rotb = fw.sbuf("rotb", [128, 128], BF16)
        fw.copy(rotb[:], rot[:])
        blk32 = fw.sbuf("blk32", [128, 128])
        fw.dma(blk32[:], consts["blk32"][:])
        blk32b = fw.sbuf("blk32b", [128, 128], BF16)
        fw.copy(blk32b[:], blk32[:])
        dmask = fw.sbuf("dmask", [128, 2])
        fw.dma(dmask[:], consts["dmask"][:])
        raw = [fw.sbuf(f"raw{i}", [128, 512]) for i in range(3)]
        tsets = prep_sets("d")
        tps = fw.psum("tps", [128, 4, 128])
        vtmp = fw.sbuf("vtmp", [128, T])
        ri = 0
        for t in range(2):
            for (s, n, is_ctx) in qblocks:
                r = raw[ri % 3]; ri += 1
                fw.dma(r[:, :n], PT[DA_OFF + t * 128:DA_OFF + (t + 1) * 128, s:s + n])
                head_norm_rope(r[:, :n], [qm[0][:, t, s:s + n], qm[1][:, t, s:s + n]], blk32b, 32,
                               pcol(l, "da_qg"), rotb, cosT, sinT, s, n, is_ctx, tsets[ri % 3], masks=dmask)
            for (s, n, is_ctx) in blocks:
                r = raw[ri % 3]; ri += 1
                fw.dma(r[:, :n], PT[DA_OFF + 256 + t * 128:DA_OFF + 256 + (t + 1) * 128, s:s + n])
                head_norm_rope(r[:, :n], [kn[:, t, s:s + n]], blk32b, 32, pcol(l, "da_kg"), rotb, cosT, sinT,
                               s, n, is_ctx, tsets[ri % 3])
        make_vdup(DA_OFF + 512, 4, Vd, vtmp, tps)
        fw.pop()
        nps = fw.psum("anps", [128, 512])
        tmps = [fw.sbuf(f"nt{i}", [128, 512]) for i in range(2)]
        sps_l = [fw.psum(f"sps{i}", [128, 512]) for i in range(3)]
        pT_l = [fw.sbuf(f"pT{i}", [128, 512], BF16) for i in range(4)]
        oacc = [fw.psum(f"oacc{m}", [128, 512]) for m in range(2)]
        dacc = [fw.psum(f"dacc{m}", [128, 512]) for m in range(2)]
        dsum_l = [[fw.sbuf(f"dsum{i}_{m}", [128, 512]) for m in range(2)] for i in range(2)]
        hq = [0]
        rec = [fw.sbuf(f"rec{m}", [128, 512]) for m in range(2)]
        o1 = fw.sbuf("o1", [128, 512])
        osb = fw.sbuf("osb", [128, 512])
        ob = [fw.sbuf(f"ob{i}", [128, 512], BF16) for i in range(2)]
        sq, rs = tmps[0], tmps[1]
        cnt = [0]
        oi = 0
        for t in range(2):
            for qb in qblocks:
                (s, n, is_ctx) = qb
                for g in range(2):
                    h = 2 * t + g
                    ph = 64 * g
                    qv = [lambda s, n, t=t, ph=ph, m=m: qm[m][ph:ph + 64, t, s:s + n] for m in range(2)]
                    kv = [lambda kt, t=t, ph=ph: kn[ph:ph + 64, t, kt * 128:(kt + 1) * 128]] * 2
                    attn_head(qv, kv, Vd[h], 2, 32 ** -0.5, qb, sps_l, pT_l, oacc, dacc, dsum_l[hq[0] % 2], cnt)
                    hq[0] += 1
                    for m in range(2):
                        fw.recip(rec[m][ph:ph + 64, :n], dacc[m][ph:ph + 64, :n])
                    fw.tt(osb[ph:ph + 64, :n], oacc[0][ph:ph + 64, :n], rec[0][ph:ph + 64, :n], ALU.mult)
                    fw.tt(o1[ph:ph + 64, :n], oacc[1][ph:ph + 64, :n], rec[1][ph:ph + 64, :n], ALU.mult)
                    fw.stt(osb[ph:ph + 64, :n], o1[ph:ph + 64, :n], nlam[ph:ph + 64, 0:1], osb[ph:ph + 64, :n],
                           ALU.mult, ALU.add)
                fw.act(sq[:, :n], osb[:, :n], AF.Square)
                fw.mm(nps[:, :n], blk64[:], sq[:, :n])
                fw.act(rs[:, :n], nps[:, :n], AF.Sqrt, bias=epsb[:, 0:1], scale=1.0 / 64)
                fw.recip(rs[:, :n], rs[:, :n])
                o = ob[oi % 2]; oi += 1
                fw.stt(o[:, :n], osb[:, :n], sg[:, 0:1], rs[:, :n], ALU.mult, ALU.mult)
                fw.dma(OT[256 + t * 128:256 + (t + 1) * 128, s:s + n], o[:, :n])
        fw.pop()

    def chunk_order(rev):
        if not rev:
            return list(range(NT))
        return list(range(NTC - 1, -1, -1)) + list(range(NT - 1, NTC - 1, -1))

    def cumsum_chunks(A, B, rev):
        cur, oth = A, B
        s = 1
        while s < 128:
            cv = cur.t[:, :].rearrange("p (c i) -> p c i", i=128)
            ov = oth.t[:, :].rearrange("p (c i) -> p c i", i=128)
            if not rev:
                fw.tt(oth.v(ov[:, :, s:]), cur.v(cv[:, :, s:]), cur.v(cv[:, :, :128 - s]), ALU.add)
                fw.copy(oth.v(ov[:, :, :s]), cur.v(cv[:, :, :s]), eng="gpsimd")
            else:
                fw.tt(oth.v(ov[:, :, :128 - s]), cur.v(cv[:, :, :128 - s]), cur.v(cv[:, :, s:]), ALU.add)
                fw.copy(oth.v(ov[:, :, 128 - s:]), cur.v(cv[:, :, 128 - s:]), eng="gpsimd")
            cur, oth = oth, cur
            s *= 2
        return cur, oth

    def gla_phase(l, b, need_ctx):
        fw.push()
        Fb = [fw.sbuf(f"F{i}", [128, T]) for i in range(4)]
        F1, F2, F3, F4 = Fb
        Vdup = fw.sbuf("Vdup", [128, NT, 4, 128], BF16)
        QM = [fw.sbuf(f"QM{h}", [128, T], BF16) for h in range(4)]
        KTt = fw.sbuf("KTt", [128, T], BF16)
        KH = fw.sbuf("KH", [128, T], BF16)
        oaccT = fw.sbuf("oaccT", [128, 2, T])
        Pend = fw.sbuf("Pend", [128, NT])
        gfb = fw.sbuf("gfb", [48, T])
        a2 = fw.sbuf("a2", [48, 128])
        nab = fw.sbuf("nab", [128, 2])
        hm4 = fw.sbuf("hm4", [128, 4])
        tri = fw.sbuf("tri", [128, 4, 128])
        fw.dma(hm4[:], consts["hmask4s"][:])
        for d in range(2):
            fw.dma(a2[32 * d:32 * d + 16, :], gla_a2_d[l, d])
            fw.dma(gfb[32 * d:32 * d + 16, :], PT[GLA_OFF + 512 + 16 * d:GLA_OFF + 528 + 16 * d, :])
        c0 = PACK["gla_ab"][0]
        fw.ts(nab[:], pk[l][:, c0:c0 + 2], -1.0, ALU.mult)
        S = fw.sbuf("Sst", [128, 64])
        Sdup = fw.sbuf("Sdup", [128, 128], BF16)
        KHt = [fw.sbuf(f"KHt{i}", [128, 640], BF16) for i in range(2)]
        for i in range(2):
            fw.memset(KHt[i][:], 0.0)
        AT = [fw.sbuf(f"AT{i}", [128, 4, 128], BF16) for i in range(2)]
        tpsb = fw.psum("tpsb", [128, 4, 128], BF16)
        tps = fw.psum("tps", [128, 4, 128])
        aps = [fw.psum(f"aps{i}", [128, 4, 128]) for i in range(2)]
        ops = [fw.psum(f"ops{i}", [128, 4, 128]) for i in range(2)]
        sps = fw.psum("sps", [128, 64])
        zps = fw.psum("zps", [128, 512])
        for vt in range(2):
            fw.dma(F1[:], PT[GLA_OFF + 256 + vt * 128:GLA_OFF + 256 + (vt + 1) * 128, :])
            for i in range(NT):
                fw.transpose(tps[:, i % 4, :], F1[:, i * 128:(i + 1) * 128], ident[:])
                for hh in range(2):
                    h = vt * 2 + hh
                    fw.copy(Vdup[:, i, h, 0:64], tps[:, i % 4, hh * 64:(hh + 1) * 64], eng="vector")
                    fw.copy(Vdup[:, i, h, 64:128], tps[:, i % 4, hh * 64:(hh + 1) * 64], eng="scalar")
        fw.dma(F1[:], PT[GLA_OFF:GLA_OFF + 128, :])
        fw.dma(F2[:], PT[GLA_OFF + 128:GLA_OFF + 256, :])
        for d in range(2):
            rev = d == 1
            fw.dma(tri[:], consts[f"tri4_{d}"][:])
            for (s, n, is_ctx) in blocks:
                fw.mm(zps[:, :n], a2[32 * d:32 * d + 16, :], gfb[32 * d:32 * d + 16, s:s + n])
                fw.act(F3[:, s:s + n], zps[:, :n], AF.Exp, bias=nab[:, d:d + 1], scale=-1.0)
            fw.act(F3[:], F3[:], AF.Ln, bias=1.0)
            fw.ts(F3[:], F3[:], -1.0 / 16.0, ALU.mult)
            bb, ff = cumsum_chunks(F3, F4, rev)
            bv = bb.t[:, :].rearrange("p (c i) -> p c i", i=128)
            eidx = 0 if rev else 127
            fw.act(Pend[:], bb.v(bv[:, :, eidx]), AF.Exp)
            fw.act(ff[:], bb[:], AF.Exp)
            for h in range(4):
                fw.stt(QM[h][:], F1[:], hm4[:, h:h + 1], ff[:], ALU.mult, ALU.mult,
                       eng=("gpsimd" if h % 2 else "vector"))
            fw.act(ff[:], bb[:], AF.Exp, scale=-1.0)
            fw.tt(KTt[:], F2[:], ff[:], ALU.mult)
            for c in range(NT):
                fw.act(ff[:, c * 128:(c + 1) * 128], bb[:, c * 128:(c + 1) * 128], AF.Exp,
                       bias=bb[:, c * 128 + eidx:c * 128 + eidx + 1], scale=-1.0)
            fw.tt(KH[:], F2[:], ff[:], ALU.mult, eng="gpsimd")
            first = True
            for ci, c in enumerate(chunk_order(rev)):
                cs = slice(c * 128, (c + 1) * 128)
                is_ctx = c < NTC
                kht = KHt[ci % 2]
                at = AT[ci % 2]
                ap_, op_ = aps[ci % 2], ops[ci % 2]
                fw.transpose(tpsb[:, 0, :], KH[:, cs], identb[:])
                kv = kht.t[:, :].rearrange("p (h x) -> p h x", x=160)
                fw.copy(kht.v(kv[:, :, 0:32]), tpsb.v(tpsb.t[:, 0, :].rearrange("p (h x) -> p h x", x=32)))
                want_out = need_ctx or not is_ctx
                if want_out:
                    for h in range(4):
                        fw.mm(ap_[:, h, :], KTt[:, cs], QM[h][:, cs])
                    fw.tt(at[:], ap_[:], tri[:], ALU.mult)
                    for h in range(4):
                        if not first:
                            fw.mm(op_[:, h, :], Sdup[:], QM[h][:, cs], start=True, stop=False)
                        fw.mm(op_[:, h, :], Vdup[:, c, h, :], at[:, h, :], start=first, stop=True)
                    o4 = op_.t[:, :, :].rearrange("p (a g) t -> p a g t", g=2)
                    for g in range(2):
                        dst = oaccT[64 * g:64 * g + 64, :, cs]
                        src = op_.v(o4[64 * g:64 * g + 64, :, g, :])
                        if d == 0:
                            fw.copy(dst, src, eng=("vector" if g == 0 else "scalar"))
                        else:
                            fw.tt(dst, src, dst, ALU.add, eng="vector")
                for h in range(4):
                    fw.mm(sps[:, :], kht[:, h * 128:(h + 1) * 128], Vdup[:, c, h, 0:64], start=(h == 0), stop=(h == 3))
                if first:
                    fw.copy(S[:], sps[:])
                else:
                    fw.stt(S[:], S[:], Pend[:, c:c + 1], sps[:], ALU.mult, ALU.add)
                fw.copy(Sdup[:, 0:64], S[:], eng="scalar")
                fw.copy(Sdup[:, 64:128], S[:], eng="gpsimd")
                first = False
        sq, rs, rr = F3, F4, F1
        ob = [fw.sbuf(f"gob{i}", [128, 512], BF16) for i in range(2)]
        oi = 0
        qblocks = (ctx_blocks if need_ctx else []) + lat_blocks
        for t in range(2):
            for (s, n, is_ctx) in qblocks:
                fw.dma(rr[:, :n], PT[GLA_OFF + 544 + t * 128:GLA_OFF + 544 + (t + 1) * 128, s:s + n])
                fw.act(rr[:, :n], rr[:, :n], AF.Silu)
                fw.act(sq[:, :n], oaccT[:, t, s:s + n], AF.Square)
                fw.mm(zps[:, :n], blk64[:], sq[:, :n])
                fw.act(rs[:, :n], zps[:, :n], AF.Sqrt, bias=epsb[:, 0:1], scale=1.0 / 64)
                fw.recip(rs[:, :n], rs[:, :n])
                fw.stt(sq[:, :n], oaccT[:, t, s:s + n], pcol(l, "gla_ng"), rs[:, :n], ALU.mult, ALU.mult)
                o = ob[oi % 2]; oi += 1
                fw.tt(o[:, :n], sq[:, :n], rr[:, :n], ALU.mult, eng="gpsimd")
                fw.dma(OT[512 + t * 128:512 + (t + 1) * 128, s:s + n], o[:, :n])
        fw.pop()

    RWS = fw.dram("RWS", [2, 2, 8, 128, T], BF16)
    VDs = fw.dram("VDs", [128, NT, 4, 128], BF16)
    BON = fw.dram("BON", [256, T], F32)
    PENDs = fw.dram("PENDs", [128, 2, 2, NT], F32)
    GATE = fw.dram("GATE", [256, T], F32)

    def shift_mix(dst, raw, l, ct):
        m0 = pcol(l, "rw_mu0", ct)
        m1 = pcol(l, "rw_mu1", ct)
        c0 = pcol(l, "rw_c0", ct)
        fw.ts(dst[:, :], raw[:, :], c0, ALU.mult)
        for (s, e) in ((0, TC), (TC, T)):
            fw.stt(dst[:, s + 1:e], raw[:, s:e - 1], m0, dst[:, s + 1:e], ALU.mult, ALU.add)
            fw.stt(dst[:, s:e - 1], raw[:, s + 1:e], m1, dst[:, s:e - 1], ALU.mult, ALU.add, eng="gpsimd")

    def rwkv_phase(l, b, need_ctx):
        fw.push()
        lora = [fw.sbuf(f"lora{i}", [128, T], BF16) for i in range(3)]
        w2s = fw.sbuf("w2s", [128, 256], BF16)
        a2s = fw.sbuf("a2s", [128, 256], BF16)
        g2s = fw.sbuf("g2s", [128, 256], BF16)
        fw.dma(w2s[:], rw_w2_d[l], eng="gpsimd")
        fw.dma(a2s[:], rw_a2_d[l], eng="gpsimd")
        fw.dma(g2s[:], rw_g2_d[l], eng="gpsimd")
        hm2 = fw.sbuf("hm2", [128, 4])
        fw.dma(hm2[:], consts["hm2"][:])
        c0 = PACK["rw_mu0"][0]
        c1 = PACK["rw_mu1"][0]
        cc = PACK["rw_c0"][0]
        fw.tt(pk[l][:, cc:cc + 9], pk[l][:, c0:c0 + 9], pk[l][:, c1:c1 + 9], ALU.add)
        fw.ts(pk[l][:, cc:cc + 9], pk[l][:, cc:cc + 9], -1.0, ALU.mult, 1.0, ALU.add)
        ck = PACK["rw_ka"][0]
        co = PACK["rw_omka"][0]
        fw.ts(pk[l][:, co:co + 2], pk[l][:, ck:ck + 2], -1.0, ALU.mult, 1.0, ALU.add)
        Bf = [fw.sbuf(f"B{i}", [128, T]) for i in range(10)]
        stgb = [fw.sbuf(f"stgb{i}", [128, T], BF16) for i in range(2)]
        zps = [fw.psum(f"zps{i}", [128, 512]) for i in range(3)]
        tps = fw.psum("tps", [128, 4, 128])
        vd = [fw.sbuf(f"vd{i}", [128, 4, 128], BF16) for i in range(2)]
        eps12 = fw.sbuf("eps12", [128, 1])
        fw.memset(eps12[:], 1e-12)
        raw, sh = Bf[0], Bf[1]
        for i, fn in ((0, AF.Tanh), (1, None), (2, AF.Sigmoid)):
            fw.dma(raw[:], PT[RW_OFF + (6 + i) * 128:RW_OFF + (7 + i) * 128, :])
            shift_mix(sh, raw, l, 6 + i)
            if fn is None:
                fw.copy(lora[i][:], sh[:])
            else:
                fw.act(lora[i][:], sh[:], fn)
        for vt in range(2):
            fw.dma(raw[:], PT[RW_OFF + 512 + vt * 128:RW_OFF + 512 + (vt + 1) * 128, :])
            shift_mix(sh, raw, l, 4 + vt)
            for i in range(NT):
                fw.transpose(tps[:, i % 4, :], sh[:, i * 128:(i + 1) * 128], ident[:])
                v_ = vd[i % 2]
                for hh in range(2):
                    fw.copy(v_[:, hh, 0:64], tps[:, i % 4, hh * 64:(hh + 1) * 64], eng="vector")
                    fw.copy(v_[:, hh, 64:128], tps[:, i % 4, hh * 64:(hh + 1) * 64], eng="scalar")
                fw.dma(VDs[:, i, 2 * vt:2 * vt + 2, :], v_[:, 0:2, :])
        Pend = fw.dram("Pend_d", [2, 2, 128, NT], F32) if False else None
        pend_s = fw.sbuf("pend_s", [128, 2, 2, NT])
        Fk, Fkk, Fr, Fbon, Ll, Aa, Bb, Fa, Fkd, Fb = Bf
        for tau in range(2):
            fw.dma(raw[:], PT[RW_OFF + 256 + tau * 128:RW_OFF + 256 + (tau + 1) * 128, :]) if False else None
            fw.dma(Ll[:], PT[RW_OFF + 256 + tau * 128:RW_OFF + 256 + (tau + 1) * 128, :])
            shift_mix(Fk, Ll, l, 2 + tau)
            fw.dma(Ll[:], PT[RW_OFF + tau * 128:RW_OFF + (tau + 1) * 128, :])
            shift_mix(Fr, Ll, l, tau)
            fw.ts(Fkk[:], Fk[:], pcol(l, "rw_kk", tau), ALU.mult)
            for (s, n, is_ctx) in blocks:
                zp = zps[0]
                fw.act(Aa[:, s:s + n], Fkk[:, s:s + n], AF.Square)
                fw.mm(zp[:, :n], blk64[:], Aa[:, s:s + n])
                fw.act(Aa[:, s:s + n], zp[:, :n], AF.Sqrt, bias=eps12[:, 0:1], scale=1.0)
            fw.recip(Aa[:], Aa[:])
            fw.tt(Fkk[:], Fkk[:], Aa[:], ALU.mult)
            for bi, (s, n, is_ctx) in enumerate(blocks):
                zp = zps[bi % 3]
                fw.mm(zp[:, :n], g2s[:, tau * 128:(tau + 1) * 128], lora[2][:, s:s + n])
                fw.copy(Aa[:, s:s + n], zp[:, :n], eng="scalar")
            fw.dma(GATE[tau * 128:(tau + 1) * 128, :], Aa[:])
            for d in range(2):
                rev = d == 1
                eidx = 0 if rev else 127
                ph = 64 * d
                for bi, (s, n, is_ctx) in enumerate(blocks):
                    zp = zps[bi % 3]
                    fw.mm(zp[:, :n], w2s[ph:ph + 64, tau * 128:(tau + 1) * 128], lora[0][ph:ph + 64, s:s + n])
                    fw.act(Ll[:, s:s + n], zp[:, :n], AF.Sigmoid, bias=pcol(l, "rw_w0", d * 2 + tau))
                    zp2 = zps[(bi + 1) % 3]
                    fw.mm(zp2[:, :n], a2s[ph:ph + 64, tau * 128:(tau + 1) * 128], lora[1][ph:ph + 64, s:s + n])
                    fw.act(Fa[:, s:s + n], zp2[:, :n], AF.Sigmoid, bias=pcol(l, "rw_a0", d * 2 + tau))
                fw.ts(Ll[:], Ll[:], -0.6065306597126334, ALU.mult)
                cur = Ll
                pp = [Aa, Bb]
                st = 1
                k_ = 0
                while st < 128:
                    oth = pp[k_ % 2]
                    cv = cur.t[:, :].rearrange("p (c i) -> p c i", i=128)
                    ov = oth.t[:, :].rearrange("p (c i) -> p c i", i=128)
                    if not rev:
                        fw.tt(oth.v(ov[:, :, st:]), cur.v(cv[:, :, st:]), cur.v(cv[:, :, :128 - st]), ALU.add)
                        fw.copy(oth.v(ov[:, :, :st]), cur.v(cv[:, :, :st]), eng="scalar")
                    else:
                        fw.tt(oth.v(ov[:, :, :128 - st]), cur.v(cv[:, :, :128 - st]), cur.v(cv[:, :, st:]), ALU.add)
                        fw.copy(oth.v(ov[:, :, 128 - st:]), cur.v(cv[:, :, 128 - st:]), eng="scalar")
                    cur = oth
                    st *= 2
                    k_ += 1
                assert cur is Aa
                cum = Aa
                cvw = cum.t[:, :].rearrange("p (c i) -> p c i", i=128)
                fw.act(pend_s[:, d, tau, :], cum.v(cvw[:, :, eidx]), AF.Exp)
                fw.tt(Ll[:], cum[:], Ll[:], ALU.subtract)
                fw.ts(Fkd[:], Fa[:], pcol(l, "rw_ka", tau), ALU.mult, pcol(l, "rw_omka", tau), ALU.add)
                fw.tt(Fkd[:], Fkd[:], Fk[:], ALU.mult, eng="gpsimd")
                fw.tt(Fb[:], Fkk[:], Fa[:], ALU.mult, eng="gpsimd")
                fw.stt(Fa[:], Fr[:], pcol(l, "rw_rk", tau), Fkd[:], ALU.mult, ALU.mult)
                for bi, (s, n, is_ctx) in enumerate(blocks):
                    zp = zps[bi % 3]
                    fw.mm(zp[:, :n], blk64[:], Fa[:, s:s + n])
                    if d == 0:
                        fw.copy(Fbon[:, s:s + n], zp[:, :n], eng="scalar")
                    else:
                        fw.tt(Fbon[:, s:s + n], zp[:, :n], Fbon[:, s:s + n], ALU.add)
                si = [0]

                def emit(arr_idx, fn):
                    o = stgb[si[0] % 2]
                    si[0] += 1
                    fn(o)
                    fw.dma(RWS[d, tau, arr_idx], o[:])
                fw.act(Bb[:], Ll[:], AF.Exp)
                for hh in range(2):
                    emit(hh, lambda o, hh=hh: fw.stt(o[:], Fkk[:], hm2[:, 2 + hh:3 + hh], Bb[:], ALU.mult, ALU.mult,
                                                    eng=("vector" if hh == 0 else "gpsimd")))
                fw.act(Bb[:], cum[:], AF.Exp)
                for hh in range(2):
                    emit(2 + hh, lambda o, hh=hh: fw.stt(o[:], Fr[:], hm2[:, hh:hh + 1], Bb[:], ALU.mult, ALU.mult,
                                                        eng=("vector" if hh == 0 else "gpsimd")))
                fw.act(Bb[:], cum[:], AF.Exp, scale=-1.0)
                emit(4, lambda o: fw.tt(o[:], Fb[:], Bb[:], ALU.mult))
                emit(5, lambda o: fw.tt(o[:], Fkd[:], Bb[:], ALU.mult, eng="gpsimd"))
                for c in range(NT):
                    fw.act(Bb[:, c * 128:(c + 1) * 128], cum[:, c * 128:(c + 1) * 128], AF.Exp,
                           bias=cum[:, c * 128 + eidx:c * 128 + eidx + 1], scale=-1.0)
                emit(6, lambda o: fw.tt(o[:], Fb[:], Bb[:], ALU.mult))
                emit(7, lambda o: fw.tt(o[:], Fkd[:], Bb[:], ALU.mult, eng="gpsimd"))
            fw.dma(Ll[:], PT[RW_OFF + 512 + tau * 128:RW_OFF + 512 + (tau + 1) * 128, :])
            shift_mix(Aa, Ll, l, 4 + tau)
            fw.tt(Aa[:], Aa[:], Fbon[:], ALU.mult)
            fw.dma(BON[tau * 128:(tau + 1) * 128, :], Aa[:])
        fw.dma(PENDs[:], pend_s[:])
        fw.pop()

        fw.push()
        pend = fw.sbuf("pend", [128, 2, 2, NT])
        fw.dma(pend[:], PENDs[:])
        Vdup = fw.sbuf("Vdup", [128, NT, 4, 128], BF16)
        fw.dma(Vdup[:], VDs[:])
        yaccT = fw.sbuf("yaccT", [128, 2, T])
        fw.memset(yaccT[:, 0, :], 0.0)
        fw.memset(yaccT[:, 1, :], 0.0, eng="gpsimd")
        I4 = fw.sbuf("I4", [128, 2, 128])
        fw.dma(I4[:], consts["I2"][:])
        identb_ = identb
        PB = [fw.psum(f"PB{i}", [128, 512]) for i in range(6)]
        pbi = [0]

        def bank():
            p = PB[pbi[0] % len(PB)]
            pbi[0] += 1
            return p

        def v4(bk, w=128):
            return bk.t[:, 0:4 * w].rearrange("p (h x) -> p h x", x=w)

        evi = [0]

        def evac(out, in_):
            e = "scalar" if evi[0] % 2 == 0 else "vector"
            evi[0] += 1
            fw.copy(out, in_, eng=e)

        def chunk_gen(d):
            rev = d == 1
            maskN = fw.sbuf(f"maskN{d}", [128, 4, 128])
            maskAB = fw.sbuf(f"maskAB{d}", [128, 2, 256])
            fw.dma(maskN[:], consts[f"maskN_{d}"][:])
            fw.dma(maskAB[:], consts[f"maskAB_{d}"][:])
            CH = [[fw.sbuf(f"CH{d}{i}_{tau}", [128, 8, 128], BF16) for tau in range(2)] for i in range(2)]
            BKt = [fw.sbuf(f"BKt{d}{i}", [128, 4, 384], BF16) for i in range(2)]
            for i in range(2):
                fw.memset(BKt[i][:], 0.0)
            X = [fw.sbuf(f"X{d}{i}", [128, 4, 128], BF16) for i in range(2)]
            XT = [fw.sbuf(f"XT{d}{i}", [128, 4, 128], BF16) for i in range(2)]
            Wt = [fw.sbuf(f"Wt{d}{i}", [128, 4, 128], BF16) for i in range(2)]
            AB = [[fw.sbuf(f"AB{d}{i}_{tau}", [128, 2, 256], BF16) for tau in range(2)] for i in range(2)]
            AK = [[fw.sbuf(f"AK{d}{i}_{tau}", [128, 2, 256], BF16) for tau in range(2)] for i in range(2)]
            Z = fw.sbuf(f"Zz{d}", [128, 4, 64], BF16)
            Udup = fw.sbuf(f"Udup{d}", [128, 4, 128], BF16)
            ST = [fw.sbuf(f"ST{d}{tau}", [128, 64]) for tau in range(2)]
            STd = [fw.sbuf(f"STd{d}{tau}", [128, 128], BF16) for tau in range(2)]
            tpsb = fw.psum(f"tpsb{d}", [128, 4, 128], BF16)
            for tau in range(2):
                fw.memset(ST[tau][:], 0.0)
                fw.memset(STd[tau][:], 0.0)
            order = chunk_order(rev)

            def load(ci):
                c = order[ci]
                cs = slice(c * 128, (c + 1) * 128)
                for tau in range(2):
                    fw.dma(CH[ci % 2][tau][:], RWS.v(RWS.t[d, tau, :, :, cs].rearrange("a p t -> p a t")))
            load(0)
            yield
            for ci, c in enumerate(order):
                cs = slice(c * 128, (c + 1) * 128)
                is_ctx = c < NTC
                want_out = need_ctx or not is_ctx
                ch = CH[ci % 2]
                bkt = BKt[ci % 2]
                if ci + 1 < len(order):
                    load(ci + 1)
                for tau in range(2):
                    fw.transpose(tpsb[:, 2 * tau, :], ch[tau][:, 6, :], identb_[:])
                    fw.transpose(tpsb[:, 2 * tau + 1, :], ch[tau][:, 7, :], identb_[:])
                bv = bkt.t[:, :, :].rearrange("p a (h x) -> p a h x", x=192)
                fw.copy(bkt.v(bv[:, :, :, 0:64]), tpsb.v(tpsb.t[:, :, :].rearrange("p a (h x) -> p a h x", x=64)),
                        eng="scalar")
                nb = bank()
                for h in range(4):
                    tau, hh = h // 2, h % 2
                    fw.mm(nb.v(v4(nb)[:, h, :]), ch[tau][:, hh, :], ch[tau][:, 4, :])
                x0 = X[0]
                fw.tt(x0[:], nb.v(v4(nb)), maskN[:], ALU.mult)
                yield
                ab, ak = AB[ci % 2], AK[ci % 2]
                for tau in range(2):
                    b2, b3 = bank(), bank()
                    for hh in range(2):
                        for (bk, arr) in ((b2, 4), (b3, 5)):
                            o = bk.t[:, :].rearrange("p (h x) -> p h x", x=256)
                            fw.mm(bk.v(o[:, hh, 0:128]), ch[tau][:, arr, :], ch[tau][:, hh, :])
                            fw.mm(bk.v(o[:, hh, 128:256]), ch[tau][:, arr, :], ch[tau][:, 2 + hh, :])
                    fw.tt(ab[tau][:], b2.v(b2.t[:, :].rearrange("p (h x) -> p h x", x=256)), maskAB[:], ALU.mult)
                    fw.tt(ak[tau][:], b3.v(b3.t[:, :].rearrange("p (h x) -> p h x", x=256)), maskAB[:], ALU.mult)
                    yield
                xt0 = XT[0]
                w0 = Wt[0]
                for tau in range(2):
                    fw.copy(xt0[:, 2 * tau:2 * tau + 2, :], ab[tau][:, :, 0:128], eng="gpsimd")
                    fw.tt(w0[:, 2 * tau:2 * tau + 2, :], ab[tau][:, :, 0:128], I4[:], ALU.add, eng="gpsimd")
                xc, xtc, wc = x0, xt0, w0
                for p in range(6):
                    xn, xtn, wn = X[(p + 1) % 2], XT[(p + 1) % 2], Wt[(p + 1) % 2]
                    bx = bank()
                    for h in range(4):
                        fw.mm(bx.v(v4(bx)[:, h, :]), xtc[:, h, :], xc[:, h, :])
                    if p < 5:
                        bxt = bank()
                        for h in range(4):
                            fw.mm(bxt.v(v4(bxt)[:, h, :]), xc[:, h, :], xtc[:, h, :])
                    evac(xn[:], bx.v(v4(bx)))
                    if p < 5:
                        evac(xtn[:], bxt.v(v4(bxt)))
                    yield
                    bw = bank()
                    for h in range(4):
                        fw.mm(bw.v(v4(bw)[:, h, :]), identb_[:], wc[:, h, :], start=True, stop=False)
                        fw.mm(bw.v(v4(bw)[:, h, :]), xn[:, h, :], wc[:, h, :], start=False, stop=True)
                    evac(wn[:], bw.v(v4(bw)))
                    xc, xtc, wc = xn, xtn, wn
                    yield
                wT = wc
                gb = bank()
                g4 = gb.t[:, 0:256].rearrange("p (h x) -> p h x", x=64)
                for h in range(4):
                    tau, hh = h // 2, h % 2
                    fw.mm(gb.v(g4[:, h, :]), ch[tau][:, hh, :], STd[tau][:, 0:64], start=True, stop=False)
                    fw.mm(gb.v(g4[:, h, :]), ak[tau][:, hh, 0:128], Vdup[:, c, h, 0:64], start=False, stop=True)
                fw.copy(Z[:], gb.v(g4), eng="scalar")
                yield
                ub = bank()
                u4 = ub.t[:, 0:256].rearrange("p (h x) -> p h x", x=64)
                for h in range(4):
                    fw.mm(ub.v(u4[:, h, :]), wT[:, h, :], Z[:, h, :])
                fw.copy(Udup[:, :, 0:64], ub.v(u4), eng="vector")
                fw.copy(Udup[:, :, 64:128], ub.v(u4), eng="scalar")
                yield
                if want_out:
                    yb = bank()
                    for h in range(4):
                        tau, hh = h // 2, h % 2
                        o = yb.v(v4(yb)[:, h, :])
                        fw.mm(o, STd[tau][:], ch[tau][:, 2 + hh, :], start=True, stop=False)
                        fw.mm(o, Udup[:, h, :], ab[tau][:, hh, 128:256], start=False, stop=False)
                        fw.mm(o, Vdup[:, c, h, :], ak[tau][:, hh, 128:256], start=False, stop=True)
                    o4 = yb.t[:, :].rearrange("p (a g t) -> p a g t", g=2, t=128)
                    for g in range(2):
                        dst = yaccT[64 * g:64 * g + 64, :, cs]
                        src = yb.v(o4[64 * g:64 * g + 64, :, g, :])
                        fw.tt(dst, src, dst, ALU.add, eng="vector")
                for tau in range(2):
                    sb = bank()
                    for hh in range(2):
                        h = 2 * tau + hh
                        fw.mm(sb[:, 0:64], bkt[:, 2 * tau, hh * 128:(hh + 1) * 128], Udup[:, h, 0:64],
                              start=(hh == 0), stop=False)
                        fw.mm(sb[:, 0:64], bkt[:, 2 * tau + 1, hh * 128:(hh + 1) * 128], Vdup[:, c, h, 0:64],
                              start=False, stop=(hh == 1))
                    fw.stt(ST[tau][:], ST[tau][:], pend[:, d, tau, c:c + 1], sb[:, 0:64], ALU.mult, ALU.add)
                    fw.copy(STd[tau][:, 0:64], ST[tau][:], eng="scalar")
                    fw.copy(STd[tau][:, 64:128], ST[tau][:], eng="gpsimd")
                yield

        gens = [chunk_gen(0), chunk_gen(1)]
        alive = [True, True]
        while any(alive):
            for gi, g in enumerate(gens):
                if alive[gi]:
                    try:
                        next(g)
                    except StopIteration:
                        alive[gi] = False
        epsln = fw.sbuf("epsln", [128, 1])
        fw.memset(epsln[:], 64e-5)
        tb = [fw.sbuf(f"r3_{i}", [128, 512]) for i in range(5)]
        ob = [fw.sbuf(f"rob{i}", [128, 512], BF16) for i in range(2)]
        oi = 0
        qblocks = (ctx_blocks if need_ctx else []) + lat_blocks
        for tau in range(2):
            for (s, n, is_ctx) in qblocks:
                yc, sq, rs, bo, ga = tb
                fw.dma(bo[:, :n], BON[tau * 128:(tau + 1) * 128, s:s + n])
                fw.dma(ga[:, :n], GATE[tau * 128:(tau + 1) * 128, s:s + n])
                mb = bank()
                fw.mm(mb[:, :n], blk64[:], yaccT[:, tau, s:s + n])
                fw.stt(yc[:, :n], mb[:, :n], -1.0 / 64, yaccT[:, tau, s:s + n], ALU.mult, ALU.add)
                fw.act(sq[:, :n], yc[:, :n], AF.Square)
                vb = bank()
                fw.mm(vb[:, :n], blk64[:], sq[:, :n])
                fw.act(rs[:, :n], vb[:, :n], AF.Sqrt, bias=epsln[:, 0:1], scale=1.0 / 64)
                fw.recip(rs[:, :n], rs[:, :n])
                fw.stt(yc[:, :n], yc[:, :n], pcol(l, "rw_ln_g", tau), rs[:, :n], ALU.mult, ALU.mult)
                fw.stt(yc[:, :n], yc[:, :n], pcol(l, "rw_ln_b", tau), bo[:, :n], ALU.add, ALU.add, eng="gpsimd")
                o = ob[oi % 2]; oi += 1
                fw.tt(o[:, :n], yc[:, :n], ga[:, :n], ALU.mult, eng="gpsimd")
                fw.dma(OT[tau * 128:(tau + 1) * 128, s:s + n], o[:, :n])
        fw.pop()

    def mixers_0(l, b, need_ctx):
        if "rwkv" in cfg.mix:
            rwkv_phase(l, b, need_ctx)
        if "gla" in cfg.mix:
            gla_phase(l, b, need_ctx)
        if "gqa" in cfg.mix:
            gqa_phase(l, b, need_ctx)
        if "da" in cfg.mix:
            da_phase(l, b, need_ctx)

    def mixers(l, b, need_ctx):
        if "rwkv" in cfg.mix:
            rwkv_phase(l, b, need_ctx)
        if "da" in cfg.mix:
            da_phase(l, b, need_ctx)
        if "gla" in cfg.mix:
            gla_phase(l, b, need_ctx)
        if "gqa" in cfg.mix:
            gqa_phase(l, b, need_ctx)

    for b in range(NB):
        fw.push()
        xT = fw.sbuf("xT_s", [128, KT, T])
        xv = xT_d.t[b].rearrange("(k p) t -> p k t", p=128)
        for k in range(KT):
            fw.dma(xT[:, k, :], xT_d.v(xv[:, k, :]))
        for l in range(L):
            need_ctx = l < L - 1
            fw.push()
            hT = fw.sbuf("hT", [128, KT, T], BF16)
            sq = fw.sbuf("sq", [128, 512])
            rstd = fw.sbuf("rstd", [128, 512])
            nps = fw.psum("nps", [128, 512])
            norm_phase(xT, hT, l, 0, b, sq, rstd, nps)
            wt = [fw.sbuf(f"wt{i}", [128, KT, 256], BF16) for i in range(3)]
            pps = [fw.psum(f"pps{i}", [128, 512]) for i in range(4)]
            stg = [fw.sbuf(f"stg{i}", [128, 512]) for i in range(4)]
            wv = w_in.t[l].rearrange("(k p) c -> p k c", p=128)
            ei = 0
            tiles_ = mixer_cols()
            groups_ = [tiles_[i:i + 2] for i in range(0, len(tiles_), 2)]
            for gi, grp in enumerate(groups_):
                g0 = grp[0][0]
                gn = sum(nc_ for _, nc_ in grp)
                wb = wt[gi % 3]
                fw.dma(wb[:, :, :gn], w_in.v(wv[:, :, g0:g0 + gn]), eng="gpsimd")
                for (c0, ncol) in grp:
                    off = c0 - g0
                    for (s, n, is_ctx) in blocks:
                        ps = pps[ei % 4]
                        st = stg[ei % 4]
                        for k in range(KT):
                            fw.mm(ps[:ncol, :n], wb[:, k, off:off + ncol], hT[:, k, s:s + n],
                                  start=(k == 0), stop=(k == KT - 1))
                        fw.copy(st[:ncol, :n], ps[:ncol, :n], eng=("vector" if ei % 2 == 0 else "scalar"))
                        fw.dma(PT[c0:c0 + ncol, s:s + n], st[:ncol, :n])
                        ei += 1
            fw.pop()
            if cfg.stop == "proj":
                break
            if cfg.stop == "ffn":
                fw.push()
                tb = fw.sbuf("tb", [128, T])
                tbb = fw.sbuf("tbb", [128, T], BF16)
                for k in range(KT):
                    fw.dma(tb[:], PT[k * 128:(k + 1) * 128, :])
                    fw.copy(tbb[:], tb[:])
                    fw.dma(OT[k * 128:(k + 1) * 128, :], tbb[:])
                fw.pop()
            else:
                mixers(l, b, need_ctx)
            if cfg.stop == "mix":
                break
            wout_phase(xT, l, b, need_ctx)
            ffn_phase(xT, l, b, need_ctx)
            if cfg.stop == "ffn":
                break
        yv = yT_d.t[b].rearrange("(k p) t -> p k t", p=128)
        for k in range(KT):
            fw.dma(yT_d.v(yv[:, k, :]), xT[:, k, TC:T])
        fw.pop()
        if cfg.stop is not None:
            break

    if cfg.stop is not None:
        dbg_pt = fw.dram("dbg_PT", [N_IN, T], F32, kind="ExternalOutput")
        fw.dma(dbg_pt[:], PT[:])
        dbg_ot = fw.dram("dbg_OT", [D, T], BF16, kind="ExternalOutput")
        fw.dma(dbg_ot[:], OT[:])
    fw.pop()
    fw.finish()
    return nc


_NC_CACHE = {}


def kernel(**inputs):
    n_cores = 8
    B = inputs["x"].shape[0]
    NB = B // n_cores
    cfg = Cfg(TC=inputs["ctx"].shape[1], TL=inputs["x"].shape[1], NB=NB, depth=DEPTH)
    nc = build(cfg)
    in_maps = [prep_inputs(inputs, cfg, i * NB) for i in range(n_cores)]
    res = run_bass_kernel_spmd(nc, in_maps, core_ids=list(range(n_cores)))
    out = np.empty((B, cfg.TL, D), np.float32)
    for i in range(n_cores):
        yT = np.asarray(res.results[i]["yT"])
        out[i * NB:(i + 1) * NB] = yT.transpose(0, 2, 1)
    return out


def prep_inputs(inp, cfg, b0):
    NB = cfg.NB
    m = {}
    x = np.asarray(inp["x"], np.float32)[b0:b0 + NB]
    ctx = np.asarray(inp["ctx"], np.float32)[b0:b0 + NB]
    xc = np.concatenate([ctx, x], axis=1)
    m["xT"] = np.ascontiguousarray(xc.transpose(0, 2, 1))
    cvec = np.concatenate([np.asarray(inp["c"], np.float32)[b0:b0 + NB],
                           np.asarray(inp["c_ctx"], np.float32)[None]], axis=0)
    m["cT"] = np.ascontiguousarray(cvec.reshape(NB + 1, KT, 128).transpose(2, 1, 0))
    L = cfg.depth
    m["pack"] = np.stack([host_pack(inp, l) for l in range(L)])
    for nm in ("rw_w2", "rw_a2"):
        m[nm] = np.ascontiguousarray(np.asarray(inp[nm], np.float32)[:L].reshape(L, 128, 256))
    m["rw_g2"] = np.ascontiguousarray(np.asarray(inp["rw_g2"], np.float32)[:L])
    m["lamb"] = np.stack([np.tile(np.asarray(inp["da_lam"], np.float32)[l].reshape(1, 128), (128, 1)) for l in range(L)])
    for nm in ("mod_w", "w_in", "w_out", "ffn_w_up", "ffn_w_down", "gla_a2"):
        m[nm] = np.ascontiguousarray(np.asarray(inp[nm], np.float32)[:L])
    for k, v in host_consts(cfg).items():
        m["c_" + k] = v
    return m
```

```python
import numpy as np
import concourse.bass as bass
import concourse.mybir as mybir
from concourse.bass_utils import run_bass_kernel_spmd

F32 = mybir.dt.float32
BF16 = mybir.dt.bfloat16
AF = mybir.ActivationFunctionType
ALU = mybir.AluOpType
AX = mybir.AxisListType

ENGS = ("tensor", "vector", "scalar", "gpsimd", "sync")


class Trk:
    __slots__ = ("name", "w", "r")

    def __init__(self, name):
        self.name = name
        self.w = None
        self.r = {}


class V:
    __slots__ = ("ap", "trk")

    def __init__(self, ap, trk):
        self.ap = ap
        self.trk = trk


class Buf:
    def __init__(self, t, name):
        self.t = t
        self.name = name
        self.trk = Trk(name)

    def __getitem__(self, idx):
        return V(self.t[idx], self.trk)

    def v(self, ap):
        return V(ap, self.trk)


def _trks(v):
    return v.trk if isinstance(v.trk, (list, tuple)) else (v.trk,)


class FW:
    def __init__(self, nc, n_dma_sems=32):
        self.nc = nc
        self.prog = {e: [] for e in ENGS}
        self.sem = {e: nc.alloc_semaphore(name=f"s_{e}") for e in ENGS}
        self.cnt = {e: 0 for e in ENGS}
        self.waited = {e: {} for e in ENGS}
        self.dsem = [nc.alloc_semaphore(name=f"d_{i}") for i in range(n_dma_sems)]
        self.dcnt = [0] * n_dma_sems
        self.dnext = 0
        self.gnext = 0
        self.semobj = {}
        for e in ENGS:
            self.semobj[("e", e)] = self.sem[e]
        for i, s in enumerate(self.dsem):
            self.semobj[("d", i)] = s
        self.ninst = 0
        self.stack = []

    def push(self):
        self.stack.append([])

    def pop(self):
        self.barrier()
        for g in reversed(self.stack.pop()):
            g.__exit__(None, None, None)

    def sbuf(self, name, shape, dtype=F32):
        self.uid = getattr(self, "uid", 0) + 1
        name = f"{name}_u{self.uid}"
        g = self.nc.sbuf_tensor(name, list(shape), dtype)
        t = g.__enter__()
        self.stack[-1].append(g)
        return Buf(t, name)

    def psum(self, name, shape, dtype=F32):
        self.uid = getattr(self, "uid", 0) + 1
        name = f"{name}_u{self.uid}"
        g = self.nc.psum_tensor(name, list(shape), dtype)
        t = g.__enter__()
        self.stack[-1].append(g)
        return Buf(t, name)

    def dram(self, name, shape, dtype=F32, kind="Internal"):
        return Buf(self.nc.dram_tensor(name, list(shape), dtype, kind=kind).ap(), name)

    def _wait(self, eng, ev):
        if ev is None:
            return
        key, val = ev
        if eng == "tensor" and key == ("e", "tensor"):
            return
        if self.waited[eng].get(key, 0) >= val:
            return
        self.waited[eng][key] = val
        self.prog[eng].append(("wait", key, val))

    def _deps(self, eng, reads, writes):
        for v in reads:
            for t in _trks(v):
                self._wait(eng, t.w)
        for v in writes:
            for t in _trks(v):
                self._wait(eng, t.w)
                for kv in list(t.r.items()):
                    self._wait(eng, kv)

    def _mark(self, ev, reads, writes):
        for v in reads:
            for t in _trks(v):
                if t.r.get(ev[0], 0) < ev[1]:
                    t.r[ev[0]] = ev[1]
        for v in writes:
            for t in _trks(v):
                t.w = ev
                t.r = {}

    def op(self, eng, meth, reads, writes, *args, **kw):
        self._deps(eng, reads, writes)
        self.cnt[eng] += 1
        ev = (("e", eng), self.cnt[eng])
        sem = self.sem[eng]
        a2 = [a.ap if isinstance(a, V) else a for a in args]
        k2 = {k: (a.ap if isinstance(a, V) else a) for k, a in kw.items()}

        def emit(e, inc, wait=None, meth=meth, a2=a2, k2=k2, sem=sem):
            ins = getattr(e, meth)(*a2, **k2)
            if wait is not None:
                ins._wait_ge(wait[0], wait[1])
            if inc:
                ins.then_inc(sem, 1)
        self.prog[eng].append(("op", emit, self.cnt[eng]))
        self._mark(ev, reads, writes)
        self.ninst += 1
        return ev

    def dma(self, out, in_, eng="sync", **kw):
        self._deps(eng, [in_], [out])
        nd = len(self.dsem)
        if eng == "gpsimd":
            k = nd - 8 + self.gnext
            self.gnext = (self.gnext + 1) % 8
        else:
            k = self.dnext
            self.dnext = (self.dnext + 1) % (nd - 8)
        if self.dcnt[k] > 0:
            self._wait(eng, (("d", k), self.dcnt[k]))
        self.dcnt[k] += 16
        ev = (("d", k), self.dcnt[k])
        sem = self.dsem[k]
        oa, ia = out.ap, in_.ap

        def emit(e, oa=oa, ia=ia, sem=sem, kw=kw):
            e.dma_start(out=oa, in_=ia, **kw).then_inc(sem, 16)
        self.prog[eng].append(("dma", emit))
        self._mark(ev, [in_], [out])
        self.ninst += 1
        return ev

    def _all_events(self):
        evs = [(("e", e), self.cnt[e]) for e in ENGS if self.cnt[e] > 0]
        evs += [(("d", i), c) for i, c in enumerate(self.dcnt) if c > 0]
        return evs

    def barrier(self):
        evs = self._all_events()
        for e in ENGS:
            for ev in evs:
                self._wait(e, ev)

    def finish(self):
        for ev in self._all_events():
            self._wait("sync", ev)
        import bisect
        needed = {e: set() for e in ENGS}
        for ename in ENGS:
            for it in self.prog[ename]:
                if it[0] == "wait" and it[1][0] == "e":
                    needed[it[1][1]].add(it[2])
        ranks = {e: sorted(needed[e]) for e in ENGS}
        self.max_sem = {e: len(ranks[e]) for e in ENGS}
        with self.nc.Block() as block:
            for ename in ENGS:
                lst = self.prog[ename]

                def body(e, lst=lst, ename=ename):
                    pending = []
                    for it in lst:
                        if it[0] == "wait":
                            key, val = it[1], it[2]
                            if key[0] == "e":
                                val = bisect.bisect_left(ranks[key[1]], val) + 1
                            pending.append((self.semobj[key], val))
                        elif it[0] == "op":
                            for (sm, vl) in pending[:-1]:
                                e.wait_ge(sm, vl)
                            it[1](e, it[2] in needed[ename], pending[-1] if pending else None)
                            pending = []
                        else:
                            for (sm, vl) in pending:
                                e.wait_ge(sm, vl)
                            pending = []
                            it[1](e)
                    for (sm, vl) in pending:
                        e.wait_ge(sm, vl)
                getattr(block, ename)(body)

    def mm(self, out, lhsT, rhs, start=True, stop=True):
        return self.op("tensor", "matmul", [lhsT, rhs], [out], out, lhsT, rhs, start=start, stop=stop)

    def transpose(self, out, in_, ident):
        return self.op("tensor", "transpose", [in_, ident], [out], out, in_, ident)

    def act(self, out, in_, func, bias=None, scale=None, accum_out=None):
        reads = [in_]
        kw = {}
        if bias is not None:
            kw["bias"] = bias
            if isinstance(bias, V):
                reads.append(bias)
        if scale is not None:
            kw["scale"] = scale
            if isinstance(scale, V):
                reads.append(scale)
        writes = [out]
        if accum_out is not None:
            kw["accum_out"] = accum_out
            writes.append(accum_out)
        return self.op("scalar", "activation", reads, writes, out, in_, func, **kw)

    def tt(self, out, in0, in1, op, eng="vector"):
        return self.op(eng, "tensor_tensor", [in0, in1], [out], out, in0, in1, op)

    def ts(self, out, in0, s1, op0, s2=None, op1=None, eng="vector"):
        reads = [in0] + [s for s in (s1, s2) if isinstance(s, V)]
        if op1 is None:
            return self.op(eng, "tensor_scalar", reads, [out], out, in0, s1, None, op0)
        return self.op(eng, "tensor_scalar", reads, [out], out, in0, s1, s2, op0, op1)

    def stt(self, out, in0, scalar, in1, op0, op1, eng="vector"):
        eng = "vector"
        reads = [in0, in1] + ([scalar] if isinstance(scalar, V) else [])
        return self.op(eng, "scalar_tensor_tensor", reads, [out], out, in0, scalar, in1, op0, op1)

    def copy(self, out, in_, eng="vector"):
        if eng == "scalar":
            return self.op("scalar", "copy", [in_], [out], out, in_)
        return self.op(eng, "tensor_copy", [in_], [out], out, in_)

    def memset(self, out, val, eng="vector"):
        return self.op(eng, "memset", [], [out], out, val)

    def recip(self, out, in_):
        return self.op("vector", "reciprocal", [in_], [out], out, in_)


D = 1024
KT = 8
N_IN = 3232
D_FF = 2816
FT = 22
GRID_W = 64
EPS = 1e-6
RW_OFF, DA_OFF, GLA_OFF, GQA_OFF = 0, 1152, 1920, 2720
DEPTH = 2


class Cfg:
    def __init__(self, TC=256, TL=2048, NB=2, depth=DEPTH, stop=None, mix=("rwkv", "da", "gla", "gqa")):
        self.TC, self.TL, self.NB, self.depth = TC, TL, NB, depth
        self.T = TC + TL
        self.stop = stop
        self.mix = mix

    def blocks(self):
        out = []
        s = 0
        while s < self.TC:
            n = min(512, self.TC - s)
            out.append((s, n, True))
            s += n
        while s < self.T:
            n = min(512, self.T - s)
            out.append((s, n, False))
            s += n
        return out


def pack_layout():
    cols = {}
    n = 0

    def add(name, k):
        nonlocal n
        cols[name] = (n, k)
        n += k
    add("nmg", 8)
    add("nfg", 8)
    add("mod_b", 48)
    add("rw_mu0", 9)
    add("rw_mu1", 9)
    add("rw_c0", 9)
    add("rw_omka", 2)
    add("rw_w0", 4)
    add("rw_a0", 4)
    add("rw_kk", 2)
    add("rw_ka", 2)
    add("rw_rk", 2)
    add("rw_ln_g", 2)
    add("rw_ln_b", 2)
    add("da_qg", 1)
    add("da_kg", 1)
    add("da_sub", 1)
    add("gla_ab", 2)
    add("gla_ng", 1)
    add("gq_qg", 1)
    add("gq_kg", 1)
    add("conv_w", 66)
    add("conv_b", 22)
    return cols, n


PACK, NPACK = pack_layout()


def host_pack(inp, l):
    P = np.zeros((128, NPACK), np.float32)

    def put(name, vec, k):
        c0, kk = PACK[name]
        assert kk == k
        P[:, c0:c0 + k] = np.asarray(vec, np.float32).reshape(k, 128).T
    put("nmg", inp["norm_mix_g"][l], 8)
    put("nfg", inp["norm_ffn_g"][l], 8)
    put("mod_b", inp["mod_b"][l], 48)
    put("rw_mu0", inp["rw_mu"][l, 0], 9)
    put("rw_mu1", inp["rw_mu"][l, 1], 9)
    put("rw_w0", inp["rw_w0"][l].reshape(-1), 4)
    put("rw_a0", inp["rw_a0"][l].reshape(-1), 4)
    put("rw_kk", inp["rw_kk"][l], 2)
    put("rw_ka", inp["rw_ka"][l], 2)
    put("rw_rk", inp["rw_rk"][l].reshape(-1), 2)
    put("rw_ln_g", inp["rw_ln_g"][l], 2)
    put("rw_ln_b", inp["rw_ln_b"][l], 2)
    put("da_qg", np.tile(inp["da_qk_g"][l, 0], 4), 1)
    put("da_kg", np.tile(inp["da_qk_g"][l, 1], 4), 1)
    put("da_sub", np.tile(inp["da_subln_g"][l], 2), 1)
    put("gla_ab", inp["gla_ab"][l].reshape(-1), 2)
    put("gla_ng", np.tile(inp["gla_norm_g"][l], 2), 1)
    put("gq_qg", np.tile(inp["gqa_qk_g"][l, 0], 2), 1)
    put("gq_kg", np.tile(inp["gqa_qk_g"][l, 1], 2), 1)
    put("conv_w", inp["ffn_conv_w"][l].reshape(-1), 66)
    put("conv_b", inp["ffn_conv_b"][l], 22)
    return P


def host_consts(cfg):
    c = {}
    c["ident"] = np.eye(128, dtype=np.float32)
    c["ones"] = np.ones((128, 128), np.float32)
    b64 = np.zeros((128, 128), np.float32)
    b64[:64, :64] = 1
    b64[64:, 64:] = 1
    c["blk64"] = b64
    b32 = np.zeros((128, 128), np.float32)
    for i in range(4):
        b32[32 * i:32 * i + 32, 32 * i:32 * i + 32] = 1
    c["blk32"] = b32
    TL = cfg.TL
    rows = TL // GRID_W
    row = np.repeat(np.arange(rows, dtype=np.float32), GRID_W)
    col = np.tile(np.arange(GRID_W, dtype=np.float32), rows)

    def tables(hd):
        nf = hd // 4
        inv = (10000.0 ** (-np.arange(nf, dtype=np.float32) / nf)).astype(np.float32)
        ang = np.concatenate([row[:, None] * inv, col[:, None] * inv], axis=-1)
        cos, sin = np.cos(ang).astype(np.float32), np.sin(ang).astype(np.float32)
        half = hd // 2
        cosf = np.concatenate([cos, cos], axis=-1)
        sinf = np.concatenate([sin, sin], axis=-1)
        rep = 128 // hd
        cT = np.tile(cosf, (1, rep)).T.copy()
        sT = np.tile(sinf, (1, rep)).T.copy()
        R = np.zeros((128, 128), np.float32)
        for m in range(128):
            if m % hd < half:
                R[m + half, m] = -1.0
            else:
                R[m - half, m] = 1.0
        return cT, sT, R
    hm = np.zeros((128, 4), np.float32)
    for p in range(128):
        hm[p, p // 32] = 1.0
    c["hmask4s"] = hm * np.float32(32 ** -0.5)
    jj, tt_ = np.meshgrid(np.arange(128), np.arange(128), indexing="ij")
    c["tri4_0"] = np.tile((jj <= tt_).astype(np.float32)[:, None, :], (1, 4, 1))
    c["tri4_1"] = np.tile((jj >= tt_).astype(np.float32)[:, None, :], (1, 4, 1))
    h2 = np.zeros((128, 4), np.float32)
    h2[:64, 0] = 1; h2[64:, 1] = 1; h2[:64, 2] = -1; h2[64:, 3] = -1
    c["hm2"] = h2
    c["I2"] = np.tile(np.eye(128, dtype=np.float32)[:, None, :], (1, 2, 1))
    c["maskN_0"] = np.tile((tt_ < jj).astype(np.float32)[:, None, :], (1, 4, 1))
    c["maskN_1"] = np.tile((tt_ > jj).astype(np.float32)[:, None, :], (1, 4, 1))
    sf, inf_ = (jj < tt_).astype(np.float32), (jj <= tt_).astype(np.float32)
    sr, inr = (jj > tt_).astype(np.float32), (jj >= tt_).astype(np.float32)
    c["maskAB_0"] = np.tile(np.concatenate([sf, inf_], 1)[:, None, :], (1, 2, 1))
    c["maskAB_1"] = np.tile(np.concatenate([sr, inr], 1)[:, None, :], (1, 2, 1))
    dm = np.zeros((128, 2), np.float32)
    for p in range(128):
        dm[p, (p % 64) // 32] = 1.0
    c["dmask"] = dm
    c["cos_gq"], c["sin_gq"], c["rot_gq"] = tables(64)
    c["cos_da"], c["sin_da"], c["rot_da"] = tables(32)
    return c


def build(cfg):
    nc = bass.Bass("TRN2", target_bir_lowering=False)
    fw = FW(nc)
    NB, T, TC, TL = cfg.NB, cfg.T, cfg.TC, cfg.TL
    NJ = NB + 1
    L = cfg.depth

    def din(name, shape, dt=F32):
        return fw.dram(name, shape, dt, kind="ExternalInput")

    xT_d = din("xT", [NB, D, T])
    cT_d = din("cT", [128, KT, NJ])
    pack_d = din("pack", [L, 128, NPACK])
    mod_w = din("mod_w", [L, D, 6 * D])
    w_in = din("w_in", [L, D, N_IN])
    w_out = din("w_out", [L, D, D])
    w_up = din("ffn_w_up", [L, D, 2 * D_FF])
    w_down = din("ffn_w_down", [L, D_FF, D])
    consts = {}
    for nm in ("ident", "ones", "blk64", "blk32", "rot_gq", "rot_da"):
        consts[nm] = din("c_" + nm, [128, 128])
    for nm in ("cos_gq", "sin_gq", "cos_da", "sin_da"):
        consts[nm] = din("c_" + nm, [128, TL])
    consts["dmask"] = din("c_dmask", [128, 2])
    lamb_d = din("lamb", [L, 128, 128])
    gla_a2_d = din("gla_a2", [L, 2, 16, 128])
    rw_w2_d = din("rw_w2", [L, 128, 256])
    rw_a2_d = din("rw_a2", [L, 128, 256])
    rw_g2_d = din("rw_g2", [L, 128, 256])
    consts["hm2"] = din("c_hm2", [128, 4])
    consts["I2"] = din("c_I2", [128, 2, 128])
    for d_ in range(2):
        consts[f"maskN_{d_}"] = din(f"c_maskN_{d_}", [128, 4, 128])
        consts[f"maskAB_{d_}"] = din(f"c_maskAB_{d_}", [128, 2, 256])
    consts["hmask4s"] = din("c_hmask4s", [128, 4])
    consts["tri4_0"] = din("c_tri4_0", [128, 4, 128])
    consts["tri4_1"] = din("c_tri4_1", [128, 4, 128])
    yT_d = fw.dram("yT", [NB, D, TL], F32, kind="ExternalOutput")
    dbg = {}

    PT = fw.dram("PT", [N_IN, T], F32)
    OT = fw.dram("OT", [D, T], BF16)
    ACT = fw.dram("ACTs", [D_FF, T], BF16)

    fw.push()
    ident = fw.sbuf("ident", [128, 128])
    ones = fw.sbuf("ones", [128, 128])
    blk64 = fw.sbuf("blk64", [128, 128])
    fw.dma(ident[:], consts["ident"][:])
    fw.dma(ones[:], consts["ones"][:])
    fw.dma(blk64[:], consts["blk64"][:])
    identb = fw.sbuf("identb", [128, 128], BF16)
    fw.copy(identb[:], ident[:])
    onesb_g = fw.sbuf("onesb_g", [128, 128], BF16)
    fw.copy(onesb_g[:], ones[:])
    blk64b = fw.sbuf("blk64b", [128, 128], BF16)
    fw.copy(blk64b[:], blk64[:])
    pk = [fw.sbuf(f"pk{l}", [128, NPACK]) for l in range(L)]
    for l in range(L):
        fw.dma(pk[l][:], pack_d[l])
    modT = [fw.sbuf(f"modT{l}", [128, 48, NJ]) for l in range(L)]
    gs = [fw.sbuf(f"gs{l}", [128, 2, KT, NJ]) for l in range(L)]
    epsb = fw.sbuf("epsb", [128, 1])
    fw.memset(epsb[:], EPS)

    def pcol(l, name, i=0):
        c0, k = PACK[name]
        return pk[l][:, c0 + i:c0 + i + 1]

    fw.push()
    cs = fw.sbuf("cs", [128, KT, NJ])
    fw.dma(cs[:], cT_d[:])
    fw.act(cs[:], cs[:], AF.Silu)
    mps = fw.psum("mps", [128, 48, NJ])
    wm = [fw.sbuf(f"wm{i}", [128, KT, 512]) for i in range(2)]
    for l in range(L):
        mwv = mod_w.t[l].rearrange("(k p) c -> p k c", p=128)
        for g in range(12):
            wb = wm[g % 2]
            fw.dma(wb[:], mod_w.v(mwv[:, :, g * 512:(g + 1) * 512]))
            for ci in range(4):
                ct = g * 4 + ci
                for k in range(KT):
                    fw.mm(mps[:, ct, :], wb[:, k, ci * 128:(ci + 1) * 128], cs[:, k, :],
                          start=(k == 0), stop=(k == KT - 1))
        c0 = PACK["mod_b"][0]
        for j in range(NJ):
            fw.tt(modT[l][:, :, j], mps[:, :, j], pk[l][:, c0:c0 + 48], ALU.add)
        for j in range(NJ):
            c0 = PACK["nmg"][0]
            fw.stt(gs[l][:, 0, :, j], modT[l][:, 8:16, j], 1.0, pk[l][:, c0:c0 + 8], ALU.add, ALU.mult)
            c0 = PACK["nfg"][0]
            fw.stt(gs[l][:, 1, :, j], modT[l][:, 32:40, j], 1.0, pk[l][:, c0:c0 + 8], ALU.add, ALU.mult)
    fw.pop()

    blocks = cfg.blocks()
    if cfg.stop == "mix":
        fw.push()
        zt = fw.sbuf("zt", [128, T], BF16)
        fw.memset(zt[:], 0.0)
        for k in range(KT):
            fw.dma(OT[k * 128:(k + 1) * 128, :], zt[:])
        fw.pop()

    def mixer_cols():
        tl = []
        for i in range(9):
            tl.append((RW_OFF + 128 * i, 128))
        for i in range(6):
            tl.append((DA_OFF + 128 * i, 128))
        for i in range(6):
            tl.append((GLA_OFF + 128 * i, 128))
        tl.append((GLA_OFF + 768, 32))
        for i in range(4):
            tl.append((GQA_OFF + 128 * i, 128))
        return tl

    def norm_phase(xT, hT, l, which, b, sq, rstd, nps):
        shift_base = 0 if which == 0 else 24
        sqb = [fw.sbuf(f"nsqb{i}", [128, 512], BF16) for i in range(3)]
        tmpf = [fw.sbuf(f"ntmp{i}", [128, 512]) for i in range(3)]
        nps2 = fw.psum("nps_b", [128, 512])
        ci = 0
        for bi, (s, n, is_ctx) in enumerate(blocks):
            j = NB if is_ctx else b
            ps = nps if bi % 2 == 0 else nps2
            for k in range(KT):
                q = sqb[ci % 3]
                ci += 1
                fw.act(q[:, :n], xT[:, k, s:s + n], AF.Square)
                fw.mm(ps[:, :n], onesb_g[:], q[:, :n], start=(k == 0), stop=(k == KT - 1))
            rs = sq if bi % 2 == 0 else rstd
            fw.act(rs[:, :n], ps[:, :n], AF.Sqrt, bias=epsb[:, 0:1], scale=1.0 / D)
            fw.recip(rs[:, :n], rs[:, :n])
            for k in range(KT):
                t_ = tmpf[ci % 3]
                ci += 1
                fw.tt(t_[:, :n], xT[:, k, s:s + n], rs[:, :n], ALU.mult)
                fw.act(hT[:, k, s:s + n], t_[:, :n], AF.Identity,
                       bias=modT[l][:, shift_base + k, j:j + 1], scale=gs[l][:, which, k, j:j + 1])

    def wout_phase(xT, l, b, need_ctx):
        fw.push()
        oT = fw.sbuf("oT", [128, KT, T], BF16)
        ov = OT.t.rearrange("(k p) t -> p k t", p=128)
        for k in range(KT):
            fw.dma(oT[:, k, :], OT.v(ov[:, k, :]))
        wt = [fw.sbuf(f"wo{i}", [128, KT, 256], BF16) for i in range(2)]
        pps = [fw.psum(f"ops{i}", [128, 512]) for i in range(4)]
        wv = w_out.t[l].rearrange("(k p) c -> p k c", p=128)
        ei = 0
        for jt in range(KT):
            wb_full = wt[(jt // 2) % 2]
            if jt % 2 == 0:
                fw.dma(wb_full[:], w_out.v(wv[:, :, jt * 128:(jt + 2) * 128]), eng="gpsimd")
            wb = Buf(wb_full.t[:, :, (jt % 2) * 128:(jt % 2 + 1) * 128], "wo_half")
            wb.trk = wb_full.trk
            for (s, n, is_ctx) in blocks:
                if is_ctx and not need_ctx:
                    continue
                j = NB if is_ctx else b
                ps = pps[ei % 4]
                ei += 1
                for k in range(KT):
                    fw.mm(ps[:, :n], wb[:, k, :], oT[:, k, s:s + n], start=(k == 0), stop=(k == KT - 1))
                fw.stt(xT[:, jt, s:s + n], ps[:, :n], modT[l][:, 16 + jt, j:j + 1], xT[:, jt, s:s + n],
                       ALU.mult, ALU.add)
        fw.pop()

    def ffn_phase(xT, l, b, need_ctx):
        segs = ([(0, TC)] if need_ctx else []) + [(TC, T)]
        fblocks = [bl for bl in blocks if (need_ctx or not bl[2])]
        fw.push()
        hT = fw.sbuf("hT2", [128, KT, T], BF16)
        sq = fw.sbuf("sq2", [128, 512])
        rstd = fw.sbuf("rstd2", [128, 512])
        nps = fw.psum("nps2", [128, 512])
        norm_phase(xT, hT, l, 1, b, sq, rstd, nps)
        wu = [fw.sbuf(f"wu{i}", [128, KT, 256], BF16) for i in range(2)]
        wg = [fw.sbuf(f"wg{i}", [128, KT, 256], BF16) for i in range(2)]
        ups = [fw.psum(f"ups{i}", [128, 512]) for i in range(2)]
        gps = [fw.psum(f"gps{i}", [128, 512]) for i in range(2)]
        uT = [fw.sbuf(f"uT{i}", [128, T]) for i in range(2)]
        gT = [fw.sbuf(f"gT{i}", [128, T]) for i in range(2)]
        tmp = [fw.sbuf(f"ftmp{i}", [128, T]) for i in range(2)]
        aT = [fw.sbuf(f"aT{i}", [128, T], BF16) for i in range(2)]
        wv = w_up.t[l].rearrange("(k p) c -> p k c", p=128)
        cw0 = PACK["conv_w"][0]
        cb0 = PACK["conv_b"][0]
        ei = 0
        def wload(p):
            fw.dma(wu[p % 2][:], w_up.v(wv[:, :, p * 256:(p + 1) * 256]), eng="gpsimd")
            fw.dma(wg[p % 2][:], w_up.v(wv[:, :, D_FF + p * 256:D_FF + (p + 1) * 256]), eng="gpsimd")
        wload(0)
        for i in range(FT):
            r = i % 2
            if i % 2 == 1 and i // 2 + 1 < FT // 2:
                wload(i // 2 + 1)
            for (s, n, is_ctx) in fblocks:
                pu, pg = ups[ei % 2], gps[ei % 2]
                ei += 1
                for k in range(KT):
                    fw.mm(pu[:, :n], wu[(i // 2) % 2][:, k, (i % 2) * 128:(i % 2 + 1) * 128], hT[:, k, s:s + n],
                          start=(k == 0), stop=(k == KT - 1))
                for k in range(KT):
                    fw.mm(pg[:, :n], wg[(i // 2) % 2][:, k, (i % 2) * 128:(i % 2 + 1) * 128], hT[:, k, s:s + n],
                          start=(k == 0), stop=(k == KT - 1))
                fw.copy(uT[r][:, s:s + n], pu[:, :n], eng="scalar")
                fw.copy(gT[r][:, s:s + n], pg[:, :n], eng="scalar")
                fw.act(tmp[r][:, s:s + n], pg[:, :n], AF.Identity, bias=pk[l][:, cb0 + i:cb0 + i + 1],
                       scale=pk[l][:, cw0 + FT + i:cw0 + FT + i + 1])
            w0 = pk[l][:, cw0 + i:cw0 + i + 1]
            w1 = pk[l][:, cw0 + FT + i:cw0 + FT + i + 1]
            w2 = pk[l][:, cw0 + 2 * FT + i:cw0 + 2 * FT + i + 1]
            cb = pk[l][:, cb0 + i:cb0 + i + 1]
            for (s, e) in segs:
                fw.stt(tmp[r][:, s + 1:e], gT[r][:, s:e - 1], w0, tmp[r][:, s + 1:e], ALU.mult, ALU.add)
                fw.stt(tmp[r][:, s:e - 1], gT[r][:, s + 1:e], w2, tmp[r][:, s:e - 1], ALU.mult, ALU.add)
                fw.act(tmp[r][:, s:e], tmp[r][:, s:e], AF.Silu)
                fw.tt(aT[r][:, s:e], tmp[r][:, s:e], uT[r][:, s:e], ALU.mult)
                fw.dma(ACT[i * 128:(i + 1) * 128, s:e], aT[r][:, s:e])
        fw.pop()
        fw.push()
        wd = [fw.sbuf(f"wd{jt}", [128, FT, 128], BF16) for jt in range(KT)]
        wdv = w_down.t[l].rearrange("(f p) c -> p f c", p=128)
        for jt in range(KT):
            fw.dma(wd[jt][:], w_down.v(wdv[:, :, jt * 128:(jt + 1) * 128]), eng="gpsimd")
        ab = [fw.sbuf(f"ab{i}", [128, FT, 512], BF16) for i in range(2)]
        dps = [fw.psum(f"dps{i}", [128, 512]) for i in range(4)]
        av = ACT.t.rearrange("(f p) t -> p f t", p=128)
        ei = 0
        for bi, (s, n, is_ctx) in enumerate(fblocks):
            j = NB if is_ctx else b
            a = ab[bi % 2]
            fw.dma(a[:, :, :n], ACT.v(av[:, :, s:s + n]))
            for jt in range(KT):
                ps = dps[ei % 4]
                ei += 1
                for f in range(FT):
                    fw.mm(ps[:, :n], wd[jt][:, f, :], a[:, f, :n], start=(f == 0), stop=(f == FT - 1))
                fw.stt(xT[:, jt, s:s + n], ps[:, :n], modT[l][:, 40 + jt, j:j + 1], xT[:, jt, s:s + n],
                       ALU.mult, ALU.add)
        fw.pop()

    NT = T // 128
    NTC = TC // 128
    lat_blocks = [bl for bl in blocks if not bl[2]]
    ctx_blocks = [bl for bl in blocks if bl[2]]

    def head_norm_rope(raw, outs, blkb, hd, g_ap, rotb, cosT, sinT, s, n, is_ctx, tset, masks=None):
        sqb, rs, qg, t1, t2, nps, npr = tset
        fw.act(sqb[:, :n], raw, AF.Square)
        fw.mm(nps[:, :n], blkb[:], sqb[:, :n])
        fw.act(rs[:, :n], nps[:, :n], AF.Sqrt, bias=epsb[:, 0:1], scale=1.0 / hd)
        fw.recip(rs[:, :n], rs[:, :n])
        fw.stt(qg[:, :n], raw, g_ap, rs[:, :n], ALU.mult, ALU.mult)
        if is_ctx:
            res = qg
        else:
            fw.mm(npr[:, :n], rotb[:], qg[:, :n])
            fw.tt(t1[:, :n], qg[:, :n], cosT[:, s - TC:s - TC + n], ALU.mult)
            fw.tt(t2[:, :n], npr[:, :n], sinT[:, s - TC:s - TC + n], ALU.mult)
            fw.tt(t1[:, :n], t1[:, :n], t2[:, :n], ALU.add, eng="gpsimd")
            res = t1
        if masks is None:
            fw.copy(outs[0], res[:, :n], eng="gpsimd")
        else:
            for m, o in enumerate(outs):
                fw.ts(o, res[:, :n], masks[:, m:m + 1], ALU.mult, eng="gpsimd")

    def prep_sets(tag):
        sets = []
        for i in range(3):
            sets.append((fw.sbuf(f"{tag}sqb{i}", [128, 512], BF16), fw.sbuf(f"{tag}rs{i}", [128, 512]),
                         fw.sbuf(f"{tag}qg{i}", [128, 512], BF16), fw.sbuf(f"{tag}t1{i}", [128, 512]),
                         fw.sbuf(f"{tag}t2{i}", [128, 512]), fw.psum(f"{tag}nps{i}", [128, 512]),
                         fw.psum(f"{tag}npr{i}", [128, 512])))
        return sets

    def make_vdup(vrow0, nheads, Vd, vtmp, tps):
        ntile = (nheads * 64) // 128
        for vt in range(ntile):
            fw.dma(vtmp[:, :], PT[vrow0 + vt * 128:vrow0 + (vt + 1) * 128, :])
            for i in range(NT):
                fw.transpose(tps[:, i % 4, :], vtmp[:, i * 128:(i + 1) * 128], ident[:])
                for hh in range(2):
                    h = vt * 2 + hh
                    fw.copy(Vd[h][:, i, 0:64], tps[:, i % 4, hh * 64:(hh + 1) * 64], eng="vector")
                    fw.copy(Vd[h][:, i, 64:128], tps[:, i % 4, hh * 64:(hh + 1) * 64], eng="scalar")

    def attn_head(qviews, kviews, Vd_h, nmaps, scale, qb, sps_l, pT_l, oacc, dacc, dsum, cnt):
        (s, n, is_ctx) = qb
        kts = list(range(NTC)) if is_ctx else list(range(NT))
        steps = [(ki, kt, m) for ki, kt in enumerate(kts) for m in range(nmaps)]
        c0 = cnt[0]
        cnt[0] += len(steps)

        def score(i):
            ki, kt, m = steps[i]
            sp = sps_l[(c0 + i) % len(sps_l)]
            fw.mm(sp[:, :n], kviews[m](kt), qviews[m](s, n))
        depth = len(sps_l) - 1
        for i in range(min(depth, len(steps))):
            score(i)
        for i, (ki, kt, m) in enumerate(steps):
            if i + depth < len(steps):
                score(i + depth)
            sp = sps_l[(c0 + i) % len(sps_l)]
            pT = pT_l[(c0 + i) % len(pT_l)]
            fw.act(pT[:, :n], sp[:, :n], AF.Exp, scale=scale)
            fw.mm(oacc[m][:, :n], Vd_h[:, kt, :], pT[:, :n], start=(ki == 0), stop=(ki == len(kts) - 1))
            ds = dsum[m]
            de = "vector" if m == 0 else "gpsimd"
            if ki == 0:
                fw.copy(ds[:, :n], pT[:, :n], eng=de)
            else:
                fw.tt(ds[:, :n], pT[:, :n], ds[:, :n], ALU.add, eng=de)
        for m in range(nmaps):
            fw.mm(dacc[m][:, :n], ones[:], dsum[m][:, :n])

    def gqa_phase(l, b, need_ctx):
        fw.push()
        onesb = onesb_g
        qn = fw.sbuf("qn", [128, 2, T], BF16)
        kd = fw.sbuf("kd", [128, 2, T], BF16)
        Vd = [fw.sbuf(f"Vd{h}", [128, NT, 128], BF16) for h in range(2)]
        qblocks = (ctx_blocks if need_ctx else []) + lat_blocks
        fw.push()
        cosT = fw.sbuf("cosT", [128, TL]); sinT = fw.sbuf("sinT", [128, TL]); rot = fw.sbuf("rot", [128, 128])
        fw.dma(cosT[:], consts["cos_gq"][:]); fw.dma(sinT[:], consts["sin_gq"][:]); fw.dma(rot[:], consts["rot_gq"][:])
        rotb = fw.sbuf("rotb", [128, 128], BF16)
        fw.copy(rotb[:], rot[:])
        raw = [fw.sbuf(f"raw{i}", [128, 512]) for i in range(3)]
        tsets = prep_sets("g")
        tps = fw.psum("tps", [128, 4, 128])
        vtmp = fw.sbuf("vtmp", [128, T])
        ri = 0
        for t in range(2):
            for (s, n, is_ctx) in qblocks:
                r = raw[ri % 3]; ri += 1
                fw.dma(r[:, :n], PT[GQA_OFF + t * 128:GQA_OFF + (t + 1) * 128, s:s + n])
                head_norm_rope(r[:, :n], [qn[:, t, s:s + n]], blk64b, 64, pcol(l, "gq_qg"), rotb, cosT, sinT,
                               s, n, is_ctx, tsets[ri % 3])
            for (s, n, is_ctx) in blocks:
                r = raw[ri % 3]; ri += 1
                for hh in range(2):
                    fw.dma(r[hh * 64:(hh + 1) * 64, :n], PT[GQA_OFF + 256 + t * 64:GQA_OFF + 256 + (t + 1) * 64, s:s + n])
                head_norm_rope(r[:, :n], [kd[:, t, s:s + n]], blk64b, 64, pcol(l, "gq_kg"), rotb, cosT, sinT,
                               s, n, is_ctx, tsets[ri % 3])
        make_vdup(GQA_OFF + 384, 2, Vd, vtmp, tps)
        fw.pop()
        sps_l = [fw.psum(f"sps{i}", [128, 512]) for i in range(4)]
        pT_l = [fw.sbuf(f"pT{i}", [128, 512], BF16) for i in range(4)]
        oacc2 = [fw.psum(f"oacc{i}", [128, 512]) for i in range(2)]
        dacc2 = [fw.psum(f"dacc{i}", [128, 512]) for i in range(2)]
        dsum_l = [fw.sbuf(f"dsum{i}", [128, 512]) for i in range(2)]
        rec2 = [fw.sbuf(f"rec{i}", [128, 512]) for i in range(2)]
        ob = [fw.sbuf(f"ob{i}", [128, 512], BF16) for i in range(2)]
        cnt = [0]
        oi = 0
        for h in range(4):
            t, g = h // 2, h % 2
            ph = 64 * g
            qv = [lambda s, n, t=t, ph=ph: qn[ph:ph + 64, t, s:s + n]]
            kv = [lambda kt, t=t, ph=ph: kd[ph:ph + 64, t, kt * 128:(kt + 1) * 128]]
            for qb in qblocks:
                (s, n, is_ctx) = qb
                oacc, dacc, rec = [oacc2[oi % 2]], [dacc2[oi % 2]], rec2[oi % 2]
                attn_head(qv, kv, Vd[t], 1, 0.125, qb, sps_l, pT_l, oacc, dacc, [dsum_l[oi % 2]], cnt)
                fw.recip(rec[ph:ph + 64, :n], dacc[0][ph:ph + 64, :n])
                o = ob[oi % 2]; oi += 1
                fw.tt(o[ph:ph + 64, :n], oacc[0][ph:ph + 64, :n], rec[ph:ph + 64, :n], ALU.mult)
                fw.dma(OT[768 + h * 64:768 + (h + 1) * 64, s:s + n], o[ph:ph + 64, :n])
        fw.pop()

    def da_phase(l, b, need_ctx):
        lam_init = 0.8 - 0.6 * float(np.exp(-0.3 * l))
        fw.push()
        onesb = onesb_g
        lamb = fw.sbuf("lamb_s", [128, 128])
        fw.dma(lamb[:], lamb_d[l])
        lt = fw.sbuf("lt", [128, 64])
        lsum = fw.sbuf("lsum", [128, 2])
        nlam = fw.sbuf("nlam", [128, 1])
        sg = fw.sbuf("sg", [128, 1])
        fw.tt(lt[:, 0:32], lamb[:, 0:32], lamb[:, 32:64], ALU.mult)
        fw.tt(lt[:, 32:64], lamb[:, 64:96], lamb[:, 96:128], ALU.mult)
        fw.op("vector", "reduce_sum", [lt[:]], [lsum[:]], lsum[:, 0:1].ap, lt[:, 0:32].ap, AX.X)
        fw.op("vector", "reduce_sum", [lt[:]], [lsum[:]], lsum[:, 1:2].ap, lt[:, 32:64].ap, AX.X)
        fw.act(lsum[:], lsum[:], AF.Exp)
        fw.stt(nlam[:], lsum[:, 1:2], -lam_init, lsum[:, 0:1], ALU.add, ALU.subtract)
        fw.ts(sg[:], pcol(l, "da_sub"), 1.0 - lam_init, ALU.mult)
        qm = [fw.sbuf(f"qm{m}", [128, 2, T], BF16) for m in range(2)]
        kn = fw.sbuf("kn", [128, 2, T], BF16)
        Vd = [fw.sbuf(f"Vd{h}", [128, NT, 128], BF16) for h in range(4)]
        qblocks = (ctx_blocks if need_ctx else []) + lat_blocks
        fw.push()
        cosT = fw.sbuf("cosT", [128, TL]); sinT = fw.sbuf("sinT", [128, TL]); rot = fw.sbuf("rot", [128, 128])
        fw.dma(cosT[:], consts["cos_da"][:]); fw.dma(sinT[:], consts["sin_da"][:]); fw.dma(rot[:], consts["rot_da"][:])
        rotb = fw.sbuf("rotb", [128, 128], BF16)
        fw.copy(rotb[:], rot[:])
        blk32 = fw.sbuf("blk32", [128, 128])
        fw.dma(blk32[:], consts["blk32"][:])
        blk32b = fw.sbuf("blk32b", [128, 128], BF16)
        fw.copy(blk32b[:], blk32[:])
        dmask = fw.sbuf("dmask", [128, 2])
        fw.dma(dmask[:], consts["dmask"][:])
        raw = [fw.sbuf(f"raw{i}", [128, 512]) for i in range(3)]
        tsets = prep_sets("d")
        tps = fw.psum("tps", [128, 4, 128])
        vtmp = fw.sbuf("vtmp", [128, T])
        ri = 0
        for t in range(2):
            for (s, n, is_ctx) in qblocks:
                r = raw[ri % 3]; ri += 1
                fw.dma(r[:, :n], PT[DA_OFF + t * 128:DA_OFF + (t + 1) * 128, s:s + n])
                head_norm_rope(r[:, :n], [qm[0][:, t, s:s + n], qm[1][:, t, s:s + n]], blk32b, 32,
                               pcol(l, "da_qg"), rotb, cosT, sinT, s, n, is_ctx, tsets[ri % 3], masks=dmask)
            for (s, n, is_ctx) in blocks:
                r = raw[ri % 3]; ri += 1
                fw.dma(r[:, :n], PT[DA_OFF + 256 + t * 128:DA_OFF + 256 + (t + 1) * 128, s:s + n])
                head_norm_rope(r[:, :n], [kn[:, t, s:s + n]], blk32b, 32, pcol(l, "da_kg"), rotb, cosT, sinT,
                               s, n, is_ctx, tsets[ri % 3])
        make_vdup(DA_OFF + 512, 4, Vd, vtmp, tps)
        fw.pop()
        nps = fw.psum("anps", [128, 512])
        tmps = [fw.sbuf(f"nt{i}", [128, 512]) for i in range(2)]
        sps_l = [fw.psum(f"sps{i}", [128, 512]) for i in range(3)]
        pT_l = [fw.sbuf(f"pT{i}", [128, 512], BF16) for i in range(4)]
        oacc = [fw.psum(f"oacc{m}", [128, 512]) for m in range(2)]
        dacc = [fw.psum(f"dacc{m}", [128, 512]) for m in range(2)]
        dsum_l = [[fw.sbuf(f"dsum{i}_{m}", [128, 512]) for m in range(2)] for i in range(2)]
        hq = [0]
        rec = [fw.sbuf(f"rec{m}", [128, 512]) for m in range(2)]
        o1 = fw.sbuf("o1", [128, 512])
        osb = fw.sbuf("osb", [128, 512])
        ob = [fw.sbuf(f"ob{i}", [128, 512], BF16) for i in range(2)]
        sq, rs = tmps[0], tmps[1]
        cnt = [0]
        oi = 0
        for t in range(2):
            for qb in qblocks:
                (s, n, is_ctx) = qb
                for g in range(2):
                    h = 2 * t + g
                    ph = 64 * g
                    qv = [lambda s, n, t=t, ph=ph, m=m: qm[m][ph:ph + 64, t, s:s + n] for m in range(2)]
                    kv = [lambda kt, t=t, ph=ph: kn[ph:ph + 64, t, kt * 128:(kt + 1) * 128]] * 2
                    attn_head(qv, kv, Vd[h], 2, 32 ** -0.5, qb, sps_l, pT_l, oacc, dacc, dsum_l[hq[0] % 2], cnt)
                    hq[0] += 1
                    for m in range(2):
                        fw.recip(rec[m][ph:ph + 64, :n], dacc[m][ph:ph + 64, :n])
                    fw.tt(osb[ph:ph + 64, :n], oacc[0][ph:ph + 64, :n], rec[0][ph:ph + 64, :n], ALU.mult)
                    fw.tt(o1[ph:ph + 64, :n], oacc[1][ph:ph + 64, :n], rec[1][ph:ph + 64, :n], ALU.mult)
                    fw.stt(osb[ph:ph + 64, :n], o1[ph:ph + 64, :n], nlam[ph:ph + 64, 0:1], osb[ph:ph + 64, :n],
                           ALU.mult, ALU.add)
                fw.act(sq[:, :n], osb[:, :n], AF.Square)
                fw.mm(nps[:, :n], blk64[:], sq[:, :n])
                fw.act(rs[:, :n], nps[:, :n], AF.Sqrt, bias=epsb[:, 0:1], scale=1.0 / 64)
                fw.recip(rs[:, :n], rs[:, :n])
                o = ob[oi % 2]; oi += 1
                fw.stt(o[:, :n], osb[:, :n], sg[:, 0:1], rs[:, :n], ALU.mult, ALU.mult)
                fw.dma(OT[256 + t * 128:256 + (t + 1) * 128, s:s + n], o[:, :n])
        fw.pop()

    def chunk_order(rev):
        if not rev:
            return list(range(NT))
        return list(range(NTC - 1, -1, -1)) + list(range(NT - 1, NTC - 1, -1))

    def cumsum_chunks(A, B, rev):
        cur, oth = A, B
        s = 1
        while s < 128:
            cv = cur.t[:, :].rearrange("p (c i) -> p c i", i=128)
            ov = oth.t[:, :].rearrange("p (c i) -> p c i", i=128)
            if not rev:
                fw.tt(oth.v(ov[:, :, s:]), cur.v(cv[:, :, s:]), cur.v(cv[:, :, :128 - s]), ALU.add)
                fw.copy(oth.v(ov[:, :, :s]), cur.v(cv[:, :, :s]), eng="gpsimd")
            else:
                fw.tt(oth.v(ov[:, :, :128 - s]), cur.v(cv[:, :, :128 - s]), cur.v(cv[:, :, s:]), ALU.add)
                fw.copy(oth.v(ov[:, :, 128 - s:]), cur.v(cv[:, :, 128 - s:]), eng="gpsimd")
            cur, oth = oth, cur
            s *= 2
        return cur, oth

    def gla_phase(l, b, need_ctx):
        fw.push()
        Fb = [fw.sbuf(f"F{i}", [128, T]) for i in range(4)]
        F1, F2, F3, F4 = Fb
        Vdup = fw.sbuf("Vdup", [128, NT, 4, 128], BF16)
        QM = [fw.sbuf(f"QM{h}", [128, T], BF16) for h in range(4)]
        KTt = fw.sbuf("KTt", [128, T], BF16)
        KH = fw.sbuf("KH", [128, T], BF16)
        oaccT = fw.sbuf("oaccT", [128, 2, T])
        Pend = fw.sbuf("Pend", [128, NT])
        gfb = fw.sbuf("gfb", [48, T])
        a2 = fw.sbuf("a2", [48, 128])
        nab = fw.sbuf("nab", [128, 2])
        hm4 = fw.sbuf("hm4", [128, 4])
        tri = fw.sbuf("tri", [128, 4, 128])
        fw.dma(hm4[:], consts["hmask4s"][:])
        for d in range(2):
            fw.dma(a2[32 * d:32 * d + 16, :], gla_a2_d[l, d])
            fw.dma(gfb[32 * d:32 * d + 16, :], PT[GLA_OFF + 512 + 16 * d:GLA_OFF + 528 + 16 * d, :])
        c0 = PACK["gla_ab"][0]
        fw.ts(nab[:], pk[l][:, c0:c0 + 2], -1.0, ALU.mult)
        S = fw.sbuf("Sst", [128, 64])
        Sdup = fw.sbuf("Sdup", [128, 128], BF16)
        KHt = [fw.sbuf(f"KHt{i}", [128, 640], BF16) for i in range(2)]
        for i in range(2):
            fw.memset(KHt[i][:], 0.0)
        AT = [fw.sbuf(f"AT{i}", [128, 4, 128], BF16) for i in range(2)]
        tpsb = fw.psum("tpsb", [128, 4, 128], BF16)
        tps = fw.psum("tps", [128, 4, 128])
        aps = [fw.psum(f"aps{i}", [128, 4, 128]) for i in range(2)]
        ops = [fw.psum(f"ops{i}", [128, 4, 128]) for i in range(2)]
        sps = fw.psum("sps", [128, 64])
        zps = fw.psum("zps", [128, 512])
        for vt in range(2):
            fw.dma(F1[:], PT[GLA_OFF + 256 + vt * 128:GLA_OFF + 256 + (vt + 1) * 128, :])
            for i in range(NT):
                fw.transpose(tps[:, i % 4, :], F1[:, i * 128:(i + 1) * 128], ident[:])
                for hh in range(2):
                    h = vt * 2 + hh
                    fw.copy(Vdup[:, i, h, 0:64], tps[:, i % 4, hh * 64:(hh + 1) * 64], eng="vector")
                    fw.copy(Vdup[:, i, h, 64:128], tps[:, i % 4, hh * 64:(hh + 1) * 64], eng="scalar")
        fw.dma(F1[:], PT[GLA_OFF:GLA_OFF + 128, :])
        fw.dma(F2[:], PT[GLA_OFF + 128:GLA_OFF + 256, :])
        for d in range(2):
            rev = d == 1
            fw.dma(tri[:], consts[f"tri4_{d}"][:])
            for (s, n, is_ctx) in blocks:
                fw.mm(zps[:, :n], a2[32 * d:32 * d + 16, :], gfb[32 * d:32 * d + 16, s:s + n])
                fw.act(F3[:, s:s + n], zps[:, :n], AF.Exp, bias=nab[:, d:d + 1], scale=-1.0)
            fw.act(F3[:], F3[:], AF.Ln, bias=1.0)
            fw.ts(F3[:], F3[:], -1.0 / 16.0, ALU.mult)
            bb, ff = cumsum_chunks(F3, F4, rev)
            bv = bb.t[:, :].rearrange("p (c i) -> p c i", i=128)
            eidx = 0 if rev else 127
            fw.act(Pend[:], bb.v(bv[:, :, eidx]), AF.Exp)
            fw.act(ff[:], bb[:], AF.Exp)
            for h in range(4):
                fw.stt(QM[h][:], F1[:], hm4[:, h:h + 1], ff[:], ALU.mult, ALU.mult,
                       eng=("gpsimd" if h % 2 else "vector"))
            fw.act(ff[:], bb[:], AF.Exp, scale=-1.0)
            fw.tt(KTt[:], F2[:], ff[:], ALU.mult)
            for c in range(NT):
                fw.act(ff[:, c * 128:(c + 1) * 128], bb[:, c * 128:(c + 1) * 128], AF.Exp,
                       bias=bb[:, c * 128 + eidx:c * 128 + eidx + 1], scale=-1.0)
            fw.tt(KH[:], F2[:], ff[:], ALU.mult, eng="gpsimd")
            first = True
            for ci, c in enumerate(chunk_order(rev)):
                cs = slice(c * 128, (c + 1) * 128)
                is_ctx = c < NTC
                kht = KHt[ci % 2]
                at = AT[ci % 2]
                ap_, op_ = aps[ci % 2], ops[ci % 2]
                fw.transpose(tpsb[:, 0, :], KH[:, cs], identb[:])
                kv = kht.t[:, :].rearrange("p (h x) -> p h x", x=160)
                fw.copy(kht.v(kv[:, :, 0:32]), tpsb.v(tpsb.t[:, 0, :].rearrange("p (h x) -> p h x", x=32)))
                want_out = need_ctx or not is_ctx
                if want_out:
                    for h in range(4):
                        fw.mm(ap_[:, h, :], KTt[:, cs], QM[h][:, cs])
                    fw.tt(at[:], ap_[:], tri[:], ALU.mult)
                    for h in range(4):
                        if not first:
                            fw.mm(op_[:, h, :], Sdup[:], QM[h][:, cs], start=True, stop=False)
                        fw.mm(op_[:, h, :], Vdup[:, c, h, :], at[:, h, :], start=first, stop=True)
                    o4 = op_.t[:, :, :].rearrange("p (a g) t -> p a g t", g=2)
                    for g in range(2):
                        dst = oaccT[64 * g:64 * g + 64, :, cs]
                        src = op_.v(o4[64 * g:64 * g + 64, :, g, :])
                        if d == 0:
                            fw.copy(dst, src, eng=("vector" if g == 0 else "scalar"))
                        else:
                            fw.tt(dst, src, dst, ALU.add, eng="vector")
                for h in range(4):
                    fw.mm(sps[:, :], kht[:, h * 128:(h + 1) * 128], Vdup[:, c, h, 0:64], start=(h == 0), stop=(h == 3))
                if first:
                    fw.copy(S[:], sps[:])
                else:
                    fw.stt(S[:], S[:], Pend[:, c:c + 1], sps[:], ALU.mult, ALU.add)
                fw.copy(Sdup[:, 0:64], S[:], eng="scalar")
                fw.copy(Sdup[:, 64:128], S[:], eng="gpsimd")
                first = False
        sq, rs, rr = F3, F4, F1
        ob = [fw.sbuf(f"gob{i}", [128, 512], BF16) for i in range(2)]
        oi = 0
        qblocks = (ctx_blocks if need_ctx else []) + lat_blocks
        for t in range(2):
            for (s, n, is_ctx) in qblocks:
                fw.dma(rr[:, :n], PT[GLA_OFF + 544 + t * 128:GLA_OFF + 544 + (t + 1) * 128, s:s + n])
                fw.act(rr[:, :n], rr[:, :n], AF.Silu)
                fw.act(sq[:, :n], oaccT[:, t, s:s + n], AF.Square)
                fw.mm(zps[:, :n], blk64[:], sq[:, :n])
                fw.act(rs[:, :n], zps[:, :n], AF.Sqrt, bias=epsb[:, 0:1], scale=1.0 / 64)
                fw.recip(rs[:, :n], rs[:, :n])
                fw.stt(sq[:, :n], oaccT[:, t, s:s + n], pcol(l, "gla_ng"), rs[:, :n], ALU.mult, ALU.mult)
                o = ob[oi % 2]; oi += 1
                fw.tt(o[:, :n], sq[:, :n], rr[:, :n], ALU.mult, eng="gpsimd")
                fw.dma(OT[512 + t * 128:512 + (t + 1) * 128, s:s + n], o[:, :n])
        fw.pop()

    RWS = fw.dram("RWS", [2, 2, 8, 128, T], BF16)
    VDs = fw.dram("VDs", [128, NT, 4, 128], BF16)
    BON = fw.dram("BON", [256, T], F32)
    PENDs = fw.dram("PENDs", [128, 2, 2, NT], F32)
    GATE = fw.dram("GATE", [256, T], F32)

    def shift_mix(dst, raw, l, ct):
        m0 = pcol(l, "rw_mu0", ct)
        m1 = pcol(l, "rw_mu1", ct)
        c0 = pcol(l, "rw_c0", ct)
        fw.ts(dst[:, :], raw[:, :], c0, ALU.mult)
        for (s, e) in ((0, TC), (TC, T)):
            fw.stt(dst[:, s + 1:e], raw[:, s:e - 1], m0, dst[:, s + 1:e], ALU.mult, ALU.add)
            fw.stt(dst[:, s:e - 1], raw[:, s + 1:e], m1, dst[:, s:e - 1], ALU.mult, ALU.add, eng="gpsimd")

    def rwkv_phase(l, b, need_ctx):
        fw.push()
        lora = [fw.sbuf(f"lora{i}", [128, T], BF16) for i in range(3)]
        w2s = fw.sbuf("w2s", [128, 256], BF16)
        a2s = fw.sbuf("a2s", [128, 256], BF16)
        g2s = fw.sbuf("g2s", [128, 256], BF16)
        fw.dma(w2s[:], rw_w2_d[l], eng="gpsimd")
        fw.dma(a2s[:], rw_a2_d[l], eng="gpsimd")
        fw.dma(g2s[:], rw_g2_d[l], eng="gpsimd")
        hm2 = fw.sbuf("hm2", [128, 4])
        fw.dma(hm2[:], consts["hm2"][:])
        c0 = PACK["rw_mu0"][0]
        c1 = PACK["rw_mu1"][0]
        cc = PACK["rw_c0"][0]
        fw.tt(pk[l][:, cc:cc + 9], pk[l][:, c0:c0 + 9], pk[l][:, c1:c1 + 9], ALU.add)
        fw.ts(pk[l][:, cc:cc + 9], pk[l][:, cc:cc + 9], -1.0, ALU.mult, 1.0, ALU.add)
        ck = PACK["rw_ka"][0]
        co = PACK["rw_omka"][0]
        fw.ts(pk[l][:, co:co + 2], pk[l][:, ck:ck + 2], -1.0, ALU.mult, 1.0, ALU.add)
        Bf = [fw.sbuf(f"B{i}", [128, T]) for i in range(10)]
        stgb = [fw.sbuf(f"stgb{i}", [128, T], BF16) for i in range(2)]
        zps = [fw.psum(f"zps{i}", [128, 512]) for i in range(3)]
        tps = fw.psum("tps", [128, 4, 128])
        vd = [fw.sbuf(f"vd{i}", [128, 4, 128], BF16) for i in range(2)]
        eps12 = fw.sbuf("eps12", [128, 1])
        fw.memset(eps12[:], 1e-12)
        raw, sh = Bf[0], Bf[1]
        for i, fn in ((0, AF.Tanh), (1, None), (2, AF.Sigmoid)):
            fw.dma(raw[:], PT[RW_OFF + (6 + i) * 128:RW_OFF + (7 + i) * 128, :])
            shift_mix(sh, raw, l, 6 + i)
            if fn is None:
                fw.copy(lora[i][:], sh[:])
            else:
                fw.act(lora[i][:], sh[:], fn)
        for vt in range(2):
            fw.dma(raw[:], PT[RW_OFF + 512 + vt * 128:RW_OFF + 512 + (vt + 1) * 128, :])
            shift_mix(sh, raw, l, 4 + vt)
            for i in range(NT):
                fw.transpose(tps[:, i % 4, :], sh[:, i * 128:(i + 1) * 128], ident[:])
                v_ = vd[i % 2]
                for hh in range(2):
                    fw.copy(v_[:, hh, 0:64], tps[:, i % 4, hh * 64:(hh + 1) * 64], eng="vector")
                    fw.copy(v_[:, hh, 64:128], tps[:, i % 4, hh * 64:(hh + 1) * 64], eng="scalar")
                fw.dma(VDs[:, i, 2 * vt:2 * vt + 2, :], v_[:, 0:2, :])
        Pend = fw.dram("Pend_d", [2, 2, 128, NT], F32) if False else None
        pend_s = fw.sbuf("pend_s", [128, 2, 2, NT])
        Fk, Fkk, Fr, Fbon, Ll, Aa, Bb, Fa, Fkd, Fb = Bf
        for tau in range(2):
            fw.dma(raw[:], PT[RW_OFF + 256 + tau * 128:RW_OFF + 256 + (tau + 1) * 128, :]) if False else None
            fw.dma(Ll[:], PT[RW_OFF + 256 + tau * 128:RW_OFF + 256 + (tau + 1) * 128, :])
            shift_mix(Fk, Ll, l, 2 + tau)
            fw.dma(Ll[:], PT[RW_OFF + tau * 128:RW_OFF + (tau + 1) * 128, :])
            shift_mix(Fr, Ll, l, tau)
            fw.ts(Fkk[:], Fk[:], pcol(l, "rw_kk", tau), ALU.mult)
            for (s, n, is_ctx) in blocks:
                zp = zps[0]
                fw.act(Aa[:, s:s + n], Fkk[:, s:s + n], AF.Square)
                fw.mm(zp[:, :n], blk64[:], Aa[:, s:s + n])
                fw.act(Aa[:, s:s + n], zp[:, :n], AF.Sqrt, bias=eps12[:, 0:1], scale=1.0)
            fw.recip(Aa[:], Aa[:])
            fw.tt(Fkk[:], Fkk[:], Aa[:], ALU.mult)
            for bi, (s, n, is_ctx) in enumerate(blocks):
                zp = zps[bi % 3]
                fw.mm(zp[:, :n], g2s[:, tau * 128:(tau + 1) * 128], lora[2][:, s:s + n])
                fw.copy(Aa[:, s:s + n], zp[:, :n], eng="scalar")
            fw.dma(GATE[tau * 128:(tau + 1) * 128, :], Aa[:])
            for d in range(2):
                rev = d == 1
                eidx = 0 if rev else 127
                ph = 64 * d
                for bi, (s, n, is_ctx) in enumerate(blocks):
                    zp = zps[bi % 3]
                    fw.mm(zp[:, :n], w2s[ph:ph + 64, tau * 128:(tau + 1) * 128], lora[0][ph:ph + 64, s:s + n])
                    fw.act(Ll[:, s:s + n], zp[:, :n], AF.Sigmoid, bias=pcol(l, "rw_w0", d * 2 + tau))
                    zp2 = zps[(bi + 1) % 3]
                    fw.mm(zp2[:, :n], a2s[ph:ph + 64, tau * 128:(tau + 1) * 128], lora[1][ph:ph + 64, s:s + n])
                    fw.act(Fa[:, s:s + n], zp2[:, :n], AF.Sigmoid, bias=pcol(l, "rw_a0", d * 2 + tau))
                fw.ts(Ll[:], Ll[:], -0.6065306597126334, ALU.mult)
                cur = Ll
                pp = [Aa, Bb]
                st = 1
                k_ = 0
                while st < 128:
                    oth = pp[k_ % 2]
                    cv = cur.t[:, :].rearrange("p (c i) -> p c i", i=128)
                    ov = oth.t[:, :].rearrange("p (c i) -> p c i", i=128)
                    if not rev:
                        fw.tt(oth.v(ov[:, :, st:]), cur.v(cv[:, :, st:]), cur.v(cv[:, :, :128 - st]), ALU.add)
                        fw.copy(oth.v(ov[:, :, :st]), cur.v(cv[:, :, :st]), eng="scalar")
                    else:
                        fw.tt(oth.v(ov[:, :, :128 - st]), cur.v(cv[:, :, :128 - st]), cur.v(cv[:, :, st:]), ALU.add)
                        fw.copy(oth.v(ov[:, :, 128 - st:]), cur.v(cv[:, :, 128 - st:]), eng="scalar")
                    cur = oth
                    st *= 2
                    k_ += 1
                assert cur is Aa
                cum = Aa
                cvw = cum.t[:, :].rearrange("p (c i) -> p c i", i=128)
                fw.act(pend_s[:, d, tau, :], cum.v(cvw[:, :, eidx]), AF.Exp)
                fw.tt(Ll[:], cum[:], Ll[:], ALU.subtract)
                fw.ts(Fkd[:], Fa[:], pcol(l, "rw_ka", tau), ALU.mult, pcol(l, "rw_omka", tau), ALU.add)
                fw.tt(Fkd[:], Fkd[:], Fk[:], ALU.mult, eng="gpsimd")
                fw.tt(Fb[:], Fkk[:], Fa[:], ALU.mult, eng="gpsimd")
                fw.stt(Fa[:], Fr[:], pcol(l, "rw_rk", tau), Fkd[:], ALU.mult, ALU.mult)
                for bi, (s, n, is_ctx) in enumerate(blocks):
                    zp = zps[bi % 3]
                    fw.mm(zp[:, :n], blk64[:], Fa[:, s:s + n])
                    if d == 0:
                        fw.copy(Fbon[:, s:s + n], zp[:, :n], eng="scalar")
                    else:
                        fw.tt(Fbon[:, s:s + n], zp[:, :n], Fbon[:, s:s + n], ALU.add)
                si = [0]

                def emit(arr_idx, fn):
                    o = stgb[si[0] % 2]
                    si[0] += 1
                    fn(o)
                    fw.dma(RWS[d, tau, arr_idx], o[:])
                fw.act(Bb[:], Ll[:], AF.Exp)
                for hh in range(2):
                    emit(hh, lambda o, hh=hh: fw.stt(o[:], Fkk[:], hm2[:, 2 + hh:3 + hh], Bb[:], ALU.mult, ALU.mult,
                                                    eng=("vector" if hh == 0 else "gpsimd")))
                fw.act(Bb[:], cum[:], AF.Exp)
                for hh in range(2):
                    emit(2 + hh, lambda o, hh=hh: fw.stt(o[:], Fr[:], hm2[:, hh:hh + 1], Bb[:], ALU.mult, ALU.mult,
                                                        eng=("vector" if hh == 0 else "gpsimd")))
                fw.act(Bb[:], cum[:], AF.Exp, scale=-1.0)
                emit(4, lambda o: fw.tt(o[:], Fb[:], Bb[:], ALU.mult))
                emit(5, lambda o: fw.tt(o[:], Fkd[:], Bb[:], ALU.mult, eng="gpsimd"))
                for c in range(NT):
                    fw.act(Bb[:, c * 128:(c + 1) * 128], cum[:, c * 128:(c + 1) * 128], AF.Exp,
                           bias=cum[:, c * 128 + eidx:c * 128 + eidx + 1], scale=-1.0)
                emit(6, lambda o: fw.tt(o[:], Fb[:], Bb[:], ALU.mult))
                emit(7, lambda o: fw.tt(o[:], Fkd[:], Bb[:], ALU.mult, eng="gpsimd"))
            fw.dma(Ll[:], PT[RW_OFF + 512 + tau * 128:RW_OFF + 512 + (tau + 1) * 128, :])
            shift_mix(Aa, Ll, l, 4 + tau)
            fw.tt(Aa[:], Aa[:], Fbon[:], ALU.mult)
            fw.dma(BON[tau * 128:(tau + 1) * 128, :], Aa[:])
        fw.dma(PENDs[:], pend_s[:])
        fw.pop()

        fw.push()
        pend = fw.sbuf("pend", [128, 2, 2, NT])
        fw.dma(pend[:], PENDs[:])
        Vdup = fw.sbuf("Vdup", [128, NT, 4, 128], BF16)
        fw.dma(Vdup[:], VDs[:])
        yaccT = fw.sbuf("yaccT", [128, 2, T])
        fw.memset(yaccT[:, 0, :], 0.0)
        fw.memset(yaccT[:, 1, :], 0.0, eng="gpsimd")
        I4 = fw.sbuf("I4", [128, 2, 128])
        fw.dma(I4[:], consts["I2"][:])
        identb_ = identb
        PB = [fw.psum(f"PB{i}", [128, 512]) for i in range(6)]
        pbi = [0]

        def bank():
            p = PB[pbi[0] % len(PB)]
            pbi[0] += 1
            return p

        def v4(bk, w=128):
            return bk.t[:, 0:4 * w].rearrange("p (h x) -> p h x", x=w)

        evi = [0]

        def evac(out, in_):
            e = "scalar" if evi[0] % 2 == 0 else "vector"
            evi[0] += 1
            fw.copy(out, in_, eng=e)

        def chunk_gen(d):
            rev = d == 1
            maskN = fw.sbuf(f"maskN{d}", [128, 4, 128])
            maskAB = fw.sbuf(f"maskAB{d}", [128, 2, 256])
            fw.dma(maskN[:], consts[f"maskN_{d}"][:])
            fw.dma(maskAB[:], consts[f"maskAB_{d}"][:])
            CH = [[fw.sbuf(f"CH{d}{i}_{tau}", [128, 8, 128], BF16) for tau in range(2)] for i in range(2)]
            BKt = [fw.sbuf(f"BKt{d}{i}", [128, 4, 384], BF16) for i in range(2)]
            for i in range(2):
                fw.memset(BKt[i][:], 0.0)
            X = [fw.sbuf(f"X{d}{i}", [128, 4, 128], BF16) for i in range(2)]
            XT = [fw.sbuf(f"XT{d}{i}", [128, 4, 128], BF16) for i in range(2)]
            Wt = [fw.sbuf(f"Wt{d}{i}", [128, 4, 128], BF16) for i in range(2)]
            AB = [[fw.sbuf(f"AB{d}{i}_{tau}", [128, 2, 256], BF16) for tau in range(2)] for i in range(2)]
            AK = [[fw.sbuf(f"AK{d}{i}_{tau}", [128, 2, 256], BF16) for tau in range(2)] for i in range(2)]
            Z = fw.sbuf(f"Zz{d}", [128, 4, 64], BF16)
            Udup = fw.sbuf(f"Udup{d}", [128, 4, 128], BF16)
            ST = [fw.sbuf(f"ST{d}{tau}", [128, 64]) for tau in range(2)]
            STd = [fw.sbuf(f"STd{d}{tau}", [128, 128], BF16) for tau in range(2)]
            tpsb = fw.psum(f"tpsb{d}", [128, 4, 128], BF16)
            for tau in range(2):
                fw.memset(ST[tau][:], 0.0)
                fw.memset(STd[tau][:], 0.0)
            order = chunk_order(rev)

            def load(ci):
                c = order[ci]
                cs = slice(c * 128, (c + 1) * 128)
                for tau in range(2):
                    fw.dma(CH[ci % 2][tau][:], RWS.v(RWS.t[d, tau, :, :, cs].rearrange("a p t -> p a t")))
            load(0)
            yield
            for ci, c in enumerate(order):
                cs = slice(c * 128, (c + 1) * 128)
                is_ctx = c < NTC
                want_out = need_ctx or not is_ctx
                ch = CH[ci % 2]
                bkt = BKt[ci % 2]
                if ci + 1 < len(order):
                    load(ci + 1)
                for tau in range(2):
                    fw.transpose(tpsb[:, 2 * tau, :], ch[tau][:, 6, :], identb_[:])
                    fw.transpose(tpsb[:, 2 * tau + 1, :], ch[tau][:, 7, :], identb_[:])
                bv = bkt.t[:, :, :].rearrange("p a (h x) -> p a h x", x=192)
                fw.copy(bkt.v(bv[:, :, :, 0:64]), tpsb.v(tpsb.t[:, :, :].rearrange("p a (h x) -> p a h x", x=64)),
                        eng="scalar")
                nb = bank()
                for h in range(4):
                    tau, hh = h // 2, h % 2
                    fw.mm(nb.v(v4(nb)[:, h, :]), ch[tau][:, hh, :], ch[tau][:, 4, :])
                x0 = X[0]
                fw.tt(x0[:], nb.v(v4(nb)), maskN[:], ALU.mult)
                yield
                ab, ak = AB[ci % 2], AK[ci % 2]
                for tau in range(2):
                    b2, b3 = bank(), bank()
                    for hh in range(2):
                        for (bk, arr) in ((b2, 4), (b3, 5)):
                            o = bk.t[:, :].rearrange("p (h x) -> p h x", x=256)
                            fw.mm(bk.v(o[:, hh, 0:128]), ch[tau][:, arr, :], ch[tau][:, hh, :])
                            fw.mm(bk.v(o[:, hh, 128:256]), ch[tau][:, arr, :], ch[tau][:, 2 + hh, :])
                    fw.tt(ab[tau][:], b2.v(b2.t[:, :].rearrange("p (h x) -> p h x", x=256)), maskAB[:], ALU.mult)
                    fw.tt(ak[tau][:], b3.v(b3.t[:, :].rearrange("p (h x) -> p h x", x=256)), maskAB[:], ALU.mult)
                    yield
                xt0 = XT[0]
                w0 = Wt[0]
                for tau in range(2):
                    fw.copy(xt0[:, 2 * tau:2 * tau + 2, :], ab[tau][:, :, 0:128], eng="gpsimd")
                    fw.tt(w0[:, 2 * tau:2 * tau + 2, :], ab[tau][:, :, 0:128], I4[:], ALU.add, eng="gpsimd")
                xc, xtc, wc = x0, xt0, w0
                for p in range(6):
                    xn, xtn, wn = X[(p + 1) % 2], XT[(p + 1) % 2], Wt[(p + 1) % 2]
                    bx = bank()
                    for h in range(4):
                        fw.mm(bx.v(v4(bx)[:, h, :]), xtc[:, h, :], xc[:, h, :])
                    if p < 5:
                        bxt = bank()
                        for h in range(4):
                            fw.mm(bxt.v(v4(bxt)[:, h, :]), xc[:, h, :], xtc[:, h, :])
                    evac(xn[:], bx.v(v4(bx)))
                    if p < 5:
                        evac(xtn[:], bxt.v(v4(bxt)))
                    yield
                    bw = bank()
                    for h in range(4):
                        fw.mm(bw.v(v4(bw)[:, h, :]), identb_[:], wc[:, h, :], start=True, stop=False)
                        fw.mm(bw.v(v4(bw)[:, h, :]), xn[:, h, :], wc[:, h, :], start=False, stop=True)
                    evac(wn[:], bw.v(v4(bw)))
                    xc, xtc, wc = xn, xtn, wn
                    yield
                wT = wc
                gb = bank()
                g4 = gb.t[:, 0:256].rearrange("p (h x) -> p h x", x=64)
                for h in range(4):
                    tau, hh = h // 2, h % 2
                    fw.mm(gb.v(g4[:, h, :]), ch[tau][:, hh, :], STd[tau][:, 0:64], start=True, stop=False)
                    fw.mm(gb.v(g4[:, h, :]), ak[tau][:, hh, 0:128], Vdup[:, c, h, 0:64], start=False, stop=True)
                fw.copy(Z[:], gb.v(g4), eng="scalar")
                yield
                ub = bank()
                u4 = ub.t[:, 0:256].rearrange("p (h x) -> p h x", x=64)
                for h in range(4):
                    fw.mm(ub.v(u4[:, h, :]), wT[:, h, :], Z[:, h, :])
                fw.copy(Udup[:, :, 0:64], ub.v(u4), eng="vector")
                fw.copy(Udup[:, :, 64:128], ub.v(u4), eng="scalar")
                yield
                if want_out:
                    yb = bank()
                    for h in range(4):
                        tau, hh = h // 2, h % 2
                        o = yb.v(v4(yb)[:, h, :])
                        fw.mm(o, STd[tau][:], ch[tau][:, 2 + hh, :], start=True, stop=False)
                        fw.mm(o, Udup[:, h, :], ab[tau][:, hh, 128:256], start=False, stop=False)
                        fw.mm(o, Vdup[:, c, h, :], ak[tau][:, hh, 128:256], start=False, stop=True)
                    o4 = yb.t[:, :].rearrange("p (a g t) -> p a g t", g=2, t=128)
                    for g in range(2):
                        dst = yaccT[64 * g:64 * g + 64, :, cs]
                        src = yb.v(o4[64 * g:64 * g + 64, :, g, :])
                        fw.tt(dst, src, dst, ALU.add, eng="vector")
                for tau in range(2):
                    sb = bank()
                    for hh in range(2):
                        h = 2 * tau + hh
                        fw.mm(sb[:, 0:64], bkt[:, 2 * tau, hh * 128:(hh + 1) * 128], Udup[:, h, 0:64],
                              start=(hh == 0), stop=False)
                        fw.mm(sb[:, 0:64], bkt[:, 2 * tau + 1, hh * 128:(hh + 1) * 128], Vdup[:, c, h, 0:64],
                              start=False, stop=(hh == 1))
                    fw.stt(ST[tau][:], ST[tau][:], pend[:, d, tau, c:c + 1], sb[:, 0:64], ALU.mult, ALU.add)
                    fw.copy(STd[tau][:, 0:64], ST[tau][:], eng="scalar")
                    fw.copy(STd[tau][:, 64:128], ST[tau][:], eng="gpsimd")
                yield

        gens = [chunk_gen(0), chunk_gen(1)]
        alive = [True, True]
        while any(alive):
            for gi, g in enumerate(gens):
                if alive[gi]:
                    try:
                        next(g)
                    except StopIteration:
                        alive[gi] = False
        epsln = fw.sbuf("epsln", [128, 1])
        fw.memset(epsln[:], 64e-5)
        tb = [fw.sbuf(f"r3_{i}", [128, 512]) for i in range(5)]
        ob = [fw.sbuf(f"rob{i}", [128, 512], BF16) for i in range(2)]
        oi = 0
        qblocks = (ctx_blocks if need_ctx else []) + lat_blocks
        for tau in range(2):
            for (s, n, is_ctx) in qblocks:
                yc, sq, rs, bo, ga = tb
                fw.dma(bo[:, :n], BON[tau * 128:(tau + 1) * 128, s:s + n])
                fw.dma(ga[:, :n], GATE[tau * 128:(tau + 1) * 128, s:s + n])
                mb = bank()
                fw.mm(mb[:, :n], blk64[:], yaccT[:, tau, s:s + n])
                fw.stt(yc[:, :n], mb[:, :n], -1.0 / 64, yaccT[:, tau, s:s + n], ALU.mult, ALU.add)
                fw.act(sq[:, :n], yc[:, :n], AF.Square)
                vb = bank()
                fw.mm(vb[:, :n], blk64[:], sq[:, :n])
                fw.act(rs[:, :n], vb[:, :n], AF.Sqrt, bias=epsln[:, 0:1], scale=1.0 / 64)
                fw.recip(rs[:, :n], rs[:, :n])
                fw.stt(yc[:, :n], yc[:, :n], pcol(l, "rw_ln_g", tau), rs[:, :n], ALU.mult, ALU.mult)
                fw.stt(yc[:, :n], yc[:, :n], pcol(l, "rw_ln_b", tau), bo[:, :n], ALU.add, ALU.add, eng="gpsimd")
                o = ob[oi % 2]; oi += 1
                fw.tt(o[:, :n], yc[:, :n], ga[:, :n], ALU.mult, eng="gpsimd")
                fw.dma(OT[tau * 128:(tau + 1) * 128, s:s + n], o[:, :n])
        fw.pop()

    def mixers_0(l, b, need_ctx):
        if "rwkv" in cfg.mix:
            rwkv_phase(l, b, need_ctx)
        if "gla" in cfg.mix:
            gla_phase(l, b, need_ctx)
        if "gqa" in cfg.mix:
            gqa_phase(l, b, need_ctx)
        if "da" in cfg.mix:
            da_phase(l, b, need_ctx)

    def mixers(l, b, need_ctx):
        if "rwkv" in cfg.mix:
            rwkv_phase(l, b, need_ctx)
        if "da" in cfg.mix:
            da_phase(l, b, need_ctx)
        if "gla" in cfg.mix:
            gla_phase(l, b, need_ctx)
        if "gqa" in cfg.mix:
            gqa_phase(l, b, need_ctx)

    for b in range(NB):
        fw.push()
        xT = fw.sbuf("xT_s", [128, KT, T])
        xv = xT_d.t[b].rearrange("(k p) t -> p k t", p=128)
        for k in range(KT):
            fw.dma(xT[:, k, :], xT_d.v(xv[:, k, :]))
        for l in range(L):
            need_ctx = l < L - 1
            fw.push()
            hT = fw.sbuf("hT", [128, KT, T], BF16)
            sq = fw.sbuf("sq", [128, 512])
            rstd = fw.sbuf("rstd", [128, 512])
            nps = fw.psum("nps", [128, 512])
            norm_phase(xT, hT, l, 0, b, sq, rstd, nps)
            wt = [fw.sbuf(f"wt{i}", [128, KT, 256], BF16) for i in range(3)]
            pps = [fw.psum(f"pps{i}", [128, 512]) for i in range(4)]
            stg = [fw.sbuf(f"stg{i}", [128, 512]) for i in range(4)]
            wv = w_in.t[l].rearrange("(k p) c -> p k c", p=128)
            ei = 0
            tiles_ = mixer_cols()
            groups_ = [tiles_[i:i + 2] for i in range(0, len(tiles_), 2)]
            for gi, grp in enumerate(groups_):
                g0 = grp[0][0]
                gn = sum(nc_ for _, nc_ in grp)
                wb = wt[gi % 3]
                fw.dma(wb[:, :, :gn], w_in.v(wv[:, :, g0:g0 + gn]), eng="gpsimd")
                for (c0, ncol) in grp:
                    off = c0 - g0
                    for (s, n, is_ctx) in blocks:
                        ps = pps[ei % 4]
                        st = stg[ei % 4]
                        for k in range(KT):
                            fw.mm(ps[:ncol, :n], wb[:, k, off:off + ncol], hT[:, k, s:s + n],
                                  start=(k == 0), stop=(k == KT - 1))
                        fw.copy(st[:ncol, :n], ps[:ncol, :n], eng=("vector" if ei % 2 == 0 else "scalar"))
                        fw.dma(PT[c0:c0 + ncol, s:s + n], st[:ncol, :n])
                        ei += 1
            fw.pop()
            if cfg.stop == "proj":
                break
            if cfg.stop == "ffn":
                fw.push()
                tb = fw.sbuf("tb", [128, T])
                tbb = fw.sbuf("tbb", [128, T], BF16)
                for k in range(KT):
                    fw.dma(tb[:], PT[k * 128:(k + 1) * 128, :])
                    fw.copy(tbb[:], tb[:])
                    fw.dma(OT[k * 128:(k + 1) * 128, :], tbb[:])
                fw.pop()
            else:
                mixers(l, b, need_ctx)
            if cfg.stop == "mix":
                break
            wout_phase(xT, l, b, need_ctx)
            ffn_phase(xT, l, b, need_ctx)
            if cfg.stop == "ffn":
                break
        yv = yT_d.t[b].rearrange("(k p) t -> p k t", p=128)
        for k in range(KT):
            fw.dma(yT_d.v(yv[:, k, :]), xT[:, k, TC:T])
        fw.pop()
        if cfg.stop is not None:
            break

    if cfg.stop is not None:
        dbg_pt = fw.dram("dbg_PT", [N_IN, T], F32, kind="ExternalOutput")
        fw.dma(dbg_pt[:], PT[:])
        dbg_ot = fw.dram("dbg_OT", [D, T], BF16, kind="ExternalOutput")
        fw.dma(dbg_ot[:], OT[:])
    fw.pop()
    fw.finish()
    return nc


_NC_CACHE = {}


def kernel(**inputs):
    n_cores = 8
    B = inputs["x"].shape[0]
    NB = B // n_cores
    cfg = Cfg(TC=inputs["ctx"].shape[1], TL=inputs["x"].shape[1], NB=NB, depth=DEPTH)
    nc = build(cfg)
    in_maps = [prep_inputs(inputs, cfg, i * NB) for i in range(n_cores)]
    res = run_bass_kernel_spmd(nc, in_maps, core_ids=list(range(n_cores)))
    out = np.empty((B, cfg.TL, D), np.float32)
    for i in range(n_cores):
        yT = np.asarray(res.results[i]["yT"])
        out[i * NB:(i + 1) * NB] = yT.transpose(0, 2, 1)
    return out


def prep_inputs(inp, cfg, b0):
    NB = cfg.NB
    m = {}
    x = np.asarray(inp["x"], np.float32)[b0:b0 + NB]
    ctx = np.asarray(inp["ctx"], np.float32)[b0:b0 + NB]
    xc = np.concatenate([ctx, x], axis=1)
    m["xT"] = np.ascontiguousarray(xc.transpose(0, 2, 1))
    cvec = np.concatenate([np.asarray(inp["c"], np.float32)[b0:b0 + NB],
                           np.asarray(inp["c_ctx"], np.float32)[None]], axis=0)
    m["cT"] = np.ascontiguousarray(cvec.reshape(NB + 1, KT, 128).transpose(2, 1, 0))
    L = cfg.depth
    m["pack"] = np.stack([host_pack(inp, l) for l in range(L)])
    for nm in ("rw_w2", "rw_a2"):
        m[nm] = np.ascontiguousarray(np.asarray(inp[nm], np.float32)[:L].reshape(L, 128, 256))
    m["rw_g2"] = np.ascontiguousarray(np.asarray(inp["rw_g2"], np.float32)[:L])
    m["lamb"] = np.stack([np.tile(np.asarray(inp["da_lam"], np.float32)[l].reshape(1, 128), (128, 1)) for l in range(L)])
    for nm in ("mod_w", "w_in", "w_out", "ffn_w_up", "ffn_w_down", "gla_a2"):
        m[nm] = np.ascontiguousarray(np.asarray(inp[nm], np.float32)[:L])
    for k, v in host_consts(cfg).items():
        m["c_" + k] = v
    return m
```

```python
import numpy as np
import concourse.bass as bass
import concourse.mybir as mybir
from concourse.bass_utils import run_bass_kernel_spmd

F32 = mybir.dt.float32
BF16 = mybir.dt.bfloat16
AF = mybir.ActivationFunctionType
ALU = mybir.AluOpType
AX = mybir.AxisListType

ENGS = ("tensor", "vector", "scalar", "gpsimd", "sync")


class Trk:
    __slots__ = ("name", "w", "r")

    def __init__(self, name):
        self.name = name
        self.w = None
        self.r = {}


class V:
    __slots__ = ("ap", "trk")

    def __init__(self, ap, trk):
        self.ap = ap
        self.trk = trk


class Buf:
    def __init__(self, t, name):
        self.t = t
        self.name = name
        self.trk = Trk(name)

    def __getitem__(self, idx):
        return V(self.t[idx], self.trk)

    def v(self, ap):
        return V(ap, self.trk)


def _trks(v):
    return v.trk if isinstance(v.trk, (list, tuple)) else (v.trk,)


class FW:
    def __init__(self, nc, n_dma_sems=32):
        self.nc = nc
        self.prog = {e: [] for e in ENGS}
        self.sem = {e: nc.alloc_semaphore(name=f"s_{e}") for e in ENGS}
        self.cnt = {e: 0 for e in ENGS}
        self.waited = {e: {} for e in ENGS}
        self.dsem = [nc.alloc_semaphore(name=f"d_{i}") for i in range(n_dma_sems)]
        self.dcnt = [0] * n_dma_sems
        self.dnext = 0
        self.gnext = 0
        self.semobj = {}
        for e in ENGS:
            self.semobj[("e", e)] = self.sem[e]
        for i, s in enumerate(self.dsem):
            self.semobj[("d", i)] = s
        self.ninst = 0
        self.stack = []

    def push(self):
        self.stack.append([])

    def pop(self):
        self.barrier()
        for g in reversed(self.stack.pop()):
            g.__exit__(None, None, None)

    def sbuf(self, name, shape, dtype=F32):
        self.uid = getattr(self, "uid", 0) + 1
        name = f"{name}_u{self.uid}"
        g = self.nc.sbuf_tensor(name, list(shape), dtype)
        t = g.__enter__()
        self.stack[-1].append(g)
        return Buf(t, name)

    def psum(self, name, shape, dtype=F32):
        self.uid = getattr(self, "uid", 0) + 1
        name = f"{name}_u{self.uid}"
        g = self.nc.psum_tensor(name, list(shape), dtype)
        t = g.__enter__()
        self.stack[-1].append(g)
        return Buf(t, name)

    def dram(self, name, shape, dtype=F32, kind="Internal"):
        return Buf(self.nc.dram_tensor(name, list(shape), dtype, kind=kind).ap(), name)

    def _wait(self, eng, ev):
        if ev is None:
            return
        key, val = ev
        if eng == "tensor" and key == ("e", "tensor"):
            return
        if self.waited[eng].get(key, 0) >= val:
            return
        self.waited[eng][key] = val
        self.prog[eng].append(("wait", key, val))

    def _deps(self, eng, reads, writes):
        for v in reads:
            for t in _trks(v):
                self._wait(eng, t.w)
        for v in writes:
            for t in _trks(v):
                self._wait(eng, t.w)
                for kv in list(t.r.items()):
                    self._wait(eng, kv)

    def _mark(self, ev, reads, writes):
        for v in reads:
            for t in _trks(v):
                if t.r.get(ev[0], 0) < ev[1]:
                    t.r[ev[0]] = ev[1]
        for v in writes:
            for t in _trks(v):
                t.w = ev
                t.r = {}

    def op(self, eng, meth, reads, writes, *args, **kw):
        self._deps(eng, reads, writes)
        self.cnt[eng] += 1
        ev = (("e", eng), self.cnt[eng])
        sem = self.sem[eng]
        a2 = [a.ap if isinstance(a, V) else a for a in args]
        k2 = {k: (a.ap if isinstance(a, V) else a) for k, a in kw.items()}

        def emit(e, inc, wait=None, meth=meth, a2=a2, k2=k2, sem=sem):
            ins = getattr(e, meth)(*a2, **k2)
            if wait is not None:
                ins._wait_ge(wait[0], wait[1])
            if inc:
                ins.then_inc(sem, 1)
        self.prog[eng].append(("op", emit, self.cnt[eng]))
        self._mark(ev, reads, writes)
        self.ninst += 1
        return ev

    def dma(self, out, in_, eng="sync", **kw):
        self._deps(eng, [in_], [out])
        nd = len(self.dsem)
        if eng == "gpsimd":
            k = nd - 8 + self.gnext
            self.gnext = (self.gnext + 1) % 8
        else:
            k = self.dnext
            self.dnext = (self.dnext + 1) % (nd - 8)
        if self.dcnt[k] > 0:
            self._wait(eng, (("d", k), self.dcnt[k]))
        self.dcnt[k] += 16
        ev = (("d", k), self.dcnt[k])
        sem = self.dsem[k]
        oa, ia = out.ap, in_.ap

        def emit(e, oa=oa, ia=ia, sem=sem, kw=kw):
            e.dma_start(out=oa, in_=ia, **kw).then_inc(sem, 16)
        self.prog[eng].append(("dma", emit))
        self._mark(ev, [in_], [out])
        self.ninst += 1
        return ev

    def _all_events(self):
        evs = [(("e", e), self.cnt[e]) for e in ENGS if self.cnt[e] > 0]
        evs += [(("d", i), c) for i, c in enumerate(self.dcnt) if c > 0]
        return evs

    def barrier(self):
        evs = self._all_events()
        for e in ENGS:
            for ev in evs:
                self._wait(e, ev)

    def finish(self):
        for ev in self._all_events():
            self._wait("sync", ev)
        import bisect
        needed = {e: set() for e in ENGS}
        for ename in ENGS:
            for it in self.prog[ename]:
                if it[0] == "wait" and it[1][0] == "e":
                    needed[it[1][1]].add(it[2])
        ranks = {e: sorted(needed[e]) for e in ENGS}
        self.max_sem = {e: len(ranks[e]) for e in ENGS}
        with self.nc.Block() as block:
            for ename in ENGS:
                lst = self.prog[ename]

                def body(e, lst=lst, ename=ename):
                    pending = []
                    for it in lst:
                        if it[0] == "wait":
                            key, val = it[1], it[2]
                            if key[0] == "e":
                                val = bisect.bisect_left(ranks[key[1]], val) + 1
                            pending.append((self.semobj[key], val))
                        elif it[0] == "op":
                            for (sm, vl) in pending[:-1]:
                                e.wait_ge(sm, vl)
                            it[1](e, it[2] in needed[ename], pending[-1] if pending else None)
                            pending = []
                        else:
                            for (sm, vl) in pending:
                                e.wait_ge(sm, vl)
                            pending = []
                            it[1](e)
                    for (sm, vl) in pending:
                        e.wait_ge(sm, vl)
                getattr(block, ename)(body)

    def mm(self, out, lhsT, rhs, start=True, stop=True):
        return self.op("tensor", "matmul", [lhsT, rhs], [out], out, lhsT, rhs, start=start, stop=stop)

    def transpose(self, out, in_, ident):
        return self.op("tensor", "transpose", [in_, ident], [out], out, in_, ident)

    def act(self, out, in_, func, bias=None, scale=None, accum_out=None):
        reads = [in_]
        kw = {}
        if bias is not None:
            kw["bias"] = bias
            if isinstance(bias, V):
                reads.append(bias)
        if scale is not None:
            kw["scale"] = scale
            if isinstance(scale, V):
                reads.append(scale)
        writes = [out]
        if accum_out is not None:
            kw["accum_out"] = accum_out
            writes.append(accum_out)
        return self.op("scalar", "activation", reads, writes, out, in_, func, **kw)

    def tt(self, out, in0, in1, op, eng="vector"):
        return self.op(eng, "tensor_tensor", [in0, in1], [out], out, in0, in1, op)

    def ts(self, out, in0, s1, op0, s2=None, op1=None, eng="vector"):
        reads = [in0] + [s for s in (s1, s2) if isinstance(s, V)]
        if op1 is None:
            return self.op(eng, "tensor_scalar", reads, [out], out, in0, s1, None, op0)
        return self.op(eng, "tensor_scalar", reads, [out], out, in0, s1, s2, op0, op1)

    def stt(self, out, in0, scalar, in1, op0, op1, eng="vector"):
        eng = "vector"
        reads = [in0, in1] + ([scalar] if isinstance(scalar, V) else [])
        return self.op(eng, "scalar_tensor_tensor", reads, [out], out, in0, scalar, in1, op0, op1)

    def copy(self, out, in_, eng="vector"):
        if eng == "scalar":
            return self.op("scalar", "copy", [in_], [out], out, in_)
        return self.op(eng, "tensor_copy", [in_], [out], out, in_)

    def memset(self, out, val, eng="vector"):
        return self.op(eng, "memset", [], [out], out, val)

    def recip(self, out, in_):
        return self.op("vector", "reciprocal", [in_], [out], out, in_)


D = 1024
KT = 8
N_IN = 3232
D_FF = 2816
FT = 22
GRID_W = 64
EPS = 1e-6
RW_OFF, DA_OFF, GLA_OFF, GQA_OFF = 0, 1152, 1920, 2720
DEPTH = 2


class Cfg:
    def __init__(self, TC=256, TL=2048, NB=2, depth=DEPTH, stop=None, mix=("rwkv", "da", "gla", "gqa")):
        self.TC, self.TL, self.NB, self.depth = TC, TL, NB, depth
        self.T = TC + TL
        self.stop = stop
        self.mix = mix

    def blocks(self):
        out = []
        s = 0
        while s < self.TC:
            n = min(512, self.TC - s)
            out.append((s, n, True))
            s += n
        while s < self.T:
            n = min(512, self.T - s)
            out.append((s, n, False))
            s += n
        return out


def pack_layout():
    cols = {}
    n = 0

    def add(name, k):
        nonlocal n
        cols[name] = (n, k)
        n += k
    add("nmg", 8)
    add("nfg", 8)
    add("mod_b", 48)
    add("rw_mu0", 9)
    add("rw_mu1", 9)
    add("rw_c0", 9)
    add("rw_omka", 2)
    add("rw_w0", 4)
    add("rw_a0", 4)
    add("rw_kk", 2)
    add("rw_ka", 2)
    add("rw_rk", 2)
    add("rw_ln_g", 2)
    add("rw_ln_b", 2)
    add("da_qg", 1)
    add("da_kg", 1)
    add("da_sub", 1)
    add("gla_ab", 2)
    add("gla_ng", 1)
    add("gq_qg", 1)
    add("gq_kg", 1)
    add("conv_w", 66)
    add("conv_b", 22)
    return cols, n


PACK, NPACK = pack_layout()


def host_pack(inp, l):
    P = np.zeros((128, NPACK), np.float32)

    def put(name, vec, k):
        c0, kk = PACK[name]
        assert kk == k
        P[:, c0:c0 + k] = np.asarray(vec, np.float32).reshape(k, 128).T
    put("nmg", inp["norm_mix_g"][l], 8)
    put("nfg", inp["norm_ffn_g"][l], 8)
    put("mod_b", inp["mod_b"][l], 48)
    put("rw_mu0", inp["rw_mu"][l, 0], 9)
    put("rw_mu1", inp["rw_mu"][l, 1], 9)
    put("rw_w0", inp["rw_w0"][l].reshape(-1), 4)
    put("rw_a0", inp["rw_a0"][l].reshape(-1), 4)
    put("rw_kk", inp["rw_kk"][l], 2)
    put("rw_ka", inp["rw_ka"][l], 2)
    put("rw_rk", inp["rw_rk"][l].reshape(-1), 2)
    put("rw_ln_g", inp["rw_ln_g"][l], 2)
    put("rw_ln_b", inp["rw_ln_b"][l], 2)
    put("da_qg", np.tile(inp["da_qk_g"][l, 0], 4), 1)
    put("da_kg", np.tile(inp["da_qk_g"][l, 1], 4), 1)
    put("da_sub", np.tile(inp["da_subln_g"][l], 2), 1)
    put("gla_ab", inp["gla_ab"][l].reshape(-1), 2)
    put("gla_ng", np.tile(inp["gla_norm_g"][l], 2), 1)
    put("gq_qg", np.tile(inp["gqa_qk_g"][l, 0], 2), 1)
    put("gq_kg", np.tile(inp["gqa_qk_g"][l, 1], 2), 1)
    put("conv_w", inp["ffn_conv_w"][l].reshape(-1), 66)
    put("conv_b", inp["ffn_conv_b"][l], 22)
    return P


def host_consts(cfg):
    c = {}
    c["ident"] = np.eye(128, dtype=np.float32)
    c["ones"] = np.ones((128, 128), np.float32)
    b64 = np.zeros((128, 128), np.float32)
    b64[:64, :64] = 1
    b64[64:, 64:] = 1
    c["blk64"] = b64
    b32 = np.zeros((128, 128), np.float32)
    for i in range(4):
        b32[32 * i:32 * i + 32, 32 * i:32 * i + 32] = 1
    c["blk32"] = b32
    TL = cfg.TL
    rows = TL // GRID_W
    row = np.repeat(np.arange(rows, dtype=np.float32), GRID_W)
    col = np.tile(np.arange(GRID_W, dtype=np.float32), rows)

    def tables(hd):
        nf = hd // 4
        inv = (10000.0 ** (-np.arange(nf, dtype=np.float32) / nf)).astype(np.float32)
        ang = np.concatenate([row[:, None] * inv, col[:, None] * inv], axis=-1)
        cos, sin = np.cos(ang).astype(np.float32), np.sin(ang).astype(np.float32)
        half = hd // 2
        cosf = np.concatenate([cos, cos], axis=-1)
        sinf = np.concatenate([sin, sin], axis=-1)
        rep = 128 // hd
        cT = np.tile(cosf, (1, rep)).T.copy()
        sT = np.tile(sinf, (1, rep)).T.copy()
        R = np.zeros((128, 128), np.float32)
        for m in range(128):
            if m % hd < half:
                R[m + half, m] = -1.0
            else:
                R[m - half, m] = 1.0
        return cT, sT, R
    hm = np.zeros((128, 4), np.float32)
    for p in range(128):
        hm[p, p // 32] = 1.0
    c["hmask4s"] = hm * np.float32(32 ** -0.5)
    jj, tt_ = np.meshgrid(np.arange(128), np.arange(128), indexing="ij")
    c["tri4_0"] = np.tile((jj <= tt_).astype(np.float32)[:, None, :], (1, 4, 1))
    c["tri4_1"] = np.tile((jj >= tt_).astype(np.float32)[:, None, :], (1, 4, 1))
    h2 = np.zeros((128, 4), np.float32)
    h2[:64, 0] = 1; h2[64:, 1] = 1; h2[:64, 2] = -1; h2[64:, 3] = -1
    c["hm2"] = h2
    c["I2"] = np.tile(np.eye(128, dtype=np.float32)[:, None, :], (1, 2, 1))
    c["maskN_0"] = np.tile((tt_ < jj).astype(np.float32)[:, None, :], (1, 4, 1))
    c["maskN_1"] = np.tile((tt_ > jj).astype(np.float32)[:, None, :], (1, 4, 1))
    sf, inf_ = (jj < tt_).astype(np.float32), (jj <= tt_).astype(np.float32)
    sr, inr = (jj > tt_).astype(np.float32), (jj >= tt_).astype(np.float32)
    c["maskAB_0"] = np.tile(np.concatenate([sf, inf_], 1)[:, None, :], (1, 2, 1))
    c["maskAB_1"] = np.tile(np.concatenate([sr, inr], 1)[:, None, :], (1, 2, 1))
    dm = np.zeros((128, 2), np.float32)
    for p in range(128):
        dm[p, (p % 64) // 32] = 1.0
    c["dmask"] = dm
    c["cos_gq"], c["sin_gq"], c["rot_gq"] = tables(64)
    c["cos_da"], c["sin_da"], c["rot_da"] = tables(32)
    return c


def build(cfg):
    nc = bass.Bass("TRN2", target_bir_lowering=False)
    fw = FW(nc)
    NB, T, TC, TL = cfg.NB, cfg.T, cfg.TC, cfg.TL
    NJ = NB + 1
    L = cfg.depth

    def din(name, shape, dt=F32):
        return fw.dram(name, shape, dt, kind="ExternalInput")

    xT_d = din("xT", [NB, D, T])
    cT_d = din("cT", [128, KT, NJ])
    pack_d = din("pack", [L, 128, NPACK])
    mod_w = din("mod_w", [L, D, 6 * D])
    w_in = din("w_in", [L, D, N_IN])
    w_out = din("w_out", [L, D, D])
    w_up = din("ffn_w_up", [L, D, 2 * D_FF])
    w_down = din("ffn_w_down", [L, D_FF, D])
    consts = {}
    for nm in ("ident", "ones", "blk64", "blk32", "rot_gq", "rot_da"):
        consts[nm] = din("c_" + nm, [128, 128])
    for nm in ("cos_gq", "sin_gq", "cos_da", "sin_da"):
        consts[nm] = din("c_" + nm, [128, TL])
    consts["dmask"] = din("c_dmask", [128, 2])
    lamb_d = din("lamb", [L, 128, 128])
    gla_a2_d = din("gla_a2", [L, 2, 16, 128])
    rw_w2_d = din("rw_w2", [L, 128, 256])
    rw_a2_d = din("rw_a2", [L, 128, 256])
    rw_g2_d = din("rw_g2", [L, 128, 256])
    consts["hm2"] = din("c_hm2", [128, 4])
    consts["I2"] = din("c_I2", [128, 2, 128])
    for d_ in range(2):
        consts[f"maskN_{d_}"] = din(f"c_maskN_{d_}", [128, 4, 128])
        consts[f"maskAB_{d_}"] = din(f"c_maskAB_{d_}", [128, 2, 256])
    consts["hmask4s"] = din("c_hmask4s", [128, 4])
    consts["tri4_0"] = din("c_tri4_0", [128, 4, 128])
    consts["tri4_1"] = din("c_tri4_1", [128, 4, 128])
    yT_d = fw.dram("yT", [NB, D, TL], F32, kind="ExternalOutput")
    dbg = {}

    PT = fw.dram("PT", [N_IN, T], F32)
    OT = fw.dram("OT", [D, T], BF16)
    ACT = fw.dram("ACTs", [D_FF, T], BF16)

    fw.push()
    ident = fw.sbuf("ident", [128, 128])
    ones = fw.sbuf("ones", [128, 128])
    blk64 = fw.sbuf("blk64", [128, 128])
    fw.dma(ident[:], consts["ident"][:])
    fw.dma(ones[:], consts["ones"][:])
    fw.dma(blk64[:], consts["blk64"][:])
    identb = fw.sbuf("identb", [128, 128], BF16)
    fw.copy(identb[:], ident[:])
    onesb_g = fw.sbuf("onesb_g", [128, 128], BF16)
    fw.copy(onesb_g[:], ones[:])
    blk64b = fw.sbuf("blk64b", [128, 128], BF16)
    fw.copy(blk64b[:], blk64[:])
    pk = [fw.sbuf(f"pk{l}", [128, NPACK]) for l in range(L)]
    for l in range(L):
        fw.dma(pk[l][:], pack_d[l])
    modT = [fw.sbuf(f"modT{l}", [128, 48, NJ]) for l in range(L)]
    gs = [fw.sbuf(f"gs{l}", [128, 2, KT, NJ]) for l in range(L)]
    epsb = fw.sbuf("epsb", [128, 1])
    fw.memset(epsb[:], EPS)

    def pcol(l, name, i=0):
        c0, k = PACK[name]
        return pk[l][:, c0 + i:c0 + i + 1]

    fw.push()
    cs = fw.sbuf("cs", [128, KT, NJ])
    fw.dma(cs[:], cT_d[:])
    fw.act(cs[:], cs[:], AF.Silu)
    mps = fw.psum("mps", [128, 48, NJ])
    wm = [fw.sbuf(f"wm{i}", [128, KT, 512]) for i in range(2)]
    for l in range(L):
        mwv = mod_w.t[l].rearrange("(k p) c -> p k c", p=128)
        for g in range(12):
            wb = wm[g % 2]
            fw.dma(wb[:], mod_w.v(mwv[:, :, g * 512:(g + 1) * 512]))
            for ci in range(4):
                ct = g * 4 + ci
                for k in range(KT):
                    fw.mm(mps[:, ct, :], wb[:, k, ci * 128:(ci + 1) * 128], cs[:, k, :],
                          start=(k == 0), stop=(k == KT - 1))
        c0 = PACK["mod_b"][0]
        for j in range(NJ):
            fw.tt(modT[l][:, :, j], mps[:, :, j], pk[l][:, c0:c0 + 48], ALU.add)
        for j in range(NJ):
            c0 = PACK["nmg"][0]
            fw.stt(gs[l][:, 0, :, j], modT[l][:, 8:16, j], 1.0, pk[l][:, c0:c0 + 8], ALU.add, ALU.mult)
            c0 = PACK["nfg"][0]
            fw.stt(gs[l][:, 1, :, j], modT[l][:, 32:40, j], 1.0, pk[l][:, c0:c0 + 8], ALU.add, ALU.mult)
    fw.pop()

    blocks = cfg.blocks()
    if cfg.stop == "mix":
        fw.push()
        zt = fw.sbuf("zt", [128, T], BF16)
        fw.memset(zt[:], 0.0)
        for k in range(KT):
            fw.dma(OT[k * 128:(k + 1) * 128, :], zt[:])
        fw.pop()

    def mixer_cols():
        tl = []
        for i in range(9):
            tl.append((RW_OFF + 128 * i, 128))
        for i in range(6):
            tl.append((DA_OFF + 128 * i, 128))
        for i in range(6):
            tl.append((GLA_OFF + 128 * i, 128))
        tl.append((GLA_OFF + 768, 32))
        for i in range(4):
            tl.append((GQA_OFF + 128 * i, 128))
        return tl

    def norm_phase(xT, hT, l, which, b, sq, rstd, nps):
        shift_base = 0 if which == 0 else 24
        sqb = [fw.sbuf(f"nsqb{i}", [128, 512], BF16) for i in range(3)]
        tmpf = [fw.sbuf(f"ntmp{i}", [128, 512]) for i in range(3)]
        nps2 = fw.psum("nps_b", [128, 512])
        ci = 0
        for bi, (s, n, is_ctx) in enumerate(blocks):
            j = NB if is_ctx else b
            ps = nps if bi % 2 == 0 else nps2
            for k in range(KT):
                q = sqb[ci % 3]
                ci += 1
                fw.act(q[:, :n], xT[:, k, s:s + n], AF.Square)
                fw.mm(ps[:, :n], onesb_g[:], q[:, :n], start=(k == 0), stop=(k == KT - 1))
            rs = sq if bi % 2 == 0 else rstd
            fw.act(rs[:, :n], ps[:, :n], AF.Sqrt, bias=epsb[:, 0:1], scale=1.0 / D)
            fw.recip(rs[:, :n], rs[:, :n])
            for k in range(KT):
                t_ = tmpf[ci % 3]
                ci += 1
                fw.tt(t_[:, :n], xT[:, k, s:s + n], rs[:, :n], ALU.mult)
                fw.act(hT[:, k, s:s + n], t_[:, :n], AF.Identity,
                       bias=modT[l][:, shift_base + k, j:j + 1], scale=gs[l][:, which, k, j:j + 1])

    def wout_phase(xT, l, b, need_ctx):
        fw.push()
        oT = fw.sbuf("oT", [128, KT, T], BF16)
        ov = OT.t.rearrange("(k p) t -> p k t", p=128)
        for k in range(KT):
            fw.dma(oT[:, k, :], OT.v(ov[:, k, :]))
        wt = [fw.sbuf(f"wo{i}", [128, KT, 256], BF16) for i in range(2)]
        pps = [fw.psum(f"ops{i}", [128, 512]) for i in range(4)]
        wv = w_out.t[l].rearrange("(k p) c -> p k c", p=128)
        ei = 0
        for jt in range(KT):
            wb_full = wt[(jt // 2) % 2]
            if jt % 2 == 0:
                fw.dma(wb_full[:], w_out.v(wv[:, :, jt * 128:(jt + 2) * 128]), eng="gpsimd")
            wb = Buf(wb_full.t[:, :, (jt % 2) * 128:(jt % 2 + 1) * 128], "wo_half")
            wb.trk = wb_full.trk
            for (s, n, is_ctx) in blocks:
                if is_ctx and not need_ctx:
                    continue
                j = NB if is_ctx else b
                ps = pps[ei % 4]
                ei += 1
                for k in range(KT):
                    fw.mm(ps[:, :n], wb[:, k, :], oT[:, k, s:s + n], start=(k == 0), stop=(k == KT - 1))
                fw.stt(xT[:, jt, s:s + n], ps[:, :n], modT[l][:, 16 + jt, j:j + 1], xT[:, jt, s:s + n],
                       ALU.mult, ALU.add)
        fw.pop()

    def ffn_phase(xT, l, b, need_ctx):
        segs = ([(0, TC)] if need_ctx else []) + [(TC, T)]
        fblocks = [bl for bl in blocks if (need_ctx or not bl[2])]
        fw.push()
        hT = fw.sbuf("hT2", [128, KT, T], BF16)
        sq = fw.sbuf("sq2", [128, 512])
        rstd = fw.sbuf("rstd2", [128, 512])
        nps = fw.psum("nps2", [128, 512])
        norm_phase(xT, hT, l, 1, b, sq, rstd, nps)
        wu = [fw.sbuf(f"wu{i}", [128, KT, 256], BF16) for i in range(2)]
        wg = [fw.sbuf(f"wg{i}", [128, KT, 256], BF16) for i in range(2)]
        ups = [fw.psum(f"ups{i}", [128, 512]) for i in range(2)]
        gps = [fw.psum(f"gps{i}", [128, 512]) for i in range(2)]
        uT = [fw.sbuf(f"uT{i}", [128, T]) for i in range(2)]
        gT = [fw.sbuf(f"gT{i}", [128, T]) for i in range(2)]
        tmp = [fw.sbuf(f"ftmp{i}", [128, T]) for i in range(2)]
        aT = [fw.sbuf(f"aT{i}", [128, T], BF16) for i in range(2)]
        wv = w_up.t[l].rearrange("(k p) c -> p k c", p=128)
        cw0 = PACK["conv_w"][0]
        cb0 = PACK["conv_b"][0]
        ei = 0
        def wload(p):
            fw.dma(wu[p % 2][:], w_up.v(wv[:, :, p * 256:(p + 1) * 256]), eng="gpsimd")
            fw.dma(wg[p % 2][:], w_up.v(wv[:, :, D_FF + p * 256:D_FF + (p + 1) * 256]), eng="gpsimd")
        wload(0)
        for i in range(FT):
            r = i % 2
            if i % 2 == 1 and i // 2 + 1 < FT // 2:
                wload(i // 2 + 1)
            for (s, n, is_ctx) in fblocks:
                pu, pg = ups[ei % 2], gps[ei % 2]
                ei += 1
                for k in range(KT):
                    fw.mm(pu[:, :n], wu[(i // 2) % 2][:, k, (i % 2) * 128:(i % 2 + 1) * 128], hT[:, k, s:s + n],
                          start=(k == 0), stop=(k == KT - 1))
                for k in range(KT):
                    fw.mm(pg[:, :n], wg[(i // 2) % 2][:, k, (i % 2) * 128:(i % 2 + 1) * 128], hT[:, k, s:s + n],
                          start=(k == 0), stop=(k == KT - 1))
                fw.copy(uT[r][:, s:s + n], pu[:, :n], eng="scalar")
                fw.copy(gT[r][:, s:s + n], pg[:, :n], eng="scalar")
                fw.act(tmp[r][:, s:s + n], pg[:, :n], AF.Identity, bias=pk[l][:, cb0 + i:cb0 + i + 1],
                       scale=pk[l][:, cw0 + FT + i:cw0 + FT + i + 1])
            w0 = pk[l][:, cw0 + i:cw0 + i + 1]
            w1 = pk[l][:, cw0 + FT + i:cw0 + FT + i + 1]
            w2 = pk[l][:, cw0 + 2 * FT + i:cw0 + 2 * FT + i + 1]
            cb = pk[l][:, cb0 + i:cb0 + i + 1]
            for (s, e) in segs:
                fw.stt(tmp[r][:, s + 1:e], gT[r][:, s:e - 1], w0, tmp[r][:, s + 1:e], ALU.mult, ALU.add)
                fw.stt(tmp[r][:, s:e - 1], gT[r][:, s + 1:e], w2, tmp[r][:, s:e - 1], ALU.mult, ALU.add)
                fw.act(tmp[r][:, s:e], tmp[r][:, s:e], AF.Silu)
                fw.tt(aT[r][:, s:e], tmp[r][:, s:e], uT[r][:, s:e], ALU.mult)
                fw.dma(ACT[i * 128:(i + 1) * 128, s:e], aT[r][:, s:e])
        fw.pop()
        fw.push()
        wd = [fw.sbuf(f"wd{jt}", [128, FT, 128], BF16) for jt in range(KT)]
        wdv = w_down.t[l].rearrange("(f p) c -> p f c", p=128)
        for jt in range(KT):
            fw.dma(wd[jt][:], w_down.v(wdv[:, :, jt * 128:(jt + 1) * 128]), eng="gpsimd")
        ab = [fw.sbuf(f"ab{i}", [128, FT, 512], BF16) for i in range(2)]
        dps = [fw.psum(f"dps{i}", [128, 512]) for i in range(4)]
        av = ACT.t.rearrange("(f p) t -> p f t", p=128)
        ei = 0
        for bi, (s, n, is_ctx) in enumerate(fblocks):
            j = NB if is_ctx else b
            a = ab[bi % 2]
            fw.dma(a[:, :, :n], ACT.v(av[:, :, s:s + n]))
            for jt in range(KT):
                ps = dps[ei % 4]
                ei += 1
                for f in range(FT):
                    fw.mm(ps[:, :n], wd[jt][:, f, :], a[:, f, :n], start=(f == 0), stop=(f == FT - 1))
                fw.stt(xT[:, jt, s:s + n], ps[:, :n], modT[l][:, 40 + jt, j:j + 1], xT[:, jt, s:s + n],
                       ALU.mult, ALU.add)
        fw.pop()

    NT = T // 128
    NTC = TC // 128
    lat_blocks = [bl for bl in blocks if not bl[2]]
    ctx_blocks = [bl for bl in blocks if bl[2]]

    def head_norm_rope(raw, outs, blkb, hd, g_ap, rotb, cosT, sinT, s, n, is_ctx, tset, masks=None):
        sqb, rs, qg, t1, t2, nps, npr = tset
        fw.act(sqb[:, :n], raw, AF.Square)
        fw.mm(nps[:, :n], blkb[:], sqb[:, :n])
        fw.act(rs[:, :n], nps[:, :n], AF.Sqrt, bias=epsb[:, 0:1], scale=1.0 / hd)
        fw.recip(rs[:, :n], rs[:, :n])
        fw.stt(qg[:, :n], raw, g_ap, rs[:, :n], ALU.mult, ALU.mult)
        if is_ctx:
            res = qg
        else:
            fw.mm(npr[:, :n], rotb[:], qg[:, :n])
            fw.tt(t1[:, :n], qg[:, :n], cosT[:, s - TC:s - TC + n], ALU.mult)
            fw.tt(t2[:, :n], npr[:, :n], sinT[:, s - TC:s - TC + n], ALU.mult)
            fw.tt(t1[:, :n], t1[:, :n], t2[:, :n], ALU.add, eng="gpsimd")
            res = t1
        if masks is None:
            fw.copy(outs[0], res[:, :n], eng="gpsimd")
        else:
            for m, o in enumerate(outs):
                fw.ts(o, res[:, :n], masks[:, m:m + 1], ALU.mult, eng="gpsimd")

    def prep_sets(tag):
        sets = []
        for i in range(3):
            sets.append((fw.sbuf(f"{tag}sqb{i}", [128, 512], BF16), fw.sbuf(f"{tag}rs{i}", [128, 512]),
                         fw.sbuf(f"{tag}qg{i}", [128, 512], BF16), fw.sbuf(f"{tag}t1{i}", [128, 512]),
                         fw.sbuf(f"{tag}t2{i}", [128, 512]), fw.psum(f"{tag}nps{i}", [128, 512]),
                         fw.psum(f"{tag}npr{i}", [128, 512])))
        return sets

    def make_vdup(vrow0, nheads, Vd, vtmp, tps):
        ntile = (nheads * 64) // 128
        for vt in range(ntile):
            fw.dma(vtmp[:, :], PT[vrow0 + vt * 128:vrow0 + (vt + 1) * 128, :])
            for i in range(NT):
                fw.transpose(tps[:, i % 4, :], vtmp[:, i * 128:(i + 1) * 128], ident[:])
                for hh in range(2):
                    h = vt * 2 + hh
                    fw.copy(Vd[h][:, i, 0:64], tps[:, i % 4, hh * 64:(hh + 1) * 64], eng="vector")
                    fw.copy(Vd[h][:, i, 64:128], tps[:, i % 4, hh * 64:(hh + 1) * 64], eng="scalar")

    def attn_head(qviews, kviews, Vd_h, nmaps, scale, qb, sps_l, pT_l, oacc, dacc, dsum, cnt):
        (s, n, is_ctx) = qb
        kts = list(range(NTC)) if is_ctx else list(range(NT))
        steps = [(ki, kt, m) for ki, kt in enumerate(kts) for m in range(nmaps)]
        c0 = cnt[0]
        cnt[0] += len(steps)

        def score(i):
            ki, kt, m = steps[i]
            sp = sps_l[(c0 + i) % len(sps_l)]
            fw.mm(sp[:, :n], kviews[m](kt), qviews[m](s, n))
        depth = len(sps_l) - 1
        for i in range(min(depth, len(steps))):
            score(i)
        for i, (ki, kt, m) in enumerate(steps):
            if i + depth < len(steps):
                score(i + depth)
            sp = sps_l[(c0 + i) % len(sps_l)]
            pT = pT_l[(c0 + i) % len(pT_l)]
            fw.act(pT[:, :n], sp[:, :n], AF.Exp, scale=scale)
            fw.mm(oacc[m][:, :n], Vd_h[:, kt, :], pT[:, :n], start=(ki == 0), stop=(ki == len(kts) - 1))
            ds = dsum[m]
            de = "vector" if m == 0 else "gpsimd"
            if ki == 0:
                fw.copy(ds[:, :n], pT[:, :n], eng=de)
            else:
                fw.tt(ds[:, :n], pT[:, :n], ds[:, :n], ALU.add, eng=de)
        for m in range(nmaps):
            fw.mm(dacc[m][:, :n], ones[:], dsum[m][:, :n])

    def gqa_phase(l, b, need_ctx):
        fw.push()
        onesb = onesb_g
        qn = fw.sbuf("qn", [128, 2, T], BF16)
        kd = fw.sbuf("kd", [128, 2, T], BF16)
        Vd = [fw.sbuf(f"Vd{h}", [128, NT, 128], BF16) for h in range(2)]
        qblocks = (ctx_blocks if need_ctx else []) + lat_blocks
        fw.push()
        cosT = fw.sbuf("cosT", [128, TL]); sinT = fw.sbuf("sinT", [128, TL]); rot = fw.sbuf("rot", [128, 128])
        fw.dma(cosT[:], consts["cos_gq"][:]); fw.dma(sinT[:], consts["sin_gq"][:]); fw.dma(rot[:], consts["rot_gq"][:])
        rotb = fw.sbuf("rotb", [128, 128], BF16)
        fw.copy(rotb[:], rot[:])
        raw = [fw.sbuf(f"raw{i}", [128, 512]) for i in range(3)]
        tsets = prep_sets("g")
        tps = fw.psum("tps", [128, 4, 128])
        vtmp = fw.sbuf("vtmp", [128, T])
        ri = 0
        for t in range(2):
            for (s, n, is_ctx) in qblocks:
                r = raw[ri % 3]; ri += 1
                fw.dma(r[:, :n], PT[GQA_OFF + t * 128:GQA_OFF + (t + 1) * 128, s:s + n])
                head_norm_rope(r[:, :n], [qn[:, t, s:s + n]], blk64b, 64, pcol(l, "gq_qg"), rotb, cosT, sinT,
                               s, n, is_ctx, tsets[ri % 3])
            for (s, n, is_ctx) in blocks:
                r = raw[ri % 3]; ri += 1
                for hh in range(2):
                    fw.dma(r[hh * 64:(hh + 1) * 64, :n], PT[GQA_OFF + 256 + t * 64:GQA_OFF + 256 + (t + 1) * 64, s:s + n])
                head_norm_rope(r[:, :n], [kd[:, t, s:s + n]], blk64b, 64, pcol(l, "gq_kg"), rotb, cosT, sinT,
                               s, n, is_ctx, tsets[ri % 3])
        make_vdup(GQA_OFF + 384, 2, Vd, vtmp, tps)
        fw.pop()
        sps_l = [fw.psum(f"sps{i}", [128, 512]) for i in range(4)]
        pT_l = [fw.sbuf(f"pT{i}", [128, 512], BF16) for i in range(4)]
        oacc2 = [fw.psum(f"oacc{i}", [128, 512]) for i in range(2)]
        dacc2 = [fw.psum(f"dacc{i}", [128, 512]) for i in range(2)]
        dsum_l = [fw.sbuf(f"dsum{i}", [128, 512]) for i in range(2)]
        rec2 = [fw.sbuf(f"rec{i}", [128, 512]) for i in range(2)]
        ob = [fw.sbuf(f"ob{i}", [128, 512], BF16) for i in range(2)]
        cnt = [0]
        oi = 0
        for h in range(4):
            t, g = h // 2, h % 2
            ph = 64 * g
            qv = [lambda s, n, t=t, ph=ph: qn[ph:ph + 64, t, s:s + n]]
            kv = [lambda kt, t=t, ph=ph: kd[ph:ph + 64, t, kt * 128:(kt + 1) * 128]]
            for qb in qblocks:
                (s, n, is_ctx) = qb
                oacc, dacc, rec = [oacc2[oi % 2]], [dacc2[oi % 2]], rec2[oi % 2]
                attn_head(qv, kv, Vd[t], 1, 0.125, qb, sps_l, pT_l, oacc, dacc, [dsum_l[oi % 2]], cnt)
                fw.recip(rec[ph:ph + 64, :n], dacc[0][ph:ph + 64, :n])
                o = ob[oi % 2]; oi += 1
                fw.tt(o[ph:ph + 64, :n], oacc[0][ph:ph + 64, :n], rec[ph:ph + 64, :n], ALU.mult)
                fw.dma(OT[768 + h * 64:768 + (h + 1) * 64, s:s + n], o[ph:ph + 64, :n])
        fw.pop()

    def da_phase(l, b, need_ctx):
        lam_init = 0.8 - 0.6 * float(np.exp(-0.3 * l))
        fw.push()
        onesb = onesb_g
        lamb = fw.sbuf("lamb_s", [128, 128])
        fw.dma(lamb[:], lamb_d[l])
        lt = fw.sbuf("lt", [128, 64])
        lsum = fw.sbuf("lsum", [128, 2])
        nlam = fw.sbuf("nlam", [128, 1])
        sg = fw.sbuf("sg", [128, 1])
        fw.tt(lt[:, 0:32], lamb[:, 0:32], lamb[:, 32:64], ALU.mult)
        fw.tt(lt[:, 32:64], lamb[:, 64:96], lamb[:, 96:128], ALU.mult)
        fw.op("vector", "reduce_sum", [lt[:]], [lsum[:]], lsum[:, 0:1].ap, lt[:, 0:32].ap, AX.X)
        fw.op("vector", "reduce_sum", [lt[:]], [lsum[:]], lsum[:, 1:2].ap, lt[:, 32:64].ap, AX.X)
        fw.act(lsum[:], lsum[:], AF.Exp)
        fw.stt(nlam[:], lsum[:, 1:2], -lam_init, lsum[:, 0:1], ALU.add, ALU.subtract)
        fw.ts(sg[:], pcol(l, "da_sub"), 1.0 - lam_init, ALU.mult)
        qm = [fw.sbuf(f"qm{m}", [128, 2, T], BF16) for m in range(2)]
        kn = fw.sbuf("kn", [128, 2, T], BF16)
        Vd = [fw.sbuf(f"Vd{h}", [128, NT, 128], BF16) for h in range(4)]
        qblocks = (ctx_blocks if need_ctx else []) + lat_blocks
        fw.push()
        cosT = fw.sbuf("cosT", [128, TL]); sinT = fw.sbuf("sinT", [128, TL]); rot = fw.sbuf("rot", [128, 128])
        fw.dma(cosT[:], consts["cos_da"][:]); fw.dma(sinT[:], consts["sin_da"][:]); fw.dma(rot[:], consts["rot_da"][:])
        rotb = fw.sbuf("rotb", [128, 128], BF16)
        fw.copy(rotb[:], rot[:])
        blk32 = fw.sbuf("blk32", [128, 128])
        fw.dma(blk32[:], consts["blk32"][:])
        blk32b = fw.sbuf("blk32b", [128, 128], BF16)
        fw.copy(blk32b[:], blk32[:])
        dmask = fw.sbuf("dmask", [128, 2])
        fw.dma(dmask[:], consts["dmask"][:])
        raw = [fw.sbuf(f"raw{i}", [128, 512]) for i in range(3)]
        tsets = prep_sets("d")
        tps = fw.psum("tps", [128, 4, 128])
        vtmp = fw.sbuf("vtmp", [128, T])
        ri = 0
        for t in range(2):
            for (s, n, is_ctx) in qblocks:
                r = raw[ri % 3]; ri += 1
                fw.dma(r[:, :n], PT[DA_OFF + t * 128:DA_OFF + (t + 1) * 128, s:s + n])
                head_norm_rope(r[:, :n], [qm[0][:, t, s:s + n], qm[1][:, t, s:s + n]], blk32b, 32,
                               pcol(l, "da_qg"), rotb, cosT, sinT, s, n, is_ctx, tsets[ri % 3], masks=dmask)
            for (s, n, is_ctx) in blocks:
                r = raw[ri % 3]; ri += 1
                fw.dma(r[:, :n], PT[DA_OFF + 256 + t * 128:DA_OFF + 256 + (t + 1) * 128, s:s + n])
                head_norm_rope(r[:, :n], [kn[:, t, s:s + n]], blk32b, 32, pcol(l, "da_kg"), rotb, cosT, sinT,
                               s, n, is_ctx, tsets[ri % 3])
        make_vdup(DA_OFF + 512, 4, Vd, vtmp, tps)
        fw.pop()
        nps = fw.psum("anps", [128, 512])
        tmps = [fw.sbuf(f"nt{i}", [128, 512]) for i in range(2)]
        sps_l = [fw.psum(f"sps{i}", [128, 512]) for i in range(3)]
        pT_l = [fw.sbuf(f"pT{i}", [128, 512], BF16) for i in range(4)]
        oacc = [fw.psum(f"oacc{m}", [128, 512]) for m in range(2)]
        dacc = [fw.psum(f"dacc{m}", [128, 512]) for m in range(2)]
        dsum_l = [[fw.sbuf(f"dsum{i}_{m}", [128, 512]) for m in range(2)] for i in range(2)]
        hq = [0]
        oev = [fw.sbuf(f"oev{m}", [128, 512]) for m in range(2)]
        dev = [fw.sbuf(f"dev{m}", [128, 512]) for m in range(2)]
        rec = [fw.sbuf(f"rec{m}", [128, 512]) for m in range(2)]
        o1 = fw.sbuf("o1", [128, 512])
        osb = fw.sbuf("osb", [128, 512])
        ob = [fw.sbuf(f"ob{i}", [128, 512], BF16) for i in range(2)]
        sq, rs = tmps[0], tmps[1]
        cnt = [0]
        oi = 0
        for t in range(2):
            for qb in qblocks:
                (s, n, is_ctx) = qb
                for g in range(2):
                    h = 2 * t + g
                    ph = 64 * g
                    qv = [lambda s, n, t=t, ph=ph, m=m: qm[m][ph:ph + 64, t, s:s + n] for m in range(2)]
                    kv = [lambda kt, t=t, ph=ph: kn[ph:ph + 64, t, kt * 128:(kt + 1) * 128]] * 2
                    attn_head(qv, kv, Vd[h], 2, 32 ** -0.5, qb, sps_l, pT_l, oacc, dacc, dsum_l[hq[0] % 2], cnt)
                    hq[0] += 1
                    for m in range(2):
                        fw.copy(oev[m][ph:ph + 64, :n], oacc[m][ph:ph + 64, :n], eng="scalar")
                        fw.copy(dev[m][ph:ph + 64, :n], dacc[m][ph:ph + 64, :n], eng="scalar")
                    for m in range(2):
                        fw.recip(rec[m][ph:ph + 64, :n], dev[m][ph:ph + 64, :n])
                    fw.tt(osb[ph:ph + 64, :n], oev[0][ph:ph + 64, :n], rec[0][ph:ph + 64, :n], ALU.mult)
                    fw.tt(o1[ph:ph + 64, :n], oev[1][ph:ph + 64, :n], rec[1][ph:ph + 64, :n], ALU.mult)
                    fw.stt(osb[ph:ph + 64, :n], o1[ph:ph + 64, :n], nlam[ph:ph + 64, 0:1], osb[ph:ph + 64, :n],
                           ALU.mult, ALU.add)
                fw.act(sq[:, :n], osb[:, :n], AF.Square)
                fw.mm(nps[:, :n], blk64[:], sq[:, :n])
                fw.act(rs[:, :n], nps[:, :n], AF.Sqrt, bias=epsb[:, 0:1], scale=1.0 / 64)
                fw.recip(rs[:, :n], rs[:, :n])
                o = ob[oi % 2]; oi += 1
                fw.stt(o[:, :n], osb[:, :n], sg[:, 0:1], rs[:, :n], ALU.mult, ALU.mult)
                fw.dma(OT[256 + t * 128:256 + (t + 1) * 128, s:s + n], o[:, :n])
        fw.pop()

    def chunk_order(rev):
        if not rev:
            return list(range(NT))
        return list(range(NTC - 1, -1, -1)) + list(range(NT - 1, NTC - 1, -1))

    def cumsum_chunks(A, B, rev):
        cur, oth = A, B
        s = 1
        while s < 128:
            cv = cur.t[:, :].rearrange("p (c i) -> p c i", i=128)
            ov = oth.t[:, :].rearrange("p (c i) -> p c i", i=128)
            if not rev:
                fw.tt(oth.v(ov[:, :, s:]), cur.v(cv[:, :, s:]), cur.v(cv[:, :, :128 - s]), ALU.add)
                fw.copy(oth.v(ov[:, :, :s]), cur.v(cv[:, :, :s]), eng="gpsimd")
            else:
                fw.tt(oth.v(ov[:, :, :128 - s]), cur.v(cv[:, :, :128 - s]), cur.v(cv[:, :, s:]), ALU.add)
                fw.copy(oth.v(ov[:, :, 128 - s:]), cur.v(cv[:, :, 128 - s:]), eng="gpsimd")
            cur, oth = oth, cur
            s *= 2
        return cur, oth

    def gla_phase(l, b, need_ctx):
        fw.push()
        Fb = [fw.sbuf(f"F{i}", [128, T]) for i in range(4)]
        F1, F2, F3, F4 = Fb
        Vdup = fw.sbuf("Vdup", [128, NT, 4, 128], BF16)
        QM = [fw.sbuf(f"QM{h}", [128, T], BF16) for h in range(4)]
        KTt = fw.sbuf("KTt", [128, T], BF16)
        KH = fw.sbuf("KH", [128, T], BF16)
        oaccT = fw.sbuf("oaccT", [128, 2, T])
        Pend = fw.sbuf("Pend", [128, NT])
        gfb = fw.sbuf("gfb", [48, T])
        a2 = fw.sbuf("a2", [48, 128])
        nab = fw.sbuf("nab", [128, 2])
        hm4 = fw.sbuf("hm4", [128, 4])
        tri = fw.sbuf("tri", [128, 4, 128])
        fw.dma(hm4[:], consts["hmask4s"][:])
        for d in range(2):
            fw.dma(a2[32 * d:32 * d + 16, :], gla_a2_d[l, d])
            fw.dma(gfb[32 * d:32 * d + 16, :], PT[GLA_OFF + 512 + 16 * d:GLA_OFF + 528 + 16 * d, :])
        c0 = PACK["gla_ab"][0]
        fw.ts(nab[:], pk[l][:, c0:c0 + 2], -1.0, ALU.mult)
        S = fw.sbuf("Sst", [128, 64])
        Sdup = fw.sbuf("Sdup", [128, 128], BF16)
        KHt = [fw.sbuf(f"KHt{i}", [128, 640], BF16) for i in range(2)]
        for i in range(2):
            fw.memset(KHt[i][:], 0.0)
        AT = [fw.sbuf(f"AT{i}", [128, 4, 128], BF16) for i in range(2)]
        tpsb = fw.psum("tpsb", [128, 4, 128], BF16)
        tps = fw.psum("tps", [128, 4, 128])
        aps = [fw.psum(f"aps{i}", [128, 4, 128]) for i in range(2)]
        ops = [fw.psum(f"ops{i}", [128, 4, 128]) for i in range(2)]
        sps = fw.psum("sps", [128, 64])
        zps = fw.psum("zps", [128, 512])
        for vt in range(2):
            fw.dma(F1[:], PT[GLA_OFF + 256 + vt * 128:GLA_OFF + 256 + (vt + 1) * 128, :])
            for i in range(NT):
                fw.transpose(tps[:, i % 4, :], F1[:, i * 128:(i + 1) * 128], ident[:])
                for hh in range(2):
                    h = vt * 2 + hh
                    fw.copy(Vdup[:, i, h, 0:64], tps[:, i % 4, hh * 64:(hh + 1) * 64], eng="vector")
                    fw.copy(Vdup[:, i, h, 64:128], tps[:, i % 4, hh * 64:(hh + 1) * 64], eng="scalar")
        fw.dma(F1[:], PT[GLA_OFF:GLA_OFF + 128, :])
        fw.dma(F2[:], PT[GLA_OFF + 128:GLA_OFF + 256, :])
        for d in range(2):
            rev = d == 1
            fw.dma(tri[:], consts[f"tri4_{d}"][:])
            for (s, n, is_ctx) in blocks:
                fw.mm(zps[:, :n], a2[32 * d:32 * d + 16, :], gfb[32 * d:32 * d + 16, s:s + n])
                fw.act(F3[:, s:s + n], zps[:, :n], AF.Exp, bias=nab[:, d:d + 1], scale=-1.0)
            fw.act(F3[:], F3[:], AF.Ln, bias=1.0)
            fw.ts(F3[:], F3[:], -1.0 / 16.0, ALU.mult)
            bb, ff = cumsum_chunks(F3, F4, rev)
            bv = bb.t[:, :].rearrange("p (c i) -> p c i", i=128)
            eidx = 0 if rev else 127
            fw.act(Pend[:], bb.v(bv[:, :, eidx]), AF.Exp)
            fw.act(ff[:], bb[:], AF.Exp)
            for h in range(4):
                fw.stt(QM[h][:], F1[:], hm4[:, h:h + 1], ff[:], ALU.mult, ALU.mult,
                       eng=("gpsimd" if h % 2 else "vector"))
            fw.act(ff[:], bb[:], AF.Exp, scale=-1.0)
            fw.tt(KTt[:], F2[:], ff[:], ALU.mult)
            for c in range(NT):
                fw.act(ff[:, c * 128:(c + 1) * 128], bb[:, c * 128:(c + 1) * 128], AF.Exp,
                       bias=bb[:, c * 128 + eidx:c * 128 + eidx + 1], scale=-1.0)
            fw.tt(KH[:], F2[:], ff[:], ALU.mult, eng="gpsimd")
            first = True
            for ci, c in enumerate(chunk_order(rev)):
                cs = slice(c * 128, (c + 1) * 128)
                is_ctx = c < NTC
                kht = KHt[ci % 2]
                at = AT[ci % 2]
                ap_, op_ = aps[ci % 2], ops[ci % 2]
                fw.transpose(tpsb[:, 0, :], KH[:, cs], identb[:])
                kv = kht.t[:, :].rearrange("p (h x) -> p h x", x=160)
                fw.copy(kht.v(kv[:, :, 0:32]), tpsb.v(tpsb.t[:, 0, :].rearrange("p (h x) -> p h x", x=32)))
                want_out = need_ctx or not is_ctx
                if want_out:
                    for h in range(4):
                        fw.mm(ap_[:, h, :], KTt[:, cs], QM[h][:, cs])
                    fw.tt(at[:], ap_[:], tri[:], ALU.mult)
                    for h in range(4):
                        if not first:
                            fw.mm(op_[:, h, :], Sdup[:], QM[h][:, cs], start=True, stop=False)
                        fw.mm(op_[:, h, :], Vdup[:, c, h, :], at[:, h, :], start=first, stop=True)
                    o4 = op_.t[:, :, :].rearrange("p (a g) t -> p a g t", g=2)
                    for g in range(2):
                        dst = oaccT[64 * g:64 * g + 64, :, cs]
                        src = op_.v(o4[64 * g:64 * g + 64, :, g, :])
                        if d == 0:
                            fw.copy(dst, src, eng=("vector" if g == 0 else "scalar"))
                        else:
                            fw.tt(dst, src, dst, ALU.add, eng="vector")
                for h in range(4):
                    fw.mm(sps[:, :], kht[:, h * 128:(h + 1) * 128], Vdup[:, c, h, 0:64], start=(h == 0), stop=(h == 3))
                if first:
                    fw.copy(S[:], sps[:])
                else:
                    fw.stt(S[:], S[:], Pend[:, c:c + 1], sps[:], ALU.mult, ALU.add)
                fw.copy(Sdup[:, 0:64], S[:], eng="scalar")
                fw.copy(Sdup[:, 64:128], S[:], eng="gpsimd")
                first = False
        sq, rs, rr = F3, F4, F1
        ob = [fw.sbuf(f"gob{i}", [128, 512], BF16) for i in range(2)]
        oi = 0
        qblocks = (ctx_blocks if need_ctx else []) + lat_blocks
        for t in range(2):
            for (s, n, is_ctx) in qblocks:
                fw.dma(rr[:, :n], PT[GLA_OFF + 544 + t * 128:GLA_OFF + 544 + (t + 1) * 128, s:s + n])
                fw.act(rr[:, :n], rr[:, :n], AF.Silu)
                fw.act(sq[:, :n], oaccT[:, t, s:s + n], AF.Square)
                fw.mm(zps[:, :n], blk64[:], sq[:, :n])
                fw.act(rs[:, :n], zps[:, :n], AF.Sqrt, bias=epsb[:, 0:1], scale=1.0 / 64)
                fw.recip(rs[:, :n], rs[:, :n])
                fw.stt(sq[:, :n], oaccT[:, t, s:s + n], pcol(l, "gla_ng"), rs[:, :n], ALU.mult, ALU.mult)
                o = ob[oi % 2]; oi += 1
                fw.tt(o[:, :n], sq[:, :n], rr[:, :n], ALU.mult, eng="gpsimd")
                fw.dma(OT[512 + t * 128:512 + (t + 1) * 128, s:s + n], o[:, :n])
        fw.pop()

    RWS = fw.dram("RWS", [2, 2, 8, 128, T], BF16)
    VDs = fw.dram("VDs", [128, NT, 4, 128], BF16)
    BON = fw.dram("BON", [256, T], F32)
    PENDs = fw.dram("PENDs", [128, 2, 2, NT], F32)
    GATE = fw.dram("GATE", [256, T], F32)

    def shift_mix(dst, raw, l, ct):
        m0 = pcol(l, "rw_mu0", ct)
        m1 = pcol(l, "rw_mu1", ct)
        c0 = pcol(l, "rw_c0", ct)
        fw.ts(dst[:, :], raw[:, :], c0, ALU.mult)
        for (s, e) in ((0, TC), (TC, T)):
            fw.stt(dst[:, s + 1:e], raw[:, s:e - 1], m0, dst[:, s + 1:e], ALU.mult, ALU.add)
            fw.stt(dst[:, s:e - 1], raw[:, s + 1:e], m1, dst[:, s:e - 1], ALU.mult, ALU.add, eng="gpsimd")

    def rwkv_phase(l, b, need_ctx):
        fw.push()
        lora = [fw.sbuf(f"lora{i}", [128, T], BF16) for i in range(3)]
        w2s = fw.sbuf("w2s", [128, 256], BF16)
        a2s = fw.sbuf("a2s", [128, 256], BF16)
        g2s = fw.sbuf("g2s", [128, 256], BF16)
        fw.dma(w2s[:], rw_w2_d[l], eng="gpsimd")
        fw.dma(a2s[:], rw_a2_d[l], eng="gpsimd")
        fw.dma(g2s[:], rw_g2_d[l], eng="gpsimd")
        hm2 = fw.sbuf("hm2", [128, 4])
        fw.dma(hm2[:], consts["hm2"][:])
        c0 = PACK["rw_mu0"][0]
        c1 = PACK["rw_mu1"][0]
        cc = PACK["rw_c0"][0]
        fw.tt(pk[l][:, cc:cc + 9], pk[l][:, c0:c0 + 9], pk[l][:, c1:c1 + 9], ALU.add)
        fw.ts(pk[l][:, cc:cc + 9], pk[l][:, cc:cc + 9], -1.0, ALU.mult, 1.0, ALU.add)
        ck = PACK["rw_ka"][0]
        co = PACK["rw_omka"][0]
        fw.ts(pk[l][:, co:co + 2], pk[l][:, ck:ck + 2], -1.0, ALU.mult, 1.0, ALU.add)
        Bf = [fw.sbuf(f"B{i}", [128, T]) for i in range(10)]
        stgb = [fw.sbuf(f"stgb{i}", [128, T], BF16) for i in range(2)]
        zps = [fw.psum(f"zps{i}", [128, 512]) for i in range(3)]
        tps = fw.psum("tps", [128, 4, 128])
        vd = [fw.sbuf(f"vd{i}", [128, 4, 128], BF16) for i in range(2)]
        eps12 = fw.sbuf("eps12", [128, 1])
        fw.memset(eps12[:], 1e-12)
        raw, sh = Bf[0], Bf[1]
        for i, fn in ((0, AF.Tanh), (1, None), (2, AF.Sigmoid)):
            fw.dma(raw[:], PT[RW_OFF + (6 + i) * 128:RW_OFF + (7 + i) * 128, :])
            shift_mix(sh, raw, l, 6 + i)
            if fn is None:
                fw.copy(lora[i][:], sh[:])
            else:
                fw.act(lora[i][:], sh[:], fn)
        for vt in range(2):
            fw.dma(raw[:], PT[RW_OFF + 512 + vt * 128:RW_OFF + 512 + (vt + 1) * 128, :])
            shift_mix(sh, raw, l, 4 + vt)
            for i in range(NT):
                fw.transpose(tps[:, i % 4, :], sh[:, i * 128:(i + 1) * 128], ident[:])
                v_ = vd[i % 2]
                for hh in range(2):
                    fw.copy(v_[:, hh, 0:64], tps[:, i % 4, hh * 64:(hh + 1) * 64], eng="vector")
                    fw.copy(v_[:, hh, 64:128], tps[:, i % 4, hh * 64:(hh + 1) * 64], eng="scalar")
                fw.dma(VDs[:, i, 2 * vt:2 * vt + 2, :], v_[:, 0:2, :])
        Pend = fw.dram("Pend_d", [2, 2, 128, NT], F32) if False else None
        pend_s = fw.sbuf("pend_s", [128, 2, 2, NT])
        Fk, Fkk, Fr, Fbon, Ll, Aa, Bb, Fa, Fkd, Fb = Bf
        for tau in range(2):
            fw.dma(raw[:], PT[RW_OFF + 256 + tau * 128:RW_OFF + 256 + (tau + 1) * 128, :]) if False else None
            fw.dma(Ll[:], PT[RW_OFF + 256 + tau * 128:RW_OFF + 256 + (tau + 1) * 128, :])
            shift_mix(Fk, Ll, l, 2 + tau)
            fw.dma(Ll[:], PT[RW_OFF + tau * 128:RW_OFF + (tau + 1) * 128, :])
            shift_mix(Fr, Ll, l, tau)
            fw.ts(Fkk[:], Fk[:], pcol(l, "rw_kk", tau), ALU.mult)
            for (s, n, is_ctx) in blocks:
                zp = zps[0]
                fw.act(Aa[:, s:s + n], Fkk[:, s:s + n], AF.Square)
                fw.mm(zp[:, :n], blk64[:], Aa[:, s:s + n])
                fw.act(Aa[:, s:s + n], zp[:, :n], AF.Sqrt, bias=eps12[:, 0:1], scale=1.0)
            fw.recip(Aa[:], Aa[:])
            fw.tt(Fkk[:], Fkk[:], Aa[:], ALU.mult)
            for bi, (s, n, is_ctx) in enumerate(blocks):
                zp = zps[bi % 3]
                fw.mm(zp[:, :n], g2s[:, tau * 128:(tau + 1) * 128], lora[2][:, s:s + n])
                fw.copy(Aa[:, s:s + n], zp[:, :n], eng="scalar")
            fw.dma(GATE[tau * 128:(tau + 1) * 128, :], Aa[:])
            for d in range(2):
                rev = d == 1
                eidx = 0 if rev else 127
                ph = 64 * d
                for bi, (s, n, is_ctx) in enumerate(blocks):
                    zp = zps[bi % 3]
                    fw.mm(zp[:, :n], w2s[ph:ph + 64, tau * 128:(tau + 1) * 128], lora[0][ph:ph + 64, s:s + n])
                    fw.act(Ll[:, s:s + n], zp[:, :n], AF.Sigmoid, bias=pcol(l, "rw_w0", d * 2 + tau))
                    zp2 = zps[(bi + 1) % 3]
                    fw.mm(zp2[:, :n], a2s[ph:ph + 64, tau * 128:(tau + 1) * 128], lora[1][ph:ph + 64, s:s + n])
                    fw.act(Fa[:, s:s + n], zp2[:, :n], AF.Sigmoid, bias=pcol(l, "rw_a0", d * 2 + tau))
                fw.ts(Ll[:], Ll[:], -0.6065306597126334, ALU.mult)
                cur = Ll
                pp = [Aa, Bb]
                st = 1
                k_ = 0
                while st < 128:
                    oth = pp[k_ % 2]
                    cv = cur.t[:, :].rearrange("p (c i) -> p c i", i=128)
                    ov = oth.t[:, :].rearrange("p (c i) -> p c i", i=128)
                    if not rev:
                        fw.tt(oth.v(ov[:, :, st:]), cur.v(cv[:, :, st:]), cur.v(cv[:, :, :128 - st]), ALU.add)
                        fw.copy(oth.v(ov[:, :, :st]), cur.v(cv[:, :, :st]), eng="scalar")
                    else:
                        fw.tt(oth.v(ov[:, :, :128 - st]), cur.v(cv[:, :, :128 - st]), cur.v(cv[:, :, st:]), ALU.add)
                        fw.copy(oth.v(ov[:, :, 128 - st:]), cur.v(cv[:, :, 128 - st:]), eng="scalar")
                    cur = oth
                    st *= 2
                    k_ += 1
                assert cur is Aa
                cum = Aa
                cvw = cum.t[:, :].rearrange("p (c i) -> p c i", i=128)
                fw.act(pend_s[:, d, tau, :], cum.v(cvw[:, :, eidx]), AF.Exp)
                fw.tt(Ll[:], cum[:], Ll[:], ALU.subtract)
                fw.ts(Fkd[:], Fa[:], pcol(l, "rw_ka", tau), ALU.mult, pcol(l, "rw_omka", tau), ALU.add)
                fw.tt(Fkd[:], Fkd[:], Fk[:], ALU.mult, eng="gpsimd")
                fw.tt(Fb[:], Fkk[:], Fa[:], ALU.mult, eng="gpsimd")
                fw.stt(Fa[:], Fr[:], pcol(l, "rw_rk", tau), Fkd[:], ALU.mult, ALU.mult)
                for bi, (s, n, is_ctx) in enumerate(blocks):
                    zp = zps[bi % 3]
                    fw.mm(zp[:, :n], blk64[:], Fa[:, s:s + n])
                    if d == 0:
                        fw.copy(Fbon[:, s:s + n], zp[:, :n], eng="scalar")
                    else:
                        fw.tt(Fbon[:, s:s + n], zp[:, :n], Fbon[:, s:s + n], ALU.add)
                si = [0]

                def emit(arr_idx, fn):
                    o = stgb[si[0] % 2]
                    si[0] += 1
                    fn(o)
                    fw.dma(RWS[d, tau, arr_idx], o[:])
                fw.act(Bb[:], Ll[:], AF.Exp)
                for hh in range(2):
                    emit(hh, lambda o, hh=hh: fw.stt(o[:], Fkk[:], hm2[:, 2 + hh:3 + hh], Bb[:], ALU.mult, ALU.mult,
                                                    eng=("vector" if hh == 0 else "gpsimd")))
                fw.act(Bb[:], cum[:], AF.Exp)
                for hh in range(2):
                    emit(2 + hh, lambda o, hh=hh: fw.stt(o[:], Fr[:], hm2[:, hh:hh + 1], Bb[:], ALU.mult, ALU.mult,
                                                        eng=("vector" if hh == 0 else "gpsimd")))
                fw.act(Bb[:], cum[:], AF.Exp, scale=-1.0)
                emit(4, lambda o: fw.tt(o[:], Fb[:], Bb[:], ALU.mult))
                emit(5, lambda o: fw.tt(o[:], Fkd[:], Bb[:], ALU.mult, eng="gpsimd"))
                for c in range(NT):
                    fw.act(Bb[:, c * 128:(c + 1) * 128], cum[:, c * 128:(c + 1) * 128], AF.Exp,
                           bias=cum[:, c * 128 + eidx:c * 128 + eidx + 1], scale=-1.0)
                emit(6, lambda o: fw.tt(o[:], Fb[:], Bb[:], ALU.mult))
                emit(7, lambda o: fw.tt(o[:], Fkd[:], Bb[:], ALU.mult, eng="gpsimd"))
            fw.dma(Ll[:], PT[RW_OFF + 512 + tau * 128:RW_OFF + 512 + (tau + 1) * 128, :])
            shift_mix(Aa, Ll, l, 4 + tau)
            fw.tt(Aa[:], Aa[:], Fbon[:], ALU.mult)
            fw.dma(BON[tau * 128:(tau + 1) * 128, :], Aa[:])
        fw.dma(PENDs[:], pend_s[:])
        fw.pop()

        fw.push()
        pend = fw.sbuf("pend", [128, 2, 2, NT])
        fw.dma(pend[:], PENDs[:])
        Vdup = fw.sbuf("Vdup", [128, NT, 4, 128], BF16)
        fw.dma(Vdup[:], VDs[:])
        yaccT = fw.sbuf("yaccT", [128, 2, T])
        fw.memset(yaccT[:, 0, :], 0.0)
        fw.memset(yaccT[:, 1, :], 0.0, eng="gpsimd")
        I4 = fw.sbuf("I4", [128, 2, 128])
        fw.dma(I4[:], consts["I2"][:])
        identb_ = identb
        PB = [fw.psum(f"PB{i}", [128, 512]) for i in range(6)]
        pbi = [0]

        def bank():
            p = PB[pbi[0] % len(PB)]
            pbi[0] += 1
            return p

        def v4(bk, w=128):
            return bk.t[:, 0:4 * w].rearrange("p (h x) -> p h x", x=w)

        evi = [0]

        def evac(out, in_):
            e = "scalar" if evi[0] % 2 == 0 else "vector"
            evi[0] += 1
            fw.copy(out, in_, eng=e)

        def chunk_gen(d):
            rev = d == 1
            maskN = fw.sbuf(f"maskN{d}", [128, 4, 128])
            maskAB = fw.sbuf(f"maskAB{d}", [128, 2, 256])
            fw.dma(maskN[:], consts[f"maskN_{d}"][:])
            fw.dma(maskAB[:], consts[f"maskAB_{d}"][:])
            CH = [[fw.sbuf(f"CH{d}{i}_{tau}", [128, 8, 128], BF16) for tau in range(2)] for i in range(2)]
            BKt = [fw.sbuf(f"BKt{d}{i}", [128, 4, 384], BF16) for i in range(2)]
            for i in range(2):
                fw.memset(BKt[i][:], 0.0)
            X = [fw.sbuf(f"X{d}{i}", [128, 4, 128], BF16) for i in range(2)]
            XT = [fw.sbuf(f"XT{d}{i}", [128, 4, 128], BF16) for i in range(2)]
            Wt = [fw.sbuf(f"Wt{d}{i}", [128, 4, 128], BF16) for i in range(2)]
            AB = [[fw.sbuf(f"AB{d}{i}_{tau}", [128, 2, 256], BF16) for tau in range(2)] for i in range(2)]
            AK = [[fw.sbuf(f"AK{d}{i}_{tau}", [128, 2, 256], BF16) for tau in range(2)] for i in range(2)]
            Z = fw.sbuf(f"Zz{d}", [128, 4, 64], BF16)
            Udup = fw.sbuf(f"Udup{d}", [128, 4, 128], BF16)
            ST = [fw.sbuf(f"ST{d}{tau}", [128, 64]) for tau in range(2)]
            STd = [fw.sbuf(f"STd{d}{tau}", [128, 128], BF16) for tau in range(2)]
            tpsb = fw.psum(f"tpsb{d}", [128, 4, 128], BF16)
            for tau in range(2):
                fw.memset(ST[tau][:], 0.0)
                fw.memset(STd[tau][:], 0.0)
            order = chunk_order(rev)

            def load(ci):
                c = order[ci]
                cs = slice(c * 128, (c + 1) * 128)
                for tau in range(2):
                    fw.dma(CH[ci % 2][tau][:], RWS.v(RWS.t[d, tau, :, :, cs].rearrange("a p t -> p a t")))
            load(0)
            yield
            for ci, c in enumerate(order):
                cs = slice(c * 128, (c + 1) * 128)
                is_ctx = c < NTC
                want_out = need_ctx or not is_ctx
                ch = CH[ci % 2]
                bkt = BKt[ci % 2]
                if ci + 1 < len(order):
                    load(ci + 1)
                for tau in range(2):
                    fw.transpose(tpsb[:, 2 * tau, :], ch[tau][:, 6, :], identb_[:])
                    fw.transpose(tpsb[:, 2 * tau + 1, :], ch[tau][:, 7, :], identb_[:])
                bv = bkt.t[:, :, :].rearrange("p a (h x) -> p a h x", x=192)
                fw.copy(bkt.v(bv[:, :, :, 0:64]), tpsb.v(tpsb.t[:, :, :].rearrange("p a (h x) -> p a h x", x=64)),
                        eng="scalar")
                nb = bank()
                for h in range(4):
                    tau, hh = h // 2, h % 2
                    fw.mm(nb.v(v4(nb)[:, h, :]), ch[tau][:, hh, :], ch[tau][:, 4, :])
                x0 = X[0]
                fw.tt(x0[:], nb.v(v4(nb)), maskN[:], ALU.mult)
                yield
                ab, ak = AB[ci % 2], AK[ci % 2]
                for tau in range(2):
                    b2, b3 = bank(), bank()
                    for hh in range(2):
                        for (bk, arr) in ((b2, 4), (b3, 5)):
                            o = bk.t[:, :].rearrange("p (h x) -> p h x", x=256)
                            fw.mm(bk.v(o[:, hh, 0:128]), ch[tau][:, arr, :], ch[tau][:, hh, :])
                            fw.mm(bk.v(o[:, hh, 128:256]), ch[tau][:, arr, :], ch[tau][:, 2 + hh, :])
                    fw.tt(ab[tau][:], b2.v(b2.t[:, :].rearrange("p (h x) -> p h x", x=256)), maskAB[:], ALU.mult)
                    fw.tt(ak[tau][:], b3.v(b3.t[:, :].rearrange("p (h x) -> p h x", x=256)), maskAB[:], ALU.mult)
                    yield
                xt0 = XT[0]
                w0 = Wt[0]
                for tau in range(2):
                    fw.copy(xt0[:, 2 * tau:2 * tau + 2, :], ab[tau][:, :, 0:128], eng="gpsimd")
                    fw.tt(w0[:, 2 * tau:2 * tau + 2, :], ab[tau][:, :, 0:128], I4[:], ALU.add, eng="gpsimd")
                xc, xtc, wc = x0, xt0, w0
                for p in range(6):
                    xn, xtn, wn = X[(p + 1) % 2], XT[(p + 1) % 2], Wt[(p + 1) % 2]
                    bx = bank()
                    for h in range(4):
                        fw.mm(bx.v(v4(bx)[:, h, :]), xtc[:, h, :], xc[:, h, :])
                    if p < 5:
                        bxt = bank()
                        for h in range(4):
                            fw.mm(bxt.v(v4(bxt)[:, h, :]), xc[:, h, :], xtc[:, h, :])
                    evac(xn[:], bx.v(v4(bx)))
                    if p < 5:
                        evac(xtn[:], bxt.v(v4(bxt)))
                    yield
                    bw = bank()
                    for h in range(4):
                        fw.mm(bw.v(v4(bw)[:, h, :]), identb_[:], wc[:, h, :], start=True, stop=False)
                        fw.mm(bw.v(v4(bw)[:, h, :]), xn[:, h, :], wc[:, h, :], start=False, stop=True)
                    evac(wn[:], bw.v(v4(bw)))
                    xc, xtc, wc = xn, xtn, wn
                    yield
                wT = wc
                gb = bank()
                g4 = gb.t[:, 0:256].rearrange("p (h x) -> p h x", x=64)
                for h in range(4):
                    tau, hh = h // 2, h % 2
                    fw.mm(gb.v(g4[:, h, :]), ch[tau][:, hh, :], STd[tau][:, 0:64], start=True, stop=False)
                    fw.mm(gb.v(g4[:, h, :]), ak[tau][:, hh, 0:128], Vdup[:, c, h, 0:64], start=False, stop=True)
                fw.copy(Z[:], gb.v(g4), eng="scalar")
                yield
                ub = bank()
                u4 = ub.t[:, 0:256].rearrange("p (h x) -> p h x", x=64)
                for h in range(4):
                    fw.mm(ub.v(u4[:, h, :]), wT[:, h, :], Z[:, h, :])
                fw.copy(Udup[:, :, 0:64], ub.v(u4), eng="vector")
                fw.copy(Udup[:, :, 64:128], ub.v(u4), eng="scalar")
                yield
                if want_out:
                    yb = bank()
                    for h in range(4):
                        tau, hh = h // 2, h % 2
                        o = yb.v(v4(yb)[:, h, :])
                        fw.mm(o, STd[tau][:], ch[tau][:, 2 + hh, :], start=True, stop=False)
                        fw.mm(o, Udup[:, h, :], ab[tau][:, hh, 128:256], start=False, stop=False)
                        fw.mm(o, Vdup[:, c, h, :], ak[tau][:, hh, 128:256], start=False, stop=True)
                    o4 = yb.t[:, :].rearrange("p (a g t) -> p a g t", g=2, t=128)
                    for g in range(2):
                        dst = yaccT[64 * g:64 * g + 64, :, cs]
                        src = yb.v(o4[64 * g:64 * g + 64, :, g, :])
                        fw.tt(dst, src, dst, ALU.add, eng="vector")
                for tau in range(2):
                    sb = bank()
                    for hh in range(2):
                        h = 2 * tau + hh
                        fw.mm(sb[:, 0:64], bkt[:, 2 * tau, hh * 128:(hh + 1) * 128], Udup[:, h, 0:64],
                              start=(hh == 0), stop=False)
                        fw.mm(sb[:, 0:64], bkt[:, 2 * tau + 1, hh * 128:(hh + 1) * 128], Vdup[:, c, h, 0:64],
                              start=False, stop=(hh == 1))
                    fw.stt(ST[tau][:], ST[tau][:], pend[:, d, tau, c:c + 1], sb[:, 0:64], ALU.mult, ALU.add)
                    fw.copy(STd[tau][:, 0:64], ST[tau][:], eng="scalar")
                    fw.copy(STd[tau][:, 64:128], ST[tau][:], eng="gpsimd")
                yield

        gens = [chunk_gen(0), chunk_gen(1)]
        alive = [True, True]
        while any(alive):
            for gi, g in enumerate(gens):
                if alive[gi]:
                    try:
                        next(g)
                    except StopIteration:
                        alive[gi] = False
        epsln = fw.sbuf("epsln", [128, 1])
        fw.memset(epsln[:], 64e-5)
        tb = [fw.sbuf(f"r3_{i}", [128, 512]) for i in range(5)]
        ob = [fw.sbuf(f"rob{i}", [128, 512], BF16) for i in range(2)]
        oi = 0
        qblocks = (ctx_blocks if need_ctx else []) + lat_blocks
        for tau in range(2):
            for (s, n, is_ctx) in qblocks:
                yc, sq, rs, bo, ga = tb
                fw.dma(bo[:, :n], BON[tau * 128:(tau + 1) * 128, s:s + n])
                fw.dma(ga[:, :n], GATE[tau * 128:(tau + 1) * 128, s:s + n])
                mb = bank()
                fw.mm(mb[:, :n], blk64[:], yaccT[:, tau, s:s + n])
                fw.stt(yc[:, :n], mb[:, :n], -1.0 / 64, yaccT[:, tau, s:s + n], ALU.mult, ALU.add)
                fw.act(sq[:, :n], yc[:, :n], AF.Square)
                vb = bank()
                fw.mm(vb[:, :n], blk64[:], sq[:, :n])
                fw.act(rs[:, :n], vb[:, :n], AF.Sqrt, bias=epsln[:, 0:1], scale=1.0 / 64)
                fw.recip(rs[:, :n], rs[:, :n])
                fw.stt(yc[:, :n], yc[:, :n], pcol(l, "rw_ln_g", tau), rs[:, :n], ALU.mult, ALU.mult)
                fw.stt(yc[:, :n], yc[:, :n], pcol(l, "rw_ln_b", tau), bo[:, :n], ALU.add, ALU.add, eng="gpsimd")
                o = ob[oi % 2]; oi += 1
                fw.tt(o[:, :n], yc[:, :n], ga[:, :n], ALU.mult, eng="gpsimd")
                fw.dma(OT[tau * 128:(tau + 1) * 128, s:s + n], o[:, :n])
        fw.pop()

    def mixers_0(l, b, need_ctx):
        if "rwkv" in cfg.mix:
            rwkv_phase(l, b, need_ctx)
        if "gla" in cfg.mix:
            gla_phase(l, b, need_ctx)
        if "gqa" in cfg.mix:
            gqa_phase(l, b, need_ctx)
        if "da" in cfg.mix:
            da_phase(l, b, need_ctx)

    def mixers(l, b, need_ctx):
        if "rwkv" in cfg.mix:
            rwkv_phase(l, b, need_ctx)
        if "da" in cfg.mix:
            da_phase(l, b, need_ctx)
        if "gla" in cfg.mix:
            gla_phase(l, b, need_ctx)
        if "gqa" in cfg.mix:
            gqa_phase(l, b, need_ctx)

    for b in range(NB):
        fw.push()
        xT = fw.sbuf("xT_s", [128, KT, T])
        xv = xT_d.t[b].rearrange("(k p) t -> p k t", p=128)
        for k in range(KT):
            fw.dma(xT[:, k, :], xT_d.v(xv[:, k, :]))
        for l in range(L):
            need_ctx = l < L - 1
            fw.push()
            hT = fw.sbuf("hT", [128, KT, T], BF16)
            sq = fw.sbuf("sq", [128, 512])
            rstd = fw.sbuf("rstd", [128, 512])
            nps = fw.psum("nps", [128, 512])
            norm_phase(xT, hT, l, 0, b, sq, rstd, nps)
            wt = [fw.sbuf(f"wt{i}", [128, KT, 256], BF16) for i in range(3)]
            pps = [fw.psum(f"pps{i}", [128, 512]) for i in range(4)]
            stg = [fw.sbuf(f"stg{i}", [128, 512]) for i in range(4)]
            wv = w_in.t[l].rearrange("(k p) c -> p k c", p=128)
            ei = 0
            tiles_ = mixer_cols()
            groups_ = [tiles_[i:i + 2] for i in range(0, len(tiles_), 2)]
            for gi, grp in enumerate(groups_):
                g0 = grp[0][0]
                gn = sum(nc_ for _, nc_ in grp)
                wb = wt[gi % 3]
                fw.dma(wb[:, :, :gn], w_in.v(wv[:, :, g0:g0 + gn]), eng="gpsimd")
                for (c0, ncol) in grp:
                    off = c0 - g0
                    for (s, n, is_ctx) in blocks:
                        ps = pps[ei % 4]
                        st = stg[ei % 4]
                        for k in range(KT):
                            fw.mm(ps[:ncol, :n], wb[:, k, off:off + ncol], hT[:, k, s:s + n],
                                  start=(k == 0), stop=(k == KT - 1))
                        fw.copy(st[:ncol, :n], ps[:ncol, :n], eng=("vector" if ei % 2 == 0 else "scalar"))
                        fw.dma(PT[c0:c0 + ncol, s:s + n], st[:ncol, :n])
                        ei += 1
            fw.pop()
            if cfg.stop == "proj":
                break
            if cfg.stop == "ffn":
                fw.push()
                tb = fw.sbuf("tb", [128, T])
                tbb = fw.sbuf("tbb", [128, T], BF16)
                for k in range(KT):
                    fw.dma(tb[:], PT[k * 128:(k + 1) * 128, :])
                    fw.copy(tbb[:], tb[:])
                    fw.dma(OT[k * 128:(k + 1) * 128, :], tbb[:])
                fw.pop()
            else:
                mixers(l, b, need_ctx)
            if cfg.stop == "mix":
                break
            wout_phase(xT, l, b, need_ctx)
            ffn_phase(xT, l, b, need_ctx)
            if cfg.stop == "ffn":
                break
        yv = yT_d.t[b].rearrange("(k p) t -> p k t", p=128)
        for k in range(KT):
            fw.dma(yT_d.v(yv[:, k, :]), xT[:, k, TC:T])
        fw.pop()
        if cfg.stop is not None:
            break

    if cfg.stop is not None:
        dbg_pt = fw.dram("dbg_PT", [N_IN, T], F32, kind="ExternalOutput")
        fw.dma(dbg_pt[:], PT[:])
        dbg_ot = fw.dram("dbg_OT", [D, T], BF16, kind="ExternalOutput")
        fw.dma(dbg_ot[:], OT[:])
    fw.pop()
    fw.finish()
    return nc


_NC_CACHE = {}


def kernel(**inputs):
    n_cores = 8
    B = inputs["x"].shape[0]
    NB = B // n_cores
    cfg = Cfg(TC=inputs["ctx"].shape[1], TL=inputs["x"].shape[1], NB=NB, depth=DEPTH)
    nc = build(cfg)
    in_maps = [prep_inputs(inputs, cfg, i * NB) for i in range(n_cores)]
    res = run_bass_kernel_spmd(nc, in_maps, core_ids=list(range(n_cores)))
    out = np.empty((B, cfg.TL, D), np.float32)
    for i in range(n_cores):
        yT = np.asarray(res.results[i]["yT"])
        out[i * NB:(i + 1) * NB] = yT.transpose(0, 2, 1)
    return out


def prep_inputs(inp, cfg, b0):
    NB = cfg.NB
    m = {}
    x = np.asarray(inp["x"], np.float32)[b0:b0 + NB]
    ctx = np.asarray(inp["ctx"], np.float32)[b0:b0 + NB]
    xc = np.concatenate([ctx, x], axis=1)
    m["xT"] = np.ascontiguousarray(xc.transpose(0, 2, 1))
    cvec = np.concatenate([np.asarray(inp["c"], np.float32)[b0:b0 + NB],
                           np.asarray(inp["c_ctx"], np.float32)[None]], axis=0)
    m["cT"] = np.ascontiguousarray(cvec.reshape(NB + 1, KT, 128).transpose(2, 1, 0))
    L = cfg.depth
    m["pack"] = np.stack([host_pack(inp, l) for l in range(L)])
    for nm in ("rw_w2", "rw_a2"):
        m[nm] = np.ascontiguousarray(np.asarray(inp[nm], np.float32)[:L].reshape(L, 128, 256))
    m["rw_g2"] = np.ascontiguousarray(np.asarray(inp["rw_g2"], np.float32)[:L])
    m["lamb"] = np.stack([np.tile(np.asarray(inp["da_lam"], np.float32)[l].reshape(1, 128), (128, 1)) for l in range(L)])
    for nm in ("mod_w", "w_in", "w_out", "ffn_w_up", "ffn_w_down", "gla_a2"):
        m[nm] = np.ascontiguousarray(np.asarray(inp[nm], np.float32)[:L])
    for k, v in host_consts(cfg).items():
        m["c_" + k] = v
    return m
```

```python
import numpy as np
import concourse.bass as bass
import concourse.mybir as mybir
from concourse.bass_utils import run_bass_kernel_spmd

F32 = mybir.dt.float32
BF16 = mybir.dt.bfloat16
AF = mybir.ActivationFunctionType
ALU = mybir.AluOpType
AX = mybir.AxisListType

ENGS = ("tensor", "vector", "scalar", "gpsimd", "sync")


class Trk:
    __slots__ = ("name", "w", "r")

    def __init__(self, name):
        self.name = name
        self.w = None
        self.r = {}


class V:
    __slots__ = ("ap", "trk")

    def __init__(self, ap, trk):
        self.ap = ap
        self.trk = trk


class Buf:
    def __init__(self, t, name):
        self.t = t
        self.name = name
        self.trk = Trk(name)

    def __getitem__(self, idx):
        return V(self.t[idx], self.trk)

    def v(self, ap):
        return V(ap, self.trk)


def _trks(v):
    return v.trk if isinstance(v.trk, (list, tuple)) else (v.trk,)


class FW:
    def __init__(self, nc, n_dma_sems=32):
        self.nc = nc
        self.prog = {e: [] for e in ENGS}
        self.sem = {e: nc.alloc_semaphore(name=f"s_{e}") for e in ENGS}
        self.cnt = {e: 0 for e in ENGS}
        self.waited = {e: {} for e in ENGS}
        self.dsem = [nc.alloc_semaphore(name=f"d_{i}") for i in range(n_dma_sems)]
        self.dcnt = [0] * n_dma_sems
        self.dnext = 0
        self.gnext = 0
        self.semobj = {}
        for e in ENGS:
            self.semobj[("e", e)] = self.sem[e]
        for i, s in enumerate(self.dsem):
            self.semobj[("d", i)] = s
        self.ninst = 0
        self.stack = []

    def push(self):
        self.stack.append([])

    def pop(self):
        self.barrier()
        for g in reversed(self.stack.pop()):
            g.__exit__(None, None, None)

    def sbuf(self, name, shape, dtype=F32):
        self.uid = getattr(self, "uid", 0) + 1
        name = f"{name}_u{self.uid}"
        g = self.nc.sbuf_tensor(name, list(shape), dtype)
        t = g.__enter__()
        self.stack[-1].append(g)
        return Buf(t, name)

    def psum(self, name, shape, dtype=F32):
        self.uid = getattr(self, "uid", 0) + 1
        name = f"{name}_u{self.uid}"
        g = self.nc.psum_tensor(name, list(shape), dtype)
        t = g.__enter__()
        self.stack[-1].append(g)
        return Buf(t, name)

    def dram(self, name, shape, dtype=F32, kind="Internal"):
        return Buf(self.nc.dram_tensor(name, list(shape), dtype, kind=kind).ap(), name)

    def _wait(self, eng, ev):
        if ev is None:
            return
        key, val = ev
        if eng == "tensor" and key == ("e", "tensor"):
            return
        if self.waited[eng].get(key, 0) >= val:
            return
        self.waited[eng][key] = val
        self.prog[eng].append(("wait", key, val))

    def _deps(self, eng, reads, writes):
        for v in reads:
            for t in _trks(v):
                self._wait(eng, t.w)
        for v in writes:
            for t in _trks(v):
                self._wait(eng, t.w)
                for kv in list(t.r.items()):
                    self._wait(eng, kv)

    def _mark(self, ev, reads, writes):
        for v in reads:
            for t in _trks(v):
                if t.r.get(ev[0], 0) < ev[1]:
                    t.r[ev[0]] = ev[1]
        for v in writes:
            for t in _trks(v):
                t.w = ev
                t.r = {}

    def op(self, eng, meth, reads, writes, *args, **kw):
        self._deps(eng, reads, writes)
        self.cnt[eng] += 1
        ev = (("e", eng), self.cnt[eng])
        sem = self.sem[eng]
        a2 = [a.ap if isinstance(a, V) else a for a in args]
        k2 = {k: (a.ap if isinstance(a, V) else a) for k, a in kw.items()}

        def emit(e, inc, wait=None, meth=meth, a2=a2, k2=k2, sem=sem):
            ins = getattr(e, meth)(*a2, **k2)
            if wait is not None:
                ins._wait_ge(wait[0], wait[1])
            if inc:
                ins.then_inc(sem, 1)
        self.prog[eng].append(("op", emit, self.cnt[eng]))
        self._mark(ev, reads, writes)
        self.ninst += 1
        return ev

    def dma(self, out, in_, eng="sync", **kw):
        self._deps(eng, [in_], [out])
        nd = len(self.dsem)
        if eng == "gpsimd":
            k = nd - 8 + self.gnext
            self.gnext = (self.gnext + 1) % 8
        else:
            k = self.dnext
            self.dnext = (self.dnext + 1) % (nd - 8)
        if self.dcnt[k] > 0:
            self._wait(eng, (("d", k), self.dcnt[k]))
        self.dcnt[k] += 16
        ev = (("d", k), self.dcnt[k])
        sem = self.dsem[k]
        oa, ia = out.ap, in_.ap

        def emit(e, oa=oa, ia=ia, sem=sem, kw=kw):
            e.dma_start(out=oa, in_=ia, **kw).then_inc(sem, 16)
        self.prog[eng].append(("dma", emit))
        self._mark(ev, [in_], [out])
        self.ninst += 1
        return ev

    def _all_events(self):
        evs = [(("e", e), self.cnt[e]) for e in ENGS if self.cnt[e] > 0]
        evs += [(("d", i), c) for i, c in enumerate(self.dcnt) if c > 0]
        return evs

    def barrier(self):
        evs = self._all_events()
        for e in ENGS:
            for ev in evs:
                self._wait(e, ev)

    def finish(self):
        for ev in self._all_events():
            self._wait("sync", ev)
        import bisect
        needed = {e: set() for e in ENGS}
        for ename in ENGS:
            for it in self.prog[ename]:
                if it[0] == "wait" and it[1][0] == "e":
                    needed[it[1][1]].add(it[2])
        ranks = {e: sorted(needed[e]) for e in ENGS}
        self.max_sem = {e: len(ranks[e]) for e in ENGS}
        with self.nc.Block() as block:
            for ename in ENGS:
                lst = self.prog[ename]

                def body(e, lst=lst, ename=ename):
                    pending = []
                    for it in lst:
                        if it[0] == "wait":
                            key, val = it[1], it[2]
                            if key[0] == "e":
                                val = bisect.bisect_left(ranks[key[1]], val) + 1
                            pending.append((self.semobj[key], val))
                        elif it[0] == "op":
                            for (sm, vl) in pending[:-1]:
                                e.wait_ge(sm, vl)
                            it[1](e, it[2] in needed[ename], pending[-1] if pending else None)
                            pending = []
                        else:
                            for (sm, vl) in pending:
                                e.wait_ge(sm, vl)
                            pending = []
                            it[1](e)
                    for (sm, vl) in pending:
                        e.wait_ge(sm, vl)
                getattr(block, ename)(body)

    def mm(self, out, lhsT, rhs, start=True, stop=True):
        return self.op("tensor", "matmul", [lhsT, rhs], [out], out, lhsT, rhs, start=start, stop=stop)

    def transpose(self, out, in_, ident):
        return self.op("tensor", "transpose", [in_, ident], [out], out, in_, ident)

    def act(self, out, in_, func, bias=None, scale=None, accum_out=None):
        reads = [in_]
        kw = {}
        if bias is not None:
            kw["bias"] = bias
            if isinstance(bias, V):
                reads.append(bias)
        if scale is not None:
            kw["scale"] = scale
            if isinstance(scale, V):
                reads.append(scale)
        writes = [out]
        if accum_out is not None:
            kw["accum_out"] = accum_out
            writes.append(accum_out)
        return self.op("scalar", "activation", reads, writes, out, in_, func, **kw)

    def tt(self, out, in0, in1, op, eng="vector"):
        return self.op(eng, "tensor_tensor", [in0, in1], [out], out, in0, in1, op)

    def ts(self, out, in0, s1, op0, s2=None, op1=None, eng="vector"):
        reads = [in0] + [s for s in (s1, s2) if isinstance(s, V)]
        if op1 is None:
            return self.op(eng, "tensor_scalar", reads, [out], out, in0, s1, None, op0)
        return self.op(eng, "tensor_scalar", reads, [out], out, in0, s1, s2, op0, op1)

    def stt(self, out, in0, scalar, in1, op0, op1, eng="vector"):
        eng = "vector"
        reads = [in0, in1] + ([scalar] if isinstance(scalar, V) else [])
        return self.op(eng, "scalar_tensor_tensor", reads, [out], out, in0, scalar, in1, op0, op1)

    def copy(self, out, in_, eng="vector"):
        if eng == "scalar":
            return self.op("scalar", "copy", [in_], [out], out, in_)
        return self.op(eng, "tensor_copy", [in_], [out], out, in_)

    def memset(self, out, val, eng="vector"):
        return self.op(eng, "memset", [], [out], out, val)

    def recip(self, out, in_):
        return self.op("vector", "reciprocal", [in_], [out], out, in_)


D = 1024
KT = 8
N_IN = 3232
D_FF = 2816
FT = 22
GRID_W = 64
EPS = 1e-6
RW_OFF, DA_OFF, GLA_OFF, GQA_OFF = 0, 1152, 1920, 2720
DEPTH = 2


class Cfg:
    def __init__(self, TC=256, TL=2048, NB=2, depth=DEPTH, stop=None, mix=("rwkv", "da", "gla", "gqa")):
        self.TC, self.TL, self.NB, self.depth = TC, TL, NB, depth
        self.T = TC + TL
        self.stop = stop
        self.mix = mix

    def blocks(self):
        out = []
        s = 0
        while s < self.TC:
            n = min(512, self.TC - s)
            out.append((s, n, True))
            s += n
        while s < self.T:
            n = min(512, self.T - s)
            out.append((s, n, False))
            s += n
        return out


def pack_layout():
    cols = {}
    n = 0

    def add(name, k):
        nonlocal n
        cols[name] = (n, k)
        n += k
    add("nmg", 8)
    add("nfg", 8)
    add("mod_b", 48)
    add("rw_mu0", 9)
    add("rw_mu1", 9)
    add("rw_c0", 9)
    add("rw_omka", 2)
    add("rw_w0", 4)
    add("rw_a0", 4)
    add("rw_kk", 2)
    add("rw_ka", 2)
    add("rw_rk", 2)
    add("rw_ln_g", 2)
    add("rw_ln_b", 2)
    add("da_qg", 1)
    add("da_kg", 1)
    add("da_sub", 1)
    add("gla_ab", 2)
    add("gla_ng", 1)
    add("gq_qg", 1)
    add("gq_kg", 1)
    add("conv_w", 66)
    add("conv_b", 22)
    return cols, n


PACK, NPACK = pack_layout()


def host_pack(inp, l):
    P = np.zeros((128, NPACK), np.float32)

    def put(name, vec, k):
        c0, kk = PACK[name]
        assert kk == k
        P[:, c0:c0 + k] = np.asarray(vec, np.float32).reshape(k, 128).T
    put("nmg", inp["norm_mix_g"][l], 8)
    put("nfg", inp["norm_ffn_g"][l], 8)
    put("mod_b", inp["mod_b"][l], 48)
    put("rw_mu0", inp["rw_mu"][l, 0], 9)
    put("rw_mu1", inp["rw_mu"][l, 1], 9)
    put("rw_w0", inp["rw_w0"][l].reshape(-1), 4)
    put("rw_a0", inp["rw_a0"][l].reshape(-1), 4)
    put("rw_kk", inp["rw_kk"][l], 2)
    put("rw_ka", inp["rw_ka"][l], 2)
    put("rw_rk", inp["rw_rk"][l].reshape(-1), 2)
    put("rw_ln_g", inp["rw_ln_g"][l], 2)
    put("rw_ln_b", inp["rw_ln_b"][l], 2)
    put("da_qg", np.tile(inp["da_qk_g"][l, 0], 4), 1)
    put("da_kg", np.tile(inp["da_qk_g"][l, 1], 4), 1)
    put("da_sub", np.tile(inp["da_subln_g"][l], 2), 1)
    put("gla_ab", inp["gla_ab"][l].reshape(-1), 2)
    put("gla_ng", np.tile(inp["gla_norm_g"][l], 2), 1)
    put("gq_qg", np.tile(inp["gqa_qk_g"][l, 0], 2), 1)
    put("gq_kg", np.tile(inp["gqa_qk_g"][l, 1], 2), 1)
    put("conv_w", inp["ffn_conv_w"][l].reshape(-1), 66)
    put("conv_b", inp["ffn_conv_b"][l], 22)
    return P


def host_consts(cfg):
    c = {}
    c["ident"] = np.eye(128, dtype=np.float32)
    c["ones"] = np.ones((128, 128), np.float32)
    b64 = np.zeros((128, 128), np.float32)
    b64[:64, :64] = 1
    b64[64:, 64:] = 1
    c["blk64"] = b64
    b32 = np.zeros((128, 128), np.float32)
    for i in range(4):
        b32[32 * i:32 * i + 32, 32 * i:32 * i + 32] = 1
    c["blk32"] = b32
    TL = cfg.TL
    rows = TL // GRID_W
    row = np.repeat(np.arange(rows, dtype=np.float32), GRID_W)
    col = np.tile(np.arange(GRID_W, dtype=np.float32), rows)

    def tables(hd):
        nf = hd // 4
        inv = (10000.0 ** (-np.arange(nf, dtype=np.float32) / nf)).astype(np.float32)
        ang = np.concatenate([row[:, None] * inv, col[:, None] * inv], axis=-1)
        cos, sin = np.cos(ang).astype(np.float32), np.sin(ang).astype(np.float32)
        half = hd // 2
        cosf = np.concatenate([cos, cos], axis=-1)
        sinf = np.concatenate([sin, sin], axis=-1)
        rep = 128 // hd
        cT = np.tile(cosf, (1, rep)).T.copy()
        sT = np.tile(sinf, (1, rep)).T.copy()
        R = np.zeros((128, 128), np.float32)
        for m in range(128):
            if m % hd < half:
                R[m + half, m] = -1.0
            else:
                R[m - half, m] = 1.0
        return cT, sT, R
    hm = np.zeros((128, 4), np.float32)
    for p in range(128):
        hm[p, p // 32] = 1.0
    c["hmask4s"] = hm * np.float32(32 ** -0.5)
    jj, tt_ = np.meshgrid(np.arange(128), np.arange(128), indexing="ij")
    c["tri4_0"] = np.tile((jj <= tt_).astype(np.float32)[:, None, :], (1, 4, 1))
    c["tri4_1"] = np.tile((jj >= tt_).astype(np.float32)[:, None, :], (1, 4, 1))
    h2 = np.zeros((128, 4), np.float32)
    h2[:64, 0] = 1; h2[64:, 1] = 1; h2[:64, 2] = -1; h2[64:, 3] = -1
    c["hm2"] = h2
    c["I2"] = np.tile(np.eye(128, dtype=np.float32)[:, None, :], (1, 2, 1))
    c["maskN_0"] = np.tile((tt_ < jj).astype(np.float32)[:, None, :], (1, 4, 1))
    c["maskN_1"] = np.tile((tt_ > jj).astype(np.float32)[:, None, :], (1, 4, 1))
    sf, inf_ = (jj < tt_).astype(np.float32), (jj <= tt_).astype(np.float32)
    sr, inr = (jj > tt_).astype(np.float32), (jj >= tt_).astype(np.float32)
    c["maskAB_0"] = np.tile(np.concatenate([sf, inf_], 1)[:, None, :], (1, 2, 1))
    c["maskAB_1"] = np.tile(np.concatenate([sr, inr], 1)[:, None, :], (1, 2, 1))
    dm = np.zeros((128, 2), np.float32)
    for p in range(128):
        dm[p, (p % 64) // 32] = 1.0
    c["dmask"] = dm
    c["cos_gq"], c["sin_gq"], c["rot_gq"] = tables(64)
    c["cos_da"], c["sin_da"], c["rot_da"] = tables(32)
    return c


def build(cfg):
    nc = bass.Bass("TRN2", target_bir_lowering=False)
    fw = FW(nc)
    NB, T, TC, TL = cfg.NB, cfg.T, cfg.TC, cfg.TL
    NJ = NB + 1
    L = cfg.depth

    def din(name, shape, dt=F32):
        return fw.dram(name, shape, dt, kind="ExternalInput")

    xT_d = din("xT", [NB, D, T])
    cT_d = din("cT", [128, KT, NJ])
    pack_d = din("pack", [L, 128, NPACK])
    mod_w = din("mod_w", [L, D, 6 * D])
    w_in = din("w_in", [L, D, N_IN])
    w_out = din("w_out", [L, D, D])
    w_up = din("ffn_w_up", [L, D, 2 * D_FF])
    w_down = din("ffn_w_down", [L, D_FF, D])
    consts = {}
    for nm in ("ident", "ones", "blk64", "blk32", "rot_gq", "rot_da"):
        consts[nm] = din("c_" + nm, [128, 128])
    for nm in ("cos_gq", "sin_gq", "cos_da", "sin_da"):
        consts[nm] = din("c_" + nm, [128, TL])
    consts["dmask"] = din("c_dmask", [128, 2])
    lamb_d = din("lamb", [L, 128, 128])
    gla_a2_d = din("gla_a2", [L, 2, 16, 128])
    rw_w2_d = din("rw_w2", [L, 128, 256])
    rw_a2_d = din("rw_a2", [L, 128, 256])
    rw_g2_d = din("rw_g2", [L, 128, 256])
    consts["hm2"] = din("c_hm2", [128, 4])
    consts["I2"] = din("c_I2", [128, 2, 128])
    for d_ in range(2):
        consts[f"maskN_{d_}"] = din(f"c_maskN_{d_}", [128, 4, 128])
        consts[f"maskAB_{d_}"] = din(f"c_maskAB_{d_}", [128, 2, 256])
    consts["hmask4s"] = din("c_hmask4s", [128, 4])
    consts["tri4_0"] = din("c_tri4_0", [128, 4, 128])
    consts["tri4_1"] = din("c_tri4_1", [128, 4, 128])
    yT_d = fw.dram("yT", [NB, D, TL], F32, kind="ExternalOutput")
    dbg = {}

    PT = fw.dram("PT", [N_IN, T], F32)
    OT = fw.dram("OT", [D, T], BF16)
    ACT = fw.dram("ACTs", [D_FF, T], BF16)

    fw.push()
    ident = fw.sbuf("ident", [128, 128])
    ones = fw.sbuf("ones", [128, 128])
    blk64 = fw.sbuf("blk64", [128, 128])
    fw.dma(ident[:], consts["ident"][:])
    fw.dma(ones[:], consts["ones"][:])
    fw.dma(blk64[:], consts["blk64"][:])
    identb = fw.sbuf("identb", [128, 128], BF16)
    fw.copy(identb[:], ident[:])
    onesb_g = fw.sbuf("onesb_g", [128, 128], BF16)
    fw.copy(onesb_g[:], ones[:])
    blk64b = fw.sbuf("blk64b", [128, 128], BF16)
    fw.copy(blk64b[:], blk64[:])
    pk = [fw.sbuf(f"pk{l}", [128, NPACK]) for l in range(L)]
    for l in range(L):
        fw.dma(pk[l][:], pack_d[l])
    modT = [fw.sbuf(f"modT{l}", [128, 48, NJ]) for l in range(L)]
    gs = [fw.sbuf(f"gs{l}", [128, 2, KT, NJ]) for l in range(L)]
    epsb = fw.sbuf("epsb", [128, 1])
    fw.memset(epsb[:], EPS)

    def pcol(l, name, i=0):
        c0, k = PACK[name]
        return pk[l][:, c0 + i:c0 + i + 1]

    fw.push()
    cs = fw.sbuf("cs", [128, KT, NJ])
    fw.dma(cs[:], cT_d[:])
    fw.act(cs[:], cs[:], AF.Silu)
    mps = fw.psum("mps", [128, 48, NJ])
    wm = [fw.sbuf(f"wm{i}", [128, KT, 512]) for i in range(2)]
    for l in range(L):
        mwv = mod_w.t[l].rearrange("(k p) c -> p k c", p=128)
        for g in range(12):
            wb = wm[g % 2]
            fw.dma(wb[:], mod_w.v(mwv[:, :, g * 512:(g + 1) * 512]))
            for ci in range(4):
                ct = g * 4 + ci
                for k in range(KT):
                    fw.mm(mps[:, ct, :], wb[:, k, ci * 128:(ci + 1) * 128], cs[:, k, :],
                          start=(k == 0), stop=(k == KT - 1))
        c0 = PACK["mod_b"][0]
        for j in range(NJ):
            fw.tt(modT[l][:, :, j], mps[:, :, j], pk[l][:, c0:c0 + 48], ALU.add)
        for j in range(NJ):
            c0 = PACK["nmg"][0]
            fw.stt(gs[l][:, 0, :, j], modT[l][:, 8:16, j], 1.0, pk[l][:, c0:c0 + 8], ALU.add, ALU.mult)
            c0 = PACK["nfg"][0]
            fw.stt(gs[l][:, 1, :, j], modT[l][:, 32:40, j], 1.0, pk[l][:, c0:c0 + 8], ALU.add, ALU.mult)
    fw.pop()

    blocks = cfg.blocks()
    if cfg.stop == "mix":
        fw.push()
        zt = fw.sbuf("zt", [128, T], BF16)
        fw.memset(zt[:], 0.0)
        for k in range(KT):
            fw.dma(OT[k * 128:(k + 1) * 128, :], zt[:])
        fw.pop()

    def mixer_cols():
        tl = []
        for i in range(9):
            tl.append((RW_OFF + 128 * i, 128))
        for i in range(6):
            tl.append((DA_OFF + 128 * i, 128))
        for i in range(6):
            tl.append((GLA_OFF + 128 * i, 128))
        tl.append((GLA_OFF + 768, 32))
        for i in range(4):
            tl.append((GQA_OFF + 128 * i, 128))
        return tl

    def norm_phase(xT, hT, l, which, b, sq, rstd, nps):
        shift_base = 0 if which == 0 else 24
        sqb = [fw.sbuf(f"nsqb{i}", [128, 512], BF16) for i in range(3)]
        tmpf = [fw.sbuf(f"ntmp{i}", [128, 512]) for i in range(3)]
        nps2 = fw.psum("nps_b", [128, 512])
        ci = 0
        for bi, (s, n, is_ctx) in enumerate(blocks):
            j = NB if is_ctx else b
            ps = nps if bi % 2 == 0 else nps2
            for k in range(KT):
                q = sqb[ci % 3]
                ci += 1
                fw.act(q[:, :n], xT[:, k, s:s + n], AF.Square)
                fw.mm(ps[:, :n], onesb_g[:], q[:, :n], start=(k == 0), stop=(k == KT - 1))
            rs = sq if bi % 2 == 0 else rstd
            fw.act(rs[:, :n], ps[:, :n], AF.Sqrt, bias=epsb[:, 0:1], scale=1.0 / D)
            fw.recip(rs[:, :n], rs[:, :n])
            for k in range(KT):
                t_ = tmpf[ci % 3]
                ci += 1
                fw.tt(t_[:, :n], xT[:, k, s:s + n], rs[:, :n], ALU.mult)
                fw.act(hT[:, k, s:s + n], t_[:, :n], AF.Identity,
                       bias=modT[l][:, shift_base + k, j:j + 1], scale=gs[l][:, which, k, j:j + 1])

    def wout_phase(xT, l, b, need_ctx):
        fw.push()
        oT = fw.sbuf("oT", [128, KT, T], BF16)
        ov = OT.t.rearrange("(k p) t -> p k t", p=128)
        for k in range(KT):
            fw.dma(oT[:, k, :], OT.v(ov[:, k, :]))
        wt = [fw.sbuf(f"wo{i}", [128, KT, 256], BF16) for i in range(2)]
        pps = [fw.psum(f"ops{i}", [128, 512]) for i in range(4)]
        wv = w_out.t[l].rearrange("(k p) c -> p k c", p=128)
        ei = 0
        for jt in range(KT):
            wb_full = wt[(jt // 2) % 2]
            if jt % 2 == 0:
                fw.dma(wb_full[:], w_out.v(wv[:, :, jt * 128:(jt + 2) * 128]), eng="gpsimd")
            wb = Buf(wb_full.t[:, :, (jt % 2) * 128:(jt % 2 + 1) * 128], "wo_half")
            wb.trk = wb_full.trk
            for (s, n, is_ctx) in blocks:
                if is_ctx and not need_ctx:
                    continue
                j = NB if is_ctx else b
                ps = pps[ei % 4]
                ei += 1
                for k in range(KT):
                    fw.mm(ps[:, :n], wb[:, k, :], oT[:, k, s:s + n], start=(k == 0), stop=(k == KT - 1))
                fw.stt(xT[:, jt, s:s + n], ps[:, :n], modT[l][:, 16 + jt, j:j + 1], xT[:, jt, s:s + n],
                       ALU.mult, ALU.add)
        fw.pop()

    def ffn_phase(xT, l, b, need_ctx):
        segs = ([(0, TC)] if need_ctx else []) + [(TC, T)]
        fblocks = [bl for bl in blocks if (need_ctx or not bl[2])]
        fw.push()
        hT = fw.sbuf("hT2", [128, KT, T], BF16)
        sq = fw.sbuf("sq2", [128, 512])
        rstd = fw.sbuf("rstd2", [128, 512])
        nps = fw.psum("nps2", [128, 512])
        norm_phase(xT, hT, l, 1, b, sq, rstd, nps)
        wu = [fw.sbuf(f"wu{i}", [128, KT, 256], BF16) for i in range(2)]
        wg = [fw.sbuf(f"wg{i}", [128, KT, 256], BF16) for i in range(2)]
        ups = [fw.psum(f"ups{i}", [128, 512]) for i in range(2)]
        gps = [fw.psum(f"gps{i}", [128, 512]) for i in range(2)]
        uT = [fw.sbuf(f"uT{i}", [128, T]) for i in range(2)]
        gT = [fw.sbuf(f"gT{i}", [128, T]) for i in range(2)]
        tmp = [fw.sbuf(f"ftmp{i}", [128, T]) for i in range(2)]
        aT = [fw.sbuf(f"aT{i}", [128, T], BF16) for i in range(2)]
        wv = w_up.t[l].rearrange("(k p) c -> p k c", p=128)
        cw0 = PACK["conv_w"][0]
        cb0 = PACK["conv_b"][0]
        ei = 0
        def wload(p):
            fw.dma(wu[p % 2][:], w_up.v(wv[:, :, p * 256:(p + 1) * 256]), eng="gpsimd")
            fw.dma(wg[p % 2][:], w_up.v(wv[:, :, D_FF + p * 256:D_FF + (p + 1) * 256]), eng="gpsimd")
        wload(0)
        for i in range(FT):
            r = i % 2
            if i % 2 == 1 and i // 2 + 1 < FT // 2:
                wload(i // 2 + 1)
            for (s, n, is_ctx) in fblocks:
                pu, pg = ups[ei % 2], gps[ei % 2]
                ei += 1
                for k in range(KT):
                    fw.mm(pu[:, :n], wu[(i // 2) % 2][:, k, (i % 2) * 128:(i % 2 + 1) * 128], hT[:, k, s:s + n],
                          start=(k == 0), stop=(k == KT - 1))
                for k in range(KT):
                    fw.mm(pg[:, :n], wg[(i // 2) % 2][:, k, (i % 2) * 128:(i % 2 + 1) * 128], hT[:, k, s:s + n],
                          start=(k == 0), stop=(k == KT - 1))
                fw.copy(uT[r][:, s:s + n], pu[:, :n], eng="scalar")
                fw.copy(gT[r][:, s:s + n], pg[:, :n], eng="scalar")
                fw.act(tmp[r][:, s:s + n], pg[:, :n], AF.Identity, bias=pk[l][:, cb0 + i:cb0 + i + 1],
                       scale=pk[l][:, cw0 + FT + i:cw0 + FT + i + 1])
            w0 = pk[l][:, cw0 + i:cw0 + i + 1]
            w1 = pk[l][:, cw0 + FT + i:cw0 + FT + i + 1]
            w2 = pk[l][:, cw0 + 2 * FT + i:cw0 + 2 * FT + i + 1]
            cb = pk[l][:, cb0 + i:cb0 + i + 1]
            for (s, e) in segs:
                fw.stt(tmp[r][:, s + 1:e], gT[r][:, s:e - 1], w0, tmp[r][:, s + 1:e], ALU.mult, ALU.add)
                fw.stt(tmp[r][:, s:e - 1], gT[r][:, s + 1:e], w2, tmp[r][:, s:e - 1], ALU.mult, ALU.add)
                fw.act(tmp[r][:, s:e], tmp[r][:, s:e], AF.Silu)
                fw.tt(aT[r][:, s:e], tmp[r][:, s:e], uT[r][:, s:e], ALU.mult)
                fw.dma(ACT[i * 128:(i + 1) * 128, s:e], aT[r][:, s:e])
        fw.pop()
        fw.push()
        wd2 = [fw.sbuf(f"wd{j2}", [128, FT, 256], BF16) for j2 in range(KT // 2)]
        wdv = w_down.t[l].rearrange("(f p) c -> p f c", p=128)
        for j2 in range(KT // 2):
            fw.dma(wd2[j2][:], w_down.v(wdv[:, :, j2 * 256:(j2 + 1) * 256]), eng="gpsimd")
        ab = [fw.sbuf(f"ab{i}", [128, FT, 512], BF16) for i in range(2)]
        dps = [fw.psum(f"dps{i}", [128, 512]) for i in range(4)]
        av = ACT.t.rearrange("(f p) t -> p f t", p=128)
        ei = 0
        for bi, (s, n, is_ctx) in enumerate(fblocks):
            j = NB if is_ctx else b
            a = ab[bi % 2]
            fw.dma(a[:, :, :n], ACT.v(av[:, :, s:s + n]))
            for jt in range(KT):
                ps = dps[ei % 4]
                ei += 1
                for f in range(FT):
                    fw.mm(ps[:, :n], wd2[jt // 2][:, f, (jt % 2) * 128:(jt % 2 + 1) * 128], a[:, f, :n],
                          start=(f == 0), stop=(f == FT - 1))
                fw.stt(xT[:, jt, s:s + n], ps[:, :n], modT[l][:, 40 + jt, j:j + 1], xT[:, jt, s:s + n],
                       ALU.mult, ALU.add)
        fw.pop()

    NT = T // 128
    NTC = TC // 128
    lat_blocks = [bl for bl in blocks if not bl[2]]
    ctx_blocks = [bl for bl in blocks if bl[2]]

    def head_norm_rope(raw, outs, blkb, hd, g_ap, rotb, cosT, sinT, s, n, is_ctx, tset, masks=None):
        sqb, rs, qg, t1, t2, nps, npr = tset
        fw.act(sqb[:, :n], raw, AF.Square)
        fw.mm(nps[:, :n], blkb[:], sqb[:, :n])
        fw.act(rs[:, :n], nps[:, :n], AF.Sqrt, bias=epsb[:, 0:1], scale=1.0 / hd)
        fw.recip(rs[:, :n], rs[:, :n])
        fw.stt(qg[:, :n], raw, g_ap, rs[:, :n], ALU.mult, ALU.mult)
        if is_ctx:
            res = qg
        else:
            fw.mm(npr[:, :n], rotb[:], qg[:, :n])
            fw.tt(t1[:, :n], qg[:, :n], cosT[:, s - TC:s - TC + n], ALU.mult)
            fw.tt(t2[:, :n], npr[:, :n], sinT[:, s - TC:s - TC + n], ALU.mult)
            fw.tt(t1[:, :n], t1[:, :n], t2[:, :n], ALU.add, eng="gpsimd")
            res = t1
        if masks is None:
            fw.copy(outs[0], res[:, :n], eng="gpsimd")
        else:
            for m, o in enumerate(outs):
                fw.ts(o, res[:, :n], masks[:, m:m + 1], ALU.mult, eng="gpsimd")

    def prep_sets(tag):
        sets = []
        for i in range(3):
            sets.append((fw.sbuf(f"{tag}sqb{i}", [128, 512], BF16), fw.sbuf(f"{tag}rs{i}", [128, 512]),
                         fw.sbuf(f"{tag}qg{i}", [128, 512], BF16), fw.sbuf(f"{tag}t1{i}", [128, 512]),
                         fw.sbuf(f"{tag}t2{i}", [128, 512]), fw.psum(f"{tag}nps{i}", [128, 512]),
                         fw.psum(f"{tag}npr{i}", [128, 512])))
        return sets

    def make_vdup(vrow0, nheads, Vd, vtmp, tps):
        ntile = (nheads * 64) // 128
        for vt in range(ntile):
            fw.dma(vtmp[:, :], PT[vrow0 + vt * 128:vrow0 + (vt + 1) * 128, :])
            for i in range(NT):
                fw.transpose(tps[:, i % 4, :], vtmp[:, i * 128:(i + 1) * 128], ident[:])
                for hh in range(2):
                    h = vt * 2 + hh
                    fw.copy(Vd[h][:, i, 0:64], tps[:, i % 4, hh * 64:(hh + 1) * 64], eng="vector")
                    fw.copy(Vd[h][:, i, 64:128], tps[:, i % 4, hh * 64:(hh + 1) * 64], eng="scalar")

    def attn_head(qviews, kviews, Vd_h, nmaps, scale, qb, sps_l, pT_l, oacc, dacc, dsum, cnt):
        (s, n, is_ctx) = qb
        kts = list(range(NTC)) if is_ctx else list(range(NT))
        steps = [(ki, kt, m) for ki, kt in enumerate(kts) for m in range(nmaps)]
        c0 = cnt[0]
        cnt[0] += len(steps)

        def score(i):
            ki, kt, m = steps[i]
            sp = sps_l[(c0 + i) % len(sps_l)]
            fw.mm(sp[:, :n], kviews[m](kt), qviews[m](s, n))
        depth = len(sps_l) - 1
        for i in range(min(depth, len(steps))):
            score(i)
        for i, (ki, kt, m) in enumerate(steps):
            if i + depth < len(steps):
                score(i + depth)
            sp = sps_l[(c0 + i) % len(sps_l)]
            pT = pT_l[(c0 + i) % len(pT_l)]
            fw.act(pT[:, :n], sp[:, :n], AF.Exp, scale=scale)
            fw.mm(oacc[m][:, :n], Vd_h[:, kt, :], pT[:, :n], start=(ki == 0), stop=(ki == len(kts) - 1))
            ds = dsum[m]
            de = "vector" if m == 0 else "gpsimd"
            if ki == 0:
                fw.copy(ds[:, :n], pT[:, :n], eng=de)
            else:
                fw.tt(ds[:, :n], pT[:, :n], ds[:, :n], ALU.add, eng=de)
        for m in range(nmaps):
            fw.mm(dacc[m][:, :n], ones[:], dsum[m][:, :n])

    def gqa_phase(l, b, need_ctx):
        fw.push()
        onesb = onesb_g
        qn = fw.sbuf("qn", [128, 2, T], BF16)
        kd = fw.sbuf("kd", [128, 2, T], BF16)
        Vd = [fw.sbuf(f"Vd{h}", [128, NT, 128], BF16) for h in range(2)]
        qblocks = (ctx_blocks if need_ctx else []) + lat_blocks
        fw.push()
        cosT = fw.sbuf("cosT", [128, TL]); sinT = fw.sbuf("sinT", [128, TL]); rot = fw.sbuf("rot", [128, 128])
        fw.dma(cosT[:], consts["cos_gq"][:]); fw.dma(sinT[:], consts["sin_gq"][:]); fw.dma(rot[:], consts["rot_gq"][:])
        rotb = fw.sbuf("rotb", [128, 128], BF16)
        fw.copy(rotb[:], rot[:])
        raw = [fw.sbuf(f"raw{i}", [128, 512]) for i in range(3)]
        tsets = prep_sets("g")
        tps = fw.psum("tps", [128, 4, 128])
        vtmp = fw.sbuf("vtmp", [128, T])
        ri = 0
        for t in range(2):
            for (s, n, is_ctx) in qblocks:
                r = raw[ri % 3]; ri += 1
                fw.dma(r[:, :n], PT[GQA_OFF + t * 128:GQA_OFF + (t + 1) * 128, s:s + n])
                head_norm_rope(r[:, :n], [qn[:, t, s:s + n]], blk64b, 64, pcol(l, "gq_qg"), rotb, cosT, sinT,
                               s, n, is_ctx, tsets[ri % 3])
            for (s, n, is_ctx) in blocks:
                r = raw[ri % 3]; ri += 1
                for hh in range(2):
                    fw.dma(r[hh * 64:(hh + 1) * 64, :n], PT[GQA_OFF + 256 + t * 64:GQA_OFF + 256 + (t + 1) * 64, s:s + n])
                head_norm_rope(r[:, :n], [kd[:, t, s:s + n]], blk64b, 64, pcol(l, "gq_kg"), rotb, cosT, sinT,
                               s, n, is_ctx, tsets[ri % 3])
        make_vdup(GQA_OFF + 384, 2, Vd, vtmp, tps)
        fw.pop()
        sps_l = [fw.psum(f"sps{i}", [128, 512]) for i in range(4)]
        pT_l = [fw.sbuf(f"pT{i}", [128, 512], BF16) for i in range(4)]
        oacc2 = [fw.psum(f"oacc{i}", [128, 512]) for i in range(2)]
        dacc2 = [fw.psum(f"dacc{i}", [128, 512]) for i in range(2)]
        dsum_l = [fw.sbuf(f"dsum{i}", [128, 512]) for i in range(2)]
        rec2 = [fw.sbuf(f"rec{i}", [128, 512]) for i in range(2)]
        ob = [fw.sbuf(f"ob{i}", [128, 512], BF16) for i in range(2)]
        cnt = [0]
        oi = 0
        for h in range(4):
            t, g = h // 2, h % 2
            ph = 64 * g
            qv = [lambda s, n, t=t, ph=ph: qn[ph:ph + 64, t, s:s + n]]
            kv = [lambda kt, t=t, ph=ph: kd[ph:ph + 64, t, kt * 128:(kt + 1) * 128]]
            for qb in qblocks:
                (s, n, is_ctx) = qb
                oacc, dacc, rec = [oacc2[oi % 2]], [dacc2[oi % 2]], rec2[oi % 2]
                attn_head(qv, kv, Vd[t], 1, 0.125, qb, sps_l, pT_l, oacc, dacc, [dsum_l[oi % 2]], cnt)
                fw.recip(rec[ph:ph + 64, :n], dacc[0][ph:ph + 64, :n])
                o = ob[oi % 2]; oi += 1
                fw.tt(o[ph:ph + 64, :n], oacc[0][ph:ph + 64, :n], rec[ph:ph + 64, :n], ALU.mult)
                fw.dma(OT[768 + h * 64:768 + (h + 1) * 64, s:s + n], o[ph:ph + 64, :n])
        fw.pop()

    def da_phase(l, b, need_ctx):
        lam_init = 0.8 - 0.6 * float(np.exp(-0.3 * l))
        fw.push()
        onesb = onesb_g
        lamb = fw.sbuf("lamb_s", [128, 128])
        fw.dma(lamb[:], lamb_d[l])
        lt = fw.sbuf("lt", [128, 64])
        lsum = fw.sbuf("lsum", [128, 2])
        nlam = fw.sbuf("nlam", [128, 1])
        sg = fw.sbuf("sg", [128, 1])
        fw.tt(lt[:, 0:32], lamb[:, 0:32], lamb[:, 32:64], ALU.mult)
        fw.tt(lt[:, 32:64], lamb[:, 64:96], lamb[:, 96:128], ALU.mult)
        fw.op("vector", "reduce_sum", [lt[:]], [lsum[:]], lsum[:, 0:1].ap, lt[:, 0:32].ap, AX.X)
        fw.op("vector", "reduce_sum", [lt[:]], [lsum[:]], lsum[:, 1:2].ap, lt[:, 32:64].ap, AX.X)
        fw.act(lsum[:], lsum[:], AF.Exp)
        fw.stt(nlam[:], lsum[:, 1:2], -lam_init, lsum[:, 0:1], ALU.add, ALU.subtract)
        fw.ts(sg[:], pcol(l, "da_sub"), 1.0 - lam_init, ALU.mult)
        qm = [fw.sbuf(f"qm{m}", [128, 2, T], BF16) for m in range(2)]
        kn = fw.sbuf("kn", [128, 2, T], BF16)
        Vd = [fw.sbuf(f"Vd{h}", [128, NT, 128], BF16) for h in range(4)]
        qblocks = (ctx_blocks if need_ctx else []) + lat_blocks
        fw.push()
        cosT = fw.sbuf("cosT", [128, TL]); sinT = fw.sbuf("sinT", [128, TL]); rot = fw.sbuf("rot", [128, 128])
        fw.dma(cosT[:], consts["cos_da"][:]); fw.dma(sinT[:], consts["sin_da"][:]); fw.dma(rot[:], consts["rot_da"][:])
        rotb = fw.sbuf("rotb", [128, 128], BF16)
        fw.copy(rotb[:], rot[:])
        blk32 = fw.sbuf("blk32", [128, 128])
        fw.dma(blk32[:], consts["blk32"][:])
        blk32b = fw.sbuf("blk32b", [128, 128], BF16)
        fw.copy(blk32b[:], blk32[:])
        dmask = fw.sbuf("dmask", [128, 2])
        fw.dma(dmask[:], consts["dmask"][:])
        raw = [fw.sbuf(f"raw{i}", [128, 512]) for i in range(3)]
        tsets = prep_sets("d")
        tps = fw.psum("tps", [128, 4, 128])
        vtmp = fw.sbuf("vtmp", [128, T])
        ri = 0
        for t in range(2):
            for (s, n, is_ctx) in qblocks:
                r = raw[ri % 3]; ri += 1
                fw.dma(r[:, :n], PT[DA_OFF + t * 128:DA_OFF + (t + 1) * 128, s:s + n])
                head_norm_rope(r[:, :n], [qm[0][:, t, s:s + n], qm[1][:, t, s:s + n]], blk32b, 32,
                               pcol(l, "da_qg"), rotb, cosT, sinT, s, n, is_ctx, tsets[ri % 3], masks=dmask)
            for (s, n, is_ctx) in blocks:
                r = raw[ri % 3]; ri += 1
                fw.dma(r[:, :n], PT[DA_OFF + 256 + t * 128:DA_OFF + 256 + (t + 1) * 128, s:s + n])
                head_norm_rope(r[:, :n], [kn[:, t, s:s + n]], blk32b, 32, pcol(l, "da_kg"), rotb, cosT, sinT,
                               s, n, is_ctx, tsets[ri % 3])
        make_vdup(DA_OFF + 512, 4, Vd, vtmp, tps)
        fw.pop()
        nps = fw.psum("anps", [128, 512])
        tmps = [fw.sbuf(f"nt{i}", [128, 512]) for i in range(2)]
        sps_l = [fw.psum(f"sps{i}", [128, 512]) for i in range(3)]
        pT_l = [fw.sbuf(f"pT{i}", [128, 512], BF16) for i in range(4)]
        oacc = [fw.psum(f"oacc{m}", [128, 512]) for m in range(2)]
        dacc = [fw.psum(f"dacc{m}", [128, 512]) for m in range(2)]
        dsum_l = [[fw.sbuf(f"dsum{i}_{m}", [128, 512]) for m in range(2)] for i in range(2)]
        hq = [0]
        rec = [fw.sbuf(f"rec{m}", [128, 512]) for m in range(2)]
        o1 = fw.sbuf("o1", [128, 512])
        osb = fw.sbuf("osb", [128, 512])
        ob = [fw.sbuf(f"ob{i}", [128, 512], BF16) for i in range(2)]
        sq, rs = tmps[0], tmps[1]
        cnt = [0]
        oi = 0
        for t in range(2):
            for qb in qblocks:
                (s, n, is_ctx) = qb
                for g in range(2):
                    h = 2 * t + g
                    ph = 64 * g
                    qv = [lambda s, n, t=t, ph=ph, m=m: qm[m][ph:ph + 64, t, s:s + n] for m in range(2)]
                    kv = [lambda kt, t=t, ph=ph: kn[ph:ph + 64, t, kt * 128:(kt + 1) * 128]] * 2
                    attn_head(qv, kv, Vd[h], 2, 32 ** -0.5, qb, sps_l, pT_l, oacc, dacc, dsum_l[hq[0] % 2], cnt)
                    hq[0] += 1
                    for m in range(2):
                        fw.recip(rec[m][ph:ph + 64, :n], dacc[m][ph:ph + 64, :n])
                    fw.tt(osb[ph:ph + 64, :n], oacc[0][ph:ph + 64, :n], rec[0][ph:ph + 64, :n], ALU.mult)
                    fw.tt(o1[ph:ph + 64, :n], oacc[1][ph:ph + 64, :n], rec[1][ph:ph + 64, :n], ALU.mult)
                    fw.stt(osb[ph:ph + 64, :n], o1[ph:ph + 64, :n], nlam[ph:ph + 64, 0:1], osb[ph:ph + 64, :n],
                           ALU.mult, ALU.add)
                fw.act(sq[:, :n], osb[:, :n], AF.Square)
                fw.mm(nps[:, :n], blk64[:], sq[:, :n])
                fw.act(rs[:, :n], nps[:, :n], AF.Sqrt, bias=epsb[:, 0:1], scale=1.0 / 64)
                fw.recip(rs[:, :n], rs[:, :n])
                o = ob[oi % 2]; oi += 1
                fw.stt(o[:, :n], osb[:, :n], sg[:, 0:1], rs[:, :n], ALU.mult, ALU.mult)
                fw.dma(OT[256 + t * 128:256 + (t + 1) * 128, s:s + n], o[:, :n])
        fw.pop()

    def chunk_order(rev):
        if not rev:
            return list(range(NT))
        return list(range(NTC - 1, -1, -1)) + list(range(NT - 1, NTC - 1, -1))

    def cumsum_chunks(A, B, rev):
        cur, oth = A, B
        s = 1
        while s < 128:
            cv = cur.t[:, :].rearrange("p (c i) -> p c i", i=128)
            ov = oth.t[:, :].rearrange("p (c i) -> p c i", i=128)
            if not rev:
                fw.tt(oth.v(ov[:, :, s:]), cur.v(cv[:, :, s:]), cur.v(cv[:, :, :128 - s]), ALU.add)
                fw.copy(oth.v(ov[:, :, :s]), cur.v(cv[:, :, :s]), eng="gpsimd")
            else:
                fw.tt(oth.v(ov[:, :, :128 - s]), cur.v(cv[:, :, :128 - s]), cur.v(cv[:, :, s:]), ALU.add)
                fw.copy(oth.v(ov[:, :, 128 - s:]), cur.v(cv[:, :, 128 - s:]), eng="gpsimd")
            cur, oth = oth, cur
            s *= 2
        return cur, oth

    def gla_phase(l, b, need_ctx):
        fw.push()
        Fb = [fw.sbuf(f"F{i}", [128, T]) for i in range(4)]
        F1, F2, F3, F4 = Fb
        Vdup = fw.sbuf("Vdup", [128, NT, 4, 128], BF16)
        QM = [fw.sbuf(f"QM{h}", [128, T], BF16) for h in range(4)]
        KTt = fw.sbuf("KTt", [128, T], BF16)
        KH = fw.sbuf("KH", [128, T], BF16)
        oaccT = fw.sbuf("oaccT", [128, 2, T])
        Pend = fw.sbuf("Pend", [128, NT])
        gfb = fw.sbuf("gfb", [48, T])
        a2 = fw.sbuf("a2", [48, 128])
        nab = fw.sbuf("nab", [128, 2])
        hm4 = fw.sbuf("hm4", [128, 4])
        tri = fw.sbuf("tri", [128, 4, 128])
        fw.dma(hm4[:], consts["hmask4s"][:])
        for d in range(2):
            fw.dma(a2[32 * d:32 * d + 16, :], gla_a2_d[l, d])
            fw.dma(gfb[32 * d:32 * d + 16, :], PT[GLA_OFF + 512 + 16 * d:GLA_OFF + 528 + 16 * d, :])
        c0 = PACK["gla_ab"][0]
        fw.ts(nab[:], pk[l][:, c0:c0 + 2], -1.0, ALU.mult)
        S = fw.sbuf("Sst", [128, 64])
        Sdup = fw.sbuf("Sdup", [128, 128], BF16)
        KHt = [fw.sbuf(f"KHt{i}", [128, 640], BF16) for i in range(2)]
        for i in range(2):
            fw.memset(KHt[i][:], 0.0)
        AT = [fw.sbuf(f"AT{i}", [128, 4, 128], BF16) for i in range(2)]
        tpsb = fw.psum("tpsb", [128, 4, 128], BF16)
        tps = fw.psum("tps", [128, 4, 128])
        aps = [fw.psum(f"aps{i}", [128, 4, 128]) for i in range(2)]
        ops = [fw.psum(f"ops{i}", [128, 4, 128]) for i in range(2)]
        sps = fw.psum("sps", [128, 64])
        zps = fw.psum("zps", [128, 512])
        for vt in range(2):
            fw.dma(F1[:], PT[GLA_OFF + 256 + vt * 128:GLA_OFF + 256 + (vt + 1) * 128, :])
            for i in range(NT):
                fw.transpose(tps[:, i % 4, :], F1[:, i * 128:(i + 1) * 128], ident[:])
                for hh in range(2):
                    h = vt * 2 + hh
                    fw.copy(Vdup[:, i, h, 0:64], tps[:, i % 4, hh * 64:(hh + 1) * 64], eng="vector")
                    fw.copy(Vdup[:, i, h, 64:128], tps[:, i % 4, hh * 64:(hh + 1) * 64], eng="scalar")
        fw.dma(F1[:], PT[GLA_OFF:GLA_OFF + 128, :])
        fw.dma(F2[:], PT[GLA_OFF + 128:GLA_OFF + 256, :])
        for d in range(2):
            rev = d == 1
            fw.dma(tri[:], consts[f"tri4_{d}"][:])
            for (s, n, is_ctx) in blocks:
                fw.mm(zps[:, :n], a2[32 * d:32 * d + 16, :], gfb[32 * d:32 * d + 16, s:s + n])
                fw.act(F3[:, s:s + n], zps[:, :n], AF.Exp, bias=nab[:, d:d + 1], scale=-1.0)
            fw.act(F3[:], F3[:], AF.Ln, bias=1.0)
            fw.ts(F3[:], F3[:], -1.0 / 16.0, ALU.mult)
            bb, ff = cumsum_chunks(F3, F4, rev)
            bv = bb.t[:, :].rearrange("p (c i) -> p c i", i=128)
            eidx = 0 if rev else 127
            fw.act(Pend[:], bb.v(bv[:, :, eidx]), AF.Exp)
            fw.act(ff[:], bb[:], AF.Exp)
            for h in range(4):
                fw.stt(QM[h][:], F1[:], hm4[:, h:h + 1], ff[:], ALU.mult, ALU.mult,
                       eng=("gpsimd" if h % 2 else "vector"))
            fw.act(ff[:], bb[:], AF.Exp, scale=-1.0)
            fw.tt(KTt[:], F2[:], ff[:], ALU.mult)
            for c in range(NT):
                fw.act(ff[:, c * 128:(c + 1) * 128], bb[:, c * 128:(c + 1) * 128], AF.Exp,
                       bias=bb[:, c * 128 + eidx:c * 128 + eidx + 1], scale=-1.0)
            fw.tt(KH[:], F2[:], ff[:], ALU.mult, eng="gpsimd")
            first = True
            for ci, c in enumerate(chunk_order(rev)):
                cs = slice(c * 128, (c + 1) * 128)
                is_ctx = c < NTC
                kht = KHt[ci % 2]
                at = AT[ci % 2]
                ap_, op_ = aps[ci % 2], ops[ci % 2]
                fw.transpose(tpsb[:, 0, :], KH[:, cs], identb[:])
                kv = kht.t[:, :].rearrange("p (h x) -> p h x", x=160)
                fw.copy(kht.v(kv[:, :, 0:32]), tpsb.v(tpsb.t[:, 0, :].rearrange("p (h x) -> p h x", x=32)))
                want_out = need_ctx or not is_ctx
                if want_out:
                    for h in range(4):
                        fw.mm(ap_[:, h, :], KTt[:, cs], QM[h][:, cs])
                    fw.tt(at[:], ap_[:], tri[:], ALU.mult)
                    for h in range(4):
                        if not first:
                            fw.mm(op_[:, h, :], Sdup[:], QM[h][:, cs], start=True, stop=False)
                        fw.mm(op_[:, h, :], Vdup[:, c, h, :], at[:, h, :], start=first, stop=True)
                    o4 = op_.t[:, :, :].rearrange("p (a g) t -> p a g t", g=2)
                    for g in range(2):
                        dst = oaccT[64 * g:64 * g + 64, :, cs]
                        src = op_.v(o4[64 * g:64 * g + 64, :, g, :])
                        if d == 0:
                            fw.copy(dst, src, eng=("vector" if g == 0 else "scalar"))
                        else:
                            fw.tt(dst, src, dst, ALU.add, eng="vector")
                for h in range(4):
                    fw.mm(sps[:, :], kht[:, h * 128:(h + 1) * 128], Vdup[:, c, h, 0:64], start=(h == 0), stop=(h == 3))
                if first:
                    fw.copy(S[:], sps[:])
                else:
                    fw.stt(S[:], S[:], Pend[:, c:c + 1], sps[:], ALU.mult, ALU.add)
                fw.copy(Sdup[:, 0:64], S[:], eng="scalar")
                fw.copy(Sdup[:, 64:128], S[:], eng="gpsimd")
                first = False
        sq, rs, rr = F3, F4, F1
        ob = [fw.sbuf(f"gob{i}", [128, 512], BF16) for i in range(2)]
        oi = 0
        qblocks = (ctx_blocks if need_ctx else []) + lat_blocks
        for t in range(2):
            for (s, n, is_ctx) in qblocks:
                fw.dma(rr[:, :n], PT[GLA_OFF + 544 + t * 128:GLA_OFF + 544 + (t + 1) * 128, s:s + n])
                fw.act(rr[:, :n], rr[:, :n], AF.Silu)
                fw.act(sq[:, :n], oaccT[:, t, s:s + n], AF.Square)
                fw.mm(zps[:, :n], blk64[:], sq[:, :n])
                fw.act(rs[:, :n], zps[:, :n], AF.Sqrt, bias=epsb[:, 0:1], scale=1.0 / 64)
                fw.recip(rs[:, :n], rs[:, :n])
                fw.stt(sq[:, :n], oaccT[:, t, s:s + n], pcol(l, "gla_ng"), rs[:, :n], ALU.mult, ALU.mult)
                o = ob[oi % 2]; oi += 1
                fw.tt(o[:, :n], sq[:, :n], rr[:, :n], ALU.mult, eng="gpsimd")
                fw.dma(OT[512 + t * 128:512 + (t + 1) * 128, s:s + n], o[:, :n])
        fw.pop()

    RWS = fw.dram("RWS", [2, 2, 8, 128, T], BF16)
    VDs = fw.dram("VDs", [128, NT, 4, 128], BF16)
    BON = fw.dram("BON", [256, T], F32)
    PENDs = fw.dram("PENDs", [128, 2, 2, NT], F32)
    GATE = fw.dram("GATE", [256, T], F32)

    def shift_mix(dst, raw, l, ct):
        m0 = pcol(l, "rw_mu0", ct)
        m1 = pcol(l, "rw_mu1", ct)
        c0 = pcol(l, "rw_c0", ct)
        fw.ts(dst[:, :], raw[:, :], c0, ALU.mult)
        for (s, e) in ((0, TC), (TC, T)):
            fw.stt(dst[:, s + 1:e], raw[:, s:e - 1], m0, dst[:, s + 1:e], ALU.mult, ALU.add)
            fw.stt(dst[:, s:e - 1], raw[:, s + 1:e], m1, dst[:, s:e - 1], ALU.mult, ALU.add, eng="gpsimd")

    def rwkv_phase(l, b, need_ctx):
        fw.push()
        lora = [fw.sbuf(f"lora{i}", [128, T], BF16) for i in range(3)]
        w2s = fw.sbuf("w2s", [128, 256], BF16)
        a2s = fw.sbuf("a2s", [128, 256], BF16)
        g2s = fw.sbuf("g2s", [128, 256], BF16)
        fw.dma(w2s[:], rw_w2_d[l], eng="gpsimd")
        fw.dma(a2s[:], rw_a2_d[l], eng="gpsimd")
        fw.dma(g2s[:], rw_g2_d[l], eng="gpsimd")
        hm2 = fw.sbuf("hm2", [128, 4])
        fw.dma(hm2[:], consts["hm2"][:])
        c0 = PACK["rw_mu0"][0]
        c1 = PACK["rw_mu1"][0]
        cc = PACK["rw_c0"][0]
        fw.tt(pk[l][:, cc:cc + 9], pk[l][:, c0:c0 + 9], pk[l][:, c1:c1 + 9], ALU.add)
        fw.ts(pk[l][:, cc:cc + 9], pk[l][:, cc:cc + 9], -1.0, ALU.mult, 1.0, ALU.add)
        ck = PACK["rw_ka"][0]
        co = PACK["rw_omka"][0]
        fw.ts(pk[l][:, co:co + 2], pk[l][:, ck:ck + 2], -1.0, ALU.mult, 1.0, ALU.add)
        Bf = [fw.sbuf(f"B{i}", [128, T]) for i in range(10)]
        stgb = [fw.sbuf(f"stgb{i}", [128, T], BF16) for i in range(2)]
        zps = [fw.psum(f"zps{i}", [128, 512]) for i in range(3)]
        tps = fw.psum("tps", [128, 4, 128])
        vd = [fw.sbuf(f"vd{i}", [128, 4, 128], BF16) for i in range(2)]
        eps12 = fw.sbuf("eps12", [128, 1])
        fw.memset(eps12[:], 1e-12)
        raw, sh = Bf[0], Bf[1]
        for i, fn in ((0, AF.Tanh), (1, None), (2, AF.Sigmoid)):
            fw.dma(raw[:], PT[RW_OFF + (6 + i) * 128:RW_OFF + (7 + i) * 128, :])
            shift_mix(sh, raw, l, 6 + i)
            if fn is None:
                fw.copy(lora[i][:], sh[:])
            else:
                fw.act(lora[i][:], sh[:], fn)
        for vt in range(2):
            fw.dma(raw[:], PT[RW_OFF + 512 + vt * 128:RW_OFF + 512 + (vt + 1) * 128, :])
            shift_mix(sh, raw, l, 4 + vt)
            for i in range(NT):
                fw.transpose(tps[:, i % 4, :], sh[:, i * 128:(i + 1) * 128], ident[:])
                v_ = vd[i % 2]
                for hh in range(2):
                    fw.copy(v_[:, hh, 0:64], tps[:, i % 4, hh * 64:(hh + 1) * 64], eng="vector")
                    fw.copy(v_[:, hh, 64:128], tps[:, i % 4, hh * 64:(hh + 1) * 64], eng="scalar")
                fw.dma(VDs[:, i, 2 * vt:2 * vt + 2, :], v_[:, 0:2, :])
        Pend = fw.dram("Pend_d", [2, 2, 128, NT], F32) if False else None
        pend_s = fw.sbuf("pend_s", [128, 2, 2, NT])
        Fk, Fkk, Fr, Fbon, Ll, Aa, Bb, Fa, Fkd, Fb = Bf
        for tau in range(2):
            fw.dma(raw[:], PT[RW_OFF + 256 + tau * 128:RW_OFF + 256 + (tau + 1) * 128, :]) if False else None
            fw.dma(Ll[:], PT[RW_OFF + 256 + tau * 128:RW_OFF + 256 + (tau + 1) * 128, :])
            shift_mix(Fk, Ll, l, 2 + tau)
            fw.dma(Ll[:], PT[RW_OFF + tau * 128:RW_OFF + (tau + 1) * 128, :])
            shift_mix(Fr, Ll, l, tau)
            fw.ts(Fkk[:], Fk[:], pcol(l, "rw_kk", tau), ALU.mult)
            for (s, n, is_ctx) in blocks:
                zp = zps[0]
                fw.act(Aa[:, s:s + n], Fkk[:, s:s + n], AF.Square)
                fw.mm(zp[:, :n], blk64[:], Aa[:, s:s + n])
                fw.act(Aa[:, s:s + n], zp[:, :n], AF.Sqrt, bias=eps12[:, 0:1], scale=1.0)
            fw.recip(Aa[:], Aa[:])
            fw.tt(Fkk[:], Fkk[:], Aa[:], ALU.mult)
            for bi, (s, n, is_ctx) in enumerate(blocks):
                zp = zps[bi % 3]
                fw.mm(zp[:, :n], g2s[:, tau * 128:(tau + 1) * 128], lora[2][:, s:s + n])
                fw.copy(Aa[:, s:s + n], zp[:, :n], eng="scalar")
            fw.dma(GATE[tau * 128:(tau + 1) * 128, :], Aa[:])
            for d in range(2):
                rev = d == 1
                eidx = 0 if rev else 127
                ph = 64 * d
                for bi, (s, n, is_ctx) in enumerate(blocks):
                    zp = zps[bi % 3]
                    fw.mm(zp[:, :n], w2s[ph:ph + 64, tau * 128:(tau + 1) * 128], lora[0][ph:ph + 64, s:s + n])
                    fw.act(Ll[:, s:s + n], zp[:, :n], AF.Sigmoid, bias=pcol(l, "rw_w0", d * 2 + tau))
                    zp2 = zps[(bi + 1) % 3]
                    fw.mm(zp2[:, :n], a2s[ph:ph + 64, tau * 128:(tau + 1) * 128], lora[1][ph:ph + 64, s:s + n])
                    fw.act(Fa[:, s:s + n], zp2[:, :n], AF.Sigmoid, bias=pcol(l, "rw_a0", d * 2 + tau))
                fw.ts(Ll[:], Ll[:], -0.6065306597126334, ALU.mult)
                cur = Ll
                pp = [Aa, Bb]
                st = 1
                k_ = 0
                while st < 128:
                    oth = pp[k_ % 2]
                    cv = cur.t[:, :].rearrange("p (c i) -> p c i", i=128)
                    ov = oth.t[:, :].rearrange("p (c i) -> p c i", i=128)
                    if not rev:
                        fw.tt(oth.v(ov[:, :, st:]), cur.v(cv[:, :, st:]), cur.v(cv[:, :, :128 - st]), ALU.add)
                        fw.copy(oth.v(ov[:, :, :st]), cur.v(cv[:, :, :st]), eng="scalar")
                    else:
                        fw.tt(oth.v(ov[:, :, :128 - st]), cur.v(cv[:, :, :128 - st]), cur.v(cv[:, :, st:]), ALU.add)
                        fw.copy(oth.v(ov[:, :, 128 - st:]), cur.v(cv[:, :, 128 - st:]), eng="scalar")
                    cur = oth
                    st *= 2
                    k_ += 1
                assert cur is Aa
                cum = Aa
                cvw = cum.t[:, :].rearrange("p (c i) -> p c i", i=128)
                fw.act(pend_s[:, d, tau, :], cum.v(cvw[:, :, eidx]), AF.Exp)
                fw.tt(Ll[:], cum[:], Ll[:], ALU.subtract)
                fw.ts(Fkd[:], Fa[:], pcol(l, "rw_ka", tau), ALU.mult, pcol(l, "rw_omka", tau), ALU.add)
                fw.tt(Fkd[:], Fkd[:], Fk[:], ALU.mult, eng="gpsimd")
                fw.tt(Fb[:], Fkk[:], Fa[:], ALU.mult, eng="gpsimd")
                fw.stt(Fa[:], Fr[:], pcol(l, "rw_rk", tau), Fkd[:], ALU.mult, ALU.mult)
                for bi, (s, n, is_ctx) in enumerate(blocks):
                    zp = zps[bi % 3]
                    fw.mm(zp[:, :n], blk64[:], Fa[:, s:s + n])
                    if d == 0:
                        fw.copy(Fbon[:, s:s + n], zp[:, :n], eng="scalar")
                    else:
                        fw.tt(Fbon[:, s:s + n], zp[:, :n], Fbon[:, s:s + n], ALU.add)
                si = [0]

                def emit(arr_idx, fn):
                    o = stgb[si[0] % 2]
                    si[0] += 1
                    fn(o)
                    fw.dma(RWS[d, tau, arr_idx], o[:])
                fw.act(Bb[:], Ll[:], AF.Exp)
                for hh in range(2):
                    emit(hh, lambda o, hh=hh: fw.stt(o[:], Fkk[:], hm2[:, 2 + hh:3 + hh], Bb[:], ALU.mult, ALU.mult,
                                                    eng=("vector" if hh == 0 else "gpsimd")))
                fw.act(Bb[:], cum[:], AF.Exp)
                for hh in range(2):
                    emit(2 + hh, lambda o, hh=hh: fw.stt(o[:], Fr[:], hm2[:, hh:hh + 1], Bb[:], ALU.mult, ALU.mult,
                                                        eng=("vector" if hh == 0 else "gpsimd")))
                fw.act(Bb[:], cum[:], AF.Exp, scale=-1.0)
                emit(4, lambda o: fw.tt(o[:], Fb[:], Bb[:], ALU.mult))
                emit(5, lambda o: fw.tt(o[:], Fkd[:], Bb[:], ALU.mult, eng="gpsimd"))
                for c in range(NT):
                    fw.act(Bb[:, c * 128:(c + 1) * 128], cum[:, c * 128:(c + 1) * 128], AF.Exp,
                           bias=cum[:, c * 128 + eidx:c * 128 + eidx + 1], scale=-1.0)
                emit(6, lambda o: fw.tt(o[:], Fb[:], Bb[:], ALU.mult))
                emit(7, lambda o: fw.tt(o[:], Fkd[:], Bb[:], ALU.mult, eng="gpsimd"))
            fw.dma(Ll[:], PT[RW_OFF + 512 + tau * 128:RW_OFF + 512 + (tau + 1) * 128, :])
            shift_mix(Aa, Ll, l, 4 + tau)
            fw.tt(Aa[:], Aa[:], Fbon[:], ALU.mult)
            fw.dma(BON[tau * 128:(tau + 1) * 128, :], Aa[:])
        fw.dma(PENDs[:], pend_s[:])
        fw.pop()

        fw.push()
        pend = fw.sbuf("pend", [128, 2, 2, NT])
        fw.dma(pend[:], PENDs[:])
        Vdup = fw.sbuf("Vdup", [128, NT, 4, 128], BF16)
        fw.dma(Vdup[:], VDs[:])
        yaccT = fw.sbuf("yaccT", [128, 2, T])
        fw.memset(yaccT[:, 0, :], 0.0)
        fw.memset(yaccT[:, 1, :], 0.0, eng="gpsimd")
        I4 = fw.sbuf("I4", [128, 2, 128])
        fw.dma(I4[:], consts["I2"][:])
        identb_ = identb
        PB = [fw.psum(f"PB{i}", [128, 512]) for i in range(6)]
        pbi = [0]

        def bank():
            p = PB[pbi[0] % len(PB)]
            pbi[0] += 1
            return p

        def v4(bk, w=128):
            return bk.t[:, 0:4 * w].rearrange("p (h x) -> p h x", x=w)

        evi = [0]

        def evac(out, in_):
            e = "scalar" if evi[0] % 2 == 0 else "vector"
            evi[0] += 1
            fw.copy(out, in_, eng=e)

        def chunk_gen(d):
            rev = d == 1
            maskN = fw.sbuf(f"maskN{d}", [128, 4, 128])
            maskAB = fw.sbuf(f"maskAB{d}", [128, 2, 256])
            fw.dma(maskN[:], consts[f"maskN_{d}"][:])
            fw.dma(maskAB[:], consts[f"maskAB_{d}"][:])
            CH = [[fw.sbuf(f"CH{d}{i}_{tau}", [128, 8, 128], BF16) for tau in range(2)] for i in range(2)]
            BKt = [fw.sbuf(f"BKt{d}{i}", [128, 4, 384], BF16) for i in range(2)]
            for i in range(2):
                fw.memset(BKt[i][:], 0.0)
            X = [fw.sbuf(f"X{d}{i}", [128, 4, 128], BF16) for i in range(2)]
            XT = [fw.sbuf(f"XT{d}{i}", [128, 4, 128], BF16) for i in range(2)]
            Wt = [fw.sbuf(f"Wt{d}{i}", [128, 4, 128], BF16) for i in range(2)]
            AB = [[fw.sbuf(f"AB{d}{i}_{tau}", [128, 2, 256], BF16) for tau in range(2)] for i in range(2)]
            AK = [[fw.sbuf(f"AK{d}{i}_{tau}", [128, 2, 256], BF16) for tau in range(2)] for i in range(2)]
            Z = fw.sbuf(f"Zz{d}", [128, 4, 64], BF16)
            Udup = fw.sbuf(f"Udup{d}", [128, 4, 128], BF16)
            ST = [fw.sbuf(f"ST{d}{tau}", [128, 64]) for tau in range(2)]
            STd = [fw.sbuf(f"STd{d}{tau}", [128, 128], BF16) for tau in range(2)]
            tpsb = fw.psum(f"tpsb{d}", [128, 4, 128], BF16)
            for tau in range(2):
                fw.memset(ST[tau][:], 0.0)
                fw.memset(STd[tau][:], 0.0)
            order = chunk_order(rev)

            def load(ci):
                c = order[ci]
                cs = slice(c * 128, (c + 1) * 128)
                for tau in range(2):
                    fw.dma(CH[ci % 2][tau][:], RWS.v(RWS.t[d, tau, :, :, cs].rearrange("a p t -> p a t")))
            load(0)
            yield
            for ci, c in enumerate(order):
                cs = slice(c * 128, (c + 1) * 128)
                is_ctx = c < NTC
                want_out = need_ctx or not is_ctx
                ch = CH[ci % 2]
                bkt = BKt[ci % 2]
                if ci + 1 < len(order):
                    load(ci + 1)
                for tau in range(2):
                    fw.transpose(tpsb[:, 2 * tau, :], ch[tau][:, 6, :], identb_[:])
                    fw.transpose(tpsb[:, 2 * tau + 1, :], ch[tau][:, 7, :], identb_[:])
                bv = bkt.t[:, :, :].rearrange("p a (h x) -> p a h x", x=192)
                fw.copy(bkt.v(bv[:, :, :, 0:64]), tpsb.v(tpsb.t[:, :, :].rearrange("p a (h x) -> p a h x", x=64)),
                        eng="scalar")
                nb = bank()
                for h in range(4):
                    tau, hh = h // 2, h % 2
                    fw.mm(nb.v(v4(nb)[:, h, :]), ch[tau][:, hh, :], ch[tau][:, 4, :])
                x0 = X[0]
                fw.tt(x0[:], nb.v(v4(nb)), maskN[:], ALU.mult)
                yield
                ab, ak = AB[ci % 2], AK[ci % 2]
                for tau in range(2):
                    b2, b3 = bank(), bank()
                    for hh in range(2):
                        for (bk, arr) in ((b2, 4), (b3, 5)):
                            o = bk.t[:, :].rearrange("p (h x) -> p h x", x=256)
                            fw.mm(bk.v(o[:, hh, 0:128]), ch[tau][:, arr, :], ch[tau][:, hh, :])
                            fw.mm(bk.v(o[:, hh, 128:256]), ch[tau][:, arr, :], ch[tau][:, 2 + hh, :])
                    fw.tt(ab[tau][:], b2.v(b2.t[:, :].rearrange("p (h x) -> p h x", x=256)), maskAB[:], ALU.mult)
                    fw.tt(ak[tau][:], b3.v(b3.t[:, :].rearrange("p (h x) -> p h x", x=256)), maskAB[:], ALU.mult)
                    yield
                xt0 = XT[0]
                w0 = Wt[0]
                for tau in range(2):
                    fw.copy(xt0[:, 2 * tau:2 * tau + 2, :], ab[tau][:, :, 0:128], eng="gpsimd")
                    fw.tt(w0[:, 2 * tau:2 * tau + 2, :], ab[tau][:, :, 0:128], I4[:], ALU.add, eng="gpsimd")
                xc, xtc, wc = x0, xt0, w0
                for p in range(6):
                    xn, xtn, wn = X[(p + 1) % 2], XT[(p + 1) % 2], Wt[(p + 1) % 2]
                    bx = bank()
                    for h in range(4):
                        fw.mm(bx.v(v4(bx)[:, h, :]), xtc[:, h, :], xc[:, h, :])
                    if p < 5:
                        bxt = bank()
                        for h in range(4):
                            fw.mm(bxt.v(v4(bxt)[:, h, :]), xc[:, h, :], xtc[:, h, :])
                    evac(xn[:], bx.v(v4(bx)))
                    if p < 5:
                        evac(xtn[:], bxt.v(v4(bxt)))
                    yield
                    bw = bank()
                    for h in range(4):
                        fw.mm(bw.v(v4(bw)[:, h, :]), identb_[:], wc[:, h, :], start=True, stop=False)
                        fw.mm(bw.v(v4(bw)[:, h, :]), xn[:, h, :], wc[:, h, :], start=False, stop=True)
                    evac(wn[:], bw.v(v4(bw)))
                    xc, xtc, wc = xn, xtn, wn
                    yield
                wT = wc
                gb = bank()
                g4 = gb.t[:, 0:256].rearrange("p (h x) -> p h x", x=64)
                for h in range(4):
                    tau, hh = h // 2, h % 2
                    fw.mm(gb.v(g4[:, h, :]), ch[tau][:, hh, :], STd[tau][:, 0:64], start=True, stop=False)
                    fw.mm(gb.v(g4[:, h, :]), ak[tau][:, hh, 0:128], Vdup[:, c, h, 0:64], start=False, stop=True)
                fw.copy(Z[:], gb.v(g4), eng="scalar")
                yield
                ub = bank()
                u4 = ub.t[:, 0:256].rearrange("p (h x) -> p h x", x=64)
                for h in range(4):
                    fw.mm(ub.v(u4[:, h, :]), wT[:, h, :], Z[:, h, :])
                fw.copy(Udup[:, :, 0:64], ub.v(u4), eng="vector")
                fw.copy(Udup[:, :, 64:128], ub.v(u4), eng="scalar")
                yield
                if want_out:
                    yb = bank()
                    for h in range(4):
                        tau, hh = h // 2, h % 2
                        o = yb.v(v4(yb)[:, h, :])
                        fw.mm(o, STd[tau][:], ch[tau][:, 2 + hh, :], start=True, stop=False)
                        fw.mm(o, Udup[:, h, :], ab[tau][:, hh, 128:256], start=False, stop=False)
                        fw.mm(o, Vdup[:, c, h, :], ak[tau][:, hh, 128:256], start=False, stop=True)
                    o4 = yb.t[:, :].rearrange("p (a g t) -> p a g t", g=2, t=128)
                    for g in range(2):
                        dst = yaccT[64 * g:64 * g + 64, :, cs]
                        src = yb.v(o4[64 * g:64 * g + 64, :, g, :])
                        fw.tt(dst, src, dst, ALU.add, eng="vector")
                for tau in range(2):
                    sb = bank()
                    for hh in range(2):
                        h = 2 * tau + hh
                        fw.mm(sb[:, 0:64], bkt[:, 2 * tau, hh * 128:(hh + 1) * 128], Udup[:, h, 0:64],
                              start=(hh == 0), stop=False)
                        fw.mm(sb[:, 0:64], bkt[:, 2 * tau + 1, hh * 128:(hh + 1) * 128], Vdup[:, c, h, 0:64],
                              start=False, stop=(hh == 1))
                    fw.stt(ST[tau][:], ST[tau][:], pend[:, d, tau, c:c + 1], sb[:, 0:64], ALU.mult, ALU.add)
                    fw.copy(STd[tau][:, 0:64], ST[tau][:], eng="scalar")
                    fw.copy(STd[tau][:, 64:128], ST[tau][:], eng="gpsimd")
                yield

        gens = [chunk_gen(0), chunk_gen(1)]
        alive = [True, True]
        while any(alive):
            for gi, g in enumerate(gens):
                if alive[gi]:
                    try:
                        next(g)
                    except StopIteration:
                        alive[gi] = False
        epsln = fw.sbuf("epsln", [128, 1])
        fw.memset(epsln[:], 64e-5)
        tb = [fw.sbuf(f"r3_{i}", [128, 512]) for i in range(5)]
        ob = [fw.sbuf(f"rob{i}", [128, 512], BF16) for i in range(2)]
        oi = 0
        qblocks = (ctx_blocks if need_ctx else []) + lat_blocks
        for tau in range(2):
            for (s, n, is_ctx) in qblocks:
                yc, sq, rs, bo, ga = tb
                fw.dma(bo[:, :n], BON[tau * 128:(tau + 1) * 128, s:s + n])
                fw.dma(ga[:, :n], GATE[tau * 128:(tau + 1) * 128, s:s + n])
                mb = bank()
                fw.mm(mb[:, :n], blk64[:], yaccT[:, tau, s:s + n])
                fw.stt(yc[:, :n], mb[:, :n], -1.0 / 64, yaccT[:, tau, s:s + n], ALU.mult, ALU.add)
                fw.act(sq[:, :n], yc[:, :n], AF.Square)
                vb = bank()
                fw.mm(vb[:, :n], blk64[:], sq[:, :n])
                fw.act(rs[:, :n], vb[:, :n], AF.Sqrt, bias=epsln[:, 0:1], scale=1.0 / 64)
                fw.recip(rs[:, :n], rs[:, :n])
                fw.stt(yc[:, :n], yc[:, :n], pcol(l, "rw_ln_g", tau), rs[:, :n], ALU.mult, ALU.mult)
                fw.stt(yc[:, :n], yc[:, :n], pcol(l, "rw_ln_b", tau), bo[:, :n], ALU.add, ALU.add, eng="gpsimd")
                o = ob[oi % 2]; oi += 1
                fw.tt(o[:, :n], yc[:, :n], ga[:, :n], ALU.mult, eng="gpsimd")
                fw.dma(OT[tau * 128:(tau + 1) * 128, s:s + n], o[:, :n])
        fw.pop()

    def mixers_0(l, b, need_ctx):
        if "rwkv" in cfg.mix:
            rwkv_phase(l, b, need_ctx)
        if "gla" in cfg.mix:
            gla_phase(l, b, need_ctx)
        if "gqa" in cfg.mix:
            gqa_phase(l, b, need_ctx)
        if "da" in cfg.mix:
            da_phase(l, b, need_ctx)

    def mixers(l, b, need_ctx):
        if "rwkv" in cfg.mix:
            rwkv_phase(l, b, need_ctx)
        if "da" in cfg.mix:
            da_phase(l, b, need_ctx)
        if "gla" in cfg.mix:
            gla_phase(l, b, need_ctx)
        if "gqa" in cfg.mix:
            gqa_phase(l, b, need_ctx)

    for b in range(NB):
        fw.push()
        xT = fw.sbuf("xT_s", [128, KT, T])
        xv = xT_d.t[b].rearrange("(k p) t -> p k t", p=128)
        for k in range(KT):
            fw.dma(xT[:, k, :], xT_d.v(xv[:, k, :]))
        for l in range(L):
            need_ctx = l < L - 1
            fw.push()
            hT = fw.sbuf("hT", [128, KT, T], BF16)
            sq = fw.sbuf("sq", [128, 512])
            rstd = fw.sbuf("rstd", [128, 512])
            nps = fw.psum("nps", [128, 512])
            norm_phase(xT, hT, l, 0, b, sq, rstd, nps)
            wt = [fw.sbuf(f"wt{i}", [128, KT, 256], BF16) for i in range(3)]
            pps = [fw.psum(f"pps{i}", [128, 512]) for i in range(4)]
            stg = [fw.sbuf(f"stg{i}", [128, 512]) for i in range(4)]
            wv = w_in.t[l].rearrange("(k p) c -> p k c", p=128)
            ei = 0
            tiles_ = mixer_cols()
            groups_ = [tiles_[i:i + 2] for i in range(0, len(tiles_), 2)]
            for gi, grp in enumerate(groups_):
                g0 = grp[0][0]
                gn = sum(nc_ for _, nc_ in grp)
                wb = wt[gi % 3]
                fw.dma(wb[:, :, :gn], w_in.v(wv[:, :, g0:g0 + gn]), eng="gpsimd")
                for (c0, ncol) in grp:
                    off = c0 - g0
                    for (s, n, is_ctx) in blocks:
                        ps = pps[ei % 4]
                        st = stg[ei % 4]
                        for k in range(KT):
                            fw.mm(ps[:ncol, :n], wb[:, k, off:off + ncol], hT[:, k, s:s + n],
                                  start=(k == 0), stop=(k == KT - 1))
                        fw.copy(st[:ncol, :n], ps[:ncol, :n], eng=("vector" if ei % 2 == 0 else "scalar"))
                        fw.dma(PT[c0:c0 + ncol, s:s + n], st[:ncol, :n])
                        ei += 1
            fw.pop()
            if cfg.stop == "proj":
                break
            if cfg.stop == "ffn":
                fw.push()
                tb = fw.sbuf("tb", [128, T])
                tbb = fw.sbuf("tbb", [128, T], BF16)
                for k in range(KT):
                    fw.dma(tb[:], PT[k * 128:(k + 1) * 128, :])
                    fw.copy(tbb[:], tb[:])
                    fw.dma(OT[k * 128:(k + 1) * 128, :], tbb[:])
                fw.pop()
            else:
                mixers(l, b, need_ctx)
            if cfg.stop == "mix":
                break
            wout_phase(xT, l, b, need_ctx)
            ffn_phase(xT, l, b, need_ctx)
            if cfg.stop == "ffn":
                break
        yv = yT_d.t[b].rearrange("(k p) t -> p k t", p=128)
        for k in range(KT):
            fw.dma(yT_d.v(yv[:, k, :]), xT[:, k, TC:T])
        fw.pop()
        if cfg.stop is not None:
            break

    if cfg.stop is not None:
        dbg_pt = fw.dram("dbg_PT", [N_IN, T], F32, kind="ExternalOutput")
        fw.dma(dbg_pt[:], PT[:])
        dbg_ot = fw.dram("dbg_OT", [D, T], BF16, kind="ExternalOutput")
        fw.dma(dbg_ot[:], OT[:])
    fw.pop()
    fw.finish()
    return nc


_NC_CACHE = {}


def kernel(**inputs):
    n_cores = 8
    B = inputs["x"].shape[0]
    NB = B // n_cores
    cfg = Cfg(TC=inputs["ctx"].shape[1], TL=inputs["x"].shape[1], NB=NB, depth=DEPTH)
    nc = build(cfg)
    in_maps = [prep_inputs(inputs, cfg, i * NB) for i in range(n_cores)]
    res = run_bass_kernel_spmd(nc, in_maps, core_ids=list(range(n_cores)))
    out = np.empty((B, cfg.TL, D), np.float32)
    for i in range(n_cores):
        yT = np.asarray(res.results[i]["yT"])
        out[i * NB:(i + 1) * NB] = yT.transpose(0, 2, 1)
    return out


def prep_inputs(inp, cfg, b0):
    NB = cfg.NB
    m = {}
    x = np.asarray(inp["x"], np.float32)[b0:b0 + NB]
    ctx = np.asarray(inp["ctx"], np.float32)[b0:b0 + NB]
    xc = np.concatenate([ctx, x], axis=1)
    m["xT"] = np.ascontiguousarray(xc.transpose(0, 2, 1))
    cvec = np.concatenate([np.asarray(inp["c"], np.float32)[b0:b0 + NB],
                           np.asarray(inp["c_ctx"], np.float32)[None]], axis=0)
    m["cT"] = np.ascontiguousarray(cvec.reshape(NB + 1, KT, 128).transpose(2, 1, 0))
    L = cfg.depth
    m["pack"] = np.stack([host_pack(inp, l) for l in range(L)])
    for nm in ("rw_w2", "rw_a2"):
        m[nm] = np.ascontiguousarray(np.asarray(inp[nm], np.float32)[:L].reshape(L, 128, 256))
    m["rw_g2"] = np.ascontiguousarray(np.asarray(inp["rw_g2"], np.float32)[:L])
    m["lamb"] = np.stack([np.tile(np.asarray(inp["da_lam"], np.float32)[l].reshape(1, 128), (128, 1)) for l in range(L)])
    for nm in ("mod_w", "w_in", "w_out", "ffn_w_up", "ffn_w_down", "gla_a2"):
        m[nm] = np.ascontiguousarray(np.asarray(inp[nm], np.float32)[:L])
    for k, v in host_consts(cfg).items():
        m["c_" + k] = v
    return m
```
